# Optimizing a Trainium2 kernel written in Bass

```python
import math
import jax, jax.numpy as jnp
from jax import lax
import numpy as np

D_MODEL = 1024
BATCH = 8
SEQ = 4096
DEPTH = 2

N_MIXERS = 2
D_FF = ((8 * D_MODEL // 3 + 127) // 128) * 128
HALF_STEP = 0.5
RMS_EPS = 1e-6
MASK_VALUE = -1e30

NSA_HEAD_DIM = 64
NSA_HEADS = D_MODEL // NSA_HEAD_DIM
NSA_KV_HEADS = NSA_HEADS // 4
NSA_GROUP = NSA_HEADS // NSA_KV_HEADS
CMP_BLOCK = 32
CMP_STRIDE = 16
CMP_HIDDEN = NSA_HEAD_DIM
SEL_BLOCK = 64
N_SELECT = 16
WINDOW = 512
Q_BLOCK = 32
FORCED_SCORE = 1e4
NSA_IN = NSA_HEADS * NSA_HEAD_DIM + 6 * NSA_KV_HEADS * NSA_HEAD_DIM + 3 * NSA_HEADS

RWKV_HEAD_DIM = 64
RWKV_HEADS = D_MODEL // RWKV_HEAD_DIM
DECAY_LORA = max(32, int(round(1.8 * D_MODEL ** 0.5 / 32)) * 32)
AAA_LORA = max(32, int(round(1.8 * D_MODEL ** 0.5 / 32)) * 32)
GATE_LORA = max(32, int(round(0.6 * D_MODEL ** 0.8 / 32)) * 32)
GN_EPS = 64e-5
RWKV_IN = 3 * D_MODEL + DECAY_LORA + AAA_LORA + GATE_LORA

kernel_name = 'hybrid_nsa_rwkv7_macaron_sandwich'


def rms_norm(x, g):
    xf = x.astype(jnp.float32)
    y = xf * lax.rsqrt(jnp.mean(xf * xf, axis=-1, keepdims=True) + RMS_EPS)
    return (y * g.astype(jnp.float32)).astype(x.dtype)


def swiglu(x, w_gu, w_down):
    gate, up = jnp.split(x @ w_gu, 2, axis=-1)
    return (jax.nn.silu(gate) * up) @ w_down


def alibi_slopes(n_heads):
    return 2.0 ** (-8.0 * jnp.arange(1, n_heads + 1, dtype=jnp.float32) / n_heads)


def masked_softmax(scores, mask):
    p = jax.nn.softmax(jnp.where(mask, scores, MASK_VALUE), axis=-1)
    return jnp.where(mask, p, 0.0)


def nsa_mixer(u, w_in, pe_k, w_ck1, w_ck2, pe_v, w_cv1, w_cv2, w_out):
    B, S, _ = u.shape
    H, HK, G, HD = NSA_HEADS, NSA_KV_HEADS, NSA_GROUP, NSA_HEAD_DIM
    KV = HK * HD
    f32 = jnp.float32
    proj = u @ w_in
    cuts = [H * HD + i * KV for i in range(7)]
    q, kc, vc, ks, vs, kw, vw, gl = jnp.split(proj, cuts, axis=-1)
    q = (q * HD ** -0.5).reshape(B, S, HK, G, HD)
    kc, vc, ks, vs, kw, vw = [t.reshape(B, S, HK, HD) for t in (kc, vc, ks, vs, kw, vw)]
    gates = jax.nn.sigmoid(gl.astype(f32)).astype(u.dtype).reshape(B, S, HK, G, 3)

    n_cmp = (S - CMP_BLOCK) // CMP_STRIDE + 1
    cmp_idx = np.arange(n_cmp)[:, None] * CMP_STRIDE + np.arange(CMP_BLOCK)[None, :]

    def compress(t, pe, w1, w2):
        blocks = t[:, cmp_idx] + pe[None, None, :, None, :]
        hid = jax.nn.silu(jnp.einsum('bnlgd,ldc->bngc', blocks, w1))
        return jnp.einsum('bngc,cd->bngd', hid, w2)

    k_cmp = compress(kc, pe_k, w_ck1, w_ck2)
    v_cmp = compress(vc, pe_v, w_cv1, w_cv2)
    cmp_end = jnp.asarray(cmp_idx[:, -1], jnp.int32)

    n_sel = S // SEL_BLOCK
    k_top = min(N_SELECT, n_sel)
    cs = np.arange(n_cmp) * CMP_STRIDE
    bs = np.arange(n_sel) * SEL_BLOCK
    overlap = np.clip(np.minimum(cs[:, None] + CMP_BLOCK, bs[None, :] + SEL_BLOCK)
                      - np.maximum(cs[:, None], bs[None, :]), 0, None)
    cmp_to_sel = jnp.asarray(overlap / CMP_BLOCK, jnp.float32)
    k_blocks = ks.reshape(B, n_sel, SEL_BLOCK, HK, HD).transpose(0, 3, 1, 2, 4)
    v_blocks = vs.reshape(B, n_sel, SEL_BLOCK, HK, HD).transpose(0, 3, 1, 2, 4)
    gather_blocks = jax.vmap(jax.vmap(lambda blk, idx: blk[idx]))
    sel_ids = jnp.arange(n_sel)

    kw_pad = jnp.pad(kw, ((0, 0), (WINDOW, 0), (0, 0), (0, 0)))
    vw_pad = jnp.pad(vw, ((0, 0), (WINDOW, 0), (0, 0), (0, 0)))

    slopes = alibi_slopes(H).reshape(HK, G)[None, :, :, None, None]

    def query_block(qi):
        t0 = qi * Q_BLOCK
        tpos = t0 + jnp.arange(Q_BLOCK)
        qb = lax.dynamic_slice_in_dim(q, t0, Q_BLOCK, axis=1)
        gb = lax.dynamic_slice_in_dim(gates, t0, Q_BLOCK, axis=1)

        dist_c = tpos[:, None] - cmp_end[None, :]
        s_c = jnp.einsum('bqgrd,bngd->bgrqn', qb, k_cmp).astype(f32) - slopes * dist_c.astype(f32)
        p_c = masked_softmax(s_c, dist_c >= 0)
        o_c = jnp.einsum('bgrqn,bngd->bqgrd', p_c.astype(u.dtype), v_cmp)

        imp = jnp.einsum('bgrqn,ns->bgqs', p_c, cmp_to_sel)
        cur = (tpos // SEL_BLOCK)[:, None]
        forced = (sel_ids == 0) | (sel_ids == cur) | (sel_ids == cur - 1)
        visible = sel_ids * SEL_BLOCK <= tpos[:, None]
        imp = jnp.where(forced, FORCED_SCORE, jnp.where(visible, imp, -1.0))
        _, top = lax.top_k(imp, k_top)
        M = k_top * SEL_BLOCK
        k_g = gather_blocks(k_blocks, top).reshape(B, HK, Q_BLOCK, M, HD)
        v_g = gather_blocks(v_blocks, top).reshape(B, HK, Q_BLOCK, M, HD)
        spos = (top[..., None] * SEL_BLOCK + jnp.arange(SEL_BLOCK)).reshape(B, HK, Q_BLOCK, M)
        dist_s = (tpos[:, None] - spos)[:, :, None]
        s_s = jnp.einsum('bqgrd,bgqmd->bgrqm', qb, k_g).astype(f32) - slopes * dist_s.astype(f32)
        p_s = masked_softmax(s_s, dist_s >= 0)
        o_s = jnp.einsum('bgrqm,bgqmd->bqgrd', p_s.astype(u.dtype), v_g)

        kwb = lax.dynamic_slice_in_dim(kw_pad, t0, Q_BLOCK + WINDOW, axis=1)
        vwb = lax.dynamic_slice_in_dim(vw_pad, t0, Q_BLOCK + WINDOW, axis=1)
        wpos = t0 - WINDOW + jnp.arange(Q_BLOCK + WINDOW)
        dist_w = tpos[:, None] - wpos[None, :]
        mask_w = (dist_w >= 0) & (dist_w < WINDOW) & (wpos >= 0)[None, :]
        s_w = jnp.einsum('bqgrd,bkgd->bgrqk', qb, kwb).astype(f32) - slopes * dist_w.astype(f32)
        p_w = masked_softmax(s_w, mask_w)
        o_w = jnp.einsum('bgrqk,bkgd->bqgrd', p_w.astype(u.dtype), vwb)

        return gb[..., 0:1] * o_c + gb[..., 1:2] * o_s + gb[..., 2:3] * o_w

    out = lax.map(query_block, jnp.arange(S // Q_BLOCK))
    out = jnp.moveaxis(out, 0, 1).reshape(B, S, H * HD)
    return out @ w_out


def rwkv7_mixer(u, mu, w_in, w0, w_w2, a0, w_a2, w_g2, k_k, k_a, r_k, gn_w, gn_b, w_out):
    B, S, D = u.shape
    H, N = RWKV_HEADS, RWKV_HEAD_DIM
    f32 = jnp.float32
    xx = jnp.pad(u, ((0, 0), (1, 0), (0, 0)))[:, :-1] - u
    offs = np.cumsum((0, D, D, D, DECAY_LORA, AAA_LORA, GATE_LORA))

    def proj(i):
        return (u + xx * mu[i]) @ w_in[:, int(offs[i]):int(offs[i + 1])]

    r, k, v = proj(0), proj(1), proj(2)
    w = -jax.nn.softplus(-(w0 + jnp.tanh(proj(3)) @ w_w2).astype(f32)) - 0.5
    a = jax.nn.sigmoid((a0 + proj(4) @ w_a2).astype(f32))
    g = jax.nn.sigmoid(proj(5)) @ w_g2

    heads = lambda t: t.reshape(B, S, H, N)
    kk = heads((k * k_k).astype(f32))
    kk = kk / jnp.maximum(jnp.linalg.norm(kk, axis=-1, keepdims=True), 1e-12)
    k_mod = k.astype(f32) * (1.0 + (a - 1.0) * k_a.astype(f32))
    decay = jnp.exp(-jnp.exp(w))
    r_h, k_h, v_h = heads(r.astype(f32)), heads(k_mod), heads(v.astype(f32))
    a_h, w_h = heads(a), heads(decay)
    tm = lambda t: jnp.moveaxis(t, 1, 0)

    def step(state, inp):
        r_t, w_t, k_t, v_t, kk_t, a_t = inp
        sa = jnp.einsum('bhvk,bhk->bhv', state, -kk_t)
        state = (state * w_t[:, :, None, :] + sa[..., None] * (kk_t * a_t)[:, :, None, :]
                 + v_t[..., None] * k_t[:, :, None, :])
        return state, jnp.einsum('bhvk,bhk->bhv', state, r_t)

    state0 = jnp.zeros((B, H, N, N), f32)
    _, y = lax.scan(step, state0, (tm(r_h), tm(w_h), tm(k_h), tm(v_h), tm(kk), tm(a_h)))
    y = jnp.moveaxis(y, 0, 1)
    mean = jnp.mean(y, axis=-1, keepdims=True)
    var = jnp.mean(jnp.square(y - mean), axis=-1, keepdims=True)
    y = ((y - mean) * lax.rsqrt(var + GN_EPS)).reshape(B, S, D) * gn_w.astype(f32) + gn_b.astype(f32)
    bonus = jnp.sum(r_h * k_h * r_k.astype(f32), axis=-1, keepdims=True) * v_h
    y = (y + bonus.reshape(B, S, D)).astype(u.dtype) * g
    return y @ w_out


def setup_inputs(seed: int = 0) -> dict:
    key = jax.random.key(seed)
    keys = iter(jax.random.split(key, 48))
    f32 = jnp.float32
    L, D, HD = DEPTH, D_MODEL, NSA_HEAD_DIM
    n_a = len(range(0, DEPTH, N_MIXERS))
    n_b = len(range(1, DEPTH, N_MIXERS))

    def dense(shape, fan_in, scale=1.0):
        return scale * fan_in ** -0.5 * jax.random.normal(next(keys), shape, f32)

    def gain(shape):
        return 1.0 + 0.02 * jax.random.normal(next(keys), shape, f32)

    def small(shape, scale):
        return scale * jax.random.normal(next(keys), shape, f32)

    def unif(shape, lo, hi):
        return jax.random.uniform(next(keys), shape, f32, lo, hi)

    return {
        'x': jax.random.normal(next(keys), (BATCH, SEQ, D), f32),
        'ffn1_norm_pre': gain((L, D)),
        'ffn1_w_gu': dense((L, D, 2 * D_FF), D),
        'ffn1_w_down': dense((L, D_FF, D), D_FF),
        'ffn1_norm_post': gain((L, D)),
        'mix_norm_pre': gain((L, D)),
        'nsa_w_in': dense((n_a, D, NSA_IN), D),
        'nsa_pe_k': small((n_a, CMP_BLOCK, HD), 0.1),
        'nsa_w_ck1': dense((n_a, CMP_BLOCK, HD, CMP_HIDDEN), CMP_BLOCK * HD),
        'nsa_w_ck2': dense((n_a, CMP_HIDDEN, HD), CMP_HIDDEN),
        'nsa_pe_v': small((n_a, CMP_BLOCK, HD), 0.1),
        'nsa_w_cv1': dense((n_a, CMP_BLOCK, HD, CMP_HIDDEN), CMP_BLOCK * HD),
        'nsa_w_cv2': dense((n_a, CMP_HIDDEN, HD), CMP_HIDDEN),
        'nsa_w_out': dense((n_a, D, D), D),
        'rwkv_mu': unif((n_b, 6, D), 0.0, 1.0),
        'rwkv_w_in': dense((n_b, D, RWKV_IN), D),
        'rwkv_w0': unif((n_b, D), -6.0, 0.0),
        'rwkv_w_w2': dense((n_b, DECAY_LORA, D), DECAY_LORA, 0.1),
        'rwkv_a0': small((n_b, D), 0.1),
        'rwkv_w_a2': dense((n_b, AAA_LORA, D), AAA_LORA, 0.1),
        'rwkv_w_g2': dense((n_b, GATE_LORA, D), GATE_LORA),
        'rwkv_k_k': 0.85 + small((n_b, D), 0.02),
        'rwkv_k_a': gain((n_b, D)),
        'rwkv_r_k': small((n_b, RWKV_HEADS, RWKV_HEAD_DIM), 0.3),
        'rwkv_gn_w': gain((n_b, D)),
        'rwkv_gn_b': small((n_b, D), 0.02),
        'rwkv_w_out': dense((n_b, D, D), D),
        'mix_norm_post': gain((L, D)),
        'ffn2_norm_pre': gain((L, D)),
        'ffn2_w_gu': dense((L, D, 2 * D_FF), D),
        'ffn2_w_down': dense((L, D_FF, D), D_FF),
        'ffn2_norm_post': gain((L, D)),
    }


def reference(x, ffn1_norm_pre, ffn1_w_gu, ffn1_w_down, ffn1_norm_post, mix_norm_pre,
              nsa_w_in, nsa_pe_k, nsa_w_ck1, nsa_w_ck2, nsa_pe_v, nsa_w_cv1, nsa_w_cv2, nsa_w_out,
              rwkv_mu, rwkv_w_in, rwkv_w0, rwkv_w_w2, rwkv_a0, rwkv_w_a2, rwkv_w_g2, rwkv_k_k,
              rwkv_k_a, rwkv_r_k, rwkv_gn_w, rwkv_gn_b, rwkv_w_out, mix_norm_post,
              ffn2_norm_pre, ffn2_w_gu, ffn2_w_down, ffn2_norm_post):
    h = x
    for i in range(DEPTH):
        f = swiglu(rms_norm(h, ffn1_norm_pre[i]), ffn1_w_gu[i], ffn1_w_down[i])
        h = h + HALF_STEP * rms_norm(f, ffn1_norm_post[i])
        u = rms_norm(h, mix_norm_pre[i])
        j = i // N_MIXERS
        if i % N_MIXERS == 0:
            m = nsa_mixer(u, nsa_w_in[j], nsa_pe_k[j], nsa_w_ck1[j], nsa_w_ck2[j],
                          nsa_pe_v[j], nsa_w_cv1[j], nsa_w_cv2[j], nsa_w_out[j])
        else:
            m = rwkv7_mixer(u, rwkv_mu[j], rwkv_w_in[j], rwkv_w0[j], rwkv_w_w2[j], rwkv_a0[j],
                            rwkv_w_a2[j], rwkv_w_g2[j], rwkv_k_k[j], rwkv_k_a[j], rwkv_r_k[j],
                            rwkv_gn_w[j], rwkv_gn_b[j], rwkv_w_out[j])
        h = h + rms_norm(m, mix_norm_post[i])
        f = swiglu(rms_norm(h, ffn2_norm_pre[i]), ffn2_w_gu[i], ffn2_w_down[i])
        h = h + HALF_STEP * rms_norm(f, ffn2_norm_post[i])
    return h
```

```python
import contextlib
import numpy as np
import concourse.bass as bass
import concourse.mybir as mybir
from concourse.bass_utils import run_bass_kernel_spmd

F32 = mybir.dt.float32
BF16 = mybir.dt.bfloat16
AF = mybir.ActivationFunctionType
ALU = mybir.AluOpType
AX = mybir.AxisListType

S_LEN = 4096
D = 1024
DFF = 2816
NCORES = 8
RMS_EPS = 1e-6


class Dep:
    __slots__ = ("w", "r")

    def __init__(self):
        self.w = None
        self.r = {}


class Sched:
    EPOCH = 16000
    NDMA = 24

    def __init__(self, nc, es):
        self.nc = nc
        self.es = es
        self.eng = {"pe": nc.tensor, "act": nc.scalar, "dve": nc.vector,
                    "pool": nc.gpsimd, "sp": nc.sync}
        self.nsem = 0
        self.sem = {e: self._newsem(e) for e in self.eng}
        self.cnt = {e: 0 for e in self.eng}
        self.waited = {e: {} for e in self.eng}
        self.dsem = {"sp": [self._newsem("dsp") for _ in range(16)],
                     "pool": [self._newsem("dpl") for _ in range(8)]}
        self.dcnt = {q: [0] * len(v) for q, v in self.dsem.items()}
        self.dnext = {q: 0 for q in self.dsem}
        self.ninst = 0
        self.nwait = 0
        self.pe_self_sync = False

    def _newsem(self, tag):
        self.nsem += 1
        return self.es.enter_context(self.nc.semaphore(f"s_{tag}_{self.nsem}"))

    def _wait(self, e, toks):
        best = {}
        for (s, v, src) in toks:
            if src == e and e == "pe" and not self.pe_self_sync:
                continue
            k = id(s)
            if k not in best or best[k][1] < v:
                best[k] = (s, v)
        w = self.waited[e]
        for k, (s, v) in best.items():
            if w.get(k, 0) >= v:
                continue
            self.eng[e].wait_ge(s, v)
            self.nwait += 1
            w[k] = v

    def _collect(self, reads, writes):
        toks = []
        for d in reads:
            if d.w is not None:
                toks.append(d.w)
        for d in writes:
            if d.w is not None:
                toks.append(d.w)
            toks.extend(d.r.values())
        return toks

    def _update(self, tok, reads, writes):
        k = id(tok[0])
        for d in reads:
            old = d.r.get(k)
            if old is None or old[1] < tok[1]:
                d.r[k] = tok
        for d in writes:
            d.w = tok
            d.r = {}

    def op(self, e, fn, reads=(), writes=()):
        toks = self._collect(reads, writes)
        self._wait(e, toks)
        ins = fn(self.eng[e])
        if self.cnt[e] >= self.EPOCH:
            self.sem[e] = self._newsem(e)
            self.cnt[e] = 0
        self.cnt[e] += 1
        ins.then_inc(self.sem[e], 1)
        self.ninst += 1
        tok = (self.sem[e], self.cnt[e], e)
        self._update(tok, reads, writes)
        return tok

    def dma(self, q, out, in_, reads=(), writes=(), **kw):
        toks = self._collect(reads, writes)
        dsem, dcnt = self.dsem[q], self.dcnt[q]
        k = self.dnext[q]
        self.dnext[q] = (k + 1) % len(dsem)
        if dcnt[k] >= self.EPOCH:
            toks.append((dsem[k], dcnt[k], None))
            self._wait(q, toks)
            toks = []
            dsem[k] = self._newsem("d" + q)
            dcnt[k] = 0
        if dcnt[k] > 0:
            toks.append((dsem[k], dcnt[k], None))
        self._wait(q, toks)
        ins = self.eng[q].dma_start(out=out, in_=in_, **kw)
        dcnt[k] += 16
        ins.then_inc(dsem[k], 16)
        self.ninst += 1
        tok = (dsem[k], dcnt[k], None)
        self._update(tok, reads, writes)
        return tok

    def _all_dma_toks(self):
        return [(self.dsem[q][k], self.dcnt[q][k], None) for q in self.dsem
                for k in range(len(self.dsem[q])) if self.dcnt[q][k] > 0]

    def barrier(self):
        toks = [(self.sem[e], self.cnt[e], None) for e in self.eng if self.cnt[e] > 0]
        toks += self._all_dma_toks()
        for e in self.eng:
            self._wait(e, toks)

    def finish(self):
        toks = self._all_dma_toks()
        toks += [(self.sem[e], self.cnt[e], None) for e in self.eng if self.cnt[e] > 0]
        self._wait("sp", toks)


class Buf:
    def __init__(self, t):
        self.t = t
        self.deps = {}

    def d(self, key=0):
        dd = self.deps.get(key)
        if dd is None:
            dd = self.deps[key] = Dep()
        return dd

    def __getitem__(self, idx):
        return self.t[idx]


def sb(S, es, name, shape, dt):
    return Buf(es.enter_context(S.nc.sbuf_tensor(name, shape, dt)))


def ps(S, es, name, shape, dt):
    return Buf(es.enter_context(S.nc.psum_tensor(name, shape, dt)))


def ffn_phase(S, h_in, h_out, w_gu, w_down, gpre_l, gpost_b, ident_d, ntiles=16, tag="f"):
    nc = S.nc
    T = 256
    NS = T // 128
    with contextlib.ExitStack() as es:
        wgu = sb(S, es, tag + "wgu", [128, 8, 2 * DFF], BF16)
        wdn = sb(S, es, tag + "wdn", [128, 22, D], BF16)
        gpre = sb(S, es, tag + "gpre", [128, 8], F32)
        gpost = sb(S, es, tag + "gpost", [128, D], F32)
        ident = sb(S, es, tag + "ident", [128, 128], BF16)
        xb = [sb(S, es, tag + f"x{i}", [128, NS, D], F32) for i in range(2)]
        xn = [sb(S, es, tag + f"xn{i}", [128, D], BF16) for i in range(2)]
        xnT = [sb(S, es, tag + f"xnT{i}", [128, 8, T], BF16) for i in range(2)]
        hT = sb(S, es, tag + "hT", [128, 22, T], BF16)
        sg = [sb(S, es, tag + f"sg{i}", [128, T], F32) for i in range(3)]
        ob = [sb(S, es, tag + f"ob{i}", [128, D], F32) for i in range(2)]
        tmp = sb(S, es, tag + "tmp", [128, D], F32)
        junk = sb(S, es, tag + "junk", [128, D], BF16)
        st = sb(S, es, tag + "st", [128, 16], F32)
        pT = ps(S, es, tag + "pT", [128, 8, 128], BF16)
        pGU = [ps(S, es, tag + f"pGU{i}", [128, 2, T], F32) for i in range(3)]
        pF = [ps(S, es, tag + f"pF{i}", [128, 512], F32) for i in range(4)]

        S.dma("sp", gpre[:], gpre_l, writes=[gpre.d()])
        S.dma("sp", gpost[:], gpost_b, writes=[gpost.d()])
        S.dma("sp", ident[:], ident_d, writes=[ident.d()])
        S.op("dve", lambda e: e.tensor_scalar(out=gpost[:], in0=gpost[:], scalar1=0.5, scalar2=None,
                                              op0=ALU.mult), reads=[gpost.d()], writes=[gpost.d()])

        slots = [(xb[i].d(("stg", s_)), xb[i][:, s_, :]) for i in range(2) for s_ in range(NS)]
        HW = 1024
        k = 0

        def conv(dst, view, dep, scale=None):
            nonlocal k
            if k % 2 == 0:
                if scale is None:
                    S.op("act", lambda e: e.activation(out=dst, in_=view, func=AF.Copy), reads=[dep])
                else:
                    S.op("act", lambda e: e.activation(out=dst, in_=view, func=AF.Copy, scale=scale),
                         reads=[dep, gpre.d()])
            else:
                if scale is None:
                    S.op("dve", lambda e: e.tensor_copy(out=dst, in_=view), reads=[dep])
                else:
                    S.op("dve", lambda e: e.tensor_scalar(out=dst, in0=view, scalar1=scale, scalar2=None,
                                                          op0=ALU.mult), reads=[dep, gpre.d()])
            k += 1

        for c in range(8):
            for o in range(0, 2 * DFF, HW):
                wdt = min(HW, 2 * DFF - o)
                dep, sv = slots[k % len(slots)]
                S.dma("sp", sv[:, 0:wdt], w_gu[c * 128:(c + 1) * 128, o:o + wdt], writes=[dep])
                conv(wgu[:, c, o:o + wdt], sv[:, 0:wdt], dep, scale=gpre[:, c:c + 1])
        for j in range(22):
            dep, sv = slots[k % len(slots)]
            S.dma("sp", sv[:, :], w_down[j * 128:(j + 1) * 128, :], writes=[dep])
            conv(wdn[:, j, :], sv[:, :], dep)
        S.barrier()

        hv_in = h_in.rearrange("(t s p) d -> t p s d", p=128, s=NS)
        hv_out = h_out.rearrange("(t s p) d -> t s p d", p=128, s=NS)

        def load(t):
            S.dma("sp", xb[t % 2][:, :, :], hv_in[t], writes=[xb[t % 2].d()])

        def prenorm(t):
            x = xb[t % 2]
            for s in range(NS):
                xnb = xn[s % 2]
                S.op("act", lambda e, x=x, s=s: e.activation(
                    out=junk[:], in_=x[:, s, :], func=AF.Square, accum_out=st[:, s:s + 1]),
                    reads=[x.d()], writes=[junk.d(), st.d(s)])
                S.op("dve", lambda e, s=s: e.tensor_scalar(
                    out=st[:, 4 + s:5 + s], in0=st[:, s:s + 1], scalar1=1.0 / D, scalar2=RMS_EPS,
                    op0=ALU.mult, op1=ALU.add), reads=[st.d(s)], writes=[st.d(4 + s)])
                S.op("act", lambda e, s=s: e.activation(
                    out=st[:, 4 + s:5 + s], in_=st[:, 4 + s:5 + s], func=AF.Sqrt),
                    reads=[st.d(4 + s)], writes=[st.d(4 + s)])
                S.op("dve", lambda e, s=s: e.reciprocal(
                    out=st[:, 4 + s:5 + s], in_=st[:, 4 + s:5 + s]),
                    reads=[st.d(4 + s)], writes=[st.d(4 + s)])
                S.op("act", lambda e, x=x, s=s, xnb=xnb: e.activation(
                    out=xnb[:], in_=x[:, s, :], func=AF.Copy, scale=st[:, 4 + s:5 + s]),
                    reads=[x.d(), st.d(4 + s)], writes=[xnb.d()])
                for c in range(8):
                    S.op("pe", lambda e, c=c, xnb=xnb: e.transpose(
                        out=pT[:, c, :], in_=xnb[:, c * 128:(c + 1) * 128], identity=ident[:]),
                        reads=[xnb.d(), ident.d()], writes=[pT.d()])
                S.op("dve", lambda e, t=t, s=s: e.tensor_copy(
                    out=xnT[t % 2][:, :, s * 128:(s + 1) * 128], in_=pT[:, :, :]),
                    reads=[pT.d()], writes=[xnT[t % 2].d()])

        def gu(t):
            xT = xnT[t % 2]
            for j in range(22):
                pg = pGU[j % 3]
                for half in range(2):
                    col = half * DFF + j * 128
                    for c in range(8):
                        S.op("pe", lambda e, c=c, col=col, half=half, pg=pg: e.matmul(
                            pg[:, half, :], wgu[:, c, col:col + 128], xT[:, c, :],
                            start=(c == 0), stop=(c == 7)),
                            reads=[wgu.d(), xT.d()], writes=[pg.d()])
                sgb = sg[j % 3]
                S.op("act", lambda e, pg=pg, sgb=sgb: e.activation(
                    out=sgb[:], in_=pg[:, 0, :], func=AF.Silu), reads=[pg.d()], writes=[sgb.d()])
                S.op("dve", lambda e, pg=pg, sgb=sgb, j=j: e.tensor_tensor(
                    out=hT[:, j, :], in0=pg[:, 1, :], in1=sgb[:], op=ALU.mult),
                    reads=[pg.d(), sgb.d()], writes=[hT.d()])

        def down(t):
            x = xb[t % 2]
            for s in range(NS):
                pf = [pF[(s % 2) * 2], pF[(s % 2) * 2 + 1]]
                for half in range(2):
                    for j in range(22):
                        S.op("pe", lambda e, j=j, s=s, half=half, pf=pf: e.matmul(
                            pf[half][:, :], hT[:, j, s * 128:(s + 1) * 128],
                            wdn[:, j, half * 512:(half + 1) * 512], start=(j == 0), stop=(j == 21)),
                            reads=[hT.d(), wdn.d()], writes=[pf[half].d()])
                for half in range(2):
                    S.op("act", lambda e, half=half, pf=pf, s=s: e.activation(
                        out=junk[:, 0:512], in_=pf[half][:, :], func=AF.Square,
                        accum_out=st[:, 8 + 2 * s + half:9 + 2 * s + half]),
                        reads=[pf[half].d()], writes=[junk.d(), st.d(8 + 2 * s + half)])
                S.op("dve", lambda e, s=s: e.tensor_tensor(
                    out=st[:, 12 + s:13 + s], in0=st[:, 8 + 2 * s:9 + 2 * s],
                    in1=st[:, 9 + 2 * s:10 + 2 * s], op=ALU.add),
                    reads=[st.d(8 + 2 * s), st.d(9 + 2 * s)], writes=[st.d(12 + s)])
                S.op("dve", lambda e, s=s: e.tensor_scalar(
                    out=st[:, 12 + s:13 + s], in0=st[:, 12 + s:13 + s], scalar1=1.0 / D, scalar2=RMS_EPS,
                    op0=ALU.mult, op1=ALU.add), reads=[st.d(12 + s)], writes=[st.d(12 + s)])
                S.op("act", lambda e, s=s: e.activation(
                    out=st[:, 12 + s:13 + s], in_=st[:, 12 + s:13 + s], func=AF.Sqrt),
                    reads=[st.d(12 + s)], writes=[st.d(12 + s)])
                S.op("dve", lambda e, s=s: e.reciprocal(
                    out=st[:, 12 + s:13 + s], in_=st[:, 12 + s:13 + s]),
                    reads=[st.d(12 + s)], writes=[st.d(12 + s)])
                for half in range(2):
                    S.op("dve", lambda e, half=half, pf=pf: e.tensor_tensor(
                        out=tmp[:, half * 512:(half + 1) * 512], in0=pf[half][:, :],
                        in1=gpost[:, half * 512:(half + 1) * 512], op=ALU.mult),
                        reads=[pf[half].d(), gpost.d()], writes=[tmp.d()])
                o = ob[s % 2]
                S.op("dve", lambda e, s=s, o=o, x=x: e.scalar_tensor_tensor(
                    out=o[:], in0=tmp[:], scalar=st[:, 12 + s:13 + s], in1=x[:, s, :],
                    op0=ALU.mult, op1=ALU.add),
                    reads=[tmp.d(), st.d(12 + s), x.d()], writes=[o.d()])
                S.dma("pool", hv_out[t, s], o[:], reads=[o.d()])

        load(0)
        prenorm(0)
        for t in range(ntiles):
            if t + 1 < ntiles:
                load(t + 1)
            gu(t)
            if t + 1 < ntiles:
                prenorm(t + 1)
            down(t)
        S.barrier()


def _consts():
    import ml_dtypes
    ident = np.eye(128, dtype=np.float32).astype(ml_dtypes.bfloat16)
    return {"ident": ident}


class Ring:
    def __init__(self, views):
        self.views = views
        self.i = 0

    def get(self):
        v = self.views[self.i]
        self.i = (self.i + 1) % len(self.views)
        return v


HP_W0, HP_A0, HP_KK, HP_KA, HP_RK, HP_GNW, HP_GNB = range(7)
GN_EPS = 64e-5


def rwkv_phase(S, h_in, h_out, P, C, yT_dram, ndc=32, tag="r", stage=9, dbg=None):
    nc = S.nc
    import os
    RWBF = False
    F32R = BF16 if RWBF else mybir.dt.float32r
    PADDED = not RWBF

    def W(n):
        return 256 if PADDED else n
    HO = 0 if PADDED else 128
    with contextlib.ExitStack() as es:
        def SB(name, shape, dt):
            return sb(S, es, tag + name, shape, dt)

        Wb = SB("Wb", [128, 8, 3360], BF16)
        ww2 = SB("ww2", [64, D], BF16)
        wa2 = SB("wa2", [64, D], BF16)
        wg2a = SB("wg2a", [128, D], BF16)
        wg2b = SB("wg2b", [32, D], BF16)
        gpre = SB("gpre", [128, 8], F32)
        mu = SB("mu", [128, 6, 8], F32)
        hp = SB("hp", [64, 7, 16], F32)
        identb = SB("identb", [128, 128], BF16)
        identf = SB("identf", [128, 128], F32)
        identr = SB("identr", [128, 256], F32R)
        ones64 = SB("ones64", [64, 64], F32)
        ones64r = SB("ones64r", [64, 64], F32R)
        mask2 = SB("mask2", [128, 256], F32)
        masksl = SB("masksl", [128, 128], F32)
        scanm = SB("scanm", [64, 512], F32)
        xb = SB("xb", [128, D], F32)
        xn = SB("xn", [128, D], BF16)
        junk = SB("junk", [128, 512], BF16)
        uTx = [SB(f"uTx{i}", [128, 8, 129], BF16) for i in range(2)]
        xx = SB("xx", [128, 8, 128], BF16)
        mixb = [SB(f"mix{i}", [128, 8, 128], BF16) for i in range(4)]
        vtok = SB("vtok", [128, D + 256], F32R)
        th3 = SB("th3", [64, 128], BF16)
        p4b = SB("p4b", [64, 128], BF16)
        s5a = SB("s5a", [128, 128], BF16)
        s5b = SB("s5b", [32, 128], BF16)
        yfin = SB("yfin", [128, 8, 128], BF16)
        st = SB("st", [128, 8], F32)
        Sst = SB("Sst", [64, 20, 64], F32R)
        gamC = SB("gamC", [64, 16], F32)
        Q = {n: SB("q_" + n, [64, 4, 128], F32) for n in ["k", "sig", "a", "cs", "kk", "t1", "eneg", "epos", "gt1"]}
        Q["eexc"] = Q["sig"]
        Q["t1r"] = SB("q_t1r", [64, 4, 128], F32R)
        Q["gt1r"] = SB("q_gt1r", [64, 4, 128], F32R)
        QP = [{n: SB(f"qp{p}_" + n, [64, 4, 128], F32) for n in ["r", "kmod", "vT", "g", "y"]} for p in range(2)]
        ARs = [SB(f"AR{p}", [64, 4, 2, 128], F32R) for p in range(2)]
        BTs = [SB(f"BTb{p}", [64, 6, 128], F32R) for p in range(2)]
        KTs = [SB(f"KTb{p}", [64, 6, 128], F32R) for p in range(2)]
        NH = 4
        PR = [[SB(f"PR{i}_{j}", [128, 256], F32R) for j in range(2)] for i in range(NH)]
        PTb = [[SB(f"PT{i}_{j}", [128, 256], F32R) for j in range(2)] for i in range(NH)]
        MRB = [SB(f"MRB{i}", [128, 256], F32R) for i in range(NH)]
        MKb = [SB(f"MK{i}", [128, 256], F32R) for i in range(NH)]
        AXb = [SB(f"AX{i}", [128, 256], F32R) for i in range(NH)]
        PQb = [SB(f"PQ{i}", [128, 320], F32R) for i in range(NH)]
        BKb = [SB(f"BK{i}", [128, 256], F32R) for i in range(NH)]
        GTb = [SB(f"GT{i}", [64, 64], F32R) for i in range(NH)]
        Hsb = [SB(f"Hs{i}", [64, 64], F32) for i in range(NH)]
        RhT = [SB(f"RhT{i}", [64, 256], F32R) for i in range(NH)]

        banks = [ps(S, es, tag + f"bk{i}", [128, 512], F32) for i in range(7)]
        bankT = ps(S, es, tag + "bkT", [128, 8, 128], BF16)
        ring_proj = Ring([(b[:, :], b.d()) for b in banks[0:1]])
        ringF = Ring([(b, b.d()) for b in banks[1:7]])

        for (t, src) in [(gpre, P["gpre_l"]), (mu, P["mu_l"]), (hp, P["hp"]),
                         (identb, C["identb"]), (identf, C["identf"]), (ones64, C["ones64"]),
                         (mask2, C["mask2"]), (masksl, C["masksl"]), (scanm, C["scanm"])]:
            S.dma("sp", t[:], src, writes=[t.d()])
        S.op("dve", lambda e: e.memset(uTx[1][:, :, 128:129], 0.0), writes=[uTx[1].d()])
        kcnt = [0]
        stg2 = Buf(xb.t)
        stg = [(xb, xb[:, 0:512]), (stg2, xb[:, 512:1024])]

        def conv(dst, view, b, scale=None):
            if scale is not None:
                S.op("act", lambda e: e.activation(out=dst, in_=view, func=AF.Copy, scale=scale),
                     reads=[b.d(), gpre.d()])
            elif kcnt[0] % 2 == 0:
                S.op("act", lambda e: e.activation(out=dst, in_=view, func=AF.Copy), reads=[b.d()])
            else:
                S.op("dve", lambda e: e.tensor_copy(out=dst, in_=view), reads=[b.d()])
            kcnt[0] += 1

        for c in range(8):
            for o in range(0, 3360, 512):
                wdt = min(512, 3360 - o)
                b, bv = stg[kcnt[0] % 2]
                S.dma("sp", bv[:, 0:wdt], P["w_in"][c * 128:(c + 1) * 128, o:o + wdt], writes=[b.d()])
                conv(Wb[:, c, o:o + wdt], bv[:, 0:wdt], b, scale=gpre[:, c:c + 1])
        for (dst, src, n) in [(ww2, P["w_w2"], 64), (wa2, P["w_a2"], 64), (wg2a, P["w_g2"][0:128, :], 128),
                              (wg2b, P["w_g2"][128:160, :], 32)]:
            for o in range(0, D, 512):
                b, bv = stg[kcnt[0] % 2]
                S.dma("sp", bv[0:n, :], src[:, o:o + 512], writes=[b.d()])
                conv(dst[0:n, o:o + 512], bv[0:n, :], b)
        S.barrier()

        S.op("dve", lambda e: e.memset(xb[:, 0:512], 0.0), writes=[xb.d()])

        def zero_r(buf, flat, nparts, width, deps):
            for o in range(0, width, 512):
                w_ = min(512, width - o)
                S.op("pool", lambda e, o=o, w_=w_: e.tensor_copy(out=flat[0:nparts, o:o + w_],
                                                                 in_=xb[0:nparts, 0:w_]),
                     reads=[xb.d()], writes=deps)
        zero_r(Sst, Sst[:].rearrange("p a b -> p (a b)"), 64, 20 * 64, [Sst.d(h) for h in range(16)])
        zero_r(identr, identr[:, 128:256], 128, 128, [identr.d()])
        S.op("dve", lambda e: e.tensor_copy(out=identr[:, 0:128], in_=identf[:]), reads=[identf.d()],
             writes=[identr.d()])
        S.op("dve", lambda e: e.tensor_copy(out=ones64r[:], in_=ones64[:]), reads=[ones64.d()],
             writes=[ones64r.d()])
        zero_r(vtok, vtok[:, :], 128, D + 256, [vtok.d()])
        for p_ in range(2):
            zero_r(BTs[p_], BTs[p_][:].rearrange("p a b -> p (a b)"), 64, 6 * 128, [BTs[p_].d()])
            zero_r(KTs[p_], KTs[p_][:].rearrange("p a b -> p (a b)"), 64, 6 * 128, [KTs[p_].d()])
        for t_ in [b for row in PTb for b in row] + MRB + MKb + AXb + BKb:
            zero_r(t_, t_[:, :], 128, 256, [t_.d()])
        for t_ in PQb:
            zero_r(t_, t_[:, :], 128, 320, [t_.d()])
        for t_ in RhT:
            zero_r(t_, t_[:, :], 64, 256, [t_.d()])
        S.barrier()

        hv_in = h_in.rearrange("(t p) d -> t p d", p=128)
        hv_out = h_out.rearrange("(t p) d -> t p d", p=128)

        def bc(idx, q):
            return hp[:, idx, 4 * q:4 * q + 4].unsqueeze(2).to_broadcast([64, 4, 128])

        def f2(b):
            return b[:].rearrange("p a b -> p (a b)")

        def rstd_ops(src_col, dst_col):
            S.op("dve", lambda e: e.tensor_scalar(out=st[:, dst_col:dst_col + 1], in0=st[:, src_col:src_col + 1],
                                                  scalar1=1.0 / D, scalar2=RMS_EPS, op0=ALU.mult, op1=ALU.add),
                 reads=[st.d(src_col)], writes=[st.d(dst_col)])
            S.op("act", lambda e: e.activation(out=st[:, dst_col:dst_col + 1], in_=st[:, dst_col:dst_col + 1],
                                               func=AF.Sqrt), reads=[st.d(dst_col)], writes=[st.d(dst_col)])
            S.op("dve", lambda e: e.reciprocal(out=st[:, dst_col:dst_col + 1], in_=st[:, dst_col:dst_col + 1]),
                 reads=[st.d(dst_col)], writes=[st.d(dst_col)])

        def mm(out, lhsT, rhs, reads, wdep_, start=True, stop=True):
            S.op("pe", lambda e: e.matmul(out, lhsT, rhs, start=start, stop=stop), reads=reads, writes=[wdep_])

        def tt(eng, out, in0, in1, op, reads, writes):
            S.op(eng, lambda e: e.tensor_tensor(out=out, in0=in0, in1=in1, op=op), reads=reads, writes=writes)

        def actf(out, in_, func, reads, writes, **kw):
            S.op("act", lambda e: e.activation(out=out, in_=in_, func=func, **kw), reads=reads, writes=writes)

        def make_mix(i, m, cur):
            for c in range(8):
                S.op("dve", lambda e, c=c: e.scalar_tensor_tensor(
                    out=m[:, c, :], in0=xx[:, c, :], scalar=mu[:, i, c:c + 1], in1=cur[:, c, 1:129],
                    op0=ALU.mult, op1=ALU.add), reads=[xx.d(), cur.d()], writes=[m.d()])
            return m

        Sf = Sst[:].rearrange("p a b -> p (a b)")
        yT_v = yT_dram.rearrange("(c p) t -> p c t", p=128)

        def front(h, slot, q):
            j = h % 4
            AR, BTb, KTb = ARs[q % 2], BTs[q % 2], KTs[q % 2]
            BTf = BTb[:].rearrange("p a b -> p (a b)")
            ARcat = AR[:, j, :, :].rearrange("p a b -> p (a b)")
            AT = AR[:, j, 0, :]
            RT = AR[:, j, 1, :]
            BTh = BTb[:, j, :]
            KTh = KTb[:, j, :]
            vpad = vtok[:, h * 64:h * 64 + W(64)]
            mrb, mk = MRB[slot], MKb[slot]
            pr0, pr1 = PR[slot]
            pt0, pt1 = PTb[slot]
            ax, pq, bk = AXb[slot], PQb[slot], BKb[slot]
            pb, pd = ringF.get()
            mm(pb[:, 0:256], BTh, ARcat, [BTb.d(), AR.d()], pd)
            tt("dve", pr0[:, 0:128], pb[:, 0:128], mask2[:, 0:128], ALU.mult, [pd], [pr0.d()])
            tt("dve", mrb[:, 128:256], pb[:, 128:256], mask2[:, 128:256], ALU.mult, [pd], [mrb.d()])
            actf(pr0[:, 128:256], identr[:, 0:128], AF.Copy, [identr.d()], [pr0.d()])
            pb, pd = ringF.get()
            mm(pb[:, 0:256], KTh, ARcat, [KTb.d(), AR.d()], pd)
            tt("dve", mk[:, 0:256], pb[:, 0:256], mask2[:], ALU.mult, [pd], [mk.d()])
            yield
            pb, pd = ringF.get()
            mm(pb[:, 0:W(128)], AT, BTf[:, j * 128:j * 128 + W(128)], [BTb.d(), AR.d()], pd)
            tt("dve", pt0[:, 0:128], pb[:, 0:128], masksl[:], ALU.mult, [pd], [pt0.d()])
            pb2, pd2 = ringF.get()
            mm(pb2[:, 0:W(64)], BTh, identr[0:64, 0:W(64)], [BTb.d()], pd2)
            mm(pb2[:, 64:64 + W(64)], KTh, identr[0:64, 0:W(64)], [KTb.d()], pd2)
            actf(bk[:, 0:128], pb2[:, 0:128], AF.Copy, [pd2], [bk.d()])
            yield
            PRc, PTc = pr0, pt0
            for jj in range(1, 7):
                PRn = pr1 if PRc is pr0 else pr0
                PTn = pt1 if PTc is pt0 else pt0
                pb, pd = ringF.get()
                mm(pb[:, 0:256], PTc[:, 0:128], PRc[:, :], [PTc.d(), PRc.d()], pd)
                if jj < 6:
                    actf(PRn[:, 0:128], pb[:, 0:128], AF.Copy, [pd], [PRn.d(), pd])
                tt("dve", PRn[:, 128:256], pb[:, 128:256], PRc[:, 128:256], ALU.add, [pd, PRc.d()], [PRn.d(), pd])
                pb2, pd2 = ringF.get()
                mm(pb2[:, 0:W(128)], PRc[:, 0:128], PTc[:, 0:W(128)], [PTc.d(), PRc.d()], pd2)
                actf(PTn[:, 0:128], pb2[:, 0:128], AF.Copy, [pd2], [PTn.d()])
                if jj == 1:
                    pb3, pd3 = ringF.get()
                    mm(pb3[:, 0:W(64)], AT, identr[0:64, 0:W(64)], [AR.d()], pd3)
                    mm(pb3[:, 64:64 + W(64)], mk[:, 0:128], vpad, [mk.d(), vtok.d()], pd3)
                    actf(ax[:, 0:128], pb3[:, 0:128], AF.Copy, [pd3], [ax.d()])
                yield
                PRc, PTc = PRn, PTn
            Rfin = pr1 if PRc is pr0 else pr0
            pb, pd = ringF.get()
            mm(pb[:, 0:256 - HO], PTc[:, 0:128], PRc[:, HO:256], [PTc.d(), PRc.d()], pd)
            tt("dve", Rfin[:, 0:128], pb[:, 128 - HO:256 - HO], PRc[:, 128:256], ALU.add, [pd, PRc.d()], [Rfin.d()])
            yield
            pb, pd = ringF.get()
            mm(pb[:, 0:W(128)], Rfin[:, 0:128], ax[:, 0:W(128)], [Rfin.d(), ax.d()], pd)
            actf(pq[:, 0:128], pb[:, 0:128], AF.Copy, [pd], [pq.d()])
            yield
            gt, hs, rh = GTb[slot], Hsb[slot], RhT[slot]
            pb, pd = ringF.get()
            mm(pb[0:64, 0:W(64)], pq[:, 0:64], bk[:, 0:W(64)], [pq.d(), bk.d()], pd)
            tt("dve", gt[:], pb[0:64, 0:64], identf[0:64, 0:64], ALU.add, [pd], [gt.d()])
            pb2, pd2 = ringF.get()
            mm(pb2[0:64, 0:W(64)], bk[:, 0:64], pq[:, 64:64 + W(64)], [pq.d(), bk.d()], pd2, True, False)
            mm(pb2[0:64, 0:W(64)], bk[:, 64:128], vpad, [bk.d(), vtok.d()], pd2, False, True)
            S.op("dve", lambda e: e.tensor_scalar(out=hs[:], in0=pb2[0:64, 0:64], scalar1=gamC[:, h:h + 1],
                                                  scalar2=None, op0=ALU.mult),
                 reads=[pd2, gamC.d(q)], writes=[hs.d()])
            pb3, pd3 = ringF.get()
            mm(pb3[0:64, 0:256 - HO], pq[:, 0:64], mrb[:, HO:256], [pq.d(), mrb.d()], pd3)
            tt("dve", rh[:, 128:256], pb3[0:64, 128 - HO:256 - HO], RT, ALU.add, [pd3, AR.d()], [rh.d()])
            yield

        def back(h, slot, q):
            j = h % 4
            y = QP[q % 2]["y"]
            vh = vtok[:, h * 64:(h + 1) * 64]
            mrb, mk, pq, gt, hs, rh = MRB[slot], MKb[slot], PQb[slot], GTb[slot], Hsb[slot], RhT[slot]
            pb, pd = ringF.get()
            mm(pb[0:64, 0:256 - HO], pq[:, 64:128], mrb[:, HO:256], [pq.d(), mrb.d()], pd, True, False)
            mm(pb[0:64, 0:256 - HO], vh, mk[:, HO:256], [vtok.d(), mk.d()], pd, False, False)
            mm(pb[0:64, 0:256 - HO], Sst[:, h, :], rh[:, HO:256], [Sst.d(h), rh.d()], pd, False, True)
            actf(y[:, j, :], pb[0:64, 128 - HO:256 - HO], AF.Copy, [pd], [y.d()])
            pb2, pd2 = ringF.get()
            mm(pb2[0:64, 0:W(64)], gt[:], Sf[:, h * 64:h * 64 + W(64)], [gt.d(), Sst.d(h)], pd2)
            S.op("dve", lambda e: e.scalar_tensor_tensor(
                out=Sst[:, h, :], in0=pb2[0:64, 0:64], scalar=gamC[:, h:h + 1], in1=hs[:],
                op0=ALU.mult, op1=ALU.add), reads=[pd2, gamC.d(q), hs.d()], writes=[Sst.d(h)])

        mixes = {}

        def stageA(dc):
            cur, prv = uTx[dc % 2], uTx[(dc + 1) % 2]
            S.dma("sp", xb[:], hv_in[dc], writes=[xb.d()])
            actf(xn[:], xb[:], AF.Square, [xb.d()], [xn.d(), st.d(0)], accum_out=st[:, 0:1])
            rstd_ops(0, 1)
            actf(xn[:], xb[:], AF.Copy, [xb.d(), st.d(1)], [xn.d()], scale=st[:, 1:2])
            for c in range(8):
                S.op("pe", lambda e, c=c: e.transpose(out=bankT[:, c, :], in_=xn[:, c * 128:(c + 1) * 128],
                                                      identity=identb[:]), reads=[xn.d()], writes=[bankT.d()])
            S.op("dve", lambda e: e.tensor_copy(out=cur[:, :, 1:129], in_=bankT[:, :, :]),
                 reads=[bankT.d()], writes=[cur.d()])
            S.op("dve", lambda e: e.tensor_copy(out=cur[:, :, 0:1], in_=prv[:, :, 128:129]),
                 reads=[prv.d()], writes=[cur.d()])
            tt("dve", xx[:], cur[:, :, 0:128], cur[:, :, 1:129], ALU.subtract, [cur.d()], [xx.d()])
            yield
            m3 = make_mix(3, mixb[3], cur)
            pv_, pd = ring_proj.get()
            for c in range(8):
                mm(pv_[0:64, 0:128], Wb[:, c, 3072:3136], m3[:, c, :], [m3.d()], pd, c == 0, c == 7)
            actf(th3[:], pv_[0:64, 0:128], AF.Tanh, [pd], [th3.d()])
            yield
            m4 = make_mix(4, mixb[3], cur)
            pv_, pd = ring_proj.get()
            for c in range(8):
                mm(pv_[0:64, 0:128], Wb[:, c, 3136:3200], m4[:, c, :], [m4.d()], pd, c == 0, c == 7)
            actf(p4b[:], pv_[0:64, 0:128], AF.Copy, [pd], [p4b.d()])
            yield
            m5 = make_mix(5, mixb[3], cur)
            pv_, pd = ring_proj.get()
            for c in range(8):
                mm(pv_[:, 0:128], Wb[:, c, 3200:3328], m5[:, c, :], [m5.d()], pd, c == 0, c == 7)
            for c in range(8):
                mm(pv_[0:32, 128:256], Wb[:, c, 3328:3360], m5[:, c, :], [m5.d()], pd, c == 0, c == 7)
            actf(s5a[:], pv_[:, 0:128], AF.Sigmoid, [pd], [s5a.d()])
            actf(s5b[:], pv_[0:32, 128:256], AF.Sigmoid, [pd], [s5b.d()])
            yield
            mixes[0] = make_mix(0, mixb[0], cur)
            yield
            mixes[1] = make_mix(1, mixb[1], cur)
            yield
            mixes[2] = make_mix(2, mixb[2], cur)
            yield

        def stageA2(dc):
            m2 = mixes[2]
            for half in range(2):
                bb, bbd = ring_proj.get()
                for c in range(8):
                    mm(bb, m2[:, c, :], Wb[:, c, 2048 + half * 512:2048 + (half + 1) * 512],
                       [m2.d()], bbd, c == 0, c == 7)
                actf(vtok[:, half * 512:(half + 1) * 512], bb, AF.Copy, [bbd], [vtok.d()])
                yield

        def prep(q):
            par = q % 2
            PENG = os.environ.get('RW_PENG', 'dve')
            AR, BTb, KTb, qp = ARs[par], BTs[par], KTs[par], QP[par]

            def evac_pairs(pv2, pd2, dst, eng="act"):
                src = pv2[:, 0:256].rearrange("p (a b) -> p a b", a=2)
                dv = dst[:].rearrange("p (a two) b -> p a two b", two=2)
                for half in range(2):
                    if eng == "act":
                        actf(dv[:, :, half, :], src[64 * half:64 * half + 64], AF.Copy, [pd2], [dst.d()])
                    else:
                        S.op("dve", lambda e, half=half: e.tensor_copy(out=dv[:, :, half, :],
                                                                       in_=src[64 * half:64 * half + 64]),
                             reads=[pd2], writes=[dst.d()])

            def proj4(mbuf, colbase, dst, eng="act"):
                pv2, pd2 = ring_proj.get()
                for pi in range(2):
                    pr = 2 * q + pi
                    for c in range(8):
                        mm(pv2[:, pi * 128:(pi + 1) * 128], Wb[:, c, colbase + pr * 128:colbase + (pr + 1) * 128],
                           mbuf[:, c, :], [mbuf.d()], pd2, c == 0, c == 7)
                evac_pairs(pv2, pd2, dst, eng)

            proj4(mixes[0], 0, qp["r"])
            yield
            proj4(mixes[1], 1024, Q["k"], "dve")
            yield
            proj4(mixes[2], 2048, qp["vT"])
            yield
            pv2, pd2 = ring_proj.get()
            for pi in range(2):
                pr = 2 * q + pi
                mm(pv2[:, pi * 128:(pi + 1) * 128], ww2[:, pr * 128:(pr + 1) * 128], th3[:], [th3.d()], pd2)
            evac_pairs(pv2, pd2, Q["sig"], "dve")
            tt(PENG, Q["sig"][:], Q["sig"][:], bc(HP_W0, q), ALU.add, [Q["sig"].d()], [Q["sig"].d()])
            actf(f2(Q["sig"]), f2(Q["sig"]), AF.Sigmoid, [Q["sig"].d()], [Q["sig"].d()])
            yield
            pv2, pd2 = ring_proj.get()
            for pi in range(2):
                pr = 2 * q + pi
                mm(pv2[:, pi * 128:(pi + 1) * 128], wa2[:, pr * 128:(pr + 1) * 128], p4b[:], [p4b.d()], pd2)
            evac_pairs(pv2, pd2, Q["a"], "dve")
            tt(PENG, Q["a"][:], Q["a"][:], bc(HP_A0, q), ALU.add, [Q["a"].d()], [Q["a"].d()])
            actf(f2(Q["a"]), f2(Q["a"]), AF.Sigmoid, [Q["a"].d()], [Q["a"].d()])
            yield
            pv2, pd2 = ring_proj.get()
            for pi in range(2):
                pr = 2 * q + pi
                mm(pv2[:, pi * 128:(pi + 1) * 128], wg2a[:, pr * 128:(pr + 1) * 128], s5a[:], [s5a.d()], pd2,
                   True, False)
                mm(pv2[:, pi * 128:(pi + 1) * 128], wg2b[:, pr * 128:(pr + 1) * 128], s5b[:], [s5b.d()], pd2,
                   False, True)
            evac_pairs(pv2, pd2, qp["g"])
            yield
            S.op(PENG, lambda e: e.tensor_scalar(out=f2(Q["sig"]), in0=f2(Q["sig"]), scalar1=-0.6065306597126334,
                                                  scalar2=None, op0=ALU.mult),
                 reads=[Q["sig"].d()], writes=[Q["sig"].d()])
            S.op("dve", lambda e: e.tensor_tensor_scan(out=f2(Q["cs"]), data0=scanm[:], data1=f2(Q["sig"]),
                                                       initial=0.0, op0=ALU.mult, op1=ALU.add),
                 reads=[Q["sig"].d()], writes=[Q["cs"].d()])
            tt(PENG, Q["kk"][:], Q["k"][:], bc(HP_KK, q), ALU.mult, [Q["k"].d()], [Q["kk"].d()])
            actf(f2(Q["t1r"]), f2(Q["kk"]), AF.Square, [Q["kk"].d()], [Q["t1r"].d()])
            yield
            pv2, pd2 = ring_proj.get()
            mm(pv2[0:64, :], ones64r[:], f2(Q["t1r"]), [Q["t1r"].d()], pd2)
            actf(f2(Q["t1"]), pv2[0:64, :], AF.Sqrt, [pd2], [Q["t1"].d()])
            S.op("dve", lambda e: e.tensor_scalar(out=f2(Q["t1"]), in0=f2(Q["t1"]), scalar1=1e-12, scalar2=None,
                                                  op0=ALU.max), reads=[Q["t1"].d()], writes=[Q["t1"].d()])
            S.op("dve", lambda e: e.reciprocal(out=f2(Q["t1"]), in_=f2(Q["t1"])),
                 reads=[Q["t1"].d()], writes=[Q["t1"].d()])
            tt(PENG, Q["kk"][:], Q["kk"][:], Q["t1"][:], ALU.mult, [Q["kk"].d(), Q["t1"].d()], [Q["kk"].d()])
            yield
            S.op("dve", lambda e: e.scalar_tensor_tensor(out=Q["t1"][:], in0=Q["a"][:], scalar=-1.0, in1=bc(HP_KA, q),
                                                         op0=ALU.add, op1=ALU.mult),
                 reads=[Q["a"].d()], writes=[Q["t1"].d()])
            S.op("dve", lambda e: e.scalar_tensor_tensor(out=f2(qp["kmod"]), in0=f2(Q["t1"]), scalar=1.0,
                                                         in1=f2(Q["k"]), op0=ALU.add, op1=ALU.mult),
                 reads=[Q["t1"].d(), Q["k"].d()], writes=[qp["kmod"].d()])
            actf(f2(Q["eneg"]), f2(Q["cs"]), AF.Exp, [Q["cs"].d()], [Q["eneg"].d()], scale=-1.0)
            actf(f2(Q["epos"]), f2(Q["cs"]), AF.Exp, [Q["cs"].d()], [Q["epos"].d()])
            yield
            tt(PENG, Q["eexc"][:], Q["cs"][:], Q["sig"][:], ALU.subtract, [Q["cs"].d(), Q["sig"].d()],
               [Q["eexc"].d()])
            actf(f2(Q["eexc"]), f2(Q["eexc"]), AF.Exp, [Q["eexc"].d()], [Q["eexc"].d()])
            S.op("dve", lambda e: e.tensor_copy(out=gamC[:, 4 * q:4 * q + 4], in_=Q["epos"][:, :, 127]),
                 reads=[Q["epos"].d()], writes=[gamC.d(q)])
            yield
            S.op("dve", lambda e: e.scalar_tensor_tensor(out=AR[:, :, 0, :], in0=Q["kk"][:], scalar=-1.0,
                                                         in1=Q["eexc"][:], op0=ALU.mult, op1=ALU.mult),
                 reads=[Q["kk"].d(), Q["eexc"].d()], writes=[AR.d()])
            tt(PENG, AR[:, :, 1, :], qp["r"][:], Q["epos"][:], ALU.mult, [qp["r"].d(), Q["epos"].d()], [AR.d()])
            yield
            tt(PENG, Q["t1"][:], Q["kk"][:], Q["a"][:], ALU.mult, [Q["kk"].d(), Q["a"].d()], [Q["t1"].d()])
            tt(PENG, BTb[:, 0:4, :], Q["t1"][:], Q["eneg"][:], ALU.mult, [Q["t1"].d(), Q["eneg"].d()], [BTb.d()])
            tt(PENG, KTb[:, 0:4, :], qp["kmod"][:], Q["eneg"][:], ALU.mult, [qp["kmod"].d(), Q["eneg"].d()],
               [KTb.d()])
            yield

        def gn(q, dc):
            qp = QP[q % 2]
            y, t1, t1r = qp["y"], Q["gt1"], Q["gt1r"]
            heads = [4 * q + j for j in range(4)]
            actf(f2(t1r), f2(y), AF.Copy, [y.d()], [t1r.d()])
            pv2, pd2 = ring_proj.get()
            mm(pv2[0:64, :], ones64r[:], f2(t1r), [t1r.d()], pd2)
            S.op("dve", lambda e: e.tensor_scalar(out=f2(t1), in0=pv2[0:64, :], scalar1=1.0 / 64, scalar2=None,
                                                  op0=ALU.mult), reads=[pd2], writes=[t1.d()])
            tt("dve", y[:], y[:], t1[:], ALU.subtract, [y.d(), t1.d()], [y.d()])
            actf(f2(t1r), f2(y), AF.Square, [y.d()], [t1r.d()])
            yield
            pv2, pd2 = ring_proj.get()
            mm(pv2[0:64, :], ones64r[:], f2(t1r), [t1r.d()], pd2)
            S.op("dve", lambda e: e.tensor_scalar(out=f2(t1), in0=pv2[0:64, :], scalar1=1.0 / 64,
                                                  scalar2=GN_EPS, op0=ALU.mult, op1=ALU.add),
                 reads=[pd2], writes=[t1.d()])
            actf(f2(t1), f2(t1), AF.Sqrt, [t1.d()], [t1.d()])
            S.op("dve", lambda e: e.reciprocal(out=f2(t1), in_=f2(t1)), reads=[t1.d()], writes=[t1.d()])
            tt("dve", y[:], y[:], t1[:], ALU.mult, [y.d(), t1.d()], [y.d()])
            yield
            tt("pool", y[:], y[:], bc(HP_GNW, q), ALU.mult, [y.d()], [y.d()])
            tt("pool", y[:], y[:], bc(HP_GNB, q), ALU.add, [y.d()], [y.d()])
            tt("pool", t1[:], qp["r"][:], qp["kmod"][:], ALU.mult, [qp["r"].d(), qp["kmod"].d()], [t1.d()])
            tt("dve", t1r[:], t1[:], bc(HP_RK, q), ALU.mult, [t1.d()], [t1r.d()])
            yield
            pv2, pd2 = ring_proj.get()
            mm(pv2[0:64, :], ones64r[:], f2(t1r), [t1r.d()], pd2)
            tt("dve", t1[:], pv2[0:64, :].rearrange("p (a b) -> p a b", a=4), qp["vT"][:], ALU.mult,
               [pd2, qp["vT"].d()], [t1.d()])
            tt("pool", y[:], y[:], t1[:], ALU.add, [y.d(), t1.d()], [y.d()])
            yield
            for j, h in enumerate(heads):
                po = 64 * (h % 2)
                tt("dve", yfin[po:po + 64, h // 2, :], y[:, j, :], qp["g"][:, j, :], ALU.mult,
                   [y.d(), qp["g"].d()], [yfin.d()])
            if q == 3:
                S.dma("pool", yT_v[:, :, dc * 128:(dc + 1) * 128], yfin[:, :, :], reads=[yfin.d()])
            yield

        def chain(*gens):
            for g_ in gens:
                if g_ is not None:
                    yield from g_

        XSTEP = int(os.environ.get("RW_XSTEP", "1"))

        def run(gens, extra=None):
            alive = [(g_, 1) for g_ in gens if g_ is not None]
            if extra is not None:
                if os.environ.get("RW_XFIRST"):
                    alive.insert(0, (extra, XSTEP))
                else:
                    alive.append((extra, XSTEP))
            while alive:
                nxt = []
                for g_, k_ in alive:
                    ok = True
                    for _ in range(k_):
                        try:
                            next(g_)
                        except StopIteration:
                            ok = False
                            break
                    if ok:
                        nxt.append((g_, k_))
                alive = nxt

        run([chain(stageA(0), stageA2(0), prep(0))])
        gn_prev = None
        for dc in range(ndc):
            more = dc + 1 < ndc
            for q in range(4):
                heads = [4 * q + j for j in range(4)]
                if q < 3:
                    extra = chain(gn(q - 1, dc) if q > 0 else None, prep(q + 1))
                else:
                    extra = chain(gn(2, dc), stageA(dc + 1) if more else None, prep(0) if more else None)
                run([front(h, j, q) for j, h in enumerate(heads)], extra)
                for j, h in enumerate(heads):
                    back(h, j, q)
            gn_prev = gn(3, dc)
            run([stageA2(dc + 1) if more else None, gn_prev])
            gn_prev = None
        S.barrier()
    outproj_phase(S, h_in, h_out, yT_dram, P["w_out"], P["gpost_b"], tag + "o", ntile=ndc)


def rwkv_consts():
    import ml_dtypes
    i = np.arange(128)
    su = (i[:, None] < i[None, :]).astype(np.float32)
    u = (i[:, None] <= i[None, :]).astype(np.float32)
    scanm = np.ones((64, 512), np.float32)
    scanm[:, ::128] = 0.0
    return {
        "identb": np.eye(128, dtype=np.float32).astype(ml_dtypes.bfloat16),
        "identf": np.eye(128, dtype=np.float32),
        "ones64": np.ones((64, 64), np.float32),
        "mask2": np.ascontiguousarray(np.concatenate([su, u], axis=1)),
        "masksl": np.ascontiguousarray(su.T),
        "scanm": scanm,
    }


def _pl(v):
    return np.ascontiguousarray(np.asarray(v, np.float32).reshape(8, 128).T)


def _bcast(v):
    return np.ascontiguousarray(np.broadcast_to(np.asarray(v, np.float32), (128, D)))


def rwkv_host_params(mix_norm_pre, mix_norm_post, mu, w0, a0, k_k, k_a, r_k, gn_w, gn_b):
    hp = np.stack([np.asarray(t, np.float32).reshape(16, 64) for t in
                   (w0, a0, k_k, k_a, r_k, gn_w, gn_b)], axis=0)
    return {
        "gpre_l": _pl(mix_norm_pre), "gpost_b": _bcast(mix_norm_post),
        "mu_l": np.ascontiguousarray(np.asarray(mu, np.float32).reshape(6, 8, 128).transpose(2, 0, 1)),
        "hp": np.ascontiguousarray(hp.transpose(2, 0, 1)),
    }


def outproj_phase(S, h_in, h_out, attnT, w_out, gpost_b, tag, ntile=32):
    P = {"w_out": w_out, "gpost_b": gpost_b}

    def mmf(out, lhsT, rhs, reads, wdep_, start=True, stop=True):
        S.op("pe", lambda e: e.matmul(out, lhsT, rhs, start=start, stop=stop), reads=reads, writes=[wdep_])
    with contextlib.ExitStack() as es:
        def SB(name, shape, dt):
            return sb(S, es, tag + name, shape, dt)
        wo = SB("wo", [128, 8, D], BF16)
        gpost = SB("gpost", [128, D], F32)
        stg = [SB(f"stg{i}", [128, 1024], F32) for i in range(2)]
        aT = [SB(f"aT{i}", [128, 8, 128], BF16) for i in range(2)]
        xb = [SB(f"xb{i}", [128, D], F32) for i in range(2)]
        tmp = [SB(f"tmp{i}", [128, D], F32) for i in range(2)]
        junk = SB("junk", [128, 512], BF16)
        st = SB("st", [128, 8], F32)
        bk = [ps(S, es, tag + f"bk{i}", [128, 512], F32) for i in range(2)]
        S.dma("sp", gpost[:], P["gpost_b"], writes=[gpost.d()])
        for c in range(8):
            b = stg[c % 2]
            S.dma("sp", b[:, :], P["w_out"][c * 128:(c + 1) * 128, :], writes=[b.d()])
            S.op("act", lambda e, c=c, b=b: e.activation(out=wo[:, c, :], in_=b[:, :], func=AF.Copy), reads=[b.d()])
        S.barrier()
        a_v = attnT.rearrange("(c p) t -> p c t", p=128)
        hv_in = h_in.rearrange("(t p) d -> t p d", p=128)
        hv_out = h_out.rearrange("(t p) d -> t p d", p=128)
        for t in range(ntile):
            a, x, tm = aT[t % 2], xb[t % 2], tmp[t % 2]
            S.dma("sp", a[:, :, :], a_v[:, :, t * 128:(t + 1) * 128], writes=[a.d()])
            S.dma("sp", x[:], hv_in[t], writes=[x.d()])
            for half in range(2):
                b = bk[half]
                for c in range(8):
                    mmf(b[:, :], a[:, c, :], wo[:, c, half * 512:(half + 1) * 512], [a.d()], b.d(), c == 0, c == 7)
                S.op("act", lambda e, b=b, half=half: e.activation(out=junk[:], in_=b[:, :], func=AF.Square,
                                                                   accum_out=st[:, half:half + 1]),
                     reads=[b.d()], writes=[junk.d(), st.d(half), b.d("port")])
                S.op("dve", lambda e, b=b, half=half, tm=tm: e.tensor_tensor(
                    out=tm[:, half * 512:(half + 1) * 512], in0=b[:, :], in1=gpost[:, half * 512:(half + 1) * 512],
                    op=ALU.mult), reads=[b.d()], writes=[tm.d(), b.d("port")])
            S.op("dve", lambda e: e.tensor_tensor(out=st[:, 2:3], in0=st[:, 0:1], in1=st[:, 1:2], op=ALU.add),
                 reads=[st.d(0), st.d(1)], writes=[st.d(2)])
            S.op("dve", lambda e: e.tensor_scalar(out=st[:, 2:3], in0=st[:, 2:3], scalar1=1.0 / D, scalar2=RMS_EPS,
                                                  op0=ALU.mult, op1=ALU.add), reads=[st.d(2)], writes=[st.d(2)])
            S.op("act", lambda e: e.activation(out=st[:, 2:3], in_=st[:, 2:3], func=AF.Sqrt),
                 reads=[st.d(2)], writes=[st.d(2)])
            S.op("dve", lambda e: e.reciprocal(out=st[:, 2:3], in_=st[:, 2:3]), reads=[st.d(2)], writes=[st.d(2)])
            S.op("dve", lambda e, tm=tm, x=x: e.scalar_tensor_tensor(out=tm[:], in0=tm[:], scalar=st[:, 2:3], in1=x[:],
                                                                    op0=ALU.mult, op1=ALU.add),
                 reads=[tm.d(), st.d(2), x.d()], writes=[tm.d()])
            S.dma("pool", hv_out[t], tm[:], reads=[tm.d()])
        S.barrier()


NSA_IN = 2608
NEG = -30000.0


def nsa_consts():
    import ml_dtypes
    bf = ml_dtypes.bfloat16
    slopes = (2.0 ** (-8.0 * np.arange(1, 17, dtype=np.float64) / 16)).astype(np.float32)
    t = np.arange(4096, dtype=np.float64)
    aq = np.zeros((16, 3, 4096), dtype=bf)
    for h in range(16):
        v = (-slopes[h].astype(np.float64) * t).astype(np.float32)
        r = v.copy()
        for k in range(3):
            p = r.astype(bf)
            aq[h, k] = p
            r = (r - p.astype(np.float32)).astype(np.float32)
    j = np.arange(128)
    i = np.arange(512)
    bias_key = np.zeros((128, 16, 32), np.float32)
    bias_cmp = np.zeros((128, 16, 2), np.float32)
    for h in range(16):
        for kt in range(32):
            bias_key[:, h, kt] = slopes[h] * (128 * kt + j)
        for nt in range(2):
            bias_cmp[:, h, nt] = slopes[h] * (16 * (128 * nt + j) + 31)
    cmpmask = np.zeros((128, 8, 512), np.float32)
    for idx in range(8):
        cmpmask[:, idx, :] = np.where(16 * j[:, None] + 31 - i[None, :] <= 512 * idx, 0.0, NEG)
    causal = np.zeros((128, 4, 512), np.float32)
    for r_ in range(4):
        causal[:, r_, :] = np.where(j[:, None] + 128 * r_ <= i[None, :], 0.0, NEG)
    win = np.zeros((128, 8, 512), np.float32)
    for w in range(8):
        dist = i[None, :] - j[:, None] - 128 * (w - 4)
        win[:, w, :] = np.where((dist >= 0) & (dist < 512), 0.0, NEG)
    E = np.zeros((64, 4096), np.float32)
    E[np.arange(4096) // 64, np.arange(4096)] = 1.0
    cs_ = np.arange(255) * 16
    bs_ = np.arange(64) * 64
    ov = np.clip(np.minimum(cs_[:, None] + 32, bs_[None, :] + 64) - np.maximum(cs_[:, None], bs_[None, :]), 0, None) / 32.0
    Caug = np.zeros((256, 65), np.float32)
    Caug[:255, :64] = ov
    Caug[:255, 64] = 1.0
    Caug = Caug.reshape(2, 128, 65).transpose(1, 0, 2)
    vis = np.zeros((128, 32, 64), np.float32)
    add = np.zeros((128, 32, 64), np.float32)
    s = np.arange(64)
    for it in range(32):
        tq = 128 * it + j
        cur = tq // 64
        visible = s[None, :] * 64 <= tq[:, None]
        a = np.where(visible, 0.0, -1.0)
        v = visible.astype(np.float32)
        for (cond, val) in [(s[None, :] == 0, 1e4), (s[None, :] == cur[:, None], 2e4),
                            (s[None, :] == cur[:, None] - 1, 3e4)]:
            a = np.where(cond, val, a)
            v = np.where(cond, 0.0, v)
        vis[:, it, :] = v
        add[:, it, :] = a
    return {
        "alibi_q": aq, "bias_key": bias_key, "bias_cmp": bias_cmp,
        "cmpmask": cmpmask.astype(bf), "causal": causal.astype(bf), "winmask": win.astype(bf),
        "E_all": np.concatenate([E[1:64], np.ones((1, 4096), np.float32)], axis=0).astype(bf), "C_aug": np.ascontiguousarray(Caug).astype(bf),
        "vis": vis, "addt": add,
        "identb": np.eye(128, dtype=np.float32).astype(bf),
    }


def nsa_host_params(mix_norm_pre, mix_norm_post, pe_k, w_ck1, pe_v, w_cv1):
    return {
        "gpre_l": _pl(mix_norm_pre), "gpost_b": _bcast(mix_norm_post),
        "pekT": np.ascontiguousarray(np.asarray(pe_k, np.float32).T),
        "pevT": np.ascontiguousarray(np.asarray(pe_v, np.float32).T),
        "wck1_l": np.ascontiguousarray(np.asarray(w_ck1, np.float32).transpose(1, 0, 2)),
        "wcv1_l": np.ascontiguousarray(np.asarray(w_cv1, np.float32).transpose(1, 0, 2)),
    }


def nsa_phase(S, h_in, h_out, P, C, scr, tag="n"):
    nc = S.nc
    projT, vtokd, gTd, attnT = scr["projT"], scr["vtok"], scr["gT"], scr["attnT"]

    def mmf(out, lhsT, rhs, reads, wdep_, start=True, stop=True):
        S.op("pe", lambda e: e.matmul(out, lhsT, rhs, start=start, stop=stop), reads=reads, writes=[wdep_])

    with contextlib.ExitStack() as es:
        def SB(name, shape, dt):
            return sb(S, es, tag + "1" + name, shape, dt)
        Wb = SB("Wb", [128, 8, NSA_IN], BF16)
        gpre = SB("gpre", [128, 8], F32)
        identb = SB("identb", [128, 128], BF16)
        stg = [SB(f"stg{i}", [128, 1024], F32) for i in range(2)]
        xb = [SB(f"xb{i}", [128, 4, D], F32) for i in range(2)]
        xn = [SB(f"xn{i}", [128, D], BF16) for i in range(2)]
        junk = SB("junk", [128, D], BF16)
        uT = [SB(f"uT{i}", [128, 8, 512], BF16) for i in range(2)]
        ev = [SB(f"ev{i}", [128, 512], BF16) for i in range(4)]
        evf = [SB(f"evf{i}", [48, 512], F32) for i in range(2)]
        evt = [SB(f"evt{i}", [128, 512], BF16) for i in range(2)]
        st = SB("st", [128, 8], F32)
        bankT = ps(S, es, tag + "1bT", [128, 8, 128], BF16)
        banks = [ps(S, es, tag + f"1bk{i}", [128, 512], F32) for i in range(4)]
        S.dma("sp", gpre[:], P["gpre_l"], writes=[gpre.d()])
        S.dma("sp", identb[:], C["identb"], writes=[identb.d()])
        k = 0
        for c in range(8):
            for o in range(0, NSA_IN, 1024):
                wdt = min(1024, NSA_IN - o)
                b = stg[k % 2]
                S.dma("sp", b[:, 0:wdt], P["w_in"][c * 128:(c + 1) * 128, o:o + wdt], writes=[b.d()])
                S.op("act", lambda e, c=c, o=o, wdt=wdt, b=b: e.activation(
                    out=Wb[:, c, o:o + wdt], in_=b[:, 0:wdt], func=AF.Copy, scale=gpre[:, c:c + 1]),
                    reads=[b.d(), gpre.d()])
                k += 1
        S.barrier()
        hv_in = h_in.rearrange("(t s p) d -> t p s d", p=128, s=4)
        chunks = [(o, 128) for o in range(0, 2560, 128)] + [(2560, 48)]
        ke = 0
        for t in range(8):
            x, u = xb[t % 2], uT[t % 2]
            S.dma("sp", x[:, :, :], hv_in[t], writes=[x.d()])
            for s in range(4):
                xnb = xn[s % 2]
                S.op("act", lambda e, x=x, s=s: e.activation(out=junk[:], in_=x[:, s, :], func=AF.Square,
                                                             accum_out=st[:, s:s + 1]),
                     reads=[x.d()], writes=[junk.d(), st.d(s)])
                S.op("dve", lambda e, s=s: e.tensor_scalar(out=st[:, 4 + s:5 + s], in0=st[:, s:s + 1],
                                                           scalar1=1.0 / D, scalar2=RMS_EPS, op0=ALU.mult,
                                                           op1=ALU.add), reads=[st.d(s)], writes=[st.d(4 + s)])
                S.op("act", lambda e, s=s: e.activation(out=st[:, 4 + s:5 + s], in_=st[:, 4 + s:5 + s], func=AF.Sqrt),
                     reads=[st.d(4 + s)], writes=[st.d(4 + s)])
                S.op("dve", lambda e, s=s: e.reciprocal(out=st[:, 4 + s:5 + s], in_=st[:, 4 + s:5 + s]),
                     reads=[st.d(4 + s)], writes=[st.d(4 + s)])
                S.op("act", lambda e, x=x, s=s, xnb=xnb: e.activation(out=xnb[:], in_=x[:, s, :], func=AF.Copy,
                                                                      scale=st[:, 4 + s:5 + s]),
                     reads=[x.d(), st.d(4 + s)], writes=[xnb.d()])
                for c in range(8):
                    S.op("pe", lambda e, c=c, xnb=xnb: e.transpose(out=bankT[:, c, :],
                                                                   in_=xnb[:, c * 128:(c + 1) * 128],
                                                                   identity=identb[:]),
                         reads=[xnb.d()], writes=[bankT.d()])
                S.op("dve", lambda e, s=s, u=u: e.tensor_copy(out=u[:, :, s * 128:(s + 1) * 128], in_=bankT[:, :, :]),
                     reads=[bankT.d()], writes=[u.d()])
            for (o, m) in chunks:
                bk = banks[ke % 3]
                for c in range(8):
                    mmf(bk[0:m, :], Wb[:, c, o:o + m], u[:, c, :], [u.d()], bk.d(), c == 0, c == 7)
                if o == 2560:
                    e_ = evf[ke % 2]
                    S.op("act", lambda e, bk=bk, e_=e_: e.activation(out=e_[:], in_=bk[0:48, :], func=AF.Sigmoid),
                         reads=[bk.d()], writes=[e_.d()])
                    S.dma("pool", gTd[:, t * 512:(t + 1) * 512], e_[:], reads=[e_.d()])
                else:
                    e_ = ev[ke % 4]
                    sc = 0.125 if o < 1024 else 1.0
                    if ke % 2 == 0:
                        S.op("act", lambda e, bk=bk, e_=e_, sc=sc: e.activation(out=e_[:], in_=bk[:, :], func=AF.Copy,
                                                                               scale=sc),
                             reads=[bk.d()], writes=[e_.d()])
                    else:
                        S.op("dve", lambda e, bk=bk, e_=e_, sc=sc: e.tensor_scalar(out=e_[:], in0=bk[:, :], scalar1=sc,
                                                                                  scalar2=None, op0=ALU.mult),
                             reads=[bk.d()], writes=[e_.d()])
                    S.dma("pool", projT[o:o + 128, t * 512:(t + 1) * 512], e_[:], reads=[e_.d()])
                ke += 1
            for s in range(4):
                bk = banks[3]
                for (jj, o) in enumerate((1792, 2304)):
                    for c in range(8):
                        mmf(bk[:, jj * 256:(jj + 1) * 256], u[:, c, s * 128:(s + 1) * 128], Wb[:, c, o:o + 256],
                            [u.d()], bk.d(), c == 0, c == 7)
                e_ = evt[s % 2]
                S.op("act", lambda e, bk=bk, e_=e_: e.activation(out=e_[:], in_=bk[:, :], func=AF.Copy),
                     reads=[bk.d()], writes=[e_.d()])
                S.dma("pool", vtokd[t * 512 + s * 128:t * 512 + (s + 1) * 128, :], e_[:], reads=[e_.d()])
        S.barrier()

    with contextlib.ExitStack() as es:
        def SB(name, shape, dt):
            return sb(S, es, tag + "2" + name, shape, dt)
        ksA = SB("ksA", [128, 4096], BF16)
        kwA = SB("kwA", [128, 4096], BF16)
        kcT = SB("kcT", [64, 4096], BF16)
        vcT = SB("vcT", [64, 4096], BF16)
        vsA = SB("vsA", [128, 32, 128], BF16)
        vwA = SB("vwA", [128, 32, 128], BF16)
        qA = [SB(f"qA{i}", [128, 4096], BF16) for i in range(4)]
        kcmpA = SB("kcmpA", [128, 256], BF16)
        vcmpA = SB("vcmpA", [128, 2, 128], BF16)
        bias_key = SB("bias_key", [128, 16, 32], F32)
        bias_cmp = SB("bias_cmp", [128, 16, 2], F32)
        cmpmask = SB("cmpmask", [128, 8, 512], BF16)
        causal = SB("causal", [128, 4, 512], BF16)
        winmask = SB("winmask", [128, 8, 512], BF16)
        C_aug = SB("C_aug", [128, 2, 65], BF16)
        vis = SB("vis", [128, 32, 64], F32)
        addt = SB("addt", [128, 32, 64], F32)
        identb = SB("identb", [128, 128], BF16)
        w1 = [SB(f"w1_{i}", [64, 32, 64], BF16) for i in range(2)]
        w2 = [SB(f"w2_{i}", [64, 64], BF16) for i in range(2)]
        peT = [SB(f"peT{i}", [64, 32], BF16) for i in range(2)]
        cbias = SB("cbias", [64, 2], F32)
        stgw = SB("stgw", [64, 2048], F32)
        hid = [SB(f"hid{i}", [64, 256], BF16) for i in range(2)]
        eTs = [SB(f"eT{i}", [128, 512], BF16) for i in range(6)]
        imp_acc = SB("imp_acc", [128, 4, 64], F32)
        imp2 = SB("imp2", [128, 64], F32)
        imp3 = SB("imp3", [128, 64], F32)
        m8 = SB("m8", [128, 16], F32)
        mk = SB("mk", [128, 64], F32)
        negq = SB("negq", [128, 64], BF16)
        rq = SB("rq", [128, 4], F32)
        gbc = [SB(f"gbc{i}", [64, 3, 512], F32) for i in range(4)]
        acc = SB("acc", [64, 512], F32)
        rd = SB("rd", [64, 512], F32)
        tb = SB("tb", [64, 512], F32)
        outb = [SB(f"outb{i}", [64, 512], BF16) for i in range(2)]
        cacc = [SB(f"cacc{i}", [64, 512], F32) for i in range(4)]
        bsc = [ps(S, es, tag + f"2sc{i}", [128, 512], F32) for i in range(4)]
        bpv = [ps(S, es, tag + f"2pv{i}", [128, 512], F32) for i in range(2)]
        bimp = ps(S, es, tag + "2imp", [128, 512], F32)
        btr = ps(S, es, tag + "2tr", [128, 512], BF16)
        bcm = bimp
        ring_sc = Ring([(b, b.d()) for b in bsc])
        ring_pv = Ring([(b, b.d()) for b in bpv])
        ring_e = Ring([(b, b.d()) for b in eTs])

        for (t_, nm) in [(bias_key, "bias_key"), (bias_cmp, "bias_cmp"), (cmpmask, "cmpmask"), (causal, "causal"),
                         (winmask, "winmask"), (C_aug, "C_aug"), (vis, "vis"), (addt, "addt"),
                         (identb, "identb")]:
            S.dma("sp", t_[:], C[nm], writes=[t_.d()])
        for bi, (w1n, w2n, pen) in enumerate([("wck1_l", "w_ck2", "pekT"), ("wcv1_l", "w_cv2", "pevT")]):
            S.dma("sp", stgw[:, :].rearrange("p (l c) -> p l c", l=32), P[w1n], writes=[stgw.d()])
            S.op("dve", lambda e, bi=bi: e.tensor_copy(out=w1[bi][:].rearrange("p l c -> p (l c)"), in_=stgw[:, :]),
                 reads=[stgw.d()], writes=[w1[bi].d()])
            S.dma("sp", stgw[:, 0:64], P[w2n], writes=[stgw.d()])
            S.op("dve", lambda e, bi=bi: e.tensor_copy(out=w2[bi][:], in_=stgw[:, 0:64]),
                 reads=[stgw.d()], writes=[w2[bi].d()])
            S.dma("sp", stgw[:, 0:32], P[pen], writes=[stgw.d()])
            S.op("dve", lambda e, bi=bi: e.tensor_copy(out=peT[bi][:], in_=stgw[:, 0:32]),
                 reads=[stgw.d()], writes=[peT[bi].d()])
            for l in range(32):
                mmf(bcm[0:64, 0:1], w1[bi][:, l, :], peT[bi][:, l:l + 1], [w1[bi].d(), peT[bi].d()], bcm.d(),
                    l == 0, l == 31)
            S.op("dve", lambda e, bi=bi: e.tensor_copy(out=cbias[:, bi:bi + 1], in_=bcm[0:64, 0:1]),
                 reads=[bcm.d()], writes=[cbias.d()])
        for t_ in (kwA, kcmpA) + tuple(qA):
            S.op("dve", lambda e, t_=t_: e.memset(t_[64:128, :], 0.0), writes=[t_.d()])
        S.dma("sp", ksA[64:128, :], C["E_all"], writes=[ksA.d()])
        S.dma("sp", kwA[127:128, :], C["E_all"][63:64, :], writes=[kwA.d()])
        S.dma("sp", kcmpA[127:128, :], C["E_all"][63:64, 0:256], writes=[kcmpA.d()])
        S.op("dve", lambda e: e.memset(negq[:], 0.0), writes=[negq.d()])
        S.op("dve", lambda e: e.memset(vcmpA[:, :, 0:64], 0.0), writes=[vcmpA.d()])
        for t_ in (vsA, vwA, vcmpA):
            S.op("dve", lambda e, t_=t_: e.memset(t_[:, :, 64:128], 1.0), writes=[t_.d()])
        S.op("dve", lambda e: e.memset(kcmpA[0:64, 255:256], 0.0), writes=[kcmpA.d()])
        vt_v = vtokd.rearrange("(kt p) d -> p kt d", p=128)
        pend = []
        import os
        LAG = int(os.environ.get('NSA_LAG', '3'))
        S.barrier()

        for g in range(4):
            S.dma("sp", ksA[0:64, :], projT[1536 + g * 64:1536 + (g + 1) * 64, :], writes=[ksA.d()])
            S.dma("sp", kwA[0:64, :], projT[2048 + g * 64:2048 + (g + 1) * 64, :], writes=[kwA.d()])
            S.dma("sp", kcT[:, :], projT[1024 + g * 64:1024 + (g + 1) * 64, :], writes=[kcT.d()])
            S.dma("sp", vcT[:, :], projT[1280 + g * 64:1280 + (g + 1) * 64, :], writes=[vcT.d()])
            S.dma("sp", vsA[:, :, 0:64], vt_v[:, :, g * 64:(g + 1) * 64], writes=[vsA.d()])
            S.dma("sp", vwA[:, :, 0:64], vt_v[:, :, 256 + g * 64:256 + (g + 1) * 64], writes=[vwA.d()])
            for r in range(4):
                h = 4 * g + r
                S.dma("sp", qA[r][0:64, :], projT[h * 64:(h + 1) * 64, :], writes=[qA[r].d()])
                S.dma("sp", qA[r][127:128, :], C["alibi_q"][h, 0:1, :], writes=[qA[r].d()])
            for bi, src in enumerate((kcT, vcT)):
                for l in range(32):
                    mmf(bcm[0:64, 0:255], w1[bi][:, l, :], src[:, l:l + 16 * 254 + 1:16], [src.d()], bcm.d(),
                        l == 0, l == 31)
                S.op("act", lambda e, bi=bi: e.activation(out=hid[bi][:, 0:255], in_=bcm[0:64, 0:255], func=AF.Silu,
                                                          bias=cbias[:, bi:bi + 1]),
                     reads=[bcm.d(), cbias.d()], writes=[hid[bi].d()])
            mmf(bcm[0:64, 0:255], w2[0][:], hid[0][:, 0:255], [hid[0].d()], bcm.d())
            S.op("dve", lambda e: e.tensor_copy(out=kcmpA[0:64, 0:255], in_=bcm[0:64, 0:255]),
                 reads=[bcm.d()], writes=[kcmpA.d()])
            for nt in range(2):
                nn = 128 if nt == 0 else 127
                mmf(bcm[0:nn, 256 + nt * 64:256 + (nt + 1) * 64], hid[1][:, nt * 128:nt * 128 + nn], w2[1][:],
                    [hid[1].d()], bcm.d())
                S.op("dve", lambda e, nt=nt, nn=nn: e.tensor_copy(out=vcmpA[0:nn, nt, 0:64],
                                                                 in_=bcm[0:nn, 256 + nt * 64:256 + (nt + 1) * 64]),
                     reads=[bcm.d()], writes=[vcmpA.d()])

            def push_branch(tiles, qc, done_cb):
                pvb, pvd = ring_pv.get()
                n = len(tiles)
                ets = []
                for ti, (kT, bias, masks, vT, rds) in enumerate(tiles):
                    sc, scd = ring_sc.get()
                    mmf(sc[:, :], kT, qc, rds, scd, True, len(masks) == 0)
                    for mi, (ml, mr, mrd) in enumerate(masks):
                        mmf(sc[:, :], ml, mr, mrd, scd, False, mi == len(masks) - 1)
                    eT, eTd = ring_e.get()
                    S.op("act", lambda e, sc=sc, eT=eT, bias=bias: e.activation(out=eT[:], in_=sc[:, :], func=AF.Exp,
                                                                               bias=bias),
                         reads=[scd], writes=[eTd])
                    ets.append((eT, eTd))
                    flush(LAG - 1)
                    last = ti == n - 1
                    pend.append((pvb, pvd, vT, eT, [eTd] + list(rds), ti == 0, last,
                                 (lambda: done_cb(pvb, pvd, ets)) if last else None))

            def flush(keep=0):
                while len(pend) > keep:
                    pvb, pvd, vT, eT, prds, first, last, cb = pend.pop(0)
                    mmf(pvb[:, :], vT, eT[:], prds, pvd, first, last)
                    if cb is not None:
                        cb()

            def combine(pvb, pvd, gb, b, accb, first):
                S.op("dve", lambda e: e.tensor_scalar(out=rd[:], in0=pvb[64:128, :], scalar1=1e-30, scalar2=None,
                                                      op0=ALU.add), reads=[pvd], writes=[rd.d()])
                S.op("dve", lambda e: e.reciprocal(out=rd[:], in_=rd[:]), reads=[rd.d()], writes=[rd.d()])
                S.op("dve", lambda e: e.tensor_tensor(out=tb[:], in0=pvb[0:64, :], in1=rd[:], op=ALU.mult),
                     reads=[pvd, rd.d()], writes=[tb.d()])
                if first:
                    S.op("pool", lambda e: e.tensor_tensor(out=accb[:], in0=tb[:], in1=gb[:, b, :], op=ALU.mult),
                         reads=[tb.d(), gb.d()], writes=[accb.d()])
                else:
                    S.op("pool", lambda e: e.tensor_tensor(out=tb[:], in0=tb[:], in1=gb[:, b, :], op=ALU.mult),
                         reads=[tb.d(), gb.d()], writes=[tb.d()])
                    S.op("pool", lambda e: e.tensor_tensor(out=accb[:], in0=accb[:], in1=tb[:], op=ALU.add),
                         reads=[tb.d(), accb.d()], writes=[accb.d()])

            for c in range(8):
                cs_ = slice(c * 512, (c + 1) * 512)
                nts = [0] if c < 4 else [0, 1]
                for r in range(4):
                    h = 4 * g + r
                    tiles = [(kcmpA[:, nt * 128:(nt + 1) * 128], bias_cmp[:, h, nt:nt + 1],
                              [(identb[:], cmpmask[:, c - 4 * nt, :], [])], vcmpA[:, nt, :],
                              [kcmpA.d(), qA[r].d(), vcmpA.d()]) for nt in nts]

                    def cmp_done(pvb, pvd, ets, r=r, h=h, nts=nts, cs_=cs_):
                        for qs in range(4):
                            for ni, nt in enumerate(nts):
                                eT, eTd = ets[ni]
                                mmf(bimp[:, qs * 65:(qs + 1) * 65], eT[:, qs * 128:(qs + 1) * 128], C_aug[:, nt, :],
                                    [eTd], bimp.d(), ni == 0, ni == len(nts) - 1)
                        S.op("dve", lambda e: e.tensor_scalar(
                            out=rq[:], in0=bimp[:, 0:260].rearrange("p (a b) -> p a b", a=4)[:, :, 64], scalar1=1e-30,
                            scalar2=None, op0=ALU.add), reads=[bimp.d()], writes=[rq.d()])
                        S.op("dve", lambda e: e.reciprocal(out=rq[:], in_=rq[:]), reads=[rq.d()], writes=[rq.d()])
                        for qs in range(4):
                            if r == 0:
                                S.op("dve", lambda e, qs=qs: e.tensor_scalar(
                                    out=imp_acc[:, qs, :], in0=bimp[:, qs * 65:qs * 65 + 64], scalar1=rq[:, qs:qs + 1],
                                    scalar2=None, op0=ALU.mult), reads=[bimp.d(), rq.d()], writes=[imp_acc.d()])
                            else:
                                S.op("dve", lambda e, qs=qs: e.scalar_tensor_tensor(
                                    out=imp_acc[:, qs, :], in0=bimp[:, qs * 65:qs * 65 + 64], scalar=rq[:, qs:qs + 1],
                                    in1=imp_acc[:, qs, :], op0=ALU.mult, op1=ALU.add),
                                    reads=[bimp.d(), rq.d(), imp_acc.d()], writes=[imp_acc.d()])
                        gb = gbc[r]
                        S.dma("sp", gb[:], gTd[3 * h:3 * h + 3, cs_].partition_broadcast(64), writes=[gb.d()])
                        combine(pvb, pvd, gb, 0, cacc[r], True)

                    push_branch(tiles, qA[r][:, cs_], cmp_done)
                flush()
                for qs in range(4):
                    it = 4 * c + qs
                    S.op("dve", lambda e, qs=qs, it=it: e.tensor_tensor(out=imp2[:], in0=imp_acc[:, qs, :],
                                                                      in1=vis[:, it, :], op=ALU.mult),
                         reads=[imp_acc.d()], writes=[imp2.d()])
                    S.op("dve", lambda e, it=it: e.tensor_tensor(out=imp2[:], in0=imp2[:], in1=addt[:, it, :],
                                                                 op=ALU.add), reads=[imp2.d()], writes=[imp2.d()])
                    S.op("dve", lambda e: e.max(out=m8[:, 0:8], in_=imp2[:]), reads=[imp2.d()], writes=[m8.d()])
                    S.op("dve", lambda e: e.match_replace(out=imp3[:], in_to_replace=m8[:, 0:8], in_values=imp2[:],
                                                          imm_value=-2.0),
                         reads=[imp2.d(), m8.d()], writes=[imp3.d()])
                    S.op("dve", lambda e: e.max(out=m8[:, 8:16], in_=imp3[:]), reads=[imp3.d()], writes=[m8.d()])
                    S.op("dve", lambda e: e.tensor_scalar(out=mk[:], in0=imp2[:], scalar1=m8[:, 15:16], scalar2=None,
                                                          op0=ALU.is_ge), reads=[imp2.d(), m8.d()], writes=[mk.d()])
                    S.op("dve", lambda e: e.tensor_scalar(out=negq[:, 0:63], in0=mk[:, 1:64], scalar1=-1.0,
                                                          scalar2=-NEG, op0=ALU.add, op1=ALU.mult),
                         reads=[mk.d()], writes=[negq.d()])
                    S.op("pe", lambda e: e.transpose(out=btr[0:64, 0:128], in_=negq[:], identity=identb[:]),
                         reads=[negq.d()], writes=[btr.d()])
                    for r in range(4):
                        S.op("act", lambda e, it=it, r=r: e.activation(out=qA[r][64:127, it * 128:(it + 1) * 128],
                                                                       in_=btr[0:63, 0:128], func=AF.Copy),
                             reads=[btr.d()], writes=[qA[r].d()])
                for r in range(4):
                    h = 4 * g + r
                    gb = gbc[r]
                    tiles = []
                    for kt in range(4 * c + 4):
                        masks = []
                        if kt >= 4 * c:
                            masks.append((identb[:], causal[:, kt - 4 * c, :], []))
                        tiles.append((ksA[:, kt * 128:(kt + 1) * 128], bias_key[:, h, kt:kt + 1], masks,
                                      vsA[:, kt, :], [ksA.d(), qA[r].d(), vsA.d()]))
                    push_branch(tiles, qA[r][:, cs_],
                                lambda pvb, pvd, ets, r=r, gb=gb: combine(pvb, pvd, gb, 1, cacc[r], False))
                    tiles = []
                    for kt in range(max(0, 4 * c - 4), 4 * c + 4):
                        tiles.append((kwA[:, kt * 128:(kt + 1) * 128], bias_key[:, h, kt:kt + 1],
                                      [(identb[:], winmask[:, kt - 4 * c + 4, :], [])], vwA[:, kt, :],
                                      [kwA.d(), qA[r].d(), vwA.d()]))

                    def win_done(pvb, pvd, ets, r=r, h=h, gb=gb, cs_=cs_):
                        combine(pvb, pvd, gb, 2, cacc[r], False)
                        ob = outb[r % 2]
                        S.op("act", lambda e: e.activation(out=ob[:], in_=cacc[r][:], func=AF.Copy),
                             reads=[cacc[r].d()], writes=[ob.d()])
                        S.dma("pool", attnT[h * 64:(h + 1) * 64, cs_], ob[:], reads=[ob.d()])

                    push_branch(tiles, qA[r][:, cs_], win_done)
            flush()
        S.barrier()

    outproj_phase(S, h_in, h_out, attnT, P["w_out"], P["gpost_b"], tag + "3")


_CACHE = {}


def build_program(phases=("f", "n", "f", "f", "r", "f")):
    nc = bass.Bass("TRN2", target_bir_lowering=False)
    A = {}

    def din(name, shape, dt=F32):
        A[name] = nc.dram_tensor(name, list(shape), dt, kind="ExternalInput").ap()
        return A[name]

    def dscr(name, shape, dt=F32):
        return nc.dram_tensor(name, list(shape), dt, kind="Internal").ap()

    din("x", [S_LEN, D])
    for li in range(2):
        for f in (1, 2):
            din(f"f{f}_{li}_wgu", [D, 2 * DFF]); din(f"f{f}_{li}_wdn", [DFF, D])
            din(f"f{f}_{li}_gpre", [128, 8]); din(f"f{f}_{li}_gpost", [128, D])
    ncst = nsa_consts()
    rcst = rwkv_consts()
    for k_, v in ncst.items():
        din("nc_" + k_, v.shape, F32 if v.dtype == np.float32 else BF16)
    for k_, v in rcst.items():
        din("rc_" + k_, v.shape, F32 if v.dtype == np.float32 else BF16)
    NP = {"w_in": din("n_w_in", [D, NSA_IN]), "w_out": din("n_w_out", [D, D]),
          "gpre_l": din("n_gpre_l", [128, 8]), "gpost_b": din("n_gpost_b", [128, D]),
          "pekT": din("n_pekT", [64, 32]), "pevT": din("n_pevT", [64, 32]),
          "wck1_l": din("n_wck1_l", [64, 32, 64]), "wcv1_l": din("n_wcv1_l", [64, 32, 64]),
          "w_ck2": din("n_w_ck2", [64, 64]), "w_cv2": din("n_w_cv2", [64, 64])}
    RP = {"w_in": din("r_w_in", [D, 3360]), "w_out": din("r_w_out", [D, D]), "w_w2": din("r_w_w2", [64, D]),
          "w_a2": din("r_w_a2", [64, D]), "w_g2": din("r_w_g2", [160, D]),
          "gpre_l": din("r_gpre_l", [128, 8]), "gpost_b": din("r_gpost_b", [128, D]),
          "mu_l": din("r_mu_l", [128, 6, 8]), "hp": din("r_hp", [64, 7, 16])}
    y = nc.dram_tensor("y", [S_LEN, D], F32, kind="ExternalOutput").ap()
    hs = [A["x"]] + [dscr(f"h{i}", [S_LEN, D]) for i in range(len(phases) - 1)] + [y]
    scr = {"projT": dscr("projT", [2560, S_LEN], BF16), "vtok": dscr("vtokd", [S_LEN, 512], BF16),
           "gT": dscr("gT", [48, S_LEN]), "attnT": dscr("attnT", [D, S_LEN], BF16)}
    NC_ = {k_: A["nc_" + k_] for k_ in ncst}
    RC_ = {k_: A["rc_" + k_] for k_ in rcst}
    es = contextlib.ExitStack()
    with es:
        S = Sched(nc, es)
        ffn_ids = [(1, 0), (2, 0), (1, 1), (2, 1)]
        fi = 0
        for pi, ph in enumerate(phases):
            hin, hout = hs[pi], hs[pi + 1]
            if ph == "f":
                f, li = ffn_ids[fi]
                fi += 1
                ffn_phase(S, hin, hout, A[f"f{f}_{li}_wgu"], A[f"f{f}_{li}_wdn"], A[f"f{f}_{li}_gpre"],
                          A[f"f{f}_{li}_gpost"], A["rc_identb"], tag=f"f{pi}")
            elif ph == "n":
                nsa_phase(S, hin, hout, NP, NC_, scr)
            elif ph == "r":
                rwkv_phase(S, hin, hout, RP, RC_, scr["attnT"])
        S.finish()
    return nc, ncst, rcst


def kernel(**inp):
    f32 = np.float32
    g = {k_: np.asarray(v) for k_, v in inp.items()}
    if "prog" not in _CACHE:
        _CACHE["prog"] = build_program()
    nc, ncst, rcst = _CACHE["prog"]
    shared = {}
    for li in range(2):
        for f in (1, 2):
            shared[f"f{f}_{li}_wgu"] = np.ascontiguousarray(g[f"ffn{f}_w_gu"][li], f32)
            shared[f"f{f}_{li}_wdn"] = np.ascontiguousarray(g[f"ffn{f}_w_down"][li], f32)
            shared[f"f{f}_{li}_gpre"] = _pl(g[f"ffn{f}_norm_pre"][li])
            shared[f"f{f}_{li}_gpost"] = _bcast(g[f"ffn{f}_norm_post"][li])
    for k_, v in ncst.items():
        shared["nc_" + k_] = v
    for k_, v in rcst.items():
        shared["rc_" + k_] = v
    nh = nsa_host_params(g["mix_norm_pre"][0], g["mix_norm_post"][0], g["nsa_pe_k"][0], g["nsa_w_ck1"][0],
                         g["nsa_pe_v"][0], g["nsa_w_cv1"][0])
    for k_, v in nh.items():
        shared["n_" + k_] = v
    shared["n_w_in"] = np.ascontiguousarray(g["nsa_w_in"][0], f32)
    shared["n_w_out"] = np.ascontiguousarray(g["nsa_w_out"][0], f32)
    shared["n_w_ck2"] = np.ascontiguousarray(g["nsa_w_ck2"][0], f32)
    shared["n_w_cv2"] = np.ascontiguousarray(g["nsa_w_cv2"][0], f32)
    rh = rwkv_host_params(g["mix_norm_pre"][1], g["mix_norm_post"][1], g["rwkv_mu"][0], g["rwkv_w0"][0],
                          g["rwkv_a0"][0], g["rwkv_k_k"][0], g["rwkv_k_a"][0], g["rwkv_r_k"][0],
                          g["rwkv_gn_w"][0], g["rwkv_gn_b"][0])
    for k_, v in rh.items():
        shared["r_" + k_] = v
    for nm in ("w_in", "w_out", "w_w2", "w_a2", "w_g2"):
        shared["r_" + nm] = np.ascontiguousarray(g["rwkv_" + nm][0], f32)
    x = np.asarray(g["x"], f32)
    in_maps = [dict(shared, x=np.ascontiguousarray(x[b])) for b in range(NCORES)]
    res = run_bass_kernel_spmd(nc, in_maps, core_ids=list(range(NCORES)))
    return np.stack([np.asarray(r["y"], f32) for r in res.results], axis=0)
```

```python
import contextlib
import numpy as np
import concourse.bass as bass
import concourse.mybir as mybir
from concourse.bass_utils import run_bass_kernel_spmd

F32 = mybir.dt.float32
BF16 = mybir.dt.bfloat16
AF = mybir.ActivationFunctionType
ALU = mybir.AluOpType
AX = mybir.AxisListType

S_LEN = 4096
D = 1024
DFF = 2816
NCORES = 8
RMS_EPS = 1e-6


class Dep:
    __slots__ = ("w", "r")

    def __init__(self):
        self.w = None
        self.r = {}


class Sched:
    EPOCH = 16000
    NDMA = 24

    def __init__(self, nc, es):
        self.nc = nc
        self.es = es
        self.eng = {"pe": nc.tensor, "act": nc.scalar, "dve": nc.vector,
                    "pool": nc.gpsimd, "sp": nc.sync}
        self.nsem = 0
        self.sem = {e: self._newsem(e) for e in self.eng}
        self.cnt = {e: 0 for e in self.eng}
        self.waited = {e: {} for e in self.eng}
        self.dsem = {"sp": [self._newsem("dsp") for _ in range(16)],
                     "pool": [self._newsem("dpl") for _ in range(8)]}
        self.dcnt = {q: [0] * len(v) for q, v in self.dsem.items()}
        self.dnext = {q: 0 for q in self.dsem}
        self.ninst = 0
        self.nwait = 0
        self.pe_self_sync = False

    def _newsem(self, tag):
        self.nsem += 1
        return self.es.enter_context(self.nc.semaphore(f"s_{tag}_{self.nsem}"))

    def _wait(self, e, toks):
        best = {}
        for (s, v, src) in toks:
            if src == e and e == "pe" and not self.pe_self_sync:
                continue
            k = id(s)
            if k not in best or best[k][1] < v:
                best[k] = (s, v)
        w = self.waited[e]
        for k, (s, v) in best.items():
            if w.get(k, 0) >= v:
                continue
            self.eng[e].wait_ge(s, v)
            self.nwait += 1
            w[k] = v

    def _collect(self, reads, writes):
        toks = []
        for d in reads:
            if d.w is not None:
                toks.append(d.w)
        for d in writes:
            if d.w is not None:
                toks.append(d.w)
            toks.extend(d.r.values())
        return toks

    def _update(self, tok, reads, writes):
        k = id(tok[0])
        for d in reads:
            old = d.r.get(k)
            if old is None or old[1] < tok[1]:
                d.r[k] = tok
        for d in writes:
            d.w = tok
            d.r = {}

    def op(self, e, fn, reads=(), writes=()):
        toks = self._collect(reads, writes)
        self._wait(e, toks)
        ins = fn(self.eng[e])
        if self.cnt[e] >= self.EPOCH:
            self.sem[e] = self._newsem(e)
            self.cnt[e] = 0
        self.cnt[e] += 1
        ins.then_inc(self.sem[e], 1)
        self.ninst += 1
        tok = (self.sem[e], self.cnt[e], e)
        self._update(tok, reads, writes)
        return tok

    def dma(self, q, out, in_, reads=(), writes=(), **kw):
        toks = self._collect(reads, writes)
        dsem, dcnt = self.dsem[q], self.dcnt[q]
        k = self.dnext[q]
        self.dnext[q] = (k + 1) % len(dsem)
        if dcnt[k] >= self.EPOCH:
            toks.append((dsem[k], dcnt[k], None))
            self._wait(q, toks)
            toks = []
            dsem[k] = self._newsem("d" + q)
            dcnt[k] = 0
        if dcnt[k] > 0:
            toks.append((dsem[k], dcnt[k], None))
        self._wait(q, toks)
        ins = self.eng[q].dma_start(out=out, in_=in_, **kw)
        dcnt[k] += 16
        ins.then_inc(dsem[k], 16)
        self.ninst += 1
        tok = (dsem[k], dcnt[k], None)
        self._update(tok, reads, writes)
        return tok

    def _all_dma_toks(self):
        return [(self.dsem[q][k], self.dcnt[q][k], None) for q in self.dsem
                for k in range(len(self.dsem[q])) if self.dcnt[q][k] > 0]

    def barrier(self):
        toks = [(self.sem[e], self.cnt[e], None) for e in self.eng if self.cnt[e] > 0]
        toks += self._all_dma_toks()
        for e in self.eng:
            self._wait(e, toks)

    def finish(self):
        toks = self._all_dma_toks()
        toks += [(self.sem[e], self.cnt[e], None) for e in self.eng if self.cnt[e] > 0]
        self._wait("sp", toks)


class Buf:
    def __init__(self, t):
        self.t = t
        self.deps = {}

    def d(self, key=0):
        dd = self.deps.get(key)
        if dd is None:
            dd = self.deps[key] = Dep()
        return dd

    def __getitem__(self, idx):
        return self.t[idx]


def sb(S, es, name, shape, dt):
    return Buf(es.enter_context(S.nc.sbuf_tensor(name, shape, dt)))


def ps(S, es, name, shape, dt):
    return Buf(es.enter_context(S.nc.psum_tensor(name, shape, dt)))


def ffn_phase(S, h_in, h_out, w_gu, w_down, gpre_l, gpost_b, ident_d, ntiles=16, tag="f"):
    nc = S.nc
    T = 256
    NS = T // 128
    with contextlib.ExitStack() as es:
        wgu = sb(S, es, tag + "wgu", [128, 8, 2 * DFF], BF16)
        wdn = sb(S, es, tag + "wdn", [128, 22, D], BF16)
        gpre = sb(S, es, tag + "gpre", [128, 8], F32)
        gpost = sb(S, es, tag + "gpost", [128, D], F32)
        ident = sb(S, es, tag + "ident", [128, 128], BF16)
        xb = [sb(S, es, tag + f"x{i}", [128, NS, D], F32) for i in range(2)]
        xn = [sb(S, es, tag + f"xn{i}", [128, D], BF16) for i in range(2)]
        xnT = [sb(S, es, tag + f"xnT{i}", [128, 8, T], BF16) for i in range(2)]
        hT = sb(S, es, tag + "hT", [128, 22, T], BF16)
        sg = [sb(S, es, tag + f"sg{i}", [128, T], F32) for i in range(3)]
        ob = [sb(S, es, tag + f"ob{i}", [128, D], F32) for i in range(2)]
        tmp = sb(S, es, tag + "tmp", [128, D], F32)
        junk = sb(S, es, tag + "junk", [128, D], BF16)
        st = sb(S, es, tag + "st", [128, 16], F32)
        pT = ps(S, es, tag + "pT", [128, 8, 128], BF16)
        pGU = [ps(S, es, tag + f"pGU{i}", [128, 2, T], F32) for i in range(3)]
        pF = [ps(S, es, tag + f"pF{i}", [128, 512], F32) for i in range(4)]

        S.dma("sp", gpre[:], gpre_l, writes=[gpre.d()])
        S.dma("sp", gpost[:], gpost_b, writes=[gpost.d()])
        S.dma("sp", ident[:], ident_d, writes=[ident.d()])
        S.op("dve", lambda e: e.tensor_scalar(out=gpost[:], in0=gpost[:], scalar1=0.5, scalar2=None,
                                              op0=ALU.mult), reads=[gpost.d()], writes=[gpost.d()])

        slots = [(xb[i].d(("stg", s_)), xb[i][:, s_, :]) for i in range(2) for s_ in range(NS)]
        HW = 1024
        k = 0

        def conv(dst, view, dep, scale=None):
            nonlocal k
            if k % 2 == 0:
                if scale is None:
                    S.op("act", lambda e: e.activation(out=dst, in_=view, func=AF.Copy), reads=[dep])
                else:
                    S.op("act", lambda e: e.activation(out=dst, in_=view, func=AF.Copy, scale=scale),
                         reads=[dep, gpre.d()])
            else:
                if scale is None:
                    S.op("dve", lambda e: e.tensor_copy(out=dst, in_=view), reads=[dep])
                else:
                    S.op("dve", lambda e: e.tensor_scalar(out=dst, in0=view, scalar1=scale, scalar2=None,
                                                          op0=ALU.mult), reads=[dep, gpre.d()])
            k += 1

        for c in range(8):
            for o in range(0, 2 * DFF, HW):
                wdt = min(HW, 2 * DFF - o)
                dep, sv = slots[k % len(slots)]
                S.dma("sp", sv[:, 0:wdt], w_gu[c * 128:(c + 1) * 128, o:o + wdt], writes=[dep])
                conv(wgu[:, c, o:o + wdt], sv[:, 0:wdt], dep, scale=gpre[:, c:c + 1])
        for j in range(22):
            dep, sv = slots[k % len(slots)]
            S.dma("sp", sv[:, :], w_down[j * 128:(j + 1) * 128, :], writes=[dep])
            conv(wdn[:, j, :], sv[:, :], dep)
        S.barrier()

        hv_in = h_in.rearrange("(t s p) d -> t p s d", p=128, s=NS)
        hv_out = h_out.rearrange("(t s p) d -> t s p d", p=128, s=NS)

        def load(t):
            S.dma("sp", xb[t % 2][:, :, :], hv_in[t], writes=[xb[t % 2].d()])

        def prenorm(t):
            x = xb[t % 2]
            for s in range(NS):
                xnb = xn[s % 2]
                S.op("act", lambda e, x=x, s=s: e.activation(
                    out=junk[:], in_=x[:, s, :], func=AF.Square, accum_out=st[:, s:s + 1]),
                    reads=[x.d()], writes=[junk.d(), st.d(s)])
                S.op("dve", lambda e, s=s: e.tensor_scalar(
                    out=st[:, 4 + s:5 + s], in0=st[:, s:s + 1], scalar1=1.0 / D, scalar2=RMS_EPS,
                    op0=ALU.mult, op1=ALU.add), reads=[st.d(s)], writes=[st.d(4 + s)])
                S.op("act", lambda e, s=s: e.activation(
                    out=st[:, 4 + s:5 + s], in_=st[:, 4 + s:5 + s], func=AF.Sqrt),
                    reads=[st.d(4 + s)], writes=[st.d(4 + s)])
                S.op("dve", lambda e, s=s: e.reciprocal(
                    out=st[:, 4 + s:5 + s], in_=st[:, 4 + s:5 + s]),
                    reads=[st.d(4 + s)], writes=[st.d(4 + s)])
                S.op("act", lambda e, x=x, s=s, xnb=xnb: e.activation(
                    out=xnb[:], in_=x[:, s, :], func=AF.Copy, scale=st[:, 4 + s:5 + s]),
                    reads=[x.d(), st.d(4 + s)], writes=[xnb.d()])
                for c in range(8):
                    S.op("pe", lambda e, c=c, xnb=xnb: e.transpose(
                        out=pT[:, c, :], in_=xnb[:, c * 128:(c + 1) * 128], identity=ident[:]),
                        reads=[xnb.d(), ident.d()], writes=[pT.d()])
                S.op("dve", lambda e, t=t, s=s: e.tensor_copy(
                    out=xnT[t % 2][:, :, s * 128:(s + 1) * 128], in_=pT[:, :, :]),
                    reads=[pT.d()], writes=[xnT[t % 2].d()])

        def gu(t):
            xT = xnT[t % 2]
            for j in range(22):
                pg = pGU[j % 3]
                for half in range(2):
                    col = half * DFF + j * 128
                    for c in range(8):
                        S.op("pe", lambda e, c=c, col=col, half=half, pg=pg: e.matmul(
                            pg[:, half, :], wgu[:, c, col:col + 128], xT[:, c, :],
                            start=(c == 0), stop=(c == 7)),
                            reads=[wgu.d(), xT.d()], writes=[pg.d()])
                sgb = sg[j % 3]
                S.op("act", lambda e, pg=pg, sgb=sgb: e.activation(
                    out=sgb[:], in_=pg[:, 0, :], func=AF.Silu), reads=[pg.d()], writes=[sgb.d()])
                S.op("dve", lambda e, pg=pg, sgb=sgb, j=j: e.tensor_tensor(
                    out=hT[:, j, :], in0=pg[:, 1, :], in1=sgb[:], op=ALU.mult),
                    reads=[pg.d(), sgb.d()], writes=[hT.d()])

        def down(t):
            x = xb[t % 2]
            for s in range(NS):
                pf = [pF[(s % 2) * 2], pF[(s % 2) * 2 + 1]]
                for half in range(2):
                    for j in range(22):
                        S.op("pe", lambda e, j=j, s=s, half=half, pf=pf: e.matmul(
                            pf[half][:, :], hT[:, j, s * 128:(s + 1) * 128],
                            wdn[:, j, half * 512:(half + 1) * 512], start=(j == 0), stop=(j == 21)),
                            reads=[hT.d(), wdn.d()], writes=[pf[half].d()])
                for half in range(2):
                    S.op("act", lambda e, half=half, pf=pf, s=s: e.activation(
                        out=junk[:, 0:512], in_=pf[half][:, :], func=AF.Square,
                        accum_out=st[:, 8 + 2 * s + half:9 + 2 * s + half]),
                        reads=[pf[half].d()], writes=[junk.d(), st.d(8 + 2 * s + half)])
                S.op("dve", lambda e, s=s: e.tensor_tensor(
                    out=st[:, 12 + s:13 + s], in0=st[:, 8 + 2 * s:9 + 2 * s],
                    in1=st[:, 9 + 2 * s:10 + 2 * s], op=ALU.add),
                    reads=[st.d(8 + 2 * s), st.d(9 + 2 * s)], writes=[st.d(12 + s)])
                S.op("dve", lambda e, s=s: e.tensor_scalar(
                    out=st[:, 12 + s:13 + s], in0=st[:, 12 + s:13 + s], scalar1=1.0 / D, scalar2=RMS_EPS,
                    op0=ALU.mult, op1=ALU.add), reads=[st.d(12 + s)], writes=[st.d(12 + s)])
                S.op("act", lambda e, s=s: e.activation(
                    out=st[:, 12 + s:13 + s], in_=st[:, 12 + s:13 + s], func=AF.Sqrt),
                    reads=[st.d(12 + s)], writes=[st.d(12 + s)])
                S.op("dve", lambda e, s=s: e.reciprocal(
                    out=st[:, 12 + s:13 + s], in_=st[:, 12 + s:13 + s]),
                    reads=[st.d(12 + s)], writes=[st.d(12 + s)])
                for half in range(2):
                    S.op("dve", lambda e, half=half, pf=pf: e.tensor_tensor(
                        out=tmp[:, half * 512:(half + 1) * 512], in0=pf[half][:, :],
                        in1=gpost[:, half * 512:(half + 1) * 512], op=ALU.mult),
                        reads=[pf[half].d(), gpost.d()], writes=[tmp.d()])
                o = ob[s % 2]
                S.op("dve", lambda e, s=s, o=o, x=x: e.scalar_tensor_tensor(
                    out=o[:], in0=tmp[:], scalar=st[:, 12 + s:13 + s], in1=x[:, s, :],
                    op0=ALU.mult, op1=ALU.add),
                    reads=[tmp.d(), st.d(12 + s), x.d()], writes=[o.d()])
                S.dma("pool", hv_out[t, s], o[:], reads=[o.d()])

        load(0)
        prenorm(0)
        for t in range(ntiles):
            if t + 1 < ntiles:
                load(t + 1)
            gu(t)
            if t + 1 < ntiles:
                prenorm(t + 1)
            down(t)
        S.barrier()


def _consts():
    import ml_dtypes
    ident = np.eye(128, dtype=np.float32).astype(ml_dtypes.bfloat16)
    return {"ident": ident}


class Ring:
    def __init__(self, views):
        self.views = views
        self.i = 0

    def get(self):
        v = self.views[self.i]
        self.i = (self.i + 1) % len(self.views)
        return v


HP_W0, HP_A0, HP_KK, HP_KA, HP_RK, HP_GNW, HP_GNB = range(7)
GN_EPS = 64e-5


def rwkv_phase(S, h_in, h_out, P, C, yT_dram, ndc=32, tag="r", stage=9, dbg=None):
    nc = S.nc
    import os
    RWBF = False
    F32R = BF16 if RWBF else mybir.dt.float32r
    PADDED = not RWBF

    def W(n):
        return 256 if PADDED else n
    HO = 0 if PADDED else 128
    with contextlib.ExitStack() as es:
        def SB(name, shape, dt):
            return sb(S, es, tag + name, shape, dt)

        Wb = SB("Wb", [128, 8, 3360], BF16)
        ww2 = SB("ww2", [64, D], BF16)
        wa2 = SB("wa2", [64, D], BF16)
        wg2a = SB("wg2a", [128, D], BF16)
        wg2b = SB("wg2b", [32, D], BF16)
        gpre = SB("gpre", [128, 8], F32)
        mu = SB("mu", [128, 6, 8], F32)
        hp = SB("hp", [64, 7, 16], F32)
        identb = SB("identb", [128, 128], BF16)
        identf = SB("identf", [128, 128], F32)
        identr = SB("identr", [128, 256], F32R)
        ones64 = SB("ones64", [64, 64], F32)
        ones64r = SB("ones64r", [64, 64], F32R)
        mask2 = SB("mask2", [128, 256], F32)
        masksl = SB("masksl", [128, 128], F32)
        scanm = SB("scanm", [64, 512], F32)
        xb = SB("xb", [128, D], F32)
        xn = SB("xn", [128, D], BF16)
        junk = SB("junk", [128, 512], BF16)
        uTx = [SB(f"uTx{i}", [128, 8, 129], BF16) for i in range(2)]
        xx = SB("xx", [128, 8, 128], BF16)
        mixb = [SB(f"mix{i}", [128, 8, 128], BF16) for i in range(4)]
        vtok = SB("vtok", [128, D + 256], F32R)
        th3 = SB("th3", [64, 128], BF16)
        p4b = SB("p4b", [64, 128], BF16)
        s5a = SB("s5a", [128, 128], BF16)
        s5b = SB("s5b", [32, 128], BF16)
        yfin = SB("yfin", [128, 8, 128], BF16)
        st = SB("st", [128, 8], F32)
        Sst = SB("Sst", [64, 20, 64], F32R)
        gamC = SB("gamC", [64, 16], F32)
        Q = {n: SB("q_" + n, [64, 4, 128], F32) for n in ["k", "sig", "a", "cs", "kk", "t1", "eneg", "epos", "gt1"]}
        Q["eexc"] = Q["sig"]
        Q["t1r"] = SB("q_t1r", [64, 4, 128], F32R)
        Q["gt1r"] = SB("q_gt1r", [64, 4, 128], F32R)
        QP = [{n: SB(f"qp{p}_" + n, [64, 4, 128], F32) for n in ["r", "kmod", "vT", "g", "y"]} for p in range(2)]
        ARs = [SB(f"AR{p}", [64, 4, 2, 128], F32R) for p in range(2)]
        BTs = [SB(f"BTb{p}", [64, 6, 128], F32R) for p in range(2)]
        KTs = [SB(f"KTb{p}", [64, 6, 128], F32R) for p in range(2)]
        NH = 4
        XB = [[SB(f"X{i}_{j}", [128, 512], F32R) for j in range(2)] for i in range(NH)]
        MRB = [SB(f"MRB{i}", [128, 256], F32R) for i in range(NH)]
        MKb = [SB(f"MK{i}", [128, 256], F32R) for i in range(NH)]
        AXb = [SB(f"AX{i}", [128, 256], F32R) for i in range(NH)]
        PQb = [SB(f"PQ{i}", [128, 320], F32R) for i in range(NH)]
        BKb = [SB(f"BK{i}", [128, 256], F32R) for i in range(NH)]
        GTb = [SB(f"GT{i}", [64, 64], F32R) for i in range(NH)]
        Hsb = [SB(f"Hs{i}", [64, 64], F32) for i in range(NH)]
        RhT = [SB(f"RhT{i}", [64, 256], F32R) for i in range(NH)]

        banks = [ps(S, es, tag + f"bk{i}", [128, 512], F32) for i in range(7)]
        bankT = ps(S, es, tag + "bkT", [128, 8, 128], BF16)
        ring_proj = Ring([(b[:, :], b.d()) for b in banks[0:1]])
        ringF = Ring([(b, b.d()) for b in banks[1:7]])

        for (t, src) in [(gpre, P["gpre_l"]), (mu, P["mu_l"]), (hp, P["hp"]),
                         (identb, C["identb"]), (identf, C["identf"]), (ones64, C["ones64"]),
                         (mask2, C["mask2"]), (masksl, C["masksl"]), (scanm, C["scanm"])]:
            S.dma("sp", t[:], src, writes=[t.d()])
        S.op("dve", lambda e: e.memset(uTx[1][:, :, 128:129], 0.0), writes=[uTx[1].d()])
        kcnt = [0]
        stg2 = Buf(xb.t)
        stg = [(xb, xb[:, 0:512]), (stg2, xb[:, 512:1024])]

        def conv(dst, view, b, scale=None):
            if scale is not None:
                S.op("act", lambda e: e.activation(out=dst, in_=view, func=AF.Copy, scale=scale),
                     reads=[b.d(), gpre.d()])
            elif kcnt[0] % 2 == 0:
                S.op("act", lambda e: e.activation(out=dst, in_=view, func=AF.Copy), reads=[b.d()])
            else:
                S.op("dve", lambda e: e.tensor_copy(out=dst, in_=view), reads=[b.d()])
            kcnt[0] += 1

        for c in range(8):
            for o in range(0, 3360, 512):
                wdt = min(512, 3360 - o)
                b, bv = stg[kcnt[0] % 2]
                S.dma("sp", bv[:, 0:wdt], P["w_in"][c * 128:(c + 1) * 128, o:o + wdt], writes=[b.d()])
                conv(Wb[:, c, o:o + wdt], bv[:, 0:wdt], b, scale=gpre[:, c:c + 1])
        for (dst, src, n) in [(ww2, P["w_w2"], 64), (wa2, P["w_a2"], 64), (wg2a, P["w_g2"][0:128, :], 128),
                              (wg2b, P["w_g2"][128:160, :], 32)]:
            for o in range(0, D, 512):
                b, bv = stg[kcnt[0] % 2]
                S.dma("sp", bv[0:n, :], src[:, o:o + 512], writes=[b.d()])
                conv(dst[0:n, o:o + 512], bv[0:n, :], b)
        S.barrier()

        S.op("dve", lambda e: e.memset(xb[:, 0:512], 0.0), writes=[xb.d()])

        def zero_r(buf, flat, nparts, width, deps):
            for o in range(0, width, 512):
                w_ = min(512, width - o)
                S.op("pool", lambda e, o=o, w_=w_: e.tensor_copy(out=flat[0:nparts, o:o + w_],
                                                                 in_=xb[0:nparts, 0:w_]),
                     reads=[xb.d()], writes=deps)
        zero_r(Sst, Sst[:].rearrange("p a b -> p (a b)"), 64, 20 * 64, [Sst.d(h) for h in range(16)])
        zero_r(identr, identr[:, 128:256], 128, 128, [identr.d()])
        S.op("dve", lambda e: e.tensor_copy(out=identr[:, 0:128], in_=identf[:]), reads=[identf.d()],
             writes=[identr.d()])
        S.op("dve", lambda e: e.tensor_copy(out=ones64r[:], in_=ones64[:]), reads=[ones64.d()],
             writes=[ones64r.d()])
        zero_r(vtok, vtok[:, :], 128, D + 256, [vtok.d()])
        for p_ in range(2):
            zero_r(BTs[p_], BTs[p_][:].rearrange("p a b -> p (a b)"), 64, 6 * 128, [BTs[p_].d()])
            zero_r(KTs[p_], KTs[p_][:].rearrange("p a b -> p (a b)"), 64, 6 * 128, [KTs[p_].d()])
        for t_ in MRB + MKb + AXb + BKb:
            zero_r(t_, t_[:, :], 128, 256, [t_.d()])
        for t_ in [b for row in XB for b in row]:
            zero_r(t_, t_[:, :], 128, 512, [t_.d()])
        for t_ in PQb:
            zero_r(t_, t_[:, :], 128, 320, [t_.d()])
        for t_ in RhT:
            zero_r(t_, t_[:, :], 64, 256, [t_.d()])
        S.barrier()

        hv_in = h_in.rearrange("(t p) d -> t p d", p=128)
        hv_out = h_out.rearrange("(t p) d -> t p d", p=128)

        def bc(idx, q):
            return hp[:, idx, 4 * q:4 * q + 4].unsqueeze(2).to_broadcast([64, 4, 128])

        def f2(b):
            return b[:].rearrange("p a b -> p (a b)")

        def rstd_ops(src_col, dst_col):
            S.op("dve", lambda e: e.tensor_scalar(out=st[:, dst_col:dst_col + 1], in0=st[:, src_col:src_col + 1],
                                                  scalar1=1.0 / D, scalar2=RMS_EPS, op0=ALU.mult, op1=ALU.add),
                 reads=[st.d(src_col)], writes=[st.d(dst_col)])
            S.op("act", lambda e: e.activation(out=st[:, dst_col:dst_col + 1], in_=st[:, dst_col:dst_col + 1],
                                               func=AF.Sqrt), reads=[st.d(dst_col)], writes=[st.d(dst_col)])
            S.op("dve", lambda e: e.reciprocal(out=st[:, dst_col:dst_col + 1], in_=st[:, dst_col:dst_col + 1]),
                 reads=[st.d(dst_col)], writes=[st.d(dst_col)])

        def mm(out, lhsT, rhs, reads, wdep_, start=True, stop=True):
            S.op("pe", lambda e: e.matmul(out, lhsT, rhs, start=start, stop=stop), reads=reads, writes=[wdep_])

        def tt(eng, out, in0, in1, op, reads, writes):
            S.op(eng, lambda e: e.tensor_tensor(out=out, in0=in0, in1=in1, op=op), reads=reads, writes=writes)

        def actf(out, in_, func, reads, writes, **kw):
            S.op("act", lambda e: e.activation(out=out, in_=in_, func=func, **kw), reads=reads, writes=writes)

        def make_mix(i, m, cur):
            for c in range(8):
                S.op("dve", lambda e, c=c: e.scalar_tensor_tensor(
                    out=m[:, c, :], in0=xx[:, c, :], scalar=mu[:, i, c:c + 1], in1=cur[:, c, 1:129],
                    op0=ALU.mult, op1=ALU.add), reads=[xx.d(), cur.d()], writes=[m.d()])
            return m

        Sf = Sst[:].rearrange("p a b -> p (a b)")
        yT_v = yT_dram.rearrange("(c p) t -> p c t", p=128)

        def front(h, slot, q):
            j = h % 4
            AR, BTb, KTb = ARs[q % 2], BTs[q % 2], KTs[q % 2]
            BTf = BTb[:].rearrange("p a b -> p (a b)")
            ARcat = AR[:, j, :, :].rearrange("p a b -> p (a b)")
            AT = AR[:, j, 0, :]
            RT = AR[:, j, 1, :]
            BTh = BTb[:, j, :]
            KTh = KTb[:, j, :]
            vpad = vtok[:, h * 64:h * 64 + W(64)]
            mrb, mk = MRB[slot], MKb[slot]
            x0, x1 = XB[slot]
            ax, pq, bk = AXb[slot], PQb[slot], BKb[slot]
            pb, pd = ringF.get()
            mm(pb[:, 0:256], BTh, ARcat, [BTb.d(), AR.d()], pd)
            tt("dve", x0[:, 0:128], pb[:, 0:128], mask2[:, 0:128], ALU.mult, [pd], [x0.d()])
            tt("dve", mrb[:, 128:256], pb[:, 128:256], mask2[:, 128:256], ALU.mult, [pd], [mrb.d()])
            actf(x0[:, 128:256], identr[:, 0:128], AF.Copy, [identr.d()], [x0.d()])
            pb, pd = ringF.get()
            mm(pb[:, 0:256], KTh, ARcat, [KTb.d(), AR.d()], pd)
            tt("dve", mk[:, 0:256], pb[:, 0:256], mask2[:], ALU.mult, [pd], [mk.d()])
            yield
            pb, pd = ringF.get()
            mm(pb[:, 0:W(128)], AT, BTf[:, j * 128:j * 128 + W(128)], [BTb.d(), AR.d()], pd)
            tt("dve", x0[:, 256:384], pb[:, 0:128], masksl[:], ALU.mult, [pd], [x0.d()])
            pb2, pd2 = ringF.get()
            mm(pb2[:, 0:W(64)], BTh, identr[0:64, 0:W(64)], [BTb.d()], pd2)
            mm(pb2[:, 64:64 + W(64)], KTh, identr[0:64, 0:W(64)], [KTb.d()], pd2)
            actf(bk[:, 0:128], pb2[:, 0:128], AF.Copy, [pd2], [bk.d()])
            yield
            Xc = x0
            for jj in range(1, 7):
                Xn = x1 if Xc is x0 else x0
                pb, pd = ringF.get()
                mm(pb[:, 0:256], Xc[:, 256:384], Xc[:, 0:256], [Xc.d()], pd)
                mm(pb[:, 256:256 + W(128)], Xc[:, 0:128], Xc[:, 256:256 + W(128)], [Xc.d()], pd)
                S.op("act", lambda e, pb=pb, Xn=Xn: e.activation(
                    out=Xn[:, :].rearrange("p (a b) -> p a b", a=2)[:, :, 0:128],
                    in_=pb[:, :].rearrange("p (a b) -> p a b", a=2)[:, :, 0:128], func=AF.Copy),
                    reads=[pd], writes=[Xn.d(), pd])
                tt("dve", Xn[:, 128:256], pb[:, 128:256], Xc[:, 128:256], ALU.add, [pd, Xc.d()], [Xn.d(), pd])
                if jj == 1:
                    pb3, pd3 = ringF.get()
                    mm(pb3[:, 0:W(64)], AT, identr[0:64, 0:W(64)], [AR.d()], pd3)
                    mm(pb3[:, 64:64 + W(64)], mk[:, 0:128], vpad, [mk.d(), vtok.d()], pd3)
                    actf(ax[:, 0:128], pb3[:, 0:128], AF.Copy, [pd3], [ax.d()])
                yield
                Xc = Xn
            Rfin = x1 if Xc is x0 else x0
            pb, pd = ringF.get()
            mm(pb[:, 0:256 - HO], Xc[:, 256:384], Xc[:, HO:256], [Xc.d()], pd)
            tt("dve", Rfin[:, 0:128], pb[:, 128 - HO:256 - HO], Xc[:, 128:256], ALU.add, [pd, Xc.d()], [Rfin.d()])
            yield
            pb, pd = ringF.get()
            mm(pb[:, 0:W(128)], Rfin[:, 0:128], ax[:, 0:W(128)], [Rfin.d(), ax.d()], pd)
            actf(pq[:, 0:128], pb[:, 0:128], AF.Copy, [pd], [pq.d()])
            yield
            gt, hs, rh = GTb[slot], Hsb[slot], RhT[slot]
            pb, pd = ringF.get()
            mm(pb[0:64, 0:W(64)], pq[:, 0:64], bk[:, 0:W(64)], [pq.d(), bk.d()], pd)
            tt("dve", gt[:], pb[0:64, 0:64], identf[0:64, 0:64], ALU.add, [pd], [gt.d()])
            pb2, pd2 = ringF.get()
            mm(pb2[0:64, 0:W(64)], bk[:, 0:64], pq[:, 64:64 + W(64)], [pq.d(), bk.d()], pd2, True, False)
            mm(pb2[0:64, 0:W(64)], bk[:, 64:128], vpad, [bk.d(), vtok.d()], pd2, False, True)
            S.op("dve", lambda e: e.tensor_scalar(out=hs[:], in0=pb2[0:64, 0:64], scalar1=gamC[:, h:h + 1],
                                                  scalar2=None, op0=ALU.mult),
                 reads=[pd2, gamC.d(q)], writes=[hs.d()])
            pb3, pd3 = ringF.get()
            mm(pb3[0:64, 0:256 - HO], pq[:, 0:64], mrb[:, HO:256], [pq.d(), mrb.d()], pd3)
            tt("dve", rh[:, 128:256], pb3[0:64, 128 - HO:256 - HO], RT, ALU.add, [pd3, AR.d()], [rh.d()])
            yield

        def back(h, slot, q):
            j = h % 4
            y = QP[q % 2]["y"]
            vh = vtok[:, h * 64:(h + 1) * 64]
            mrb, mk, pq, gt, hs, rh = MRB[slot], MKb[slot], PQb[slot], GTb[slot], Hsb[slot], RhT[slot]
            pb, pd = ringF.get()
            mm(pb[0:64, 0:256 - HO], pq[:, 64:128], mrb[:, HO:256], [pq.d(), mrb.d()], pd, True, False)
            mm(pb[0:64, 0:256 - HO], vh, mk[:, HO:256], [vtok.d(), mk.d()], pd, False, False)
            mm(pb[0:64, 0:256 - HO], Sst[:, h, :], rh[:, HO:256], [Sst.d(h), rh.d()], pd, False, True)
            actf(y[:, j, :], pb[0:64, 128 - HO:256 - HO], AF.Copy, [pd], [y.d()])
            pb2, pd2 = ringF.get()
            mm(pb2[0:64, 0:W(64)], gt[:], Sf[:, h * 64:h * 64 + W(64)], [gt.d(), Sst.d(h)], pd2)
            S.op("dve", lambda e: e.scalar_tensor_tensor(
                out=Sst[:, h, :], in0=pb2[0:64, 0:64], scalar=gamC[:, h:h + 1], in1=hs[:],
                op0=ALU.mult, op1=ALU.add), reads=[pd2, gamC.d(q), hs.d()], writes=[Sst.d(h)])

        mixes = {}

        def stageA(dc):
            cur, prv = uTx[dc % 2], uTx[(dc + 1) % 2]
            S.dma("sp", xb[:], hv_in[dc], writes=[xb.d()])
            actf(xn[:], xb[:], AF.Square, [xb.d()], [xn.d(), st.d(0)], accum_out=st[:, 0:1])
            rstd_ops(0, 1)
            actf(xn[:], xb[:], AF.Copy, [xb.d(), st.d(1)], [xn.d()], scale=st[:, 1:2])
            for c in range(8):
                S.op("pe", lambda e, c=c: e.transpose(out=bankT[:, c, :], in_=xn[:, c * 128:(c + 1) * 128],
                                                      identity=identb[:]), reads=[xn.d()], writes=[bankT.d()])
            S.op("dve", lambda e: e.tensor_copy(out=cur[:, :, 1:129], in_=bankT[:, :, :]),
                 reads=[bankT.d()], writes=[cur.d()])
            S.op("dve", lambda e: e.tensor_copy(out=cur[:, :, 0:1], in_=prv[:, :, 128:129]),
                 reads=[prv.d()], writes=[cur.d()])
            tt("dve", xx[:], cur[:, :, 0:128], cur[:, :, 1:129], ALU.subtract, [cur.d()], [xx.d()])
            yield
            m3 = make_mix(3, mixb[3], cur)
            pv_, pd = ring_proj.get()
            for c in range(8):
                mm(pv_[0:64, 0:128], Wb[:, c, 3072:3136], m3[:, c, :], [m3.d()], pd, c == 0, c == 7)
            actf(th3[:], pv_[0:64, 0:128], AF.Tanh, [pd], [th3.d()])
            yield
            m4 = make_mix(4, mixb[3], cur)
            pv_, pd = ring_proj.get()
            for c in range(8):
                mm(pv_[0:64, 0:128], Wb[:, c, 3136:3200], m4[:, c, :], [m4.d()], pd, c == 0, c == 7)
            actf(p4b[:], pv_[0:64, 0:128], AF.Copy, [pd], [p4b.d()])
            yield
            m5 = make_mix(5, mixb[3], cur)
            pv_, pd = ring_proj.get()
            for c in range(8):
                mm(pv_[:, 0:128], Wb[:, c, 3200:3328], m5[:, c, :], [m5.d()], pd, c == 0, c == 7)
            for c in range(8):
                mm(pv_[0:32, 128:256], Wb[:, c, 3328:3360], m5[:, c, :], [m5.d()], pd, c == 0, c == 7)
            actf(s5a[:], pv_[:, 0:128], AF.Sigmoid, [pd], [s5a.d()])
            actf(s5b[:], pv_[0:32, 128:256], AF.Sigmoid, [pd], [s5b.d()])
            yield
            mixes[0] = make_mix(0, mixb[0], cur)
            yield
            mixes[1] = make_mix(1, mixb[1], cur)
            yield
            mixes[2] = make_mix(2, mixb[2], cur)
            yield

        def stageA2(dc):
            m2 = mixes[2]
            for half in range(2):
                bb, bbd = ring_proj.get()
                for c in range(8):
                    mm(bb, m2[:, c, :], Wb[:, c, 2048 + half * 512:2048 + (half + 1) * 512],
                       [m2.d()], bbd, c == 0, c == 7)
                actf(vtok[:, half * 512:(half + 1) * 512], bb, AF.Copy, [bbd], [vtok.d()])
                yield

        def prep(q):
            par = q % 2
            PENG = os.environ.get('RW_PENG', 'dve')
            AR, BTb, KTb, qp = ARs[par], BTs[par], KTs[par], QP[par]

            def evac_pairs(pv2, pd2, dst, eng="act"):
                src = pv2[:, 0:256].rearrange("p (a b) -> p a b", a=2)
                dv = dst[:].rearrange("p (a two) b -> p a two b", two=2)
                for half in range(2):
                    if eng == "act":
                        actf(dv[:, :, half, :], src[64 * half:64 * half + 64], AF.Copy, [pd2], [dst.d()])
                    else:
                        S.op("dve", lambda e, half=half: e.tensor_copy(out=dv[:, :, half, :],
                                                                       in_=src[64 * half:64 * half + 64]),
                             reads=[pd2], writes=[dst.d()])

            def proj4(mbuf, colbase, dst, eng="act"):
                pv2, pd2 = ring_proj.get()
                for pi in range(2):
                    pr = 2 * q + pi
                    for c in range(8):
                        mm(pv2[:, pi * 128:(pi + 1) * 128], Wb[:, c, colbase + pr * 128:colbase + (pr + 1) * 128],
                           mbuf[:, c, :], [mbuf.d()], pd2, c == 0, c == 7)
                evac_pairs(pv2, pd2, dst, eng)

            proj4(mixes[0], 0, qp["r"])
            yield
            proj4(mixes[1], 1024, Q["k"], "dve")
            yield
            proj4(mixes[2], 2048, qp["vT"])
            yield
            pv2, pd2 = ring_proj.get()
            for pi in range(2):
                pr = 2 * q + pi
                mm(pv2[:, pi * 128:(pi + 1) * 128], ww2[:, pr * 128:(pr + 1) * 128], th3[:], [th3.d()], pd2)
            evac_pairs(pv2, pd2, Q["sig"], "dve")
            tt(PENG, Q["sig"][:], Q["sig"][:], bc(HP_W0, q), ALU.add, [Q["sig"].d()], [Q["sig"].d()])
            actf(f2(Q["sig"]), f2(Q["sig"]), AF.Sigmoid, [Q["sig"].d()], [Q["sig"].d()])
            yield
            pv2, pd2 = ring_proj.get()
            for pi in range(2):
                pr = 2 * q + pi
                mm(pv2[:, pi * 128:(pi + 1) * 128], wa2[:, pr * 128:(pr + 1) * 128], p4b[:], [p4b.d()], pd2)
            evac_pairs(pv2, pd2, Q["a"], "dve")
            tt(PENG, Q["a"][:], Q["a"][:], bc(HP_A0, q), ALU.add, [Q["a"].d()], [Q["a"].d()])
            actf(f2(Q["a"]), f2(Q["a"]), AF.Sigmoid, [Q["a"].d()], [Q["a"].d()])
            yield
            pv2, pd2 = ring_proj.get()
            for pi in range(2):
                pr = 2 * q + pi
                mm(pv2[:, pi * 128:(pi + 1) * 128], wg2a[:, pr * 128:(pr + 1) * 128], s5a[:], [s5a.d()], pd2,
                   True, False)
                mm(pv2[:, pi * 128:(pi + 1) * 128], wg2b[:, pr * 128:(pr + 1) * 128], s5b[:], [s5b.d()], pd2,
                   False, True)
            evac_pairs(pv2, pd2, qp["g"])
            yield
            S.op(PENG, lambda e: e.tensor_scalar(out=f2(Q["sig"]), in0=f2(Q["sig"]), scalar1=-0.6065306597126334,
                                                  scalar2=None, op0=ALU.mult),
                 reads=[Q["sig"].d()], writes=[Q["sig"].d()])
            S.op("dve", lambda e: e.tensor_tensor_scan(out=f2(Q["cs"]), data0=scanm[:], data1=f2(Q["sig"]),
                                                       initial=0.0, op0=ALU.mult, op1=ALU.add),
                 reads=[Q["sig"].d()], writes=[Q["cs"].d()])
            tt(PENG, Q["kk"][:], Q["k"][:], bc(HP_KK, q), ALU.mult, [Q["k"].d()], [Q["kk"].d()])
            actf(f2(Q["t1r"]), f2(Q["kk"]), AF.Square, [Q["kk"].d()], [Q["t1r"].d()])
            yield
            pv2, pd2 = ring_proj.get()
            mm(pv2[0:64, :], ones64r[:], f2(Q["t1r"]), [Q["t1r"].d()], pd2)
            actf(f2(Q["t1"]), pv2[0:64, :], AF.Sqrt, [pd2], [Q["t1"].d()])
            S.op("dve", lambda e: e.tensor_scalar(out=f2(Q["t1"]), in0=f2(Q["t1"]), scalar1=1e-12, scalar2=None,
                                                  op0=ALU.max), reads=[Q["t1"].d()], writes=[Q["t1"].d()])
            S.op("dve", lambda e: e.reciprocal(out=f2(Q["t1"]), in_=f2(Q["t1"])),
                 reads=[Q["t1"].d()], writes=[Q["t1"].d()])
            tt(PENG, Q["kk"][:], Q["kk"][:], Q["t1"][:], ALU.mult, [Q["kk"].d(), Q["t1"].d()], [Q["kk"].d()])
            yield
            S.op("dve", lambda e: e.scalar_tensor_tensor(out=Q["t1"][:], in0=Q["a"][:], scalar=-1.0, in1=bc(HP_KA, q),
                                                         op0=ALU.add, op1=ALU.mult),
                 reads=[Q["a"].d()], writes=[Q["t1"].d()])
            S.op("dve", lambda e: e.scalar_tensor_tensor(out=f2(qp["kmod"]), in0=f2(Q["t1"]), scalar=1.0,
                                                         in1=f2(Q["k"]), op0=ALU.add, op1=ALU.mult),
                 reads=[Q["t1"].d(), Q["k"].d()], writes=[qp["kmod"].d()])
            actf(f2(Q["eneg"]), f2(Q["cs"]), AF.Exp, [Q["cs"].d()], [Q["eneg"].d()], scale=-1.0)
            actf(f2(Q["epos"]), f2(Q["cs"]), AF.Exp, [Q["cs"].d()], [Q["epos"].d()])
            yield
            tt(PENG, Q["eexc"][:], Q["cs"][:], Q["sig"][:], ALU.subtract, [Q["cs"].d(), Q["sig"].d()],
               [Q["eexc"].d()])
            actf(f2(Q["eexc"]), f2(Q["eexc"]), AF.Exp, [Q["eexc"].d()], [Q["eexc"].d()])
            S.op("dve", lambda e: e.tensor_copy(out=gamC[:, 4 * q:4 * q + 4], in_=Q["epos"][:, :, 127]),
                 reads=[Q["epos"].d()], writes=[gamC.d(q)])
            yield
            S.op("dve", lambda e: e.scalar_tensor_tensor(out=AR[:, :, 0, :], in0=Q["kk"][:], scalar=-1.0,
                                                         in1=Q["eexc"][:], op0=ALU.mult, op1=ALU.mult),
                 reads=[Q["kk"].d(), Q["eexc"].d()], writes=[AR.d()])
            tt(PENG, AR[:, :, 1, :], qp["r"][:], Q["epos"][:], ALU.mult, [qp["r"].d(), Q["epos"].d()], [AR.d()])
            yield
            tt(PENG, Q["t1"][:], Q["kk"][:], Q["a"][:], ALU.mult, [Q["kk"].d(), Q["a"].d()], [Q["t1"].d()])
            tt(PENG, BTb[:, 0:4, :], Q["t1"][:], Q["eneg"][:], ALU.mult, [Q["t1"].d(), Q["eneg"].d()], [BTb.d()])
            tt(PENG, KTb[:, 0:4, :], qp["kmod"][:], Q["eneg"][:], ALU.mult, [qp["kmod"].d(), Q["eneg"].d()],
               [KTb.d()])
            yield

        def gn(q, dc):
            qp = QP[q % 2]
            y, t1, t1r = qp["y"], Q["gt1"], Q["gt1r"]
            heads = [4 * q + j for j in range(4)]
            actf(f2(t1r), f2(y), AF.Copy, [y.d()], [t1r.d()])
            pv2, pd2 = ring_proj.get()
            mm(pv2[0:64, :], ones64r[:], f2(t1r), [t1r.d()], pd2)
            S.op("dve", lambda e: e.tensor_scalar(out=f2(t1), in0=pv2[0:64, :], scalar1=1.0 / 64, scalar2=None,
                                                  op0=ALU.mult), reads=[pd2], writes=[t1.d()])
            tt("dve", y[:], y[:], t1[:], ALU.subtract, [y.d(), t1.d()], [y.d()])
            actf(f2(t1r), f2(y), AF.Square, [y.d()], [t1r.d()])
            yield
            pv2, pd2 = ring_proj.get()
            mm(pv2[0:64, :], ones64r[:], f2(t1r), [t1r.d()], pd2)
            S.op("dve", lambda e: e.tensor_scalar(out=f2(t1), in0=pv2[0:64, :], scalar1=1.0 / 64,
                                                  scalar2=GN_EPS, op0=ALU.mult, op1=ALU.add),
                 reads=[pd2], writes=[t1.d()])
            actf(f2(t1), f2(t1), AF.Sqrt, [t1.d()], [t1.d()])
            S.op("dve", lambda e: e.reciprocal(out=f2(t1), in_=f2(t1)), reads=[t1.d()], writes=[t1.d()])
            tt("dve", y[:], y[:], t1[:], ALU.mult, [y.d(), t1.d()], [y.d()])
            yield
            tt("pool", y[:], y[:], bc(HP_GNW, q), ALU.mult, [y.d()], [y.d()])
            tt("pool", y[:], y[:], bc(HP_GNB, q), ALU.add, [y.d()], [y.d()])
            tt("pool", t1[:], qp["r"][:], qp["kmod"][:], ALU.mult, [qp["r"].d(), qp["kmod"].d()], [t1.d()])
            tt("dve", t1r[:], t1[:], bc(HP_RK, q), ALU.mult, [t1.d()], [t1r.d()])
            yield
            pv2, pd2 = ring_proj.get()
            mm(pv2[0:64, :], ones64r[:], f2(t1r), [t1r.d()], pd2)
            tt("dve", t1[:], pv2[0:64, :].rearrange("p (a b) -> p a b", a=4), qp["vT"][:], ALU.mult,
               [pd2, qp["vT"].d()], [t1.d()])
            tt("pool", y[:], y[:], t1[:], ALU.add, [y.d(), t1.d()], [y.d()])
            yield
            for j, h in enumerate(heads):
                po = 64 * (h % 2)
                tt("dve", yfin[po:po + 64, h // 2, :], y[:, j, :], qp["g"][:, j, :], ALU.mult,
                   [y.d(), qp["g"].d()], [yfin.d()])
            if q == 3:
                S.dma("pool", yT_v[:, :, dc * 128:(dc + 1) * 128], yfin[:, :, :], reads=[yfin.d()])
            yield

        def chain(*gens):
            for g_ in gens:
                if g_ is not None:
                    yield from g_

        XSTEP = int(os.environ.get("RW_XSTEP", "1"))

        def run(gens, extra=None):
            alive = [(g_, 1) for g_ in gens if g_ is not None]
            if extra is not None:
                if os.environ.get("RW_XFIRST"):
                    alive.insert(0, (extra, XSTEP))
                else:
                    alive.append((extra, XSTEP))
            while alive:
                nxt = []
                for g_, k_ in alive:
                    ok = True
                    for _ in range(k_):
                        try:
                            next(g_)
                        except StopIteration:
                            ok = False
                            break
                    if ok:
                        nxt.append((g_, k_))
                alive = nxt

        run([chain(stageA(0), stageA2(0), prep(0))])
        gn_prev = None
        for dc in range(ndc):
            more = dc + 1 < ndc
            for q in range(4):
                heads = [4 * q + j for j in range(4)]
                if q < 3:
                    extra = chain(gn(q - 1, dc) if q > 0 else None, prep(q + 1))
                else:
                    extra = chain(gn(2, dc), stageA(dc + 1) if more else None, prep(0) if more else None)
                run([front(h, j, q) for j, h in enumerate(heads)], extra)
                for j, h in enumerate(heads):
                    back(h, j, q)
            gn_prev = gn(3, dc)
            run([stageA2(dc + 1) if more else None, gn_prev])
            gn_prev = None
        S.barrier()
    outproj_phase(S, h_in, h_out, yT_dram, P["w_out"], P["gpost_b"], tag + "o", ntile=ndc)


def rwkv_consts():
    import ml_dtypes
    i = np.arange(128)
    su = (i[:, None] < i[None, :]).astype(np.float32)
    u = (i[:, None] <= i[None, :]).astype(np.float32)
    scanm = np.ones((64, 512), np.float32)
    scanm[:, ::128] = 0.0
    return {
        "identb": np.eye(128, dtype=np.float32).astype(ml_dtypes.bfloat16),
        "identf": np.eye(128, dtype=np.float32),
        "ones64": np.ones((64, 64), np.float32),
        "mask2": np.ascontiguousarray(np.concatenate([su, u], axis=1)),
        "masksl": np.ascontiguousarray(su.T),
        "scanm": scanm,
    }


def _pl(v):
    return np.ascontiguousarray(np.asarray(v, np.float32).reshape(8, 128).T)


def _bcast(v):
    return np.ascontiguousarray(np.broadcast_to(np.asarray(v, np.float32), (128, D)))


def rwkv_host_params(mix_norm_pre, mix_norm_post, mu, w0, a0, k_k, k_a, r_k, gn_w, gn_b):
    hp = np.stack([np.asarray(t, np.float32).reshape(16, 64) for t in
                   (w0, a0, k_k, k_a, r_k, gn_w, gn_b)], axis=0)
    return {
        "gpre_l": _pl(mix_norm_pre), "gpost_b": _bcast(mix_norm_post),
        "mu_l": np.ascontiguousarray(np.asarray(mu, np.float32).reshape(6, 8, 128).transpose(2, 0, 1)),
        "hp": np.ascontiguousarray(hp.transpose(2, 0, 1)),
    }


def outproj_phase(S, h_in, h_out, attnT, w_out, gpost_b, tag, ntile=32):
    P = {"w_out": w_out, "gpost_b": gpost_b}

    def mmf(out, lhsT, rhs, reads, wdep_, start=True, stop=True):
        S.op("pe", lambda e: e.matmul(out, lhsT, rhs, start=start, stop=stop), reads=reads, writes=[wdep_])
    with contextlib.ExitStack() as es:
        def SB(name, shape, dt):
            return sb(S, es, tag + name, shape, dt)
        wo = SB("wo", [128, 8, D], BF16)
        gpost = SB("gpost", [128, D], F32)
        stg = [SB(f"stg{i}", [128, 1024], F32) for i in range(2)]
        aT = [SB(f"aT{i}", [128, 8, 128], BF16) for i in range(2)]
        xb = [SB(f"xb{i}", [128, D], F32) for i in range(2)]
        tmp = [SB(f"tmp{i}", [128, D], F32) for i in range(2)]
        junk = SB("junk", [128, 512], BF16)
        st = SB("st", [128, 8], F32)
        bk = [ps(S, es, tag + f"bk{i}", [128, 512], F32) for i in range(2)]
        S.dma("sp", gpost[:], P["gpost_b"], writes=[gpost.d()])
        for c in range(8):
            b = stg[c % 2]
            S.dma("sp", b[:, :], P["w_out"][c * 128:(c + 1) * 128, :], writes=[b.d()])
            S.op("act", lambda e, c=c, b=b: e.activation(out=wo[:, c, :], in_=b[:, :], func=AF.Copy), reads=[b.d()])
        S.barrier()
        a_v = attnT.rearrange("(c p) t -> p c t", p=128)
        hv_in = h_in.rearrange("(t p) d -> t p d", p=128)
        hv_out = h_out.rearrange("(t p) d -> t p d", p=128)
        for t in range(ntile):
            a, x, tm = aT[t % 2], xb[t % 2], tmp[t % 2]
            S.dma("sp", a[:, :, :], a_v[:, :, t * 128:(t + 1) * 128], writes=[a.d()])
            S.dma("sp", x[:], hv_in[t], writes=[x.d()])
            for half in range(2):
                b = bk[half]
                for c in range(8):
                    mmf(b[:, :], a[:, c, :], wo[:, c, half * 512:(half + 1) * 512], [a.d()], b.d(), c == 0, c == 7)
                S.op("act", lambda e, b=b, half=half: e.activation(out=junk[:], in_=b[:, :], func=AF.Square,
                                                                   accum_out=st[:, half:half + 1]),
                     reads=[b.d()], writes=[junk.d(), st.d(half), b.d("port")])
                S.op("dve", lambda e, b=b, half=half, tm=tm: e.tensor_tensor(
                    out=tm[:, half * 512:(half + 1) * 512], in0=b[:, :], in1=gpost[:, half * 512:(half + 1) * 512],
                    op=ALU.mult), reads=[b.d()], writes=[tm.d(), b.d("port")])
            S.op("dve", lambda e: e.tensor_tensor(out=st[:, 2:3], in0=st[:, 0:1], in1=st[:, 1:2], op=ALU.add),
                 reads=[st.d(0), st.d(1)], writes=[st.d(2)])
            S.op("dve", lambda e: e.tensor_scalar(out=st[:, 2:3], in0=st[:, 2:3], scalar1=1.0 / D, scalar2=RMS_EPS,
                                                  op0=ALU.mult, op1=ALU.add), reads=[st.d(2)], writes=[st.d(2)])
            S.op("act", lambda e: e.activation(out=st[:, 2:3], in_=st[:, 2:3], func=AF.Sqrt),
                 reads=[st.d(2)], writes=[st.d(2)])
            S.op("dve", lambda e: e.reciprocal(out=st[:, 2:3], in_=st[:, 2:3]), reads=[st.d(2)], writes=[st.d(2)])
            S.op("dve", lambda e, tm=tm, x=x: e.scalar_tensor_tensor(out=tm[:], in0=tm[:], scalar=st[:, 2:3], in1=x[:],
                                                                    op0=ALU.mult, op1=ALU.add),
                 reads=[tm.d(), st.d(2), x.d()], writes=[tm.d()])
            S.dma("pool", hv_out[t], tm[:], reads=[tm.d()])
        S.barrier()


NSA_IN = 2608
NEG = -30000.0


def nsa_consts():
    import ml_dtypes
    bf = ml_dtypes.bfloat16
    slopes = (2.0 ** (-8.0 * np.arange(1, 17, dtype=np.float64) / 16)).astype(np.float32)
    t = np.arange(4096, dtype=np.float64)
    aq = np.zeros((16, 3, 4096), dtype=bf)
    for h in range(16):
        v = (-slopes[h].astype(np.float64) * t).astype(np.float32)
        r = v.copy()
        for k in range(3):
            p = r.astype(bf)
            aq[h, k] = p
            r = (r - p.astype(np.float32)).astype(np.float32)
    j = np.arange(128)
    i = np.arange(512)
    bias_key = np.zeros((128, 16, 32), np.float32)
    bias_cmp = np.zeros((128, 16, 2), np.float32)
    for h in range(16):
        for kt in range(32):
            bias_key[:, h, kt] = slopes[h] * (128 * kt + j)
        for nt in range(2):
            bias_cmp[:, h, nt] = slopes[h] * (16 * (128 * nt + j) + 31)
    cmpmask = np.zeros((128, 8, 512), np.float32)
    for idx in range(8):
        cmpmask[:, idx, :] = np.where(16 * j[:, None] + 31 - i[None, :] <= 512 * idx, 0.0, NEG)
    causal = np.zeros((128, 4, 512), np.float32)
    for r_ in range(4):
        causal[:, r_, :] = np.where(j[:, None] + 128 * r_ <= i[None, :], 0.0, NEG)
    win = np.zeros((128, 8, 512), np.float32)
    for w in range(8):
        dist = i[None, :] - j[:, None] - 128 * (w - 4)
        win[:, w, :] = np.where((dist >= 0) & (dist < 512), 0.0, NEG)
    E = np.zeros((64, 4096), np.float32)
    E[np.arange(4096) // 64, np.arange(4096)] = 1.0
    cs_ = np.arange(255) * 16
    bs_ = np.arange(64) * 64
    ov = np.clip(np.minimum(cs_[:, None] + 32, bs_[None, :] + 64) - np.maximum(cs_[:, None], bs_[None, :]), 0, None) / 32.0
    Caug = np.zeros((256, 65), np.float32)
    Caug[:255, :64] = ov
    Caug[:255, 64] = 1.0
    Caug = Caug.reshape(2, 128, 65).transpose(1, 0, 2)
    vis = np.zeros((128, 32, 64), np.float32)
    add = np.zeros((128, 32, 64), np.float32)
    s = np.arange(64)
    for it in range(32):
        tq = 128 * it + j
        cur = tq // 64
        visible = s[None, :] * 64 <= tq[:, None]
        a = np.where(visible, 0.0, -1.0)
        v = visible.astype(np.float32)
        for (cond, val) in [(s[None, :] == 0, 1e4), (s[None, :] == cur[:, None], 2e4),
                            (s[None, :] == cur[:, None] - 1, 3e4)]:
            a = np.where(cond, val, a)
            v = np.where(cond, 0.0, v)
        vis[:, it, :] = v
        add[:, it, :] = a
    return {
        "alibi_q": aq, "bias_key": bias_key, "bias_cmp": bias_cmp,
        "cmpmask": cmpmask.astype(bf), "causal": causal.astype(bf), "winmask": win.astype(bf),
        "E_all": np.concatenate([E[1:64], np.ones((1, 4096), np.float32)], axis=0).astype(bf), "C_aug": np.ascontiguousarray(Caug).astype(bf),
        "vis": vis, "addt": add,
        "identb": np.eye(128, dtype=np.float32).astype(bf),
    }


def nsa_host_params(mix_norm_pre, mix_norm_post, pe_k, w_ck1, pe_v, w_cv1):
    return {
        "gpre_l": _pl(mix_norm_pre), "gpost_b": _bcast(mix_norm_post),
        "pekT": np.ascontiguousarray(np.asarray(pe_k, np.float32).T),
        "pevT": np.ascontiguousarray(np.asarray(pe_v, np.float32).T),
        "wck1_l": np.ascontiguousarray(np.asarray(w_ck1, np.float32).transpose(1, 0, 2)),
        "wcv1_l": np.ascontiguousarray(np.asarray(w_cv1, np.float32).transpose(1, 0, 2)),
    }


def nsa_phase(S, h_in, h_out, P, C, scr, tag="n"):
    nc = S.nc
    projT, vtokd, gTd, attnT = scr["projT"], scr["vtok"], scr["gT"], scr["attnT"]

    def mmf(out, lhsT, rhs, reads, wdep_, start=True, stop=True):
        S.op("pe", lambda e: e.matmul(out, lhsT, rhs, start=start, stop=stop), reads=reads, writes=[wdep_])

    with contextlib.ExitStack() as es:
        def SB(name, shape, dt):
            return sb(S, es, tag + "1" + name, shape, dt)
        Wb = SB("Wb", [128, 8, NSA_IN], BF16)
        gpre = SB("gpre", [128, 8], F32)
        identb = SB("identb", [128, 128], BF16)
        stg = [SB(f"stg{i}", [128, 1024], F32) for i in range(2)]
        xb = [SB(f"xb{i}", [128, 4, D], F32) for i in range(2)]
        xn = [SB(f"xn{i}", [128, D], BF16) for i in range(2)]
        junk = SB("junk", [128, D], BF16)
        uT = [SB(f"uT{i}", [128, 8, 512], BF16) for i in range(2)]
        ev = [SB(f"ev{i}", [128, 512], BF16) for i in range(4)]
        evf = [SB(f"evf{i}", [48, 512], F32) for i in range(2)]
        evt = [SB(f"evt{i}", [128, 512], BF16) for i in range(2)]
        st = SB("st", [128, 8], F32)
        bankT = ps(S, es, tag + "1bT", [128, 8, 128], BF16)
        banks = [ps(S, es, tag + f"1bk{i}", [128, 512], F32) for i in range(4)]
        S.dma("sp", gpre[:], P["gpre_l"], writes=[gpre.d()])
        S.dma("sp", identb[:], C["identb"], writes=[identb.d()])
        k = 0
        for c in range(8):
            for o in range(0, NSA_IN, 1024):
                wdt = min(1024, NSA_IN - o)
                b = stg[k % 2]
                S.dma("sp", b[:, 0:wdt], P["w_in"][c * 128:(c + 1) * 128, o:o + wdt], writes=[b.d()])
                S.op("act", lambda e, c=c, o=o, wdt=wdt, b=b: e.activation(
                    out=Wb[:, c, o:o + wdt], in_=b[:, 0:wdt], func=AF.Copy, scale=gpre[:, c:c + 1]),
                    reads=[b.d(), gpre.d()])
                k += 1
        S.barrier()
        hv_in = h_in.rearrange("(t s p) d -> t p s d", p=128, s=4)
        chunks = [(o, 128) for o in range(0, 2560, 128)] + [(2560, 48)]
        ke = 0
        for t in range(8):
            x, u = xb[t % 2], uT[t % 2]
            S.dma("sp", x[:, :, :], hv_in[t], writes=[x.d()])
            for s in range(4):
                xnb = xn[s % 2]
                S.op("act", lambda e, x=x, s=s: e.activation(out=junk[:], in_=x[:, s, :], func=AF.Square,
                                                             accum_out=st[:, s:s + 1]),
                     reads=[x.d()], writes=[junk.d(), st.d(s)])
                S.op("dve", lambda e, s=s: e.tensor_scalar(out=st[:, 4 + s:5 + s], in0=st[:, s:s + 1],
                                                           scalar1=1.0 / D, scalar2=RMS_EPS, op0=ALU.mult,
                                                           op1=ALU.add), reads=[st.d(s)], writes=[st.d(4 + s)])
                S.op("act", lambda e, s=s: e.activation(out=st[:, 4 + s:5 + s], in_=st[:, 4 + s:5 + s], func=AF.Sqrt),
                     reads=[st.d(4 + s)], writes=[st.d(4 + s)])
                S.op("dve", lambda e, s=s: e.reciprocal(out=st[:, 4 + s:5 + s], in_=st[:, 4 + s:5 + s]),
                     reads=[st.d(4 + s)], writes=[st.d(4 + s)])
                S.op("act", lambda e, x=x, s=s, xnb=xnb: e.activation(out=xnb[:], in_=x[:, s, :], func=AF.Copy,
                                                                      scale=st[:, 4 + s:5 + s]),
                     reads=[x.d(), st.d(4 + s)], writes=[xnb.d()])
                for c in range(8):
                    S.op("pe", lambda e, c=c, xnb=xnb: e.transpose(out=bankT[:, c, :],
                                                                   in_=xnb[:, c * 128:(c + 1) * 128],
                                                                   identity=identb[:]),
                         reads=[xnb.d()], writes=[bankT.d()])
                S.op("dve", lambda e, s=s, u=u: e.tensor_copy(out=u[:, :, s * 128:(s + 1) * 128], in_=bankT[:, :, :]),
                     reads=[bankT.d()], writes=[u.d()])
            for (o, m) in chunks:
                bk = banks[ke % 3]
                for c in range(8):
                    mmf(bk[0:m, :], Wb[:, c, o:o + m], u[:, c, :], [u.d()], bk.d(), c == 0, c == 7)
                if o == 2560:
                    e_ = evf[ke % 2]
                    S.op("act", lambda e, bk=bk, e_=e_: e.activation(out=e_[:], in_=bk[0:48, :], func=AF.Sigmoid),
                         reads=[bk.d()], writes=[e_.d()])
                    S.dma("pool", gTd[:, t * 512:(t + 1) * 512], e_[:], reads=[e_.d()])
                else:
                    e_ = ev[ke % 4]
                    sc = 0.125 if o < 1024 else 1.0
                    if ke % 2 == 0:
                        S.op("act", lambda e, bk=bk, e_=e_, sc=sc: e.activation(out=e_[:], in_=bk[:, :], func=AF.Copy,
                                                                               scale=sc),
                             reads=[bk.d()], writes=[e_.d()])
                    else:
                        S.op("dve", lambda e, bk=bk, e_=e_, sc=sc: e.tensor_scalar(out=e_[:], in0=bk[:, :], scalar1=sc,
                                                                                  scalar2=None, op0=ALU.mult),
                             reads=[bk.d()], writes=[e_.d()])
                    S.dma("pool", projT[o:o + 128, t * 512:(t + 1) * 512], e_[:], reads=[e_.d()])
                ke += 1
            for s in range(4):
                bk = banks[3]
                for (jj, o) in enumerate((1792, 2304)):
                    for c in range(8):
                        mmf(bk[:, jj * 256:(jj + 1) * 256], u[:, c, s * 128:(s + 1) * 128], Wb[:, c, o:o + 256],
                            [u.d()], bk.d(), c == 0, c == 7)
                e_ = evt[s % 2]
                S.op("act", lambda e, bk=bk, e_=e_: e.activation(out=e_[:], in_=bk[:, :], func=AF.Copy),
                     reads=[bk.d()], writes=[e_.d()])
                S.dma("pool", vtokd[t * 512 + s * 128:t * 512 + (s + 1) * 128, :], e_[:], reads=[e_.d()])
        S.barrier()

    with contextlib.ExitStack() as es:
        def SB(name, shape, dt):
            return sb(S, es, tag + "2" + name, shape, dt)
        ksA = SB("ksA", [128, 4096], BF16)
        kwA = SB("kwA", [128, 4096], BF16)
        kcT = SB("kcT", [64, 4096], BF16)
        vcT = SB("vcT", [64, 4096], BF16)
        vsA = SB("vsA", [128, 32, 128], BF16)
        vwA = SB("vwA", [128, 32, 128], BF16)
        qA = [SB(f"qA{i}", [128, 4096], BF16) for i in range(4)]
        kcmpA = SB("kcmpA", [128, 256], BF16)
        vcmpA = SB("vcmpA", [128, 2, 128], BF16)
        bias_key = SB("bias_key", [128, 16, 32], F32)
        bias_cmp = SB("bias_cmp", [128, 16, 2], F32)
        cmpmask = SB("cmpmask", [128, 8, 512], BF16)
        causal = SB("causal", [128, 4, 512], BF16)
        winmask = SB("winmask", [128, 8, 512], BF16)
        C_aug = SB("C_aug", [128, 2, 65], BF16)
        vis = SB("vis", [128, 32, 64], F32)
        addt = SB("addt", [128, 32, 64], F32)
        identb = SB("identb", [128, 128], BF16)
        w1 = [SB(f"w1_{i}", [64, 32, 64], BF16) for i in range(2)]
        w2 = [SB(f"w2_{i}", [64, 64], BF16) for i in range(2)]
        peT = [SB(f"peT{i}", [64, 32], BF16) for i in range(2)]
        cbias = SB("cbias", [64, 2], F32)
        stgw = SB("stgw", [64, 2048], F32)
        hid = [SB(f"hid{i}", [64, 256], BF16) for i in range(2)]
        eTs = [SB(f"eT{i}", [128, 512], BF16) for i in range(6)]
        imp_acc = SB("imp_acc", [128, 4, 64], F32)
        imp2 = SB("imp2", [128, 64], F32)
        imp3 = SB("imp3", [128, 64], F32)
        m8 = SB("m8", [128, 16], F32)
        mk = SB("mk", [128, 64], F32)
        negq = SB("negq", [128, 64], BF16)
        rq = SB("rq", [128, 4], F32)
        gbc = [SB(f"gbc{i}", [64, 3, 512], F32) for i in range(4)]
        acc = SB("acc", [64, 512], F32)
        rd = SB("rd", [64, 512], F32)
        tb = SB("tb", [64, 512], F32)
        outb = [SB(f"outb{i}", [64, 512], BF16) for i in range(2)]
        cacc = [SB(f"cacc{i}", [64, 512], F32) for i in range(4)]
        bsc = [ps(S, es, tag + f"2sc{i}", [128, 512], F32) for i in range(4)]
        bpv = [ps(S, es, tag + f"2pv{i}", [128, 512], F32) for i in range(2)]
        bimp = ps(S, es, tag + "2imp", [128, 512], F32)
        btr = ps(S, es, tag + "2tr", [128, 512], BF16)
        bcm = bimp
        ring_sc = Ring([(b, b.d()) for b in bsc])
        ring_pv = Ring([(b, b.d()) for b in bpv])
        ring_e = Ring([(b, b.d()) for b in eTs])

        for (t_, nm) in [(bias_key, "bias_key"), (bias_cmp, "bias_cmp"), (cmpmask, "cmpmask"), (causal, "causal"),
                         (winmask, "winmask"), (C_aug, "C_aug"), (vis, "vis"), (addt, "addt"),
                         (identb, "identb")]:
            S.dma("sp", t_[:], C[nm], writes=[t_.d()])
        for bi, (w1n, w2n, pen) in enumerate([("wck1_l", "w_ck2", "pekT"), ("wcv1_l", "w_cv2", "pevT")]):
            S.dma("sp", stgw[:, :].rearrange("p (l c) -> p l c", l=32), P[w1n], writes=[stgw.d()])
            S.op("dve", lambda e, bi=bi: e.tensor_copy(out=w1[bi][:].rearrange("p l c -> p (l c)"), in_=stgw[:, :]),
                 reads=[stgw.d()], writes=[w1[bi].d()])
            S.dma("sp", stgw[:, 0:64], P[w2n], writes=[stgw.d()])
            S.op("dve", lambda e, bi=bi: e.tensor_copy(out=w2[bi][:], in_=stgw[:, 0:64]),
                 reads=[stgw.d()], writes=[w2[bi].d()])
            S.dma("sp", stgw[:, 0:32], P[pen], writes=[stgw.d()])
            S.op("dve", lambda e, bi=bi: e.tensor_copy(out=peT[bi][:], in_=stgw[:, 0:32]),
                 reads=[stgw.d()], writes=[peT[bi].d()])
            for l in range(32):
                mmf(bcm[0:64, 0:1], w1[bi][:, l, :], peT[bi][:, l:l + 1], [w1[bi].d(), peT[bi].d()], bcm.d(),
                    l == 0, l == 31)
            S.op("dve", lambda e, bi=bi: e.tensor_copy(out=cbias[:, bi:bi + 1], in_=bcm[0:64, 0:1]),
                 reads=[bcm.d()], writes=[cbias.d()])
        for t_ in (kwA, kcmpA) + tuple(qA):
            S.op("dve", lambda e, t_=t_: e.memset(t_[64:128, :], 0.0), writes=[t_.d()])
        S.dma("sp", ksA[64:128, :], C["E_all"], writes=[ksA.d()])
        S.dma("sp", kwA[127:128, :], C["E_all"][63:64, :], writes=[kwA.d()])
        S.dma("sp", kcmpA[127:128, :], C["E_all"][63:64, 0:256], writes=[kcmpA.d()])
        S.op("dve", lambda e: e.memset(negq[:], 0.0), writes=[negq.d()])
        S.op("dve", lambda e: e.memset(vcmpA[:, :, 0:64], 0.0), writes=[vcmpA.d()])
        for t_ in (vsA, vwA, vcmpA):
            S.op("dve", lambda e, t_=t_: e.memset(t_[:, :, 64:128], 1.0), writes=[t_.d()])
        S.op("dve", lambda e: e.memset(kcmpA[0:64, 255:256], 0.0), writes=[kcmpA.d()])
        vt_v = vtokd.rearrange("(kt p) d -> p kt d", p=128)
        pend = []
        import os
        LAG = int(os.environ.get('NSA_LAG', '3'))
        S.barrier()

        for g in range(4):
            S.dma("sp", ksA[0:64, :], projT[1536 + g * 64:1536 + (g + 1) * 64, :], writes=[ksA.d()])
            S.dma("sp", kwA[0:64, :], projT[2048 + g * 64:2048 + (g + 1) * 64, :], writes=[kwA.d()])
            S.dma("sp", kcT[:, :], projT[1024 + g * 64:1024 + (g + 1) * 64, :], writes=[kcT.d()])
            S.dma("sp", vcT[:, :], projT[1280 + g * 64:1280 + (g + 1) * 64, :], writes=[vcT.d()])
            S.dma("sp", vsA[:, :, 0:64], vt_v[:, :, g * 64:(g + 1) * 64], writes=[vsA.d()])
            S.dma("sp", vwA[:, :, 0:64], vt_v[:, :, 256 + g * 64:256 + (g + 1) * 64], writes=[vwA.d()])
            for r in range(4):
                h = 4 * g + r
                S.dma("sp", qA[r][0:64, :], projT[h * 64:(h + 1) * 64, :], writes=[qA[r].d()])
                S.dma("sp", qA[r][127:128, :], C["alibi_q"][h, 0:1, :], writes=[qA[r].d()])
            for bi, src in enumerate((kcT, vcT)):
                for l in range(32):
                    mmf(bcm[0:64, 0:255], w1[bi][:, l, :], src[:, l:l + 16 * 254 + 1:16], [src.d()], bcm.d(),
                        l == 0, l == 31)
                S.op("act", lambda e, bi=bi: e.activation(out=hid[bi][:, 0:255], in_=bcm[0:64, 0:255], func=AF.Silu,
                                                          bias=cbias[:, bi:bi + 1]),
                     reads=[bcm.d(), cbias.d()], writes=[hid[bi].d()])
            mmf(bcm[0:64, 0:255], w2[0][:], hid[0][:, 0:255], [hid[0].d()], bcm.d())
            S.op("dve", lambda e: e.tensor_copy(out=kcmpA[0:64, 0:255], in_=bcm[0:64, 0:255]),
                 reads=[bcm.d()], writes=[kcmpA.d()])
            for nt in range(2):
                nn = 128 if nt == 0 else 127
                mmf(bcm[0:nn, 256 + nt * 64:256 + (nt + 1) * 64], hid[1][:, nt * 128:nt * 128 + nn], w2[1][:],
                    [hid[1].d()], bcm.d())
                S.op("dve", lambda e, nt=nt, nn=nn: e.tensor_copy(out=vcmpA[0:nn, nt, 0:64],
                                                                 in_=bcm[0:nn, 256 + nt * 64:256 + (nt + 1) * 64]),
                     reads=[bcm.d()], writes=[vcmpA.d()])

            def push_branch(tiles, qc, done_cb):
                pvb, pvd = ring_pv.get()
                n = len(tiles)
                ets = []
                for ti, (kT, bias, masks, vT, rds) in enumerate(tiles):
                    sc, scd = ring_sc.get()
                    mmf(sc[:, :], kT, qc, rds, scd, True, len(masks) == 0)
                    for mi, (ml, mr, mrd) in enumerate(masks):
                        mmf(sc[:, :], ml, mr, mrd, scd, False, mi == len(masks) - 1)
                    eT, eTd = ring_e.get()
                    S.op("act", lambda e, sc=sc, eT=eT, bias=bias: e.activation(out=eT[:], in_=sc[:, :], func=AF.Exp,
                                                                               bias=bias),
                         reads=[scd], writes=[eTd])
                    ets.append((eT, eTd))
                    flush(LAG - 1)
                    last = ti == n - 1
                    pend.append((pvb, pvd, vT, eT, [eTd] + list(rds), ti == 0, last,
                                 (lambda: done_cb(pvb, pvd, ets)) if last else None))

            def flush(keep=0):
                while len(pend) > keep:
                    pvb, pvd, vT, eT, prds, first, last, cb = pend.pop(0)
                    mmf(pvb[:, :], vT, eT[:], prds, pvd, first, last)
                    if cb is not None:
                        cb()

            def combine(pvb, pvd, gb, b, accb, first):
                S.op("dve", lambda e: e.tensor_scalar(out=rd[:], in0=pvb[64:128, :], scalar1=1e-30, scalar2=None,
                                                      op0=ALU.add), reads=[pvd], writes=[rd.d()])
                S.op("dve", lambda e: e.reciprocal(out=rd[:], in_=rd[:]), reads=[rd.d()], writes=[rd.d()])
                S.op("dve", lambda e: e.tensor_tensor(out=tb[:], in0=pvb[0:64, :], in1=rd[:], op=ALU.mult),
                     reads=[pvd, rd.d()], writes=[tb.d()])
                if first:
                    S.op("pool", lambda e: e.tensor_tensor(out=accb[:], in0=tb[:], in1=gb[:, b, :], op=ALU.mult),
                         reads=[tb.d(), gb.d()], writes=[accb.d()])
                else:
                    S.op("pool", lambda e: e.tensor_tensor(out=tb[:], in0=tb[:], in1=gb[:, b, :], op=ALU.mult),
                         reads=[tb.d(), gb.d()], writes=[tb.d()])
                    S.op("pool", lambda e: e.tensor_tensor(out=accb[:], in0=accb[:], in1=tb[:], op=ALU.add),
                         reads=[tb.d(), accb.d()], writes=[accb.d()])

            for c in range(8):
                cs_ = slice(c * 512, (c + 1) * 512)
                nts = [0] if c < 4 else [0, 1]
                for r in range(4):
                    h = 4 * g + r
                    tiles = [(kcmpA[:, nt * 128:(nt + 1) * 128], bias_cmp[:, h, nt:nt + 1],
                              [(identb[:], cmpmask[:, c - 4 * nt, :], [])], vcmpA[:, nt, :],
                              [kcmpA.d(), qA[r].d(), vcmpA.d()]) for nt in nts]

                    def cmp_done(pvb, pvd, ets, r=r, h=h, nts=nts, cs_=cs_):
                        for qs in range(4):
                            for ni, nt in enumerate(nts):
                                eT, eTd = ets[ni]
                                mmf(bimp[:, qs * 65:(qs + 1) * 65], eT[:, qs * 128:(qs + 1) * 128], C_aug[:, nt, :],
                                    [eTd], bimp.d(), ni == 0, ni == len(nts) - 1)
                        S.op("dve", lambda e: e.tensor_scalar(
                            out=rq[:], in0=bimp[:, 0:260].rearrange("p (a b) -> p a b", a=4)[:, :, 64], scalar1=1e-30,
                            scalar2=None, op0=ALU.add), reads=[bimp.d()], writes=[rq.d()])
                        S.op("dve", lambda e: e.reciprocal(out=rq[:], in_=rq[:]), reads=[rq.d()], writes=[rq.d()])
                        for qs in range(4):
                            if r == 0:
                                S.op("dve", lambda e, qs=qs: e.tensor_scalar(
                                    out=imp_acc[:, qs, :], in0=bimp[:, qs * 65:qs * 65 + 64], scalar1=rq[:, qs:qs + 1],
                                    scalar2=None, op0=ALU.mult), reads=[bimp.d(), rq.d()], writes=[imp_acc.d()])
                            else:
                                S.op("dve", lambda e, qs=qs: e.scalar_tensor_tensor(
                                    out=imp_acc[:, qs, :], in0=bimp[:, qs * 65:qs * 65 + 64], scalar=rq[:, qs:qs + 1],
                                    in1=imp_acc[:, qs, :], op0=ALU.mult, op1=ALU.add),
                                    reads=[bimp.d(), rq.d(), imp_acc.d()], writes=[imp_acc.d()])
                        gb = gbc[r]
                        S.dma("sp", gb[:], gTd[3 * h:3 * h + 3, cs_].partition_broadcast(64), writes=[gb.d()])
                        combine(pvb, pvd, gb, 0, cacc[r], True)

                    push_branch(tiles, qA[r][:, cs_], cmp_done)
                flush()
                for qs in range(4):
                    it = 4 * c + qs
                    S.op("dve", lambda e, qs=qs, it=it: e.tensor_tensor(out=imp2[:], in0=imp_acc[:, qs, :],
                                                                      in1=vis[:, it, :], op=ALU.mult),
                         reads=[imp_acc.d()], writes=[imp2.d()])
                    S.op("dve", lambda e, it=it: e.tensor_tensor(out=imp2[:], in0=imp2[:], in1=addt[:, it, :],
                                                                 op=ALU.add), reads=[imp2.d()], writes=[imp2.d()])
                    S.op("dve", lambda e: e.max(out=m8[:, 0:8], in_=imp2[:]), reads=[imp2.d()], writes=[m8.d()])
                    S.op("dve", lambda e: e.match_replace(out=imp3[:], in_to_replace=m8[:, 0:8], in_values=imp2[:],
                                                          imm_value=-2.0),
                         reads=[imp2.d(), m8.d()], writes=[imp3.d()])
                    S.op("dve", lambda e: e.max(out=m8[:, 8:16], in_=imp3[:]), reads=[imp3.d()], writes=[m8.d()])
                    S.op("dve", lambda e: e.tensor_scalar(out=mk[:], in0=imp2[:], scalar1=m8[:, 15:16], scalar2=None,
                                                          op0=ALU.is_ge), reads=[imp2.d(), m8.d()], writes=[mk.d()])
                    S.op("dve", lambda e: e.tensor_scalar(out=negq[:, 0:63], in0=mk[:, 1:64], scalar1=-1.0,
                                                          scalar2=-NEG, op0=ALU.add, op1=ALU.mult),
                         reads=[mk.d()], writes=[negq.d()])
                    S.op("pe", lambda e: e.transpose(out=btr[0:64, 0:128], in_=negq[:], identity=identb[:]),
                         reads=[negq.d()], writes=[btr.d()])
                    for r in range(4):
                        S.op("act", lambda e, it=it, r=r: e.activation(out=qA[r][64:127, it * 128:(it + 1) * 128],
                                                                       in_=btr[0:63, 0:128], func=AF.Copy),
                             reads=[btr.d()], writes=[qA[r].d()])
                for r in range(4):
                    h = 4 * g + r
                    gb = gbc[r]
                    tiles = []
                    for kt in range(4 * c + 4):
                        masks = []
                        if kt >= 4 * c:
                            masks.append((identb[:], causal[:, kt - 4 * c, :], []))
                        tiles.append((ksA[:, kt * 128:(kt + 1) * 128], bias_key[:, h, kt:kt + 1], masks,
                                      vsA[:, kt, :], [ksA.d(), qA[r].d(), vsA.d()]))
                    push_branch(tiles, qA[r][:, cs_],
                                lambda pvb, pvd, ets, r=r, gb=gb: combine(pvb, pvd, gb, 1, cacc[r], False))
                    tiles = []
                    for kt in range(max(0, 4 * c - 4), 4 * c + 4):
                        tiles.append((kwA[:, kt * 128:(kt + 1) * 128], bias_key[:, h, kt:kt + 1],
                                      [(identb[:], winmask[:, kt - 4 * c + 4, :], [])], vwA[:, kt, :],
                                      [kwA.d(), qA[r].d(), vwA.d()]))

                    def win_done(pvb, pvd, ets, r=r, h=h, gb=gb, cs_=cs_):
                        combine(pvb, pvd, gb, 2, cacc[r], False)
                        ob = outb[r % 2]
                        S.op("act", lambda e: e.activation(out=ob[:], in_=cacc[r][:], func=AF.Copy),
                             reads=[cacc[r].d()], writes=[ob.d()])
                        S.dma("pool", attnT[h * 64:(h + 1) * 64, cs_], ob[:], reads=[ob.d()])

                    push_branch(tiles, qA[r][:, cs_], win_done)
            flush()
        S.barrier()

    outproj_phase(S, h_in, h_out, attnT, P["w_out"], P["gpost_b"], tag + "3")


_CACHE = {}


def build_program(phases=("f", "n", "f", "f", "r", "f")):
    nc = bass.Bass("TRN2", target_bir_lowering=False)
    A = {}

    def din(name, shape, dt=F32):
        A[name] = nc.dram_tensor(name, list(shape), dt, kind="ExternalInput").ap()
        return A[name]

    def dscr(name, shape, dt=F32):
        return nc.dram_tensor(name, list(shape), dt, kind="Internal").ap()

    din("x", [S_LEN, D])
    for li in range(2):
        for f in (1, 2):
            din(f"f{f}_{li}_wgu", [D, 2 * DFF]); din(f"f{f}_{li}_wdn", [DFF, D])
            din(f"f{f}_{li}_gpre", [128, 8]); din(f"f{f}_{li}_gpost", [128, D])
    ncst = nsa_consts()
    rcst = rwkv_consts()
    for k_, v in ncst.items():
        din("nc_" + k_, v.shape, F32 if v.dtype == np.float32 else BF16)
    for k_, v in rcst.items():
        din("rc_" + k_, v.shape, F32 if v.dtype == np.float32 else BF16)
    NP = {"w_in": din("n_w_in", [D, NSA_IN]), "w_out": din("n_w_out", [D, D]),
          "gpre_l": din("n_gpre_l", [128, 8]), "gpost_b": din("n_gpost_b", [128, D]),
          "pekT": din("n_pekT", [64, 32]), "pevT": din("n_pevT", [64, 32]),
          "wck1_l": din("n_wck1_l", [64, 32, 64]), "wcv1_l": din("n_wcv1_l", [64, 32, 64]),
          "w_ck2": din("n_w_ck2", [64, 64]), "w_cv2": din("n_w_cv2", [64, 64])}
    RP = {"w_in": din("r_w_in", [D, 3360]), "w_out": din("r_w_out", [D, D]), "w_w2": din("r_w_w2", [64, D]),
          "w_a2": din("r_w_a2", [64, D]), "w_g2": din("r_w_g2", [160, D]),
          "gpre_l": din("r_gpre_l", [128, 8]), "gpost_b": din("r_gpost_b", [128, D]),
          "mu_l": din("r_mu_l", [128, 6, 8]), "hp": din("r_hp", [64, 7, 16])}
    y = nc.dram_tensor("y", [S_LEN, D], F32, kind="ExternalOutput").ap()
    hs = [A["x"]] + [dscr(f"h{i}", [S_LEN, D]) for i in range(len(phases) - 1)] + [y]
    scr = {"projT": dscr("projT", [2560, S_LEN], BF16), "vtok": dscr("vtokd", [S_LEN, 512], BF16),
           "gT": dscr("gT", [48, S_LEN]), "attnT": dscr("attnT", [D, S_LEN], BF16)}
    NC_ = {k_: A["nc_" + k_] for k_ in ncst}
    RC_ = {k_: A["rc_" + k_] for k_ in rcst}
    es = contextlib.ExitStack()
    with es:
        S = Sched(nc, es)
        ffn_ids = [(1, 0), (2, 0), (1, 1), (2, 1)]
        fi = 0
        for pi, ph in enumerate(phases):
            hin, hout = hs[pi], hs[pi + 1]
            if ph == "f":
                f, li = ffn_ids[fi]
                fi += 1
                ffn_phase(S, hin, hout, A[f"f{f}_{li}_wgu"], A[f"f{f}_{li}_wdn"], A[f"f{f}_{li}_gpre"],
                          A[f"f{f}_{li}_gpost"], A["rc_identb"], tag=f"f{pi}")
            elif ph == "n":
                nsa_phase(S, hin, hout, NP, NC_, scr)
            elif ph == "r":
                rwkv_phase(S, hin, hout, RP, RC_, scr["attnT"])
        S.finish()
    return nc, ncst, rcst


def kernel(**inp):
    f32 = np.float32
    g = {k_: np.asarray(v) for k_, v in inp.items()}
    if "prog" not in _CACHE:
        _CACHE["prog"] = build_program()
    nc, ncst, rcst = _CACHE["prog"]
    shared = {}
    for li in range(2):
        for f in (1, 2):
            shared[f"f{f}_{li}_wgu"] = np.ascontiguousarray(g[f"ffn{f}_w_gu"][li], f32)
            shared[f"f{f}_{li}_wdn"] = np.ascontiguousarray(g[f"ffn{f}_w_down"][li], f32)
            shared[f"f{f}_{li}_gpre"] = _pl(g[f"ffn{f}_norm_pre"][li])
            shared[f"f{f}_{li}_gpost"] = _bcast(g[f"ffn{f}_norm_post"][li])
    for k_, v in ncst.items():
        shared["nc_" + k_] = v
    for k_, v in rcst.items():
        shared["rc_" + k_] = v
    nh = nsa_host_params(g["mix_norm_pre"][0], g["mix_norm_post"][0], g["nsa_pe_k"][0], g["nsa_w_ck1"][0],
                         g["nsa_pe_v"][0], g["nsa_w_cv1"][0])
    for k_, v in nh.items():
        shared["n_" + k_] = v
    shared["n_w_in"] = np.ascontiguousarray(g["nsa_w_in"][0], f32)
    shared["n_w_out"] = np.ascontiguousarray(g["nsa_w_out"][0], f32)
    shared["n_w_ck2"] = np.ascontiguousarray(g["nsa_w_ck2"][0], f32)
    shared["n_w_cv2"] = np.ascontiguousarray(g["nsa_w_cv2"][0], f32)
    rh = rwkv_host_params(g["mix_norm_pre"][1], g["mix_norm_post"][1], g["rwkv_mu"][0], g["rwkv_w0"][0],
                          g["rwkv_a0"][0], g["rwkv_k_k"][0], g["rwkv_k_a"][0], g["rwkv_r_k"][0],
                          g["rwkv_gn_w"][0], g["rwkv_gn_b"][0])
    for k_, v in rh.items():
        shared["r_" + k_] = v
    for nm in ("w_in", "w_out", "w_w2", "w_a2", "w_g2"):
        shared["r_" + nm] = np.ascontiguousarray(g["rwkv_" + nm][0], f32)
    x = np.asarray(g["x"], f32)
    in_maps = [dict(shared, x=np.ascontiguousarray(x[b])) for b in range(NCORES)]
    res = run_bass_kernel_spmd(nc, in_maps, core_ids=list(range(NCORES)))
    return np.stack([np.asarray(r["y"], f32) for r in res.results], axis=0)
```

```python
import contextlib
import numpy as np
import concourse.bass as bass
import concourse.mybir as mybir
from concourse.bass_utils import run_bass_kernel_spmd

F32 = mybir.dt.float32
BF16 = mybir.dt.bfloat16
AF = mybir.ActivationFunctionType
ALU = mybir.AluOpType
AX = mybir.AxisListType

S_LEN = 4096
D = 1024
DFF = 2816
NCORES = 8
RMS_EPS = 1e-6


class Dep:
    __slots__ = ("w", "r")

    def __init__(self):
        self.w = None
        self.r = {}


class Sched:
    EPOCH = 16000
    NDMA = 24

    def __init__(self, nc, es):
        self.nc = nc
        self.es = es
        self.eng = {"pe": nc.tensor, "act": nc.scalar, "dve": nc.vector,
                    "pool": nc.gpsimd, "sp": nc.sync}
        self.nsem = 0
        self.sem = {e: self._newsem(e) for e in self.eng}
        self.cnt = {e: 0 for e in self.eng}
        self.waited = {e: {} for e in self.eng}
        self.dsem = {"sp": [self._newsem("dsp") for _ in range(16)],
                     "pool": [self._newsem("dpl") for _ in range(8)]}
        self.dcnt = {q: [0] * len(v) for q, v in self.dsem.items()}
        self.dnext = {q: 0 for q in self.dsem}
        self.ninst = 0
        self.nwait = 0
        self.pe_self_sync = False

    def _newsem(self, tag):
        self.nsem += 1
        return self.es.enter_context(self.nc.semaphore(f"s_{tag}_{self.nsem}"))

    def _wait(self, e, toks):
        best = {}
        for (s, v, src) in toks:
            if src == e and e == "pe" and not self.pe_self_sync:
                continue
            k = id(s)
            if k not in best or best[k][1] < v:
                best[k] = (s, v)
        w = self.waited[e]
        for k, (s, v) in best.items():
            if w.get(k, 0) >= v:
                continue
            self.eng[e].wait_ge(s, v)
            self.nwait += 1
            w[k] = v

    def _collect(self, reads, writes):
        toks = []
        for d in reads:
            if d.w is not None:
                toks.append(d.w)
        for d in writes:
            if d.w is not None:
                toks.append(d.w)
            toks.extend(d.r.values())
        return toks

    def _update(self, tok, reads, writes):
        k = id(tok[0])
        for d in reads:
            old = d.r.get(k)
            if old is None or old[1] < tok[1]:
                d.r[k] = tok
        for d in writes:
            d.w = tok
            d.r = {}

    def op(self, e, fn, reads=(), writes=()):
        toks = self._collect(reads, writes)
        self._wait(e, toks)
        ins = fn(self.eng[e])
        if self.cnt[e] >= self.EPOCH:
            self.sem[e] = self._newsem(e)
            self.cnt[e] = 0
        self.cnt[e] += 1
        ins.then_inc(self.sem[e], 1)
        self.ninst += 1
        tok = (self.sem[e], self.cnt[e], e)
        self._update(tok, reads, writes)
        return tok

    def dma(self, q, out, in_, reads=(), writes=(), **kw):
        toks = self._collect(reads, writes)
        dsem, dcnt = self.dsem[q], self.dcnt[q]
        k = self.dnext[q]
        self.dnext[q] = (k + 1) % len(dsem)
        if dcnt[k] >= self.EPOCH:
            toks.append((dsem[k], dcnt[k], None))
            self._wait(q, toks)
            toks = []
            dsem[k] = self._newsem("d" + q)
            dcnt[k] = 0
        if dcnt[k] > 0:
            toks.append((dsem[k], dcnt[k], None))
        self._wait(q, toks)
        ins = self.eng[q].dma_start(out=out, in_=in_, **kw)
        dcnt[k] += 16
        ins.then_inc(dsem[k], 16)
        self.ninst += 1
        tok = (dsem[k], dcnt[k], None)
        self._update(tok, reads, writes)
        return tok

    def _all_dma_toks(self):
        return [(self.dsem[q][k], self.dcnt[q][k], None) for q in self.dsem
                for k in range(len(self.dsem[q])) if self.dcnt[q][k] > 0]

    def barrier(self):
        toks = [(self.sem[e], self.cnt[e], None) for e in self.eng if self.cnt[e] > 0]
        toks += self._all_dma_toks()
        for e in self.eng:
            self._wait(e, toks)

    def finish(self):
        toks = self._all_dma_toks()
        toks += [(self.sem[e], self.cnt[e], None) for e in self.eng if self.cnt[e] > 0]
        self._wait("sp", toks)


class Buf:
    def __init__(self, t):
        self.t = t
        self.deps = {}

    def d(self, key=0):
        dd = self.deps.get(key)
        if dd is None:
            dd = self.deps[key] = Dep()
        return dd

    def __getitem__(self, idx):
        return self.t[idx]


def sb(S, es, name, shape, dt):
    return Buf(es.enter_context(S.nc.sbuf_tensor(name, shape, dt)))


def ps(S, es, name, shape, dt):
    return Buf(es.enter_context(S.nc.psum_tensor(name, shape, dt)))


def ffn_phase(S, h_in, h_out, w_gu, w_down, gpre_l, gpost_b, ident_d, ntiles=16, tag="f"):
    nc = S.nc
    T = 256
    NS = T // 128
    with contextlib.ExitStack() as es:
        wgu = sb(S, es, tag + "wgu", [128, 8, 2 * DFF], BF16)
        wdn = sb(S, es, tag + "wdn", [128, 22, D], BF16)
        gpre = sb(S, es, tag + "gpre", [128, 8], F32)
        gpost = sb(S, es, tag + "gpost", [128, D], F32)
        ident = sb(S, es, tag + "ident", [128, 128], BF16)
        xb = [sb(S, es, tag + f"x{i}", [128, NS, D], F32) for i in range(2)]
        xn = [sb(S, es, tag + f"xn{i}", [128, D], BF16) for i in range(2)]
        xnT = [sb(S, es, tag + f"xnT{i}", [128, 8, T], BF16) for i in range(2)]
        hT = sb(S, es, tag + "hT", [128, 22, T], BF16)
        sg = [sb(S, es, tag + f"sg{i}", [128, T], F32) for i in range(3)]
        ob = [sb(S, es, tag + f"ob{i}", [128, D], F32) for i in range(2)]
        tmp = sb(S, es, tag + "tmp", [128, D], F32)
        junk = sb(S, es, tag + "junk", [128, D], BF16)
        st = sb(S, es, tag + "st", [128, 16], F32)
        pT = ps(S, es, tag + "pT", [128, 8, 128], BF16)
        pGU = [ps(S, es, tag + f"pGU{i}", [128, 2, T], F32) for i in range(3)]
        pF = [ps(S, es, tag + f"pF{i}", [128, 512], F32) for i in range(4)]

        S.dma("sp", gpre[:], gpre_l, writes=[gpre.d()])
        S.dma("sp", gpost[:], gpost_b, writes=[gpost.d()])
        S.dma("sp", ident[:], ident_d, writes=[ident.d()])
        S.op("dve", lambda e: e.tensor_scalar(out=gpost[:], in0=gpost[:], scalar1=0.5, scalar2=None,
                                              op0=ALU.mult), reads=[gpost.d()], writes=[gpost.d()])

        slots = [(xb[i].d(("stg", s_)), xb[i][:, s_, :]) for i in range(2) for s_ in range(NS)]
        HW = 1024
        k = 0

        def conv(dst, view, dep, scale=None):
            nonlocal k
            if k % 2 == 0:
                if scale is None:
                    S.op("act", lambda e: e.activation(out=dst, in_=view, func=AF.Copy), reads=[dep])
                else:
                    S.op("act", lambda e: e.activation(out=dst, in_=view, func=AF.Copy, scale=scale),
                         reads=[dep, gpre.d()])
            else:
                if scale is None:
                    S.op("dve", lambda e: e.tensor_copy(out=dst, in_=view), reads=[dep])
                else:
                    S.op("dve", lambda e: e.tensor_scalar(out=dst, in0=view, scalar1=scale, scalar2=None,
                                                          op0=ALU.mult), reads=[dep, gpre.d()])
            k += 1

        for c in range(8):
            for o in range(0, 2 * DFF, HW):
                wdt = min(HW, 2 * DFF - o)
                dep, sv = slots[k % len(slots)]
                S.dma("sp", sv[:, 0:wdt], w_gu[c * 128:(c + 1) * 128, o:o + wdt], writes=[dep])
                conv(wgu[:, c, o:o + wdt], sv[:, 0:wdt], dep, scale=gpre[:, c:c + 1])
        for j in range(22):
            dep, sv = slots[k % len(slots)]
            S.dma("sp", sv[:, :], w_down[j * 128:(j + 1) * 128, :], writes=[dep])
            conv(wdn[:, j, :], sv[:, :], dep)
        S.barrier()

        hv_in = h_in.rearrange("(t s p) d -> t p s d", p=128, s=NS)
        hv_out = h_out.rearrange("(t s p) d -> t s p d", p=128, s=NS)

        def load(t):
            S.dma("sp", xb[t % 2][:, :, :], hv_in[t], writes=[xb[t % 2].d()])

        def prenorm(t):
            x = xb[t % 2]
            for s in range(NS):
                xnb = xn[s % 2]
                S.op("act", lambda e, x=x, s=s: e.activation(
                    out=junk[:], in_=x[:, s, :], func=AF.Square, accum_out=st[:, s:s + 1]),
                    reads=[x.d()], writes=[junk.d(), st.d(s)])
                S.op("dve", lambda e, s=s: e.tensor_scalar(
                    out=st[:, 4 + s:5 + s], in0=st[:, s:s + 1], scalar1=1.0 / D, scalar2=RMS_EPS,
                    op0=ALU.mult, op1=ALU.add), reads=[st.d(s)], writes=[st.d(4 + s)])
                S.op("act", lambda e, s=s: e.activation(
                    out=st[:, 4 + s:5 + s], in_=st[:, 4 + s:5 + s], func=AF.Sqrt),
                    reads=[st.d(4 + s)], writes=[st.d(4 + s)])
                S.op("dve", lambda e, s=s: e.reciprocal(
                    out=st[:, 4 + s:5 + s], in_=st[:, 4 + s:5 + s]),
                    reads=[st.d(4 + s)], writes=[st.d(4 + s)])
                S.op("act", lambda e, x=x, s=s, xnb=xnb: e.activation(
                    out=xnb[:], in_=x[:, s, :], func=AF.Copy, scale=st[:, 4 + s:5 + s]),
                    reads=[x.d(), st.d(4 + s)], writes=[xnb.d()])
                for c in range(8):
                    S.op("pe", lambda e, c=c, xnb=xnb: e.transpose(
                        out=pT[:, c, :], in_=xnb[:, c * 128:(c + 1) * 128], identity=ident[:]),
                        reads=[xnb.d(), ident.d()], writes=[pT.d()])
                S.op("dve", lambda e, t=t, s=s: e.tensor_copy(
                    out=xnT[t % 2][:, :, s * 128:(s + 1) * 128], in_=pT[:, :, :]),
                    reads=[pT.d()], writes=[xnT[t % 2].d()])

        def gu(t):
            xT = xnT[t % 2]
            for j in range(22):
                pg = pGU[j % 3]
                for half in range(2):
                    col = half * DFF + j * 128
                    for c in range(8):
                        S.op("pe", lambda e, c=c, col=col, half=half, pg=pg: e.matmul(
                            pg[:, half, :], wgu[:, c, col:col + 128], xT[:, c, :],
                            start=(c == 0), stop=(c == 7)),
                            reads=[wgu.d(), xT.d()], writes=[pg.d()])
                sgb = sg[j % 3]
                S.op("act", lambda e, pg=pg, sgb=sgb: e.activation(
                    out=sgb[:], in_=pg[:, 0, :], func=AF.Silu), reads=[pg.d()], writes=[sgb.d()])
                S.op("dve", lambda e, pg=pg, sgb=sgb, j=j: e.tensor_tensor(
                    out=hT[:, j, :], in0=pg[:, 1, :], in1=sgb[:], op=ALU.mult),
                    reads=[pg.d(), sgb.d()], writes=[hT.d()])

        def down(t):
            x = xb[t % 2]
            for s in range(NS):
                pf = [pF[(s % 2) * 2], pF[(s % 2) * 2 + 1]]
                for half in range(2):
                    for j in range(22):
                        S.op("pe", lambda e, j=j, s=s, half=half, pf=pf: e.matmul(
                            pf[half][:, :], hT[:, j, s * 128:(s + 1) * 128],
                            wdn[:, j, half * 512:(half + 1) * 512], start=(j == 0), stop=(j == 21)),
                            reads=[hT.d(), wdn.d()], writes=[pf[half].d()])
                for half in range(2):
                    S.op("act", lambda e, half=half, pf=pf, s=s: e.activation(
                        out=junk[:, 0:512], in_=pf[half][:, :], func=AF.Square,
                        accum_out=st[:, 8 + 2 * s + half:9 + 2 * s + half]),
                        reads=[pf[half].d()], writes=[junk.d(), st.d(8 + 2 * s + half)])
                S.op("dve", lambda e, s=s: e.tensor_tensor(
                    out=st[:, 12 + s:13 + s], in0=st[:, 8 + 2 * s:9 + 2 * s],
                    in1=st[:, 9 + 2 * s:10 + 2 * s], op=ALU.add),
                    reads=[st.d(8 + 2 * s), st.d(9 + 2 * s)], writes=[st.d(12 + s)])
                S.op("dve", lambda e, s=s: e.tensor_scalar(
                    out=st[:, 12 + s:13 + s], in0=st[:, 12 + s:13 + s], scalar1=1.0 / D, scalar2=RMS_EPS,
                    op0=ALU.mult, op1=ALU.add), reads=[st.d(12 + s)], writes=[st.d(12 + s)])
                S.op("act", lambda e, s=s: e.activation(
                    out=st[:, 12 + s:13 + s], in_=st[:, 12 + s:13 + s], func=AF.Sqrt),
                    reads=[st.d(12 + s)], writes=[st.d(12 + s)])
                S.op("dve", lambda e, s=s: e.reciprocal(
                    out=st[:, 12 + s:13 + s], in_=st[:, 12 + s:13 + s]),
                    reads=[st.d(12 + s)], writes=[st.d(12 + s)])
                for half in range(2):
                    S.op("dve", lambda e, half=half, pf=pf: e.tensor_tensor(
                        out=tmp[:, half * 512:(half + 1) * 512], in0=pf[half][:, :],
                        in1=gpost[:, half * 512:(half + 1) * 512], op=ALU.mult),
                        reads=[pf[half].d(), gpost.d()], writes=[tmp.d()])
                o = ob[s % 2]
                S.op("dve", lambda e, s=s, o=o, x=x: e.scalar_tensor_tensor(
                    out=o[:], in0=tmp[:], scalar=st[:, 12 + s:13 + s], in1=x[:, s, :],
                    op0=ALU.mult, op1=ALU.add),
                    reads=[tmp.d(), st.d(12 + s), x.d()], writes=[o.d()])
                S.dma("pool", hv_out[t, s], o[:], reads=[o.d()])

        load(0)
        prenorm(0)
        for t in range(ntiles):
            if t + 1 < ntiles:
                load(t + 1)
            gu(t)
            if t + 1 < ntiles:
                prenorm(t + 1)
            down(t)
        S.barrier()


def _consts():
    import ml_dtypes
    ident = np.eye(128, dtype=np.float32).astype(ml_dtypes.bfloat16)
    return {"ident": ident}


class Ring:
    def __init__(self, views):
        self.views = views
        self.i = 0

    def get(self):
        v = self.views[self.i]
        self.i = (self.i + 1) % len(self.views)
        return v


HP_W0, HP_A0, HP_KK, HP_KA, HP_RK, HP_GNW, HP_GNB = range(7)
GN_EPS = 64e-5


def rwkv_phase(S, h_in, h_out, P, C, yT_dram, ndc=32, tag="r", stage=9, dbg=None):
    nc = S.nc
    import os
    RWBF = False
    F32R = BF16 if RWBF else mybir.dt.float32r
    PADDED = not RWBF

    def W(n):
        return 256 if PADDED else n
    HO = 0 if PADDED else 128
    with contextlib.ExitStack() as es:
        def SB(name, shape, dt):
            return sb(S, es, tag + name, shape, dt)

        Wb = SB("Wb", [128, 8, 3360], BF16)
        ww2 = SB("ww2", [64, D], BF16)
        wa2 = SB("wa2", [64, D], BF16)
        wg2a = SB("wg2a", [128, D], BF16)
        wg2b = SB("wg2b", [32, D], BF16)
        gpre = SB("gpre", [128, 8], F32)
        mu = SB("mu", [128, 6, 8], F32)
        hp = SB("hp", [64, 7, 16], F32)
        identb = SB("identb", [128, 128], BF16)
        identf = SB("identf", [128, 128], F32)
        identr = SB("identr", [128, 256], F32R)
        ones64 = SB("ones64", [64, 64], F32)
        ones64r = SB("ones64r", [64, 64], F32R)
        mask2 = SB("mask2", [128, 256], F32)
        masksl = SB("masksl", [128, 128], F32)
        scanm = SB("scanm", [64, 512], F32)
        xb = SB("xb", [128, D], F32)
        xn = SB("xn", [128, D], BF16)
        junk = SB("junk", [128, 512], BF16)
        uTx = [SB(f"uTx{i}", [128, 8, 129], BF16) for i in range(2)]
        xx = SB("xx", [128, 8, 128], BF16)
        mixb = [SB(f"mix{i}", [128, 8, 128], BF16) for i in range(4)]
        vtok = SB("vtok", [128, D + 256], F32R)
        th3 = SB("th3", [64, 128], BF16)
        p4b = SB("p4b", [64, 128], BF16)
        s5a = SB("s5a", [128, 128], BF16)
        s5b = SB("s5b", [32, 128], BF16)
        yfin = SB("yfin", [128, 8, 128], BF16)
        st = SB("st", [128, 8], F32)
        Sst = SB("Sst", [64, 20, 64], F32R)
        gamC = SB("gamC", [64, 16], F32)
        Q = {n: SB("q_" + n, [64, 4, 128], F32) for n in ["k", "sig", "a", "cs", "kk", "t1", "eneg", "epos", "gt1"]}
        Q["eexc"] = Q["sig"]
        Q["t1r"] = SB("q_t1r", [64, 4, 128], F32R)
        Q["gt1r"] = SB("q_gt1r", [64, 4, 128], F32R)
        QP = [{n: SB(f"qp{p}_" + n, [64, 4, 128], F32) for n in ["r", "kmod", "vT", "g", "y"]} for p in range(2)]
        ARs = [SB(f"AR{p}", [64, 4, 2, 128], F32R) for p in range(2)]
        BTs = [SB(f"BTb{p}", [64, 6, 128], F32R) for p in range(2)]
        KTs = [SB(f"KTb{p}", [64, 6, 128], F32R) for p in range(2)]
        NH = 4
        XB = [[SB(f"X{i}_{j}", [128, 512], F32R) for j in range(2)] for i in range(NH)]
        MRB = [SB(f"MRB{i}", [128, 256], F32R) for i in range(NH)]
        MKb = [SB(f"MK{i}", [128, 256], F32R) for i in range(NH)]
        AXb = [SB(f"AX{i}", [128, 256], F32R) for i in range(NH)]
        PQb = [SB(f"PQ{i}", [128, 320], F32R) for i in range(NH)]
        BKb = [SB(f"BK{i}", [128, 256], F32R) for i in range(NH)]
        GTb = [SB(f"GT{i}", [64, 64], F32R) for i in range(NH)]
        Hsb = [SB(f"Hs{i}", [64, 64], F32) for i in range(NH)]
        RhT = [SB(f"RhT{i}", [64, 256], F32R) for i in range(NH)]

        banks = [ps(S, es, tag + f"bk{i}", [128, 512], F32) for i in range(7)]
        bankT = ps(S, es, tag + "bkT", [128, 8, 128], BF16)
        ring_proj = Ring([(b[:, :], b.d()) for b in banks[0:1]])
        ringF = Ring([(b, b.d()) for b in banks[1:7]])

        for (t, src) in [(gpre, P["gpre_l"]), (mu, P["mu_l"]), (hp, P["hp"]),
                         (identb, C["identb"]), (identf, C["identf"]), (ones64, C["ones64"]),
                         (mask2, C["mask2"]), (masksl, C["masksl"]), (scanm, C["scanm"])]:
            S.dma("sp", t[:], src, writes=[t.d()])
        S.op("dve", lambda e: e.memset(uTx[1][:, :, 128:129], 0.0), writes=[uTx[1].d()])
        kcnt = [0]
        stg2 = Buf(xb.t)
        stg = [(xb, xb[:, 0:512]), (stg2, xb[:, 512:1024])]

        def conv(dst, view, b, scale=None):
            if scale is not None:
                S.op("act", lambda e: e.activation(out=dst, in_=view, func=AF.Copy, scale=scale),
                     reads=[b.d(), gpre.d()])
            elif kcnt[0] % 2 == 0:
                S.op("act", lambda e: e.activation(out=dst, in_=view, func=AF.Copy), reads=[b.d()])
            else:
                S.op("dve", lambda e: e.tensor_copy(out=dst, in_=view), reads=[b.d()])
            kcnt[0] += 1

        for c in range(8):
            for o in range(0, 3360, 512):
                wdt = min(512, 3360 - o)
                b, bv = stg[kcnt[0] % 2]
                S.dma("sp", bv[:, 0:wdt], P["w_in"][c * 128:(c + 1) * 128, o:o + wdt], writes=[b.d()])
                conv(Wb[:, c, o:o + wdt], bv[:, 0:wdt], b, scale=gpre[:, c:c + 1])
        for (dst, src, n) in [(ww2, P["w_w2"], 64), (wa2, P["w_a2"], 64), (wg2a, P["w_g2"][0:128, :], 128),
                              (wg2b, P["w_g2"][128:160, :], 32)]:
            for o in range(0, D, 512):
                b, bv = stg[kcnt[0] % 2]
                S.dma("sp", bv[0:n, :], src[:, o:o + 512], writes=[b.d()])
                conv(dst[0:n, o:o + 512], bv[0:n, :], b)
        S.barrier()

        S.op("dve", lambda e: e.memset(xb[:, 0:512], 0.0), writes=[xb.d()])

        def zero_r(buf, flat, nparts, width, deps):
            for o in range(0, width, 512):
                w_ = min(512, width - o)
                S.op("pool", lambda e, o=o, w_=w_: e.tensor_copy(out=flat[0:nparts, o:o + w_],
                                                                 in_=xb[0:nparts, 0:w_]),
                     reads=[xb.d()], writes=deps)
        zero_r(Sst, Sst[:].rearrange("p a b -> p (a b)"), 64, 20 * 64, [Sst.d(h) for h in range(16)])
        zero_r(identr, identr[:, 128:256], 128, 128, [identr.d()])
        S.op("dve", lambda e: e.tensor_copy(out=identr[:, 0:128], in_=identf[:]), reads=[identf.d()],
             writes=[identr.d()])
        S.op("dve", lambda e: e.tensor_copy(out=ones64r[:], in_=ones64[:]), reads=[ones64.d()],
             writes=[ones64r.d()])
        zero_r(vtok, vtok[:, :], 128, D + 256, [vtok.d()])
        for p_ in range(2):
            zero_r(BTs[p_], BTs[p_][:].rearrange("p a b -> p (a b)"), 64, 6 * 128, [BTs[p_].d()])
            zero_r(KTs[p_], KTs[p_][:].rearrange("p a b -> p (a b)"), 64, 6 * 128, [KTs[p_].d()])
        for t_ in MRB + MKb + AXb + BKb:
            zero_r(t_, t_[:, :], 128, 256, [t_.d()])
        for t_ in [b for row in XB for b in row]:
            zero_r(t_, t_[:, :], 128, 512, [t_.d()])
        for t_ in PQb:
            zero_r(t_, t_[:, :], 128, 320, [t_.d()])
        for t_ in RhT:
            zero_r(t_, t_[:, :], 64, 256, [t_.d()])
        S.barrier()

        hv_in = h_in.rearrange("(t p) d -> t p d", p=128)
        hv_out = h_out.rearrange("(t p) d -> t p d", p=128)

        def bc(idx, q):
            return hp[:, idx, 4 * q:4 * q + 4].unsqueeze(2).to_broadcast([64, 4, 128])

        def f2(b):
            return b[:].rearrange("p a b -> p (a b)")

        def rstd_ops(src_col, dst_col):
            S.op("dve", lambda e: e.tensor_scalar(out=st[:, dst_col:dst_col + 1], in0=st[:, src_col:src_col + 1],
                                                  scalar1=1.0 / D, scalar2=RMS_EPS, op0=ALU.mult, op1=ALU.add),
                 reads=[st.d(src_col)], writes=[st.d(dst_col)])
            S.op("act", lambda e: e.activation(out=st[:, dst_col:dst_col + 1], in_=st[:, dst_col:dst_col + 1],
                                               func=AF.Sqrt), reads=[st.d(dst_col)], writes=[st.d(dst_col)])
            S.op("dve", lambda e: e.reciprocal(out=st[:, dst_col:dst_col + 1], in_=st[:, dst_col:dst_col + 1]),
                 reads=[st.d(dst_col)], writes=[st.d(dst_col)])

        def mm(out, lhsT, rhs, reads, wdep_, start=True, stop=True):
            S.op("pe", lambda e: e.matmul(out, lhsT, rhs, start=start, stop=stop), reads=reads, writes=[wdep_])

        def tt(eng, out, in0, in1, op, reads, writes):
            S.op(eng, lambda e: e.tensor_tensor(out=out, in0=in0, in1=in1, op=op), reads=reads, writes=writes)

        def actf(out, in_, func, reads, writes, **kw):
            S.op("act", lambda e: e.activation(out=out, in_=in_, func=func, **kw), reads=reads, writes=writes)

        def make_mix(i, m, cur):
            for c in range(8):
                S.op("dve", lambda e, c=c: e.scalar_tensor_tensor(
                    out=m[:, c, :], in0=xx[:, c, :], scalar=mu[:, i, c:c + 1], in1=cur[:, c, 1:129],
                    op0=ALU.mult, op1=ALU.add), reads=[xx.d(), cur.d()], writes=[m.d()])
            return m

        Sf = Sst[:].rearrange("p a b -> p (a b)")
        yT_v = yT_dram.rearrange("(c p) t -> p c t", p=128)

        def front(h, slot, q):
            j = h % 4
            AR, BTb, KTb = ARs[q % 2], BTs[q % 2], KTs[q % 2]
            BTf = BTb[:].rearrange("p a b -> p (a b)")
            ARcat = AR[:, j, :, :].rearrange("p a b -> p (a b)")
            AT = AR[:, j, 0, :]
            RT = AR[:, j, 1, :]
            BTh = BTb[:, j, :]
            KTh = KTb[:, j, :]
            vpad = vtok[:, h * 64:h * 64 + W(64)]
            mrb, mk = MRB[slot], MKb[slot]
            x0, x1 = XB[slot]
            ax, pq, bk = AXb[slot], PQb[slot], BKb[slot]
            pb, pd = ringF.get()
            mm(pb[:, 0:256], BTh, ARcat, [BTb.d(), AR.d()], pd)
            tt("dve", x0[:, 0:128], pb[:, 0:128], mask2[:, 0:128], ALU.mult, [pd], [x0.d()])
            tt("dve", mrb[:, 128:256], pb[:, 128:256], mask2[:, 128:256], ALU.mult, [pd], [mrb.d()])
            actf(x0[:, 128:256], identr[:, 0:128], AF.Copy, [identr.d()], [x0.d()])
            pb, pd = ringF.get()
            mm(pb[:, 0:256], KTh, ARcat, [KTb.d(), AR.d()], pd)
            tt("dve", mk[:, 0:256], pb[:, 0:256], mask2[:], ALU.mult, [pd], [mk.d()])
            yield
            pb, pd = ringF.get()
            mm(pb[:, 0:W(128)], AT, BTf[:, j * 128:j * 128 + W(128)], [BTb.d(), AR.d()], pd)
            tt("dve", x0[:, 256:384], pb[:, 0:128], masksl[:], ALU.mult, [pd], [x0.d()])
            pb2, pd2 = ringF.get()
            mm(pb2[:, 0:W(64)], BTh, identr[0:64, 0:W(64)], [BTb.d()], pd2)
            mm(pb2[:, 64:64 + W(64)], KTh, identr[0:64, 0:W(64)], [KTb.d()], pd2)
            actf(bk[:, 0:128], pb2[:, 0:128], AF.Copy, [pd2], [bk.d()])
            yield
            Xc = x0
            for jj in range(1, 7):
                Xn = x1 if Xc is x0 else x0
                pb, pd = ringF.get()
                mm(pb[:, 0:256], Xc[:, 256:384], Xc[:, 0:256], [Xc.d()], pd)
                mm(pb[:, 256:256 + W(128)], Xc[:, 0:128], Xc[:, 256:256 + W(128)], [Xc.d()], pd)
                S.op("act", lambda e, pb=pb, Xn=Xn: e.activation(
                    out=Xn[:, :].rearrange("p (a b) -> p a b", a=2)[:, :, 0:128],
                    in_=pb[:, :].rearrange("p (a b) -> p a b", a=2)[:, :, 0:128], func=AF.Copy),
                    reads=[pd], writes=[Xn.d(), pd])
                tt("dve", Xn[:, 128:256], pb[:, 128:256], Xc[:, 128:256], ALU.add, [pd, Xc.d()], [Xn.d(), pd])
                if jj == 1:
                    pb3, pd3 = ringF.get()
                    mm(pb3[:, 0:W(64)], AT, identr[0:64, 0:W(64)], [AR.d()], pd3)
                    mm(pb3[:, 64:64 + W(64)], mk[:, 0:128], vpad, [mk.d(), vtok.d()], pd3)
                    actf(ax[:, 0:128], pb3[:, 0:128], AF.Copy, [pd3], [ax.d()])
                yield
                Xc = Xn
            Rfin = x1 if Xc is x0 else x0
            pb, pd = ringF.get()
            mm(pb[:, 0:256 - HO], Xc[:, 256:384], Xc[:, HO:256], [Xc.d()], pd)
            tt("dve", Rfin[:, 0:128], pb[:, 128 - HO:256 - HO], Xc[:, 128:256], ALU.add, [pd, Xc.d()], [Rfin.d()])
            yield
            pb, pd = ringF.get()
            mm(pb[:, 0:W(128)], Rfin[:, 0:128], ax[:, 0:W(128)], [Rfin.d(), ax.d()], pd)
            actf(pq[:, 0:128], pb[:, 0:128], AF.Copy, [pd], [pq.d()])
            yield
            gt, hs, rh = GTb[slot], Hsb[slot], RhT[slot]
            pb, pd = ringF.get()
            mm(pb[0:64, 0:W(64)], pq[:, 0:64], bk[:, 0:W(64)], [pq.d(), bk.d()], pd)
            tt("dve", gt[:], pb[0:64, 0:64], identf[0:64, 0:64], ALU.add, [pd], [gt.d()])
            pb2, pd2 = ringF.get()
            mm(pb2[0:64, 0:W(64)], bk[:, 0:64], pq[:, 64:64 + W(64)], [pq.d(), bk.d()], pd2, True, False)
            mm(pb2[0:64, 0:W(64)], bk[:, 64:128], vpad, [bk.d(), vtok.d()], pd2, False, True)
            S.op("dve", lambda e: e.tensor_scalar(out=hs[:], in0=pb2[0:64, 0:64], scalar1=gamC[:, h:h + 1],
                                                  scalar2=None, op0=ALU.mult),
                 reads=[pd2, gamC.d(q)], writes=[hs.d()])
            pb3, pd3 = ringF.get()
            mm(pb3[0:64, 0:256 - HO], pq[:, 0:64], mrb[:, HO:256], [pq.d(), mrb.d()], pd3)
            tt("dve", rh[:, 128:256], pb3[0:64, 128 - HO:256 - HO], RT, ALU.add, [pd3, AR.d()], [rh.d()])
            yield

        def back(h, slot, q):
            j = h % 4
            y = QP[q % 2]["y"]
            vh = vtok[:, h * 64:(h + 1) * 64]
            mrb, mk, pq, gt, hs, rh = MRB[slot], MKb[slot], PQb[slot], GTb[slot], Hsb[slot], RhT[slot]
            pb, pd = ringF.get()
            mm(pb[0:64, 0:256 - HO], pq[:, 64:128], mrb[:, HO:256], [pq.d(), mrb.d()], pd, True, False)
            mm(pb[0:64, 0:256 - HO], vh, mk[:, HO:256], [vtok.d(), mk.d()], pd, False, False)
            mm(pb[0:64, 0:256 - HO], Sst[:, h, :], rh[:, HO:256], [Sst.d(h), rh.d()], pd, False, True)
            actf(y[:, j, :], pb[0:64, 128 - HO:256 - HO], AF.Copy, [pd], [y.d()])
            pb2, pd2 = ringF.get()
            mm(pb2[0:64, 0:W(64)], gt[:], Sf[:, h * 64:h * 64 + W(64)], [gt.d(), Sst.d(h)], pd2)
            S.op("dve", lambda e: e.scalar_tensor_tensor(
                out=Sst[:, h, :], in0=pb2[0:64, 0:64], scalar=gamC[:, h:h + 1], in1=hs[:],
                op0=ALU.mult, op1=ALU.add), reads=[pd2, gamC.d(q), hs.d()], writes=[Sst.d(h)])

        mixes = {}

        def stageA(dc):
            cur, prv = uTx[dc % 2], uTx[(dc + 1) % 2]
            S.dma("sp", xb[:], hv_in[dc], writes=[xb.d()])
            actf(xn[:], xb[:], AF.Square, [xb.d()], [xn.d(), st.d(0)], accum_out=st[:, 0:1])
            rstd_ops(0, 1)
            actf(xn[:], xb[:], AF.Copy, [xb.d(), st.d(1)], [xn.d()], scale=st[:, 1:2])
            for c in range(8):
                S.op("pe", lambda e, c=c: e.transpose(out=bankT[:, c, :], in_=xn[:, c * 128:(c + 1) * 128],
                                                      identity=identb[:]), reads=[xn.d()], writes=[bankT.d()])
            S.op("dve", lambda e: e.tensor_copy(out=cur[:, :, 1:129], in_=bankT[:, :, :]),
                 reads=[bankT.d()], writes=[cur.d()])
            S.op("dve", lambda e: e.tensor_copy(out=cur[:, :, 0:1], in_=prv[:, :, 128:129]),
                 reads=[prv.d()], writes=[cur.d()])
            tt("dve", xx[:], cur[:, :, 0:128], cur[:, :, 1:129], ALU.subtract, [cur.d()], [xx.d()])
            yield
            m3 = make_mix(3, mixb[3], cur)
            pv_, pd = ring_proj.get()
            for c in range(8):
                mm(pv_[0:64, 0:128], Wb[:, c, 3072:3136], m3[:, c, :], [m3.d()], pd, c == 0, c == 7)
            actf(th3[:], pv_[0:64, 0:128], AF.Tanh, [pd], [th3.d()])
            yield
            m4 = make_mix(4, mixb[3], cur)
            pv_, pd = ring_proj.get()
            for c in range(8):
                mm(pv_[0:64, 0:128], Wb[:, c, 3136:3200], m4[:, c, :], [m4.d()], pd, c == 0, c == 7)
            actf(p4b[:], pv_[0:64, 0:128], AF.Copy, [pd], [p4b.d()])
            yield
            m5 = make_mix(5, mixb[3], cur)
            pv_, pd = ring_proj.get()
            for c in range(8):
                mm(pv_[:, 0:128], Wb[:, c, 3200:3328], m5[:, c, :], [m5.d()], pd, c == 0, c == 7)
            for c in range(8):
                mm(pv_[0:32, 128:256], Wb[:, c, 3328:3360], m5[:, c, :], [m5.d()], pd, c == 0, c == 7)
            actf(s5a[:], pv_[:, 0:128], AF.Sigmoid, [pd], [s5a.d()])
            actf(s5b[:], pv_[0:32, 128:256], AF.Sigmoid, [pd], [s5b.d()])
            yield
            mixes[0] = make_mix(0, mixb[0], cur)
            yield
            mixes[1] = make_mix(1, mixb[1], cur)
            yield
            mixes[2] = make_mix(2, mixb[2], cur)
            yield

        def stageA2(dc):
            m2 = mixes[2]
            for half in range(2):
                bb, bbd = ring_proj.get()
                for c in range(8):
                    mm(bb, m2[:, c, :], Wb[:, c, 2048 + half * 512:2048 + (half + 1) * 512],
                       [m2.d()], bbd, c == 0, c == 7)
                actf(vtok[:, half * 512:(half + 1) * 512], bb, AF.Copy, [bbd], [vtok.d()])
                yield

        def prep(q):
            par = q % 2
            PENG = os.environ.get('RW_PENG', 'dve')
            AR, BTb, KTb, qp = ARs[par], BTs[par], KTs[par], QP[par]

            def evac_pairs(pv2, pd2, dst, eng="act"):
                src = pv2[:, 0:256].rearrange("p (a b) -> p a b", a=2)
                dv = dst[:].rearrange("p (a two) b -> p a two b", two=2)
                for half in range(2):
                    if eng == "act":
                        actf(dv[:, :, half, :], src[64 * half:64 * half + 64], AF.Copy, [pd2], [dst.d()])
                    else:
                        S.op("dve", lambda e, half=half: e.tensor_copy(out=dv[:, :, half, :],
                                                                       in_=src[64 * half:64 * half + 64]),
                             reads=[pd2], writes=[dst.d()])

            def proj4(mbuf, colbase, dst, eng="act"):
                pv2, pd2 = ring_proj.get()
                for pi in range(2):
                    pr = 2 * q + pi
                    for c in range(8):
                        mm(pv2[:, pi * 128:(pi + 1) * 128], Wb[:, c, colbase + pr * 128:colbase + (pr + 1) * 128],
                           mbuf[:, c, :], [mbuf.d()], pd2, c == 0, c == 7)
                evac_pairs(pv2, pd2, dst, eng)

            proj4(mixes[0], 0, qp["r"])
            yield
            proj4(mixes[1], 1024, Q["k"], "dve")
            yield
            proj4(mixes[2], 2048, qp["vT"])
            yield
            pv2, pd2 = ring_proj.get()
            for pi in range(2):
                pr = 2 * q + pi
                mm(pv2[:, pi * 128:(pi + 1) * 128], ww2[:, pr * 128:(pr + 1) * 128], th3[:], [th3.d()], pd2)
            evac_pairs(pv2, pd2, Q["sig"], "dve")
            tt(PENG, Q["sig"][:], Q["sig"][:], bc(HP_W0, q), ALU.add, [Q["sig"].d()], [Q["sig"].d()])
            actf(f2(Q["sig"]), f2(Q["sig"]), AF.Sigmoid, [Q["sig"].d()], [Q["sig"].d()])
            yield
            pv2, pd2 = ring_proj.get()
            for pi in range(2):
                pr = 2 * q + pi
                mm(pv2[:, pi * 128:(pi + 1) * 128], wa2[:, pr * 128:(pr + 1) * 128], p4b[:], [p4b.d()], pd2)
            evac_pairs(pv2, pd2, Q["a"], "dve")
            tt(PENG, Q["a"][:], Q["a"][:], bc(HP_A0, q), ALU.add, [Q["a"].d()], [Q["a"].d()])
            actf(f2(Q["a"]), f2(Q["a"]), AF.Sigmoid, [Q["a"].d()], [Q["a"].d()])
            yield
            pv2, pd2 = ring_proj.get()
            for pi in range(2):
                pr = 2 * q + pi
                mm(pv2[:, pi * 128:(pi + 1) * 128], wg2a[:, pr * 128:(pr + 1) * 128], s5a[:], [s5a.d()], pd2,
                   True, False)
                mm(pv2[:, pi * 128:(pi + 1) * 128], wg2b[:, pr * 128:(pr + 1) * 128], s5b[:], [s5b.d()], pd2,
                   False, True)
            evac_pairs(pv2, pd2, qp["g"])
            yield
            S.op(PENG, lambda e: e.tensor_scalar(out=f2(Q["sig"]), in0=f2(Q["sig"]), scalar1=-0.6065306597126334,
                                                  scalar2=None, op0=ALU.mult),
                 reads=[Q["sig"].d()], writes=[Q["sig"].d()])
            S.op("dve", lambda e: e.tensor_tensor_scan(out=f2(Q["cs"]), data0=scanm[:], data1=f2(Q["sig"]),
                                                       initial=0.0, op0=ALU.mult, op1=ALU.add),
                 reads=[Q["sig"].d()], writes=[Q["cs"].d()])
            tt(PENG, Q["kk"][:], Q["k"][:], bc(HP_KK, q), ALU.mult, [Q["k"].d()], [Q["kk"].d()])
            actf(f2(Q["t1r"]), f2(Q["kk"]), AF.Square, [Q["kk"].d()], [Q["t1r"].d()])
            yield
            pv2, pd2 = ring_proj.get()
            mm(pv2[0:64, :], ones64r[:], f2(Q["t1r"]), [Q["t1r"].d()], pd2)
            actf(f2(Q["t1"]), pv2[0:64, :], AF.Sqrt, [pd2], [Q["t1"].d()])
            S.op("dve", lambda e: e.tensor_scalar(out=f2(Q["t1"]), in0=f2(Q["t1"]), scalar1=1e-12, scalar2=None,
                                                  op0=ALU.max), reads=[Q["t1"].d()], writes=[Q["t1"].d()])
            S.op("dve", lambda e: e.reciprocal(out=f2(Q["t1"]), in_=f2(Q["t1"])),
                 reads=[Q["t1"].d()], writes=[Q["t1"].d()])
            tt(PENG, Q["kk"][:], Q["kk"][:], Q["t1"][:], ALU.mult, [Q["kk"].d(), Q["t1"].d()], [Q["kk"].d()])
            yield
            S.op("dve", lambda e: e.scalar_tensor_tensor(out=Q["t1"][:], in0=Q["a"][:], scalar=-1.0, in1=bc(HP_KA, q),
                                                         op0=ALU.add, op1=ALU.mult),
                 reads=[Q["a"].d()], writes=[Q["t1"].d()])
            S.op("dve", lambda e: e.scalar_tensor_tensor(out=f2(qp["kmod"]), in0=f2(Q["t1"]), scalar=1.0,
                                                         in1=f2(Q["k"]), op0=ALU.add, op1=ALU.mult),
                 reads=[Q["t1"].d(), Q["k"].d()], writes=[qp["kmod"].d()])
            actf(f2(Q["eneg"]), f2(Q["cs"]), AF.Exp, [Q["cs"].d()], [Q["eneg"].d()], scale=-1.0)
            actf(f2(Q["epos"]), f2(Q["cs"]), AF.Exp, [Q["cs"].d()], [Q["epos"].d()])
            yield
            tt(PENG, Q["eexc"][:], Q["cs"][:], Q["sig"][:], ALU.subtract, [Q["cs"].d(), Q["sig"].d()],
               [Q["eexc"].d()])
            actf(f2(Q["eexc"]), f2(Q["eexc"]), AF.Exp, [Q["eexc"].d()], [Q["eexc"].d()])
            S.op("dve", lambda e: e.tensor_copy(out=gamC[:, 4 * q:4 * q + 4], in_=Q["epos"][:, :, 127]),
                 reads=[Q["epos"].d()], writes=[gamC.d(q)])
            yield
            S.op("dve", lambda e: e.scalar_tensor_tensor(out=AR[:, :, 0, :], in0=Q["kk"][:], scalar=-1.0,
                                                         in1=Q["eexc"][:], op0=ALU.mult, op1=ALU.mult),
                 reads=[Q["kk"].d(), Q["eexc"].d()], writes=[AR.d()])
            tt(PENG, AR[:, :, 1, :], qp["r"][:], Q["epos"][:], ALU.mult, [qp["r"].d(), Q["epos"].d()], [AR.d()])
            yield
            tt(PENG, Q["t1"][:], Q["kk"][:], Q["a"][:], ALU.mult, [Q["kk"].d(), Q["a"].d()], [Q["t1"].d()])
            tt(PENG, BTb[:, 0:4, :], Q["t1"][:], Q["eneg"][:], ALU.mult, [Q["t1"].d(), Q["eneg"].d()], [BTb.d()])
            tt(PENG, KTb[:, 0:4, :], qp["kmod"][:], Q["eneg"][:], ALU.mult, [qp["kmod"].d(), Q["eneg"].d()],
               [KTb.d()])
            yield

        def gn(q, dc):
            qp = QP[q % 2]
            y, t1, t1r = qp["y"], Q["gt1"], Q["gt1r"]
            heads = [4 * q + j for j in range(4)]
            actf(f2(t1r), f2(y), AF.Copy, [y.d()], [t1r.d()])
            pv2, pd2 = ring_proj.get()
            mm(pv2[0:64, :], ones64r[:], f2(t1r), [t1r.d()], pd2)
            S.op("dve", lambda e: e.tensor_scalar(out=f2(t1), in0=pv2[0:64, :], scalar1=1.0 / 64, scalar2=None,
                                                  op0=ALU.mult), reads=[pd2], writes=[t1.d()])
            tt("dve", y[:], y[:], t1[:], ALU.subtract, [y.d(), t1.d()], [y.d()])
            actf(f2(t1r), f2(y), AF.Square, [y.d()], [t1r.d()])
            yield
            pv2, pd2 = ring_proj.get()
            mm(pv2[0:64, :], ones64r[:], f2(t1r), [t1r.d()], pd2)
            S.op("dve", lambda e: e.tensor_scalar(out=f2(t1), in0=pv2[0:64, :], scalar1=1.0 / 64,
                                                  scalar2=GN_EPS, op0=ALU.mult, op1=ALU.add),
                 reads=[pd2], writes=[t1.d()])
            actf(f2(t1), f2(t1), AF.Sqrt, [t1.d()], [t1.d()])
            S.op("dve", lambda e: e.reciprocal(out=f2(t1), in_=f2(t1)), reads=[t1.d()], writes=[t1.d()])
            tt("dve", y[:], y[:], t1[:], ALU.mult, [y.d(), t1.d()], [y.d()])
            yield
            tt("pool", y[:], y[:], bc(HP_GNW, q), ALU.mult, [y.d()], [y.d()])
            tt("pool", y[:], y[:], bc(HP_GNB, q), ALU.add, [y.d()], [y.d()])
            tt("pool", t1[:], qp["r"][:], qp["kmod"][:], ALU.mult, [qp["r"].d(), qp["kmod"].d()], [t1.d()])
            tt("dve", t1r[:], t1[:], bc(HP_RK, q), ALU.mult, [t1.d()], [t1r.d()])
            yield
            pv2, pd2 = ring_proj.get()
            mm(pv2[0:64, :], ones64r[:], f2(t1r), [t1r.d()], pd2)
            tt("dve", t1[:], pv2[0:64, :].rearrange("p (a b) -> p a b", a=4), qp["vT"][:], ALU.mult,
               [pd2, qp["vT"].d()], [t1.d()])
            tt("pool", y[:], y[:], t1[:], ALU.add, [y.d(), t1.d()], [y.d()])
            yield
            for j, h in enumerate(heads):
                po = 64 * (h % 2)
                tt("dve", yfin[po:po + 64, h // 2, :], y[:, j, :], qp["g"][:, j, :], ALU.mult,
                   [y.d(), qp["g"].d()], [yfin.d()])
            if q == 3:
                S.dma("pool", yT_v[:, :, dc * 128:(dc + 1) * 128], yfin[:, :, :], reads=[yfin.d()])
            yield

        def chain(*gens):
            for g_ in gens:
                if g_ is not None:
                    yield from g_

        XSTEP = int(os.environ.get("RW_XSTEP", "1"))

        def run(gens, extra=None):
            alive = [(g_, 1) for g_ in gens if g_ is not None]
            if extra is not None:
                if os.environ.get("RW_XFIRST"):
                    alive.insert(0, (extra, XSTEP))
                else:
                    alive.append((extra, XSTEP))
            while alive:
                nxt = []
                for g_, k_ in alive:
                    ok = True
                    for _ in range(k_):
                        try:
                            next(g_)
                        except StopIteration:
                            ok = False
                            break
                    if ok:
                        nxt.append((g_, k_))
                alive = nxt

        run([chain(stageA(0), stageA2(0), prep(0))])
        gn_prev = None
        for dc in range(ndc):
            more = dc + 1 < ndc
            for q in range(4):
                heads = [4 * q + j for j in range(4)]
                if q < 3:
                    extra = chain(gn(q - 1, dc) if q > 0 else None, prep(q + 1))
                else:
                    extra = chain(gn(2, dc), stageA(dc + 1) if more else None, prep(0) if more else None)
                run([front(h, j, q) for j, h in enumerate(heads)], extra)
                for j, h in enumerate(heads):
                    back(h, j, q)
            gn_prev = gn(3, dc)
            run([stageA2(dc + 1) if more else None, gn_prev])
            gn_prev = None
        S.barrier()
    outproj_phase(S, h_in, h_out, yT_dram, P["w_out"], P["gpost_b"], tag + "o", ntile=ndc)


def rwkv_consts():
    import ml_dtypes
    i = np.arange(128)
    su = (i[:, None] < i[None, :]).astype(np.float32)
    u = (i[:, None] <= i[None, :]).astype(np.float32)
    scanm = np.ones((64, 512), np.float32)
    scanm[:, ::128] = 0.0
    return {
        "identb": np.eye(128, dtype=np.float32).astype(ml_dtypes.bfloat16),
        "identf": np.eye(128, dtype=np.float32),
        "ones64": np.ones((64, 64), np.float32),
        "mask2": np.ascontiguousarray(np.concatenate([su, u], axis=1)),
        "masksl": np.ascontiguousarray(su.T),
        "scanm": scanm,
    }


def _pl(v):
    return np.ascontiguousarray(np.asarray(v, np.float32).reshape(8, 128).T)


def _bcast(v):
    return np.ascontiguousarray(np.broadcast_to(np.asarray(v, np.float32), (128, D)))


def rwkv_host_params(mix_norm_pre, mix_norm_post, mu, w0, a0, k_k, k_a, r_k, gn_w, gn_b):
    hp = np.stack([np.asarray(t, np.float32).reshape(16, 64) for t in
                   (w0, a0, k_k, k_a, r_k, gn_w, gn_b)], axis=0)
    return {
        "gpre_l": _pl(mix_norm_pre), "gpost_b": _bcast(mix_norm_post),
        "mu_l": np.ascontiguousarray(np.asarray(mu, np.float32).reshape(6, 8, 128).transpose(2, 0, 1)),
        "hp": np.ascontiguousarray(hp.transpose(2, 0, 1)),
    }


def outproj_phase(S, h_in, h_out, attnT, w_out, gpost_b, tag, ntile=32):
    P = {"w_out": w_out, "gpost_b": gpost_b}

    def mmf(out, lhsT, rhs, reads, wdep_, start=True, stop=True):
        S.op("pe", lambda e: e.matmul(out, lhsT, rhs, start=start, stop=stop), reads=reads, writes=[wdep_])
    with contextlib.ExitStack() as es:
        def SB(name, shape, dt):
            return sb(S, es, tag + name, shape, dt)
        wo = SB("wo", [128, 8, D], BF16)
        gpost = SB("gpost", [128, D], F32)
        stg = [SB(f"stg{i}", [128, 1024], F32) for i in range(2)]
        G = 4 if ntile % 4 == 0 else 1
        aT = [SB(f"aT{i}", [128, 8, 128 * G], BF16) for i in range(2)]
        xb = [SB(f"xb{i}", [128, D], F32) for i in range(2)]
        tmp = [SB(f"tmp{i}", [128, D], F32) for i in range(2)]
        junk = SB("junk", [128, 512], BF16)
        st = SB("st", [128, 8], F32)
        bk = [ps(S, es, tag + f"bk{i}", [128, 512], F32) for i in range(2)]
        S.dma("sp", gpost[:], P["gpost_b"], writes=[gpost.d()])
        for c in range(8):
            b = stg[c % 2]
            S.dma("sp", b[:, :], P["w_out"][c * 128:(c + 1) * 128, :], writes=[b.d()])
            S.op("act", lambda e, c=c, b=b: e.activation(out=wo[:, c, :], in_=b[:, :], func=AF.Copy), reads=[b.d()])
        S.barrier()
        a_v = attnT.rearrange("(c p) t -> p c t", p=128)
        hv_in = h_in.rearrange("(t p) d -> t p d", p=128)
        hv_out = h_out.rearrange("(t p) d -> t p d", p=128)
        for t in range(ntile):
            a, x, tm = aT[(t // G) % 2], xb[t % 2], tmp[t % 2]
            so = (t % G) * 128
            if t % G == 0:
                S.dma("sp", a[:, :, :], a_v[:, :, t * 128:(t + G) * 128], writes=[a.d()])
            S.dma("sp", x[:], hv_in[t], writes=[x.d()])
            for half in range(2):
                b = bk[half]
                for c in range(8):
                    mmf(b[:, :], a[:, c, so:so + 128], wo[:, c, half * 512:(half + 1) * 512], [a.d()], b.d(),
                        c == 0, c == 7)
                S.op("act", lambda e, b=b, half=half: e.activation(out=junk[:], in_=b[:, :], func=AF.Square,
                                                                   accum_out=st[:, half:half + 1]),
                     reads=[b.d()], writes=[junk.d(), st.d(half), b.d("port")])
                S.op("dve", lambda e, b=b, half=half, tm=tm: e.tensor_tensor(
                    out=tm[:, half * 512:(half + 1) * 512], in0=b[:, :], in1=gpost[:, half * 512:(half + 1) * 512],
                    op=ALU.mult), reads=[b.d()], writes=[tm.d(), b.d("port")])
            S.op("dve", lambda e: e.tensor_tensor(out=st[:, 2:3], in0=st[:, 0:1], in1=st[:, 1:2], op=ALU.add),
                 reads=[st.d(0), st.d(1)], writes=[st.d(2)])
            S.op("dve", lambda e: e.tensor_scalar(out=st[:, 2:3], in0=st[:, 2:3], scalar1=1.0 / D, scalar2=RMS_EPS,
                                                  op0=ALU.mult, op1=ALU.add), reads=[st.d(2)], writes=[st.d(2)])
            S.op("act", lambda e: e.activation(out=st[:, 2:3], in_=st[:, 2:3], func=AF.Sqrt),
                 reads=[st.d(2)], writes=[st.d(2)])
            S.op("dve", lambda e: e.reciprocal(out=st[:, 2:3], in_=st[:, 2:3]), reads=[st.d(2)], writes=[st.d(2)])
            S.op("dve", lambda e, tm=tm, x=x: e.scalar_tensor_tensor(out=tm[:], in0=tm[:], scalar=st[:, 2:3], in1=x[:],
                                                                    op0=ALU.mult, op1=ALU.add),
                 reads=[tm.d(), st.d(2), x.d()], writes=[tm.d()])
            S.dma("pool", hv_out[t], tm[:], reads=[tm.d()])
        S.barrier()


NSA_IN = 2608
NEG = -30000.0


def nsa_consts():
    import ml_dtypes
    bf = ml_dtypes.bfloat16
    slopes = (2.0 ** (-8.0 * np.arange(1, 17, dtype=np.float64) / 16)).astype(np.float32)
    t = np.arange(4096, dtype=np.float64)
    aq = np.zeros((16, 3, 4096), dtype=bf)
    for h in range(16):
        v = (-slopes[h].astype(np.float64) * t).astype(np.float32)
        r = v.copy()
        for k in range(3):
            p = r.astype(bf)
            aq[h, k] = p
            r = (r - p.astype(np.float32)).astype(np.float32)
    j = np.arange(128)
    i = np.arange(512)
    bias_key = np.zeros((128, 16, 32), np.float32)
    bias_cmp = np.zeros((128, 16, 2), np.float32)
    for h in range(16):
        for kt in range(32):
            bias_key[:, h, kt] = slopes[h] * (128 * kt + j)
        for nt in range(2):
            bias_cmp[:, h, nt] = slopes[h] * (16 * (128 * nt + j) + 31)
    cmpmask = np.zeros((128, 8, 512), np.float32)
    for idx in range(8):
        cmpmask[:, idx, :] = np.where(16 * j[:, None] + 31 - i[None, :] <= 512 * idx, 0.0, NEG)
    causal = np.zeros((128, 4, 512), np.float32)
    for r_ in range(4):
        causal[:, r_, :] = np.where(j[:, None] + 128 * r_ <= i[None, :], 0.0, NEG)
    win = np.zeros((128, 8, 512), np.float32)
    for w in range(8):
        dist = i[None, :] - j[:, None] - 128 * (w - 4)
        win[:, w, :] = np.where((dist >= 0) & (dist < 512), 0.0, NEG)
    E = np.zeros((64, 4096), np.float32)
    E[np.arange(4096) // 64, np.arange(4096)] = 1.0
    cs_ = np.arange(255) * 16
    bs_ = np.arange(64) * 64
    ov = np.clip(np.minimum(cs_[:, None] + 32, bs_[None, :] + 64) - np.maximum(cs_[:, None], bs_[None, :]), 0, None) / 32.0
    Caug = np.zeros((256, 65), np.float32)
    Caug[:255, :64] = ov
    Caug[:255, 64] = 1.0
    Caug = Caug.reshape(2, 128, 65).transpose(1, 0, 2)
    vis = np.zeros((128, 32, 64), np.float32)
    add = np.zeros((128, 32, 64), np.float32)
    s = np.arange(64)
    for it in range(32):
        tq = 128 * it + j
        cur = tq // 64
        visible = s[None, :] * 64 <= tq[:, None]
        a = np.where(visible, 0.0, -1.0)
        v = visible.astype(np.float32)
        for (cond, val) in [(s[None, :] == 0, 1e4), (s[None, :] == cur[:, None], 2e4),
                            (s[None, :] == cur[:, None] - 1, 3e4)]:
            a = np.where(cond, val, a)
            v = np.where(cond, 0.0, v)
        vis[:, it, :] = v
        add[:, it, :] = a
    return {
        "alibi_q": aq, "bias_key": bias_key, "bias_cmp": bias_cmp,
        "cmpmask": cmpmask.astype(bf), "causal": causal.astype(bf), "winmask": win.astype(bf),
        "E_all": np.concatenate([E[1:64], np.ones((1, 4096), np.float32)], axis=0).astype(bf), "C_aug": np.ascontiguousarray(Caug).astype(bf),
        "vis": vis, "addt": add,
        "identb": np.eye(128, dtype=np.float32).astype(bf),
    }


def nsa_host_params(mix_norm_pre, mix_norm_post, pe_k, w_ck1, pe_v, w_cv1):
    return {
        "gpre_l": _pl(mix_norm_pre), "gpost_b": _bcast(mix_norm_post),
        "pekT": np.ascontiguousarray(np.asarray(pe_k, np.float32).T),
        "pevT": np.ascontiguousarray(np.asarray(pe_v, np.float32).T),
        "wck1_l": np.ascontiguousarray(np.asarray(w_ck1, np.float32).transpose(1, 0, 2)),
        "wcv1_l": np.ascontiguousarray(np.asarray(w_cv1, np.float32).transpose(1, 0, 2)),
    }


def nsa_phase(S, h_in, h_out, P, C, scr, tag="n"):
    nc = S.nc
    projT, vtokd, gTd, attnT = scr["projT"], scr["vtok"], scr["gT"], scr["attnT"]

    def mmf(out, lhsT, rhs, reads, wdep_, start=True, stop=True):
        S.op("pe", lambda e: e.matmul(out, lhsT, rhs, start=start, stop=stop), reads=reads, writes=[wdep_])

    with contextlib.ExitStack() as es:
        def SB(name, shape, dt):
            return sb(S, es, tag + "1" + name, shape, dt)
        Wb = SB("Wb", [128, 8, NSA_IN], BF16)
        gpre = SB("gpre", [128, 8], F32)
        identb = SB("identb", [128, 128], BF16)
        stg = [SB(f"stg{i}", [128, 1024], F32) for i in range(2)]
        xb = [SB(f"xb{i}", [128, 4, D], F32) for i in range(2)]
        xn = [SB(f"xn{i}", [128, D], BF16) for i in range(2)]
        junk = SB("junk", [128, D], BF16)
        uT = [SB(f"uT{i}", [128, 8, 512], BF16) for i in range(2)]
        ev = [SB(f"ev{i}", [128, 512], BF16) for i in range(4)]
        evf = [SB(f"evf{i}", [48, 512], F32) for i in range(2)]
        evt = [SB(f"evt{i}", [128, 512], BF16) for i in range(2)]
        st = SB("st", [128, 8], F32)
        bankT = ps(S, es, tag + "1bT", [128, 8, 128], BF16)
        banks = [ps(S, es, tag + f"1bk{i}", [128, 512], F32) for i in range(4)]
        S.dma("sp", gpre[:], P["gpre_l"], writes=[gpre.d()])
        S.dma("sp", identb[:], C["identb"], writes=[identb.d()])
        k = 0
        for c in range(8):
            for o in range(0, NSA_IN, 1024):
                wdt = min(1024, NSA_IN - o)
                b = stg[k % 2]
                S.dma("sp", b[:, 0:wdt], P["w_in"][c * 128:(c + 1) * 128, o:o + wdt], writes=[b.d()])
                S.op("act", lambda e, c=c, o=o, wdt=wdt, b=b: e.activation(
                    out=Wb[:, c, o:o + wdt], in_=b[:, 0:wdt], func=AF.Copy, scale=gpre[:, c:c + 1]),
                    reads=[b.d(), gpre.d()])
                k += 1
        S.barrier()
        hv_in = h_in.rearrange("(t s p) d -> t p s d", p=128, s=4)
        chunks = [(o, 128) for o in range(0, 2560, 128)] + [(2560, 48)]
        ke = 0
        for t in range(8):
            x, u = xb[t % 2], uT[t % 2]
            S.dma("sp", x[:, :, :], hv_in[t], writes=[x.d()])
            for s in range(4):
                xnb = xn[s % 2]
                S.op("act", lambda e, x=x, s=s: e.activation(out=junk[:], in_=x[:, s, :], func=AF.Square,
                                                             accum_out=st[:, s:s + 1]),
                     reads=[x.d()], writes=[junk.d(), st.d(s)])
                S.op("dve", lambda e, s=s: e.tensor_scalar(out=st[:, 4 + s:5 + s], in0=st[:, s:s + 1],
                                                           scalar1=1.0 / D, scalar2=RMS_EPS, op0=ALU.mult,
                                                           op1=ALU.add), reads=[st.d(s)], writes=[st.d(4 + s)])
                S.op("act", lambda e, s=s: e.activation(out=st[:, 4 + s:5 + s], in_=st[:, 4 + s:5 + s], func=AF.Sqrt),
                     reads=[st.d(4 + s)], writes=[st.d(4 + s)])
                S.op("dve", lambda e, s=s: e.reciprocal(out=st[:, 4 + s:5 + s], in_=st[:, 4 + s:5 + s]),
                     reads=[st.d(4 + s)], writes=[st.d(4 + s)])
                S.op("act", lambda e, x=x, s=s, xnb=xnb: e.activation(out=xnb[:], in_=x[:, s, :], func=AF.Copy,
                                                                      scale=st[:, 4 + s:5 + s]),
                     reads=[x.d(), st.d(4 + s)], writes=[xnb.d()])
                for c in range(8):
                    S.op("pe", lambda e, c=c, xnb=xnb: e.transpose(out=bankT[:, c, :],
                                                                   in_=xnb[:, c * 128:(c + 1) * 128],
                                                                   identity=identb[:]),
                         reads=[xnb.d()], writes=[bankT.d()])
                S.op("dve", lambda e, s=s, u=u: e.tensor_copy(out=u[:, :, s * 128:(s + 1) * 128], in_=bankT[:, :, :]),
                     reads=[bankT.d()], writes=[u.d()])
            for (o, m) in chunks:
                bk = banks[ke % 3]
                for c in range(8):
                    mmf(bk[0:m, :], Wb[:, c, o:o + m], u[:, c, :], [u.d()], bk.d(), c == 0, c == 7)
                if o == 2560:
                    e_ = evf[ke % 2]
                    S.op("act", lambda e, bk=bk, e_=e_: e.activation(out=e_[:], in_=bk[0:48, :], func=AF.Sigmoid),
                         reads=[bk.d()], writes=[e_.d()])
                    S.dma("pool", gTd[:, t * 512:(t + 1) * 512], e_[:], reads=[e_.d()])
                else:
                    e_ = ev[ke % 4]
                    sc = 0.125 if o < 1024 else 1.0
                    if ke % 2 == 0:
                        S.op("act", lambda e, bk=bk, e_=e_, sc=sc: e.activation(out=e_[:], in_=bk[:, :], func=AF.Copy,
                                                                               scale=sc),
                             reads=[bk.d()], writes=[e_.d()])
                    else:
                        S.op("dve", lambda e, bk=bk, e_=e_, sc=sc: e.tensor_scalar(out=e_[:], in0=bk[:, :], scalar1=sc,
                                                                                  scalar2=None, op0=ALU.mult),
                             reads=[bk.d()], writes=[e_.d()])
                    S.dma("pool", projT[o:o + 128, t * 512:(t + 1) * 512], e_[:], reads=[e_.d()])
                ke += 1
            for s in range(4):
                bk = banks[3]
                for (jj, o) in enumerate((1792, 2304)):
                    for c in range(8):
                        mmf(bk[:, jj * 256:(jj + 1) * 256], u[:, c, s * 128:(s + 1) * 128], Wb[:, c, o:o + 256],
                            [u.d()], bk.d(), c == 0, c == 7)
                e_ = evt[s % 2]
                S.op("act", lambda e, bk=bk, e_=e_: e.activation(out=e_[:], in_=bk[:, :], func=AF.Copy),
                     reads=[bk.d()], writes=[e_.d()])
                S.dma("pool", vtokd[t * 512 + s * 128:t * 512 + (s + 1) * 128, :], e_[:], reads=[e_.d()])
        S.barrier()

    with contextlib.ExitStack() as es:
        def SB(name, shape, dt):
            return sb(S, es, tag + "2" + name, shape, dt)
        ksA = SB("ksA", [128, 4096], BF16)
        kwA = SB("kwA", [128, 4096], BF16)
        kcT = SB("kcT", [64, 4096], BF16)
        vcT = SB("vcT", [64, 4096], BF16)
        vsA = SB("vsA", [128, 32, 128], BF16)
        vwA = SB("vwA", [128, 32, 128], BF16)
        qA = [SB(f"qA{i}", [128, 4096], BF16) for i in range(4)]
        kcmpA = SB("kcmpA", [128, 256], BF16)
        vcmpA = SB("vcmpA", [128, 2, 128], BF16)
        bias_key = SB("bias_key", [128, 16, 32], F32)
        bias_cmp = SB("bias_cmp", [128, 16, 2], F32)
        cmpmask = SB("cmpmask", [128, 8, 512], BF16)
        causal = SB("causal", [128, 4, 512], BF16)
        winmask = SB("winmask", [128, 8, 512], BF16)
        C_aug = SB("C_aug", [128, 2, 65], BF16)
        vis = SB("vis", [128, 32, 64], F32)
        addt = SB("addt", [128, 32, 64], F32)
        identb = SB("identb", [128, 128], BF16)
        w1 = [SB(f"w1_{i}", [64, 32, 64], BF16) for i in range(2)]
        w2 = [SB(f"w2_{i}", [64, 64], BF16) for i in range(2)]
        peT = [SB(f"peT{i}", [64, 32], BF16) for i in range(2)]
        cbias = SB("cbias", [64, 2], F32)
        stgw = SB("stgw", [64, 2048], F32)
        hid = [SB(f"hid{i}", [64, 256], BF16) for i in range(2)]
        eTs = [SB(f"eT{i}", [128, 512], BF16) for i in range(8)]
        imp_acc = SB("imp_acc", [128, 4, 64], F32)
        imp2 = SB("imp2", [128, 64], F32)
        imp3 = SB("imp3", [128, 64], F32)
        m8 = SB("m8", [128, 16], F32)
        mk = SB("mk", [128, 64], F32)
        negq = SB("negq", [128, 64], BF16)
        rq = SB("rq", [128, 4], F32)
        gbc = [SB(f"gbc{i}", [64, 3, 512], F32) for i in range(4)]
        acc = SB("acc", [64, 512], F32)
        rd = SB("rd", [64, 512], F32)
        tb = SB("tb", [64, 512], F32)
        outb = [SB(f"outb{i}", [64, 512], BF16) for i in range(2)]
        cacc = [SB(f"cacc{i}", [64, 512], F32) for i in range(4)]
        bsc = [ps(S, es, tag + f"2sc{i}", [128, 512], F32) for i in range(4)]
        bpv = [ps(S, es, tag + f"2pv{i}", [128, 512], F32) for i in range(2)]
        bimp = ps(S, es, tag + "2imp", [128, 512], F32)
        btr = ps(S, es, tag + "2tr", [128, 512], BF16)
        bcm = bimp
        ring_sc = Ring([(b, b.d()) for b in bsc])
        ring_pv = Ring([(b, b.d()) for b in bpv])
        ring_e = Ring([(b, b.d()) for b in eTs])

        for (t_, nm) in [(bias_key, "bias_key"), (bias_cmp, "bias_cmp"), (cmpmask, "cmpmask"), (causal, "causal"),
                         (winmask, "winmask"), (C_aug, "C_aug"), (vis, "vis"), (addt, "addt"),
                         (identb, "identb")]:
            S.dma("sp", t_[:], C[nm], writes=[t_.d()])
        for bi, (w1n, w2n, pen) in enumerate([("wck1_l", "w_ck2", "pekT"), ("wcv1_l", "w_cv2", "pevT")]):
            S.dma("sp", stgw[:, :].rearrange("p (l c) -> p l c", l=32), P[w1n], writes=[stgw.d()])
            S.op("dve", lambda e, bi=bi: e.tensor_copy(out=w1[bi][:].rearrange("p l c -> p (l c)"), in_=stgw[:, :]),
                 reads=[stgw.d()], writes=[w1[bi].d()])
            S.dma("sp", stgw[:, 0:64], P[w2n], writes=[stgw.d()])
            S.op("dve", lambda e, bi=bi: e.tensor_copy(out=w2[bi][:], in_=stgw[:, 0:64]),
                 reads=[stgw.d()], writes=[w2[bi].d()])
            S.dma("sp", stgw[:, 0:32], P[pen], writes=[stgw.d()])
            S.op("dve", lambda e, bi=bi: e.tensor_copy(out=peT[bi][:], in_=stgw[:, 0:32]),
                 reads=[stgw.d()], writes=[peT[bi].d()])
            for l in range(32):
                mmf(bcm[0:64, 0:1], w1[bi][:, l, :], peT[bi][:, l:l + 1], [w1[bi].d(), peT[bi].d()], bcm.d(),
                    l == 0, l == 31)
            S.op("dve", lambda e, bi=bi: e.tensor_copy(out=cbias[:, bi:bi + 1], in_=bcm[0:64, 0:1]),
                 reads=[bcm.d()], writes=[cbias.d()])
        for t_ in (kwA, kcmpA) + tuple(qA):
            S.op("dve", lambda e, t_=t_: e.memset(t_[64:128, :], 0.0), writes=[t_.d()])
        S.dma("sp", ksA[64:128, :], C["E_all"], writes=[ksA.d()])
        S.dma("sp", kwA[127:128, :], C["E_all"][63:64, :], writes=[kwA.d()])
        S.dma("sp", kcmpA[127:128, :], C["E_all"][63:64, 0:256], writes=[kcmpA.d()])
        S.op("dve", lambda e: e.memset(negq[:], 0.0), writes=[negq.d()])
        S.op("dve", lambda e: e.memset(vcmpA[:, :, 0:64], 0.0), writes=[vcmpA.d()])
        for t_ in (vsA, vwA, vcmpA):
            S.op("dve", lambda e, t_=t_: e.memset(t_[:, :, 64:128], 1.0), writes=[t_.d()])
        S.op("dve", lambda e: e.memset(kcmpA[0:64, 255:256], 0.0), writes=[kcmpA.d()])
        vt_v = vtokd.rearrange("(kt p) d -> p kt d", p=128)
        pend = []
        import os
        LAG = int(os.environ.get('NSA_LAG', '4'))
        S.barrier()

        for g in range(4):
            S.dma("sp", ksA[0:64, :], projT[1536 + g * 64:1536 + (g + 1) * 64, :], writes=[ksA.d()])
            S.dma("sp", kwA[0:64, :], projT[2048 + g * 64:2048 + (g + 1) * 64, :], writes=[kwA.d()])
            S.dma("sp", kcT[:, :], projT[1024 + g * 64:1024 + (g + 1) * 64, :], writes=[kcT.d()])
            S.dma("sp", vcT[:, :], projT[1280 + g * 64:1280 + (g + 1) * 64, :], writes=[vcT.d()])
            S.dma("sp", vsA[:, :, 0:64], vt_v[:, :, g * 64:(g + 1) * 64], writes=[vsA.d()])
            S.dma("sp", vwA[:, :, 0:64], vt_v[:, :, 256 + g * 64:256 + (g + 1) * 64], writes=[vwA.d()])
            for r in range(4):
                h = 4 * g + r
                S.dma("sp", qA[r][0:64, :], projT[h * 64:(h + 1) * 64, :], writes=[qA[r].d()])
                S.dma("sp", qA[r][127:128, :], C["alibi_q"][h, 0:1, :], writes=[qA[r].d()])
            for bi, src in enumerate((kcT, vcT)):
                for l in range(32):
                    mmf(bcm[0:64, 0:255], w1[bi][:, l, :], src[:, l:l + 16 * 254 + 1:16], [src.d()], bcm.d(),
                        l == 0, l == 31)
                S.op("act", lambda e, bi=bi: e.activation(out=hid[bi][:, 0:255], in_=bcm[0:64, 0:255], func=AF.Silu,
                                                          bias=cbias[:, bi:bi + 1]),
                     reads=[bcm.d(), cbias.d()], writes=[hid[bi].d()])
            mmf(bcm[0:64, 0:255], w2[0][:], hid[0][:, 0:255], [hid[0].d()], bcm.d())
            S.op("dve", lambda e: e.tensor_copy(out=kcmpA[0:64, 0:255], in_=bcm[0:64, 0:255]),
                 reads=[bcm.d()], writes=[kcmpA.d()])
            for nt in range(2):
                nn = 128 if nt == 0 else 127
                mmf(bcm[0:nn, 256 + nt * 64:256 + (nt + 1) * 64], hid[1][:, nt * 128:nt * 128 + nn], w2[1][:],
                    [hid[1].d()], bcm.d())
                S.op("dve", lambda e, nt=nt, nn=nn: e.tensor_copy(out=vcmpA[0:nn, nt, 0:64],
                                                                 in_=bcm[0:nn, 256 + nt * 64:256 + (nt + 1) * 64]),
                     reads=[bcm.d()], writes=[vcmpA.d()])

            def push_branch(tiles, qc, done_cb):
                pvb, pvd = ring_pv.get()
                n = len(tiles)
                ets = []
                for ti, (kT, bias, masks, vT, rds) in enumerate(tiles):
                    sc, scd = ring_sc.get()
                    mmf(sc[:, :], kT, qc, rds, scd, True, len(masks) == 0)
                    for mi, (ml, mr, mrd) in enumerate(masks):
                        mmf(sc[:, :], ml, mr, mrd, scd, False, mi == len(masks) - 1)
                    eT, eTd = ring_e.get()
                    S.op("act", lambda e, sc=sc, eT=eT, bias=bias: e.activation(out=eT[:], in_=sc[:, :], func=AF.Exp,
                                                                               bias=bias),
                         reads=[scd], writes=[eTd])
                    ets.append((eT, eTd))
                    flush(LAG - 1)
                    last = ti == n - 1
                    pend.append((pvb, pvd, vT, eT, [eTd] + list(rds), ti == 0, last,
                                 (lambda: done_cb(pvb, pvd, ets)) if last else None))

            def flush(keep=0):
                while len(pend) > keep:
                    pvb, pvd, vT, eT, prds, first, last, cb = pend.pop(0)
                    mmf(pvb[:, :], vT, eT[:], prds, pvd, first, last)
                    if cb is not None:
                        cb()

            def combine(pvb, pvd, gb, b, accb, first):
                if first:
                    S.op("dve", lambda e: e.tensor_scalar(out=rd[:], in0=pvb[64:128, :], scalar1=1e-30, scalar2=None,
                                                          op0=ALU.add), reads=[pvd], writes=[rd.d()])
                    S.op("dve", lambda e: e.reciprocal(out=rd[:], in_=rd[:]), reads=[rd.d()], writes=[rd.d()])
                else:
                    S.op("dve", lambda e: e.reciprocal(out=rd[:], in_=pvb[64:128, :]), reads=[pvd], writes=[rd.d()])
                S.op("dve", lambda e: e.tensor_tensor(out=tb[:], in0=pvb[0:64, :], in1=rd[:], op=ALU.mult),
                     reads=[pvd, rd.d()], writes=[tb.d()])
                if first:
                    S.op("pool", lambda e: e.tensor_tensor(out=accb[:], in0=tb[:], in1=gb[:, b, :], op=ALU.mult),
                         reads=[tb.d(), gb.d()], writes=[accb.d()])
                else:
                    S.op("pool", lambda e: e.tensor_tensor(out=tb[:], in0=tb[:], in1=gb[:, b, :], op=ALU.mult),
                         reads=[tb.d(), gb.d()], writes=[tb.d()])
                    S.op("pool", lambda e: e.tensor_tensor(out=accb[:], in0=accb[:], in1=tb[:], op=ALU.add),
                         reads=[tb.d(), accb.d()], writes=[accb.d()])

            for c in range(8):
                cs_ = slice(c * 512, (c + 1) * 512)
                nts = [0] if c < 4 else [0, 1]
                for r in range(4):
                    h = 4 * g + r
                    tiles = [(kcmpA[:, nt * 128:(nt + 1) * 128], bias_cmp[:, h, nt:nt + 1],
                              [(identb[:], cmpmask[:, c - 4 * nt, :], [])], vcmpA[:, nt, :],
                              [kcmpA.d(), qA[r].d(), vcmpA.d()]) for nt in nts]

                    def cmp_done(pvb, pvd, ets, r=r, h=h, nts=nts, cs_=cs_):
                        for qs in range(4):
                            for ni, nt in enumerate(nts):
                                eT, eTd = ets[ni]
                                mmf(bimp[:, qs * 65:(qs + 1) * 65], eT[:, qs * 128:(qs + 1) * 128], C_aug[:, nt, :],
                                    [eTd], bimp.d(), ni == 0, ni == len(nts) - 1)
                        S.op("dve", lambda e: e.tensor_scalar(
                            out=rq[:], in0=bimp[:, 0:260].rearrange("p (a b) -> p a b", a=4)[:, :, 64], scalar1=1e-30,
                            scalar2=None, op0=ALU.add), reads=[bimp.d()], writes=[rq.d()])
                        S.op("dve", lambda e: e.reciprocal(out=rq[:], in_=rq[:]), reads=[rq.d()], writes=[rq.d()])
                        for qs in range(4):
                            if r == 0:
                                S.op("dve", lambda e, qs=qs: e.tensor_scalar(
                                    out=imp_acc[:, qs, :], in0=bimp[:, qs * 65:qs * 65 + 64], scalar1=rq[:, qs:qs + 1],
                                    scalar2=None, op0=ALU.mult), reads=[bimp.d(), rq.d()], writes=[imp_acc.d()])
                            else:
                                S.op("dve", lambda e, qs=qs: e.scalar_tensor_tensor(
                                    out=imp_acc[:, qs, :], in0=bimp[:, qs * 65:qs * 65 + 64], scalar=rq[:, qs:qs + 1],
                                    in1=imp_acc[:, qs, :], op0=ALU.mult, op1=ALU.add),
                                    reads=[bimp.d(), rq.d(), imp_acc.d()], writes=[imp_acc.d()])
                        gb = gbc[r]
                        S.dma("sp", gb[:], gTd[3 * h:3 * h + 3, cs_].partition_broadcast(64), writes=[gb.d()])
                        combine(pvb, pvd, gb, 0, cacc[r], True)

                    push_branch(tiles, qA[r][:, cs_], cmp_done)
                flush()
                for qs in range(4):
                    it = 4 * c + qs
                    S.op("dve", lambda e, qs=qs, it=it: e.tensor_tensor(out=imp2[:], in0=imp_acc[:, qs, :],
                                                                      in1=vis[:, it, :], op=ALU.mult),
                         reads=[imp_acc.d()], writes=[imp2.d()])
                    S.op("dve", lambda e, it=it: e.tensor_tensor(out=imp2[:], in0=imp2[:], in1=addt[:, it, :],
                                                                 op=ALU.add), reads=[imp2.d()], writes=[imp2.d()])
                    S.op("dve", lambda e: e.max(out=m8[:, 0:8], in_=imp2[:]), reads=[imp2.d()], writes=[m8.d()])
                    S.op("dve", lambda e: e.match_replace(out=imp3[:], in_to_replace=m8[:, 0:8], in_values=imp2[:],
                                                          imm_value=-2.0),
                         reads=[imp2.d(), m8.d()], writes=[imp3.d()])
                    S.op("dve", lambda e: e.max(out=m8[:, 8:16], in_=imp3[:]), reads=[imp3.d()], writes=[m8.d()])
                    S.op("dve", lambda e: e.tensor_scalar(out=mk[:], in0=imp2[:], scalar1=m8[:, 15:16], scalar2=None,
                                                          op0=ALU.is_ge), reads=[imp2.d(), m8.d()], writes=[mk.d()])
                    S.op("dve", lambda e: e.tensor_scalar(out=negq[:, 0:63], in0=mk[:, 1:64], scalar1=-1.0,
                                                          scalar2=-NEG, op0=ALU.add, op1=ALU.mult),
                         reads=[mk.d()], writes=[negq.d()])
                    S.op("pe", lambda e: e.transpose(out=btr[0:64, 0:128], in_=negq[:], identity=identb[:]),
                         reads=[negq.d()], writes=[btr.d()])
                    for r in range(4):
                        S.op("act", lambda e, it=it, r=r: e.activation(out=qA[r][64:127, it * 128:(it + 1) * 128],
                                                                       in_=btr[0:63, 0:128], func=AF.Copy),
                             reads=[btr.d()], writes=[qA[r].d()])
                for r in range(4):
                    h = 4 * g + r
                    gb = gbc[r]
                    tiles = []
                    for kt in range(4 * c + 4):
                        masks = []
                        if kt >= 4 * c:
                            masks.append((identb[:], causal[:, kt - 4 * c, :], []))
                        tiles.append((ksA[:, kt * 128:(kt + 1) * 128], bias_key[:, h, kt:kt + 1], masks,
                                      vsA[:, kt, :], [ksA.d(), qA[r].d(), vsA.d()]))
                    push_branch(tiles, qA[r][:, cs_],
                                lambda pvb, pvd, ets, r=r, gb=gb: combine(pvb, pvd, gb, 1, cacc[r], False))
                    tiles = []
                    for kt in range(max(0, 4 * c - 4), 4 * c + 4):
                        tiles.append((kwA[:, kt * 128:(kt + 1) * 128], bias_key[:, h, kt:kt + 1],
                                      [(identb[:], winmask[:, kt - 4 * c + 4, :], [])], vwA[:, kt, :],
                                      [kwA.d(), qA[r].d(), vwA.d()]))

                    def win_done(pvb, pvd, ets, r=r, h=h, gb=gb, cs_=cs_):
                        combine(pvb, pvd, gb, 2, cacc[r], False)
                        ob = outb[r % 2]
                        S.op("act", lambda e: e.activation(out=ob[:], in_=cacc[r][:], func=AF.Copy),
                             reads=[cacc[r].d()], writes=[ob.d()])
                        S.dma("pool", attnT[h * 64:(h + 1) * 64, cs_], ob[:], reads=[ob.d()])

                    push_branch(tiles, qA[r][:, cs_], win_done)
            flush()
        S.barrier()

    outproj_phase(S, h_in, h_out, attnT, P["w_out"], P["gpost_b"], tag + "3")


_CACHE = {}


def build_program(phases=("f", "n", "f", "f", "r", "f")):
    nc = bass.Bass("TRN2", target_bir_lowering=False)
    A = {}

    def din(name, shape, dt=F32):
        A[name] = nc.dram_tensor(name, list(shape), dt, kind="ExternalInput").ap()
        return A[name]

    def dscr(name, shape, dt=F32):
        return nc.dram_tensor(name, list(shape), dt, kind="Internal").ap()

    din("x", [S_LEN, D])
    for li in range(2):
        for f in (1, 2):
            din(f"f{f}_{li}_wgu", [D, 2 * DFF]); din(f"f{f}_{li}_wdn", [DFF, D])
            din(f"f{f}_{li}_gpre", [128, 8]); din(f"f{f}_{li}_gpost", [128, D])
    ncst = nsa_consts()
    rcst = rwkv_consts()
    for k_, v in ncst.items():
        din("nc_" + k_, v.shape, F32 if v.dtype == np.float32 else BF16)
    for k_, v in rcst.items():
        din("rc_" + k_, v.shape, F32 if v.dtype == np.float32 else BF16)
    NP = {"w_in": din("n_w_in", [D, NSA_IN]), "w_out": din("n_w_out", [D, D]),
          "gpre_l": din("n_gpre_l", [128, 8]), "gpost_b": din("n_gpost_b", [128, D]),
          "pekT": din("n_pekT", [64, 32]), "pevT": din("n_pevT", [64, 32]),
          "wck1_l": din("n_wck1_l", [64, 32, 64]), "wcv1_l": din("n_wcv1_l", [64, 32, 64]),
          "w_ck2": din("n_w_ck2", [64, 64]), "w_cv2": din("n_w_cv2", [64, 64])}
    RP = {"w_in": din("r_w_in", [D, 3360]), "w_out": din("r_w_out", [D, D]), "w_w2": din("r_w_w2", [64, D]),
          "w_a2": din("r_w_a2", [64, D]), "w_g2": din("r_w_g2", [160, D]),
          "gpre_l": din("r_gpre_l", [128, 8]), "gpost_b": din("r_gpost_b", [128, D]),
          "mu_l": din("r_mu_l", [128, 6, 8]), "hp": din("r_hp", [64, 7, 16])}
    y = nc.dram_tensor("y", [S_LEN, D], F32, kind="ExternalOutput").ap()
    hs = [A["x"]] + [dscr(f"h{i}", [S_LEN, D]) for i in range(len(phases) - 1)] + [y]
    scr = {"projT": dscr("projT", [2560, S_LEN], BF16), "vtok": dscr("vtokd", [S_LEN, 512], BF16),
           "gT": dscr("gT", [48, S_LEN]), "attnT": dscr("attnT", [D, S_LEN], BF16)}
    NC_ = {k_: A["nc_" + k_] for k_ in ncst}
    RC_ = {k_: A["rc_" + k_] for k_ in rcst}
    es = contextlib.ExitStack()
    with es:
        S = Sched(nc, es)
        ffn_ids = [(1, 0), (2, 0), (1, 1), (2, 1)]
        fi = 0
        for pi, ph in enumerate(phases):
            hin, hout = hs[pi], hs[pi + 1]
            if ph == "f":
                f, li = ffn_ids[fi]
                fi += 1
                ffn_phase(S, hin, hout, A[f"f{f}_{li}_wgu"], A[f"f{f}_{li}_wdn"], A[f"f{f}_{li}_gpre"],
                          A[f"f{f}_{li}_gpost"], A["rc_identb"], tag=f"f{pi}")
            elif ph == "n":
                nsa_phase(S, hin, hout, NP, NC_, scr)
            elif ph == "r":
                rwkv_phase(S, hin, hout, RP, RC_, scr["attnT"])
        S.finish()
    return nc, ncst, rcst


def kernel(**inp):
    f32 = np.float32
    g = {k_: np.asarray(v) for k_, v in inp.items()}
    if "prog" not in _CACHE:
        _CACHE["prog"] = build_program()
    nc, ncst, rcst = _CACHE["prog"]
    shared = {}
    for li in range(2):
        for f in (1, 2):
            shared[f"f{f}_{li}_wgu"] = np.ascontiguousarray(g[f"ffn{f}_w_gu"][li], f32)
            shared[f"f{f}_{li}_wdn"] = np.ascontiguousarray(g[f"ffn{f}_w_down"][li], f32)
            shared[f"f{f}_{li}_gpre"] = _pl(g[f"ffn{f}_norm_pre"][li])
            shared[f"f{f}_{li}_gpost"] = _bcast(g[f"ffn{f}_norm_post"][li])
    for k_, v in ncst.items():
        shared["nc_" + k_] = v
    for k_, v in rcst.items():
        shared["rc_" + k_] = v
    nh = nsa_host_params(g["mix_norm_pre"][0], g["mix_norm_post"][0], g["nsa_pe_k"][0], g["nsa_w_ck1"][0],
                         g["nsa_pe_v"][0], g["nsa_w_cv1"][0])
    for k_, v in nh.items():
        shared["n_" + k_] = v
    shared["n_w_in"] = np.ascontiguousarray(g["nsa_w_in"][0], f32)
    shared["n_w_out"] = np.ascontiguousarray(g["nsa_w_out"][0], f32)
    shared["n_w_ck2"] = np.ascontiguousarray(g["nsa_w_ck2"][0], f32)
    shared["n_w_cv2"] = np.ascontiguousarray(g["nsa_w_cv2"][0], f32)
    rh = rwkv_host_params(g["mix_norm_pre"][1], g["mix_norm_post"][1], g["rwkv_mu"][0], g["rwkv_w0"][0],
                          g["rwkv_a0"][0], g["rwkv_k_k"][0], g["rwkv_k_a"][0], g["rwkv_r_k"][0],
                          g["rwkv_gn_w"][0], g["rwkv_gn_b"][0])
    for k_, v in rh.items():
        shared["r_" + k_] = v
    for nm in ("w_in", "w_out", "w_w2", "w_a2", "w_g2"):
        shared["r_" + nm] = np.ascontiguousarray(g["rwkv_" + nm][0], f32)
    x = np.asarray(g["x"], f32)
    in_maps = [dict(shared, x=np.ascontiguousarray(x[b])) for b in range(NCORES)]
    res = run_bass_kernel_spmd(nc, in_maps, core_ids=list(range(NCORES)))
    return np.stack([np.asarray(r["y"], f32) for r in res.results], axis=0)
```

```python
import contextlib
import numpy as np
import concourse.bass as bass
import concourse.mybir as mybir
from concourse.bass_utils import run_bass_kernel_spmd

F32 = mybir.dt.float32
BF16 = mybir.dt.bfloat16
AF = mybir.ActivationFunctionType
ALU = mybir.AluOpType
AX = mybir.AxisListType

S_LEN = 4096
D = 1024
DFF = 2816
NCORES = 8
RMS_EPS = 1e-6


class Dep:
    __slots__ = ("w", "r")

    def __init__(self):
        self.w = None
        self.r = {}


class Sched:
    EPOCH = 16000
    NDMA = 24

    def __init__(self, nc, es):
        self.nc = nc
        self.es = es
        self.eng = {"pe": nc.tensor, "act": nc.scalar, "dve": nc.vector,
                    "pool": nc.gpsimd, "sp": nc.sync}
        self.nsem = 0
        self.sem = {e: self._newsem(e) for e in self.eng}
        self.cnt = {e: 0 for e in self.eng}
        self.waited = {e: {} for e in self.eng}
        self.dsem = {"sp": [self._newsem("dsp") for _ in range(16)],
                     "pool": [self._newsem("dpl") for _ in range(8)]}
        self.dcnt = {q: [0] * len(v) for q, v in self.dsem.items()}
        self.dnext = {q: 0 for q in self.dsem}
        self.ninst = 0
        self.nwait = 0
        self.pe_self_sync = False

    def _newsem(self, tag):
        self.nsem += 1
        return self.es.enter_context(self.nc.semaphore(f"s_{tag}_{self.nsem}"))

    def _wait(self, e, toks):
        best = {}
        for (s, v, src) in toks:
            if src == e and e == "pe" and not self.pe_self_sync:
                continue
            k = id(s)
            if k not in best or best[k][1] < v:
                best[k] = (s, v)
        w = self.waited[e]
        for k, (s, v) in best.items():
            if w.get(k, 0) >= v:
                continue
            self.eng[e].wait_ge(s, v)
            self.nwait += 1
            w[k] = v

    def _collect(self, reads, writes):
        toks = []
        for d in reads:
            if d.w is not None:
                toks.append(d.w)
        for d in writes:
            if d.w is not None:
                toks.append(d.w)
            toks.extend(d.r.values())
        return toks

    def _update(self, tok, reads, writes):
        k = id(tok[0])
        for d in reads:
            old = d.r.get(k)
            if old is None or old[1] < tok[1]:
                d.r[k] = tok
        for d in writes:
            d.w = tok
            d.r = {}

    def op(self, e, fn, reads=(), writes=()):
        toks = self._collect(reads, writes)
        self._wait(e, toks)
        ins = fn(self.eng[e])
        if self.cnt[e] >= self.EPOCH:
            self.sem[e] = self._newsem(e)
            self.cnt[e] = 0
        self.cnt[e] += 1
        ins.then_inc(self.sem[e], 1)
        self.ninst += 1
        tok = (self.sem[e], self.cnt[e], e)
        self._update(tok, reads, writes)
        return tok

    def dma(self, q, out, in_, reads=(), writes=(), **kw):
        toks = self._collect(reads, writes)
        dsem, dcnt = self.dsem[q], self.dcnt[q]
        k = self.dnext[q]
        self.dnext[q] = (k + 1) % len(dsem)
        if dcnt[k] >= self.EPOCH:
            toks.append((dsem[k], dcnt[k], None))
            self._wait(q, toks)
            toks = []
            dsem[k] = self._newsem("d" + q)
            dcnt[k] = 0
        if dcnt[k] > 0:
            toks.append((dsem[k], dcnt[k], None))
        self._wait(q, toks)
        ins = self.eng[q].dma_start(out=out, in_=in_, **kw)
        dcnt[k] += 16
        ins.then_inc(dsem[k], 16)
        self.ninst += 1
        tok = (dsem[k], dcnt[k], None)
        self._update(tok, reads, writes)
        return tok

    def _all_dma_toks(self):
        return [(self.dsem[q][k], self.dcnt[q][k], None) for q in self.dsem
                for k in range(len(self.dsem[q])) if self.dcnt[q][k] > 0]

    def barrier(self):
        toks = [(self.sem[e], self.cnt[e], None) for e in self.eng if self.cnt[e] > 0]
        toks += self._all_dma_toks()
        for e in self.eng:
            self._wait(e, toks)

    def finish(self):
        toks = self._all_dma_toks()
        toks += [(self.sem[e], self.cnt[e], None) for e in self.eng if self.cnt[e] > 0]
        self._wait("sp", toks)


class Buf:
    def __init__(self, t):
        self.t = t
        self.deps = {}

    def d(self, key=0):
        dd = self.deps.get(key)
        if dd is None:
            dd = self.deps[key] = Dep()
        return dd

    def __getitem__(self, idx):
        return self.t[idx]


def sb(S, es, name, shape, dt):
    return Buf(es.enter_context(S.nc.sbuf_tensor(name, shape, dt)))


def ps(S, es, name, shape, dt):
    return Buf(es.enter_context(S.nc.psum_tensor(name, shape, dt)))


def ffn_phase(S, h_in, h_out, w_gu, w_down, gpre_l, gpost_b, ident_d, ntiles=16, tag="f"):
    nc = S.nc
    T = 256
    NS = T // 128
    with contextlib.ExitStack() as es:
        wgu = sb(S, es, tag + "wgu", [128, 8, 2 * DFF], BF16)
        wdn = sb(S, es, tag + "wdn", [128, 22, D], BF16)
        gpre = sb(S, es, tag + "gpre", [128, 8], F32)
        gpost = sb(S, es, tag + "gpost", [128, D], F32)
        ident = sb(S, es, tag + "ident", [128, 128], BF16)
        xb = [sb(S, es, tag + f"x{i}", [128, NS, D], F32) for i in range(2)]
        xn = [sb(S, es, tag + f"xn{i}", [128, D], BF16) for i in range(2)]
        xnT = [sb(S, es, tag + f"xnT{i}", [128, 8, T], BF16) for i in range(2)]
        hT = sb(S, es, tag + "hT", [128, 22, T], BF16)
        sg = [sb(S, es, tag + f"sg{i}", [128, T], F32) for i in range(3)]
        ob = [sb(S, es, tag + f"ob{i}", [128, D], F32) for i in range(2)]
        tmp = sb(S, es, tag + "tmp", [128, D], F32)
        junk = sb(S, es, tag + "junk", [128, D], BF16)
        st = sb(S, es, tag + "st", [128, 16], F32)
        pT = ps(S, es, tag + "pT", [128, 8, 128], BF16)
        pGU = [ps(S, es, tag + f"pGU{i}", [128, 2, T], F32) for i in range(3)]
        pF = [ps(S, es, tag + f"pF{i}", [128, 512], F32) for i in range(4)]

        S.dma("sp", gpre[:], gpre_l, writes=[gpre.d()])
        S.dma("sp", gpost[:], gpost_b, writes=[gpost.d()])
        S.dma("sp", ident[:], ident_d, writes=[ident.d()])
        S.op("dve", lambda e: e.tensor_scalar(out=gpost[:], in0=gpost[:], scalar1=0.5, scalar2=None,
                                              op0=ALU.mult), reads=[gpost.d()], writes=[gpost.d()])

        slots = [(xb[i].d(("stg", s_)), xb[i][:, s_, :]) for i in range(2) for s_ in range(NS)]
        HW = 1024
        k = 0

        def conv(dst, view, dep, scale=None):
            nonlocal k
            if k % 2 == 0:
                if scale is None:
                    S.op("act", lambda e: e.activation(out=dst, in_=view, func=AF.Copy), reads=[dep])
                else:
                    S.op("act", lambda e: e.activation(out=dst, in_=view, func=AF.Copy, scale=scale),
                         reads=[dep, gpre.d()])
            else:
                if scale is None:
                    S.op("dve", lambda e: e.tensor_copy(out=dst, in_=view), reads=[dep])
                else:
                    S.op("dve", lambda e: e.tensor_scalar(out=dst, in0=view, scalar1=scale, scalar2=None,
                                                          op0=ALU.mult), reads=[dep, gpre.d()])
            k += 1

        for c in range(8):
            for o in range(0, 2 * DFF, HW):
                wdt = min(HW, 2 * DFF - o)
                dep, sv = slots[k % len(slots)]
                S.dma("sp", sv[:, 0:wdt], w_gu[c * 128:(c + 1) * 128, o:o + wdt], writes=[dep])
                conv(wgu[:, c, o:o + wdt], sv[:, 0:wdt], dep, scale=gpre[:, c:c + 1])
        for j in range(22):
            dep, sv = slots[k % len(slots)]
            S.dma("sp", sv[:, :], w_down[j * 128:(j + 1) * 128, :], writes=[dep])
            conv(wdn[:, j, :], sv[:, :], dep)
        S.barrier()

        hv_in = h_in.rearrange("(t s p) d -> t p s d", p=128, s=NS)
        hv_out = h_out.rearrange("(t s p) d -> t s p d", p=128, s=NS)

        def load(t):
            S.dma("sp", xb[t % 2][:, :, :], hv_in[t], writes=[xb[t % 2].d()])

        def prenorm(t):
            x = xb[t % 2]
            for s in range(NS):
                xnb = xn[s % 2]
                S.op("act", lambda e, x=x, s=s: e.activation(
                    out=junk[:], in_=x[:, s, :], func=AF.Square, accum_out=st[:, s:s + 1]),
                    reads=[x.d()], writes=[junk.d(), st.d(s)])
                S.op("dve", lambda e, s=s: e.tensor_scalar(
                    out=st[:, 4 + s:5 + s], in0=st[:, s:s + 1], scalar1=1.0 / D, scalar2=RMS_EPS,
                    op0=ALU.mult, op1=ALU.add), reads=[st.d(s)], writes=[st.d(4 + s)])
                S.op("act", lambda e, s=s: e.activation(
                    out=st[:, 4 + s:5 + s], in_=st[:, 4 + s:5 + s], func=AF.Sqrt),
                    reads=[st.d(4 + s)], writes=[st.d(4 + s)])
                S.op("dve", lambda e, s=s: e.reciprocal(
                    out=st[:, 4 + s:5 + s], in_=st[:, 4 + s:5 + s]),
                    reads=[st.d(4 + s)], writes=[st.d(4 + s)])
                S.op("act", lambda e, x=x, s=s, xnb=xnb: e.activation(
                    out=xnb[:], in_=x[:, s, :], func=AF.Copy, scale=st[:, 4 + s:5 + s]),
                    reads=[x.d(), st.d(4 + s)], writes=[xnb.d()])
                for c in range(8):
                    S.op("pe", lambda e, c=c, xnb=xnb: e.transpose(
                        out=pT[:, c, :], in_=xnb[:, c * 128:(c + 1) * 128], identity=ident[:]),
                        reads=[xnb.d(), ident.d()], writes=[pT.d()])
                S.op("dve", lambda e, t=t, s=s: e.tensor_copy(
                    out=xnT[t % 2][:, :, s * 128:(s + 1) * 128], in_=pT[:, :, :]),
                    reads=[pT.d()], writes=[xnT[t % 2].d()])

        def gu(t):
            xT = xnT[t % 2]
            for j in range(22):
                pg = pGU[j % 3]
                for half in range(2):
                    col = half * DFF + j * 128
                    for c in range(8):
                        S.op("pe", lambda e, c=c, col=col, half=half, pg=pg: e.matmul(
                            pg[:, half, :], wgu[:, c, col:col + 128], xT[:, c, :],
                            start=(c == 0), stop=(c == 7)),
                            reads=[wgu.d(), xT.d()], writes=[pg.d()])
                sgb = sg[j % 3]
                S.op("act", lambda e, pg=pg, sgb=sgb: e.activation(
                    out=sgb[:], in_=pg[:, 0, :], func=AF.Silu), reads=[pg.d()], writes=[sgb.d()])
                S.op("dve", lambda e, pg=pg, sgb=sgb, j=j: e.tensor_tensor(
                    out=hT[:, j, :], in0=pg[:, 1, :], in1=sgb[:], op=ALU.mult),
                    reads=[pg.d(), sgb.d()], writes=[hT.d()])

        def down(t):
            x = xb[t % 2]
            for s in range(NS):
                pf = [pF[(s % 2) * 2], pF[(s % 2) * 2 + 1]]
                for half in range(2):
                    for j in range(22):
                        S.op("pe", lambda e, j=j, s=s, half=half, pf=pf: e.matmul(
                            pf[half][:, :], hT[:, j, s * 128:(s + 1) * 128],
                            wdn[:, j, half * 512:(half + 1) * 512], start=(j == 0), stop=(j == 21)),
                            reads=[hT.d(), wdn.d()], writes=[pf[half].d()])
                for half in range(2):
                    S.op("act", lambda e, half=half, pf=pf, s=s: e.activation(
                        out=junk[:, 0:512], in_=pf[half][:, :], func=AF.Square,
                        accum_out=st[:, 8 + 2 * s + half:9 + 2 * s + half]),
                        reads=[pf[half].d()], writes=[junk.d(), st.d(8 + 2 * s + half)])
                S.op("dve", lambda e, s=s: e.tensor_tensor(
                    out=st[:, 12 + s:13 + s], in0=st[:, 8 + 2 * s:9 + 2 * s],
                    in1=st[:, 9 + 2 * s:10 + 2 * s], op=ALU.add),
                    reads=[st.d(8 + 2 * s), st.d(9 + 2 * s)], writes=[st.d(12 + s)])
                S.op("dve", lambda e, s=s: e.tensor_scalar(
                    out=st[:, 12 + s:13 + s], in0=st[:, 12 + s:13 + s], scalar1=1.0 / D, scalar2=RMS_EPS,
                    op0=ALU.mult, op1=ALU.add), reads=[st.d(12 + s)], writes=[st.d(12 + s)])
                S.op("act", lambda e, s=s: e.activation(
                    out=st[:, 12 + s:13 + s], in_=st[:, 12 + s:13 + s], func=AF.Sqrt),
                    reads=[st.d(12 + s)], writes=[st.d(12 + s)])
                S.op("dve", lambda e, s=s: e.reciprocal(
                    out=st[:, 12 + s:13 + s], in_=st[:, 12 + s:13 + s]),
                    reads=[st.d(12 + s)], writes=[st.d(12 + s)])
                for half in range(2):
                    S.op("dve", lambda e, half=half, pf=pf: e.tensor_tensor(
                        out=tmp[:, half * 512:(half + 1) * 512], in0=pf[half][:, :],
                        in1=gpost[:, half * 512:(half + 1) * 512], op=ALU.mult),
                        reads=[pf[half].d(), gpost.d()], writes=[tmp.d()])
                o = ob[s % 2]
                S.op("dve", lambda e, s=s, o=o, x=x: e.scalar_tensor_tensor(
                    out=o[:], in0=tmp[:], scalar=st[:, 12 + s:13 + s], in1=x[:, s, :],
                    op0=ALU.mult, op1=ALU.add),
                    reads=[tmp.d(), st.d(12 + s), x.d()], writes=[o.d()])
                S.dma("pool", hv_out[t, s], o[:], reads=[o.d()])

        load(0)
        prenorm(0)
        for t in range(ntiles):
            if t + 1 < ntiles:
                load(t + 1)
            gu(t)
            if t + 1 < ntiles:
                prenorm(t + 1)
            down(t)
        S.barrier()


def _consts():
    import ml_dtypes
    ident = np.eye(128, dtype=np.float32).astype(ml_dtypes.bfloat16)
    return {"ident": ident}


class Ring:
    def __init__(self, views):
        self.views = views
        self.i = 0

    def get(self):
        v = self.views[self.i]
        self.i = (self.i + 1) % len(self.views)
        return v


HP_W0, HP_A0, HP_KK, HP_KA, HP_RK, HP_GNW, HP_GNB = range(7)
GN_EPS = 64e-5


def rwkv_phase(S, h_in, h_out, P, C, yT_dram, ndc=32, tag="r", stage=9, dbg=None):
    nc = S.nc
    import os
    RWBF = False
    F32R = BF16 if RWBF else mybir.dt.float32r
    PADDED = not RWBF

    def W(n):
        return 256 if PADDED else n
    HO = 0 if PADDED else 128
    with contextlib.ExitStack() as es:
        def SB(name, shape, dt):
            return sb(S, es, tag + name, shape, dt)

        Wb = SB("Wb", [128, 8, 3360], BF16)
        ww2 = SB("ww2", [64, D], BF16)
        wa2 = SB("wa2", [64, D], BF16)
        wg2a = SB("wg2a", [128, D], BF16)
        wg2b = SB("wg2b", [32, D], BF16)
        gpre = SB("gpre", [128, 8], F32)
        mu = SB("mu", [128, 6, 8], F32)
        hp = SB("hp", [64, 7, 16], F32)
        identb = SB("identb", [128, 128], BF16)
        identf = SB("identf", [128, 128], F32)
        identr = SB("identr", [128, 256], F32R)
        ones64 = SB("ones64", [64, 64], F32)
        ones64r = SB("ones64r", [64, 64], F32R)
        mask2 = SB("mask2", [128, 256], F32)
        masksl = SB("masksl", [128, 128], F32)
        scanm = SB("scanm", [64, 512], F32)
        xb = SB("xb", [128, D], F32)
        xn = SB("xn", [128, D], BF16)
        junk = SB("junk", [128, 512], BF16)
        uTx = [SB(f"uTx{i}", [128, 8, 129], BF16) for i in range(2)]
        xx = SB("xx", [128, 8, 128], BF16)
        mixb = [SB(f"mix{i}", [128, 8, 128], BF16) for i in range(4)]
        vtok = SB("vtok", [128, D + 256], F32R)
        th3 = SB("th3", [64, 128], BF16)
        p4b = SB("p4b", [64, 128], BF16)
        s5a = SB("s5a", [128, 128], BF16)
        s5b = SB("s5b", [32, 128], BF16)
        yfin = SB("yfin", [128, 8, 128], BF16)
        st = SB("st", [128, 8], F32)
        Sst = SB("Sst", [64, 20, 64], F32R)
        gamC = SB("gamC", [64, 16], F32)
        Q = {n: SB("q_" + n, [64, 4, 128], F32) for n in ["k", "sig", "a", "cs", "kk", "t1", "eneg", "epos", "gt1"]}
        Q["eexc"] = Q["sig"]
        Q["t1r"] = SB("q_t1r", [64, 4, 128], F32R)
        Q["gt1r"] = SB("q_gt1r", [64, 4, 128], F32R)
        QP = [{n: SB(f"qp{p}_" + n, [64, 4, 128], F32) for n in ["r", "kmod", "vT", "g", "y"]} for p in range(2)]
        ARs = [SB(f"AR{p}", [64, 4, 2, 128], F32R) for p in range(2)]
        BTs = [SB(f"BTb{p}", [64, 6, 128], F32R) for p in range(2)]
        KTs = [SB(f"KTb{p}", [64, 6, 128], F32R) for p in range(2)]
        NH = 4
        XB = [[SB(f"X{i}_{j}", [128, 512], F32R) for j in range(2)] for i in range(NH)]
        MRB = [SB(f"MRB{i}", [128, 256], F32R) for i in range(NH)]
        MKb = [SB(f"MK{i}", [128, 256], F32R) for i in range(NH)]
        AXb = [SB(f"AX{i}", [128, 256], F32R) for i in range(NH)]
        PQb = [SB(f"PQ{i}", [128, 320], F32R) for i in range(NH)]
        BKb = [SB(f"BK{i}", [128, 256], F32R) for i in range(NH)]
        GTb = [SB(f"GT{i}", [64, 64], F32R) for i in range(NH)]
        Hsb = [SB(f"Hs{i}", [64, 64], F32) for i in range(NH)]
        RhT = [SB(f"RhT{i}", [64, 256], F32R) for i in range(NH)]

        banks = [ps(S, es, tag + f"bk{i}", [128, 512], F32) for i in range(7)]
        bankT = ps(S, es, tag + "bkT", [128, 8, 128], BF16)
        ring_proj = Ring([(b[:, :], b.d()) for b in banks[0:1]])
        ringF = Ring([(b, b.d()) for b in banks[1:7]])

        for (t, src) in [(gpre, P["gpre_l"]), (mu, P["mu_l"]), (hp, P["hp"]),
                         (identb, C["identb"]), (identf, C["identf"]), (ones64, C["ones64"]),
                         (mask2, C["mask2"]), (masksl, C["masksl"]), (scanm, C["scanm"])]:
            S.dma("sp", t[:], src, writes=[t.d()])
        S.op("dve", lambda e: e.memset(uTx[1][:, :, 128:129], 0.0), writes=[uTx[1].d()])
        kcnt = [0]
        stg2 = Buf(xb.t)
        stg = [(xb, xb[:, 0:512]), (stg2, xb[:, 512:1024])]

        def conv(dst, view, b, scale=None):
            if scale is not None:
                S.op("act", lambda e: e.activation(out=dst, in_=view, func=AF.Copy, scale=scale),
                     reads=[b.d(), gpre.d()])
            elif kcnt[0] % 2 == 0:
                S.op("act", lambda e: e.activation(out=dst, in_=view, func=AF.Copy), reads=[b.d()])
            else:
                S.op("dve", lambda e: e.tensor_copy(out=dst, in_=view), reads=[b.d()])
            kcnt[0] += 1

        for c in range(8):
            for o in range(0, 3360, 512):
                wdt = min(512, 3360 - o)
                b, bv = stg[kcnt[0] % 2]
                S.dma("sp", bv[:, 0:wdt], P["w_in"][c * 128:(c + 1) * 128, o:o + wdt], writes=[b.d()])
                conv(Wb[:, c, o:o + wdt], bv[:, 0:wdt], b, scale=gpre[:, c:c + 1])
        for (dst, src, n) in [(ww2, P["w_w2"], 64), (wa2, P["w_a2"], 64), (wg2a, P["w_g2"][0:128, :], 128),
                              (wg2b, P["w_g2"][128:160, :], 32)]:
            for o in range(0, D, 512):
                b, bv = stg[kcnt[0] % 2]
                S.dma("sp", bv[0:n, :], src[:, o:o + 512], writes=[b.d()])
                conv(dst[0:n, o:o + 512], bv[0:n, :], b)
        S.barrier()

        S.op("dve", lambda e: e.memset(xb[:, 0:512], 0.0), writes=[xb.d()])

        def zero_r(buf, flat, nparts, width, deps):
            for o in range(0, width, 512):
                w_ = min(512, width - o)
                S.op("pool", lambda e, o=o, w_=w_: e.tensor_copy(out=flat[0:nparts, o:o + w_],
                                                                 in_=xb[0:nparts, 0:w_]),
                     reads=[xb.d()], writes=deps)
        zero_r(Sst, Sst[:].rearrange("p a b -> p (a b)"), 64, 20 * 64, [Sst.d(h) for h in range(16)])
        zero_r(identr, identr[:, 128:256], 128, 128, [identr.d()])
        S.op("dve", lambda e: e.tensor_copy(out=identr[:, 0:128], in_=identf[:]), reads=[identf.d()],
             writes=[identr.d()])
        S.op("dve", lambda e: e.tensor_copy(out=ones64r[:], in_=ones64[:]), reads=[ones64.d()],
             writes=[ones64r.d()])
        zero_r(vtok, vtok[:, :], 128, D + 256, [vtok.d()])
        for p_ in range(2):
            zero_r(BTs[p_], BTs[p_][:].rearrange("p a b -> p (a b)"), 64, 6 * 128, [BTs[p_].d()])
            zero_r(KTs[p_], KTs[p_][:].rearrange("p a b -> p (a b)"), 64, 6 * 128, [KTs[p_].d()])
        for t_ in MRB + MKb + AXb + BKb:
            zero_r(t_, t_[:, :], 128, 256, [t_.d()])
        for t_ in [b for row in XB for b in row]:
            zero_r(t_, t_[:, :], 128, 512, [t_.d()])
        for t_ in PQb:
            zero_r(t_, t_[:, :], 128, 320, [t_.d()])
        for t_ in RhT:
            zero_r(t_, t_[:, :], 64, 256, [t_.d()])
        S.barrier()

        hv_in = h_in.rearrange("(t p) d -> t p d", p=128)
        hv_out = h_out.rearrange("(t p) d -> t p d", p=128)

        def bc(idx, q):
            return hp[:, idx, 4 * q:4 * q + 4].unsqueeze(2).to_broadcast([64, 4, 128])

        def f2(b):
            return b[:].rearrange("p a b -> p (a b)")

        def rstd_ops(src_col, dst_col):
            S.op("dve", lambda e: e.tensor_scalar(out=st[:, dst_col:dst_col + 1], in0=st[:, src_col:src_col + 1],
                                                  scalar1=1.0 / D, scalar2=RMS_EPS, op0=ALU.mult, op1=ALU.add),
                 reads=[st.d(src_col)], writes=[st.d(dst_col)])
            S.op("act", lambda e: e.activation(out=st[:, dst_col:dst_col + 1], in_=st[:, dst_col:dst_col + 1],
                                               func=AF.Sqrt), reads=[st.d(dst_col)], writes=[st.d(dst_col)])
            S.op("dve", lambda e: e.reciprocal(out=st[:, dst_col:dst_col + 1], in_=st[:, dst_col:dst_col + 1]),
                 reads=[st.d(dst_col)], writes=[st.d(dst_col)])

        def mm(out, lhsT, rhs, reads, wdep_, start=True, stop=True):
            S.op("pe", lambda e: e.matmul(out, lhsT, rhs, start=start, stop=stop), reads=reads, writes=[wdep_])

        def tt(eng, out, in0, in1, op, reads, writes):
            S.op(eng, lambda e: e.tensor_tensor(out=out, in0=in0, in1=in1, op=op), reads=reads, writes=writes)

        def actf(out, in_, func, reads, writes, **kw):
            S.op("act", lambda e: e.activation(out=out, in_=in_, func=func, **kw), reads=reads, writes=writes)

        def make_mix(i, m, cur):
            for c in range(8):
                S.op("dve", lambda e, c=c: e.scalar_tensor_tensor(
                    out=m[:, c, :], in0=xx[:, c, :], scalar=mu[:, i, c:c + 1], in1=cur[:, c, 1:129],
                    op0=ALU.mult, op1=ALU.add), reads=[xx.d(), cur.d()], writes=[m.d()])
            return m

        Sf = Sst[:].rearrange("p a b -> p (a b)")
        yT_v = yT_dram.rearrange("(c p) t -> p c t", p=128)

        def front(h, slot, q):
            j = h % 4
            AR, BTb, KTb = ARs[q % 2], BTs[q % 2], KTs[q % 2]
            BTf = BTb[:].rearrange("p a b -> p (a b)")
            ARcat = AR[:, j, :, :].rearrange("p a b -> p (a b)")
            AT = AR[:, j, 0, :]
            RT = AR[:, j, 1, :]
            BTh = BTb[:, j, :]
            KTh = KTb[:, j, :]
            vpad = vtok[:, h * 64:h * 64 + W(64)]
            mrb, mk = MRB[slot], MKb[slot]
            x0, x1 = XB[slot]
            ax, pq, bk = AXb[slot], PQb[slot], BKb[slot]
            pb, pd = ringF.get()
            mm(pb[:, 0:256], BTh, ARcat, [BTb.d(), AR.d()], pd)
            tt("dve", x0[:, 0:128], pb[:, 0:128], mask2[:, 0:128], ALU.mult, [pd], [x0.d()])
            tt("dve", mrb[:, 128:256], pb[:, 128:256], mask2[:, 128:256], ALU.mult, [pd], [mrb.d()])
            actf(x0[:, 128:256], identr[:, 0:128], AF.Copy, [identr.d()], [x0.d()])
            pb, pd = ringF.get()
            mm(pb[:, 0:256], KTh, ARcat, [KTb.d(), AR.d()], pd)
            tt("dve", mk[:, 0:256], pb[:, 0:256], mask2[:], ALU.mult, [pd], [mk.d()])
            yield
            pb, pd = ringF.get()
            mm(pb[:, 0:W(128)], AT, BTf[:, j * 128:j * 128 + W(128)], [BTb.d(), AR.d()], pd)
            tt("dve", x0[:, 256:384], pb[:, 0:128], masksl[:], ALU.mult, [pd], [x0.d()])
            pb2, pd2 = ringF.get()
            mm(pb2[:, 0:W(64)], BTh, identr[0:64, 0:W(64)], [BTb.d()], pd2)
            mm(pb2[:, 64:64 + W(64)], KTh, identr[0:64, 0:W(64)], [KTb.d()], pd2)
            actf(bk[:, 0:128], pb2[:, 0:128], AF.Copy, [pd2], [bk.d()])
            yield
            Xc = x0
            for jj in range(1, 7):
                Xn = x1 if Xc is x0 else x0
                pb, pd = ringF.get()
                mm(pb[:, 0:256], Xc[:, 256:384], Xc[:, 0:256], [Xc.d()], pd, True, False)
                mm(pb[:, 128:384], identr[:, 0:128], Xc[:, 128:384], [Xc.d(), identr.d()], pd, False, True)
                mm(pb[:, 256:256 + W(128)], Xc[:, 0:128], Xc[:, 256:256 + W(128)], [Xc.d()], pd)
                actf(Xn[:, 0:384], pb[:, 0:384], AF.Copy, [pd], [Xn.d()])
                if jj == 1:
                    pb3, pd3 = ringF.get()
                    mm(pb3[:, 0:W(64)], AT, identr[0:64, 0:W(64)], [AR.d()], pd3)
                    mm(pb3[:, 64:64 + W(64)], mk[:, 0:128], vpad, [mk.d(), vtok.d()], pd3)
                    actf(ax[:, 0:128], pb3[:, 0:128], AF.Copy, [pd3], [ax.d()])
                yield
                Xc = Xn
            Rfin = x1 if Xc is x0 else x0
            pb, pd = ringF.get()
            mm(pb[:, 0:256 - HO], Xc[:, 256:384], Xc[:, HO:256], [Xc.d()], pd)
            tt("dve", Rfin[:, 0:128], pb[:, 128 - HO:256 - HO], Xc[:, 128:256], ALU.add, [pd, Xc.d()], [Rfin.d()])
            yield
            pb, pd = ringF.get()
            mm(pb[:, 0:W(128)], Rfin[:, 0:128], ax[:, 0:W(128)], [Rfin.d(), ax.d()], pd)
            actf(pq[:, 0:128], pb[:, 0:128], AF.Copy, [pd], [pq.d()])
            yield
            gt, hs, rh = GTb[slot], Hsb[slot], RhT[slot]
            pb, pd = ringF.get()
            mm(pb[0:64, 0:W(64)], pq[:, 0:64], bk[:, 0:W(64)], [pq.d(), bk.d()], pd)
            tt("dve", gt[:], pb[0:64, 0:64], identf[0:64, 0:64], ALU.add, [pd], [gt.d()])
            pb2, pd2 = ringF.get()
            mm(pb2[0:64, 0:W(64)], bk[:, 0:64], pq[:, 64:64 + W(64)], [pq.d(), bk.d()], pd2, True, False)
            mm(pb2[0:64, 0:W(64)], bk[:, 64:128], vpad, [bk.d(), vtok.d()], pd2, False, True)
            S.op("dve", lambda e: e.tensor_scalar(out=hs[:], in0=pb2[0:64, 0:64], scalar1=gamC[:, h:h + 1],
                                                  scalar2=None, op0=ALU.mult),
                 reads=[pd2, gamC.d(q)], writes=[hs.d()])
            pb3, pd3 = ringF.get()
            mm(pb3[0:64, 0:256 - HO], pq[:, 0:64], mrb[:, HO:256], [pq.d(), mrb.d()], pd3)
            tt("dve", rh[:, 128:256], pb3[0:64, 128 - HO:256 - HO], RT, ALU.add, [pd3, AR.d()], [rh.d()])
            yield

        def back(h, slot, q):
            j = h % 4
            y = QP[q % 2]["y"]
            vh = vtok[:, h * 64:(h + 1) * 64]
            mrb, mk, pq, gt, hs, rh = MRB[slot], MKb[slot], PQb[slot], GTb[slot], Hsb[slot], RhT[slot]
            pb, pd = ringF.get()
            mm(pb[0:64, 0:256 - HO], pq[:, 64:128], mrb[:, HO:256], [pq.d(), mrb.d()], pd, True, False)
            mm(pb[0:64, 0:256 - HO], vh, mk[:, HO:256], [vtok.d(), mk.d()], pd, False, False)
            mm(pb[0:64, 0:256 - HO], Sst[:, h, :], rh[:, HO:256], [Sst.d(h), rh.d()], pd, False, True)
            actf(y[:, j, :], pb[0:64, 128 - HO:256 - HO], AF.Copy, [pd], [y.d()])
            pb2, pd2 = ringF.get()
            mm(pb2[0:64, 0:W(64)], gt[:], Sf[:, h * 64:h * 64 + W(64)], [gt.d(), Sst.d(h)], pd2)
            S.op("dve", lambda e: e.scalar_tensor_tensor(
                out=Sst[:, h, :], in0=pb2[0:64, 0:64], scalar=gamC[:, h:h + 1], in1=hs[:],
                op0=ALU.mult, op1=ALU.add), reads=[pd2, gamC.d(q), hs.d()], writes=[Sst.d(h)])

        mixes = {}

        def stageA(dc):
            cur, prv = uTx[dc % 2], uTx[(dc + 1) % 2]
            S.dma("sp", xb[:], hv_in[dc], writes=[xb.d()])
            actf(xn[:], xb[:], AF.Square, [xb.d()], [xn.d(), st.d(0)], accum_out=st[:, 0:1])
            rstd_ops(0, 1)
            actf(xn[:], xb[:], AF.Copy, [xb.d(), st.d(1)], [xn.d()], scale=st[:, 1:2])
            for c in range(8):
                S.op("pe", lambda e, c=c: e.transpose(out=bankT[:, c, :], in_=xn[:, c * 128:(c + 1) * 128],
                                                      identity=identb[:]), reads=[xn.d()], writes=[bankT.d()])
            S.op("dve", lambda e: e.tensor_copy(out=cur[:, :, 1:129], in_=bankT[:, :, :]),
                 reads=[bankT.d()], writes=[cur.d()])
            S.op("dve", lambda e: e.tensor_copy(out=cur[:, :, 0:1], in_=prv[:, :, 128:129]),
                 reads=[prv.d()], writes=[cur.d()])
            tt("dve", xx[:], cur[:, :, 0:128], cur[:, :, 1:129], ALU.subtract, [cur.d()], [xx.d()])
            yield
            m3 = make_mix(3, mixb[3], cur)
            pv_, pd = ring_proj.get()
            for c in range(8):
                mm(pv_[0:64, 0:128], Wb[:, c, 3072:3136], m3[:, c, :], [m3.d()], pd, c == 0, c == 7)
            actf(th3[:], pv_[0:64, 0:128], AF.Tanh, [pd], [th3.d()])
            yield
            m4 = make_mix(4, mixb[3], cur)
            pv_, pd = ring_proj.get()
            for c in range(8):
                mm(pv_[0:64, 0:128], Wb[:, c, 3136:3200], m4[:, c, :], [m4.d()], pd, c == 0, c == 7)
            actf(p4b[:], pv_[0:64, 0:128], AF.Copy, [pd], [p4b.d()])
            yield
            m5 = make_mix(5, mixb[3], cur)
            pv_, pd = ring_proj.get()
            for c in range(8):
                mm(pv_[:, 0:128], Wb[:, c, 3200:3328], m5[:, c, :], [m5.d()], pd, c == 0, c == 7)
            for c in range(8):
                mm(pv_[0:32, 128:256], Wb[:, c, 3328:3360], m5[:, c, :], [m5.d()], pd, c == 0, c == 7)
            actf(s5a[:], pv_[:, 0:128], AF.Sigmoid, [pd], [s5a.d()])
            actf(s5b[:], pv_[0:32, 128:256], AF.Sigmoid, [pd], [s5b.d()])
            yield
            mixes[0] = make_mix(0, mixb[0], cur)
            yield
            mixes[1] = make_mix(1, mixb[1], cur)
            yield
            mixes[2] = make_mix(2, mixb[2], cur)
            yield

        def stageA2(dc):
            m2 = mixes[2]
            for half in range(2):
                bb, bbd = ring_proj.get()
                for c in range(8):
                    mm(bb, m2[:, c, :], Wb[:, c, 2048 + half * 512:2048 + (half + 1) * 512],
                       [m2.d()], bbd, c == 0, c == 7)
                actf(vtok[:, half * 512:(half + 1) * 512], bb, AF.Copy, [bbd], [vtok.d()])
                yield

        def prep(q):
            par = q % 2
            PENG = os.environ.get('RW_PENG', 'dve')
            AR, BTb, KTb, qp = ARs[par], BTs[par], KTs[par], QP[par]

            def evac_pairs(pv2, pd2, dst, eng="act"):
                src = pv2[:, 0:256].rearrange("p (a b) -> p a b", a=2)
                dv = dst[:].rearrange("p (a two) b -> p a two b", two=2)
                for half in range(2):
                    if eng == "act":
                        actf(dv[:, :, half, :], src[64 * half:64 * half + 64], AF.Copy, [pd2], [dst.d()])
                    else:
                        S.op("dve", lambda e, half=half: e.tensor_copy(out=dv[:, :, half, :],
                                                                       in_=src[64 * half:64 * half + 64]),
                             reads=[pd2], writes=[dst.d()])

            def proj4(mbuf, colbase, dst, eng="act"):
                pv2, pd2 = ring_proj.get()
                for pi in range(2):
                    pr = 2 * q + pi
                    for c in range(8):
                        mm(pv2[:, pi * 128:(pi + 1) * 128], Wb[:, c, colbase + pr * 128:colbase + (pr + 1) * 128],
                           mbuf[:, c, :], [mbuf.d()], pd2, c == 0, c == 7)
                evac_pairs(pv2, pd2, dst, eng)

            proj4(mixes[0], 0, qp["r"])
            yield
            proj4(mixes[1], 1024, Q["k"], "dve")
            yield
            proj4(mixes[2], 2048, qp["vT"])
            yield
            pv2, pd2 = ring_proj.get()
            for pi in range(2):
                pr = 2 * q + pi
                mm(pv2[:, pi * 128:(pi + 1) * 128], ww2[:, pr * 128:(pr + 1) * 128], th3[:], [th3.d()], pd2)
            evac_pairs(pv2, pd2, Q["sig"], "dve")
            tt(PENG, Q["sig"][:], Q["sig"][:], bc(HP_W0, q), ALU.add, [Q["sig"].d()], [Q["sig"].d()])
            actf(f2(Q["sig"]), f2(Q["sig"]), AF.Sigmoid, [Q["sig"].d()], [Q["sig"].d()])
            yield
            pv2, pd2 = ring_proj.get()
            for pi in range(2):
                pr = 2 * q + pi
                mm(pv2[:, pi * 128:(pi + 1) * 128], wa2[:, pr * 128:(pr + 1) * 128], p4b[:], [p4b.d()], pd2)
            evac_pairs(pv2, pd2, Q["a"], "dve")
            tt(PENG, Q["a"][:], Q["a"][:], bc(HP_A0, q), ALU.add, [Q["a"].d()], [Q["a"].d()])
            actf(f2(Q["a"]), f2(Q["a"]), AF.Sigmoid, [Q["a"].d()], [Q["a"].d()])
            yield
            pv2, pd2 = ring_proj.get()
            for pi in range(2):
                pr = 2 * q + pi
                mm(pv2[:, pi * 128:(pi + 1) * 128], wg2a[:, pr * 128:(pr + 1) * 128], s5a[:], [s5a.d()], pd2,
                   True, False)
                mm(pv2[:, pi * 128:(pi + 1) * 128], wg2b[:, pr * 128:(pr + 1) * 128], s5b[:], [s5b.d()], pd2,
                   False, True)
            evac_pairs(pv2, pd2, qp["g"])
            yield
            S.op(PENG, lambda e: e.tensor_scalar(out=f2(Q["sig"]), in0=f2(Q["sig"]), scalar1=-0.6065306597126334,
                                                  scalar2=None, op0=ALU.mult),
                 reads=[Q["sig"].d()], writes=[Q["sig"].d()])
            S.op("dve", lambda e: e.tensor_tensor_scan(out=f2(Q["cs"]), data0=scanm[:], data1=f2(Q["sig"]),
                                                       initial=0.0, op0=ALU.mult, op1=ALU.add),
                 reads=[Q["sig"].d()], writes=[Q["cs"].d()])
            tt(PENG, Q["kk"][:], Q["k"][:], bc(HP_KK, q), ALU.mult, [Q["k"].d()], [Q["kk"].d()])
            actf(f2(Q["t1r"]), f2(Q["kk"]), AF.Square, [Q["kk"].d()], [Q["t1r"].d()])
            yield
            pv2, pd2 = ring_proj.get()
            mm(pv2[0:64, :], ones64r[:], f2(Q["t1r"]), [Q["t1r"].d()], pd2)
            actf(f2(Q["t1"]), pv2[0:64, :], AF.Sqrt, [pd2], [Q["t1"].d()])
            S.op("dve", lambda e: e.tensor_scalar(out=f2(Q["t1"]), in0=f2(Q["t1"]), scalar1=1e-12, scalar2=None,
                                                  op0=ALU.max), reads=[Q["t1"].d()], writes=[Q["t1"].d()])
            S.op("dve", lambda e: e.reciprocal(out=f2(Q["t1"]), in_=f2(Q["t1"])),
                 reads=[Q["t1"].d()], writes=[Q["t1"].d()])
            tt(PENG, Q["kk"][:], Q["kk"][:], Q["t1"][:], ALU.mult, [Q["kk"].d(), Q["t1"].d()], [Q["kk"].d()])
            yield
            S.op("dve", lambda e: e.scalar_tensor_tensor(out=Q["t1"][:], in0=Q["a"][:], scalar=-1.0, in1=bc(HP_KA, q),
                                                         op0=ALU.add, op1=ALU.mult),
                 reads=[Q["a"].d()], writes=[Q["t1"].d()])
            S.op("dve", lambda e: e.scalar_tensor_tensor(out=f2(qp["kmod"]), in0=f2(Q["t1"]), scalar=1.0,
                                                         in1=f2(Q["k"]), op0=ALU.add, op1=ALU.mult),
                 reads=[Q["t1"].d(), Q["k"].d()], writes=[qp["kmod"].d()])
            actf(f2(Q["eneg"]), f2(Q["cs"]), AF.Exp, [Q["cs"].d()], [Q["eneg"].d()], scale=-1.0)
            actf(f2(Q["epos"]), f2(Q["cs"]), AF.Exp, [Q["cs"].d()], [Q["epos"].d()])
            yield
            tt(PENG, Q["eexc"][:], Q["cs"][:], Q["sig"][:], ALU.subtract, [Q["cs"].d(), Q["sig"].d()],
               [Q["eexc"].d()])
            actf(f2(Q["eexc"]), f2(Q["eexc"]), AF.Exp, [Q["eexc"].d()], [Q["eexc"].d()])
            S.op("dve", lambda e: e.tensor_copy(out=gamC[:, 4 * q:4 * q + 4], in_=Q["epos"][:, :, 127]),
                 reads=[Q["epos"].d()], writes=[gamC.d(q)])
            yield
            S.op("dve", lambda e: e.scalar_tensor_tensor(out=AR[:, :, 0, :], in0=Q["kk"][:], scalar=-1.0,
                                                         in1=Q["eexc"][:], op0=ALU.mult, op1=ALU.mult),
                 reads=[Q["kk"].d(), Q["eexc"].d()], writes=[AR.d()])
            tt(PENG, AR[:, :, 1, :], qp["r"][:], Q["epos"][:], ALU.mult, [qp["r"].d(), Q["epos"].d()], [AR.d()])
            yield
            tt(PENG, Q["t1"][:], Q["kk"][:], Q["a"][:], ALU.mult, [Q["kk"].d(), Q["a"].d()], [Q["t1"].d()])
            tt(PENG, BTb[:, 0:4, :], Q["t1"][:], Q["eneg"][:], ALU.mult, [Q["t1"].d(), Q["eneg"].d()], [BTb.d()])
            tt(PENG, KTb[:, 0:4, :], qp["kmod"][:], Q["eneg"][:], ALU.mult, [qp["kmod"].d(), Q["eneg"].d()],
               [KTb.d()])
            yield

        def gn(q, dc):
            qp = QP[q % 2]
            y, t1, t1r = qp["y"], Q["gt1"], Q["gt1r"]
            heads = [4 * q + j for j in range(4)]
            actf(f2(t1r), f2(y), AF.Copy, [y.d()], [t1r.d()])
            pv2, pd2 = ring_proj.get()
            mm(pv2[0:64, :], ones64r[:], f2(t1r), [t1r.d()], pd2)
            S.op("dve", lambda e: e.tensor_scalar(out=f2(t1), in0=pv2[0:64, :], scalar1=1.0 / 64, scalar2=None,
                                                  op0=ALU.mult), reads=[pd2], writes=[t1.d()])
            tt("dve", y[:], y[:], t1[:], ALU.subtract, [y.d(), t1.d()], [y.d()])
            actf(f2(t1r), f2(y), AF.Square, [y.d()], [t1r.d()])
            yield
            pv2, pd2 = ring_proj.get()
            mm(pv2[0:64, :], ones64r[:], f2(t1r), [t1r.d()], pd2)
            S.op("dve", lambda e: e.tensor_scalar(out=f2(t1), in0=pv2[0:64, :], scalar1=1.0 / 64,
                                                  scalar2=GN_EPS, op0=ALU.mult, op1=ALU.add),
                 reads=[pd2], writes=[t1.d()])
            actf(f2(t1), f2(t1), AF.Sqrt, [t1.d()], [t1.d()])
            S.op("dve", lambda e: e.reciprocal(out=f2(t1), in_=f2(t1)), reads=[t1.d()], writes=[t1.d()])
            tt("dve", y[:], y[:], t1[:], ALU.mult, [y.d(), t1.d()], [y.d()])
            yield
            tt("pool", y[:], y[:], bc(HP_GNW, q), ALU.mult, [y.d()], [y.d()])
            tt("pool", y[:], y[:], bc(HP_GNB, q), ALU.add, [y.d()], [y.d()])
            tt("pool", t1[:], qp["r"][:], qp["kmod"][:], ALU.mult, [qp["r"].d(), qp["kmod"].d()], [t1.d()])
            tt("dve", t1r[:], t1[:], bc(HP_RK, q), ALU.mult, [t1.d()], [t1r.d()])
            yield
            pv2, pd2 = ring_proj.get()
            mm(pv2[0:64, :], ones64r[:], f2(t1r), [t1r.d()], pd2)
            tt("dve", t1[:], pv2[0:64, :].rearrange("p (a b) -> p a b", a=4), qp["vT"][:], ALU.mult,
               [pd2, qp["vT"].d()], [t1.d()])
            tt("pool", y[:], y[:], t1[:], ALU.add, [y.d(), t1.d()], [y.d()])
            yield
            for j, h in enumerate(heads):
                po = 64 * (h % 2)
                tt("dve", yfin[po:po + 64, h // 2, :], y[:, j, :], qp["g"][:, j, :], ALU.mult,
                   [y.d(), qp["g"].d()], [yfin.d()])
            if q == 3:
                S.dma("pool", yT_v[:, :, dc * 128:(dc + 1) * 128], yfin[:, :, :], reads=[yfin.d()])
            yield

        def chain(*gens):
            for g_ in gens:
                if g_ is not None:
                    yield from g_

        XSTEP = int(os.environ.get("RW_XSTEP", "1"))

        def run(gens, extra=None):
            alive = [(g_, 1) for g_ in gens if g_ is not None]
            if extra is not None:
                if os.environ.get("RW_XFIRST"):
                    alive.insert(0, (extra, XSTEP))
                else:
                    alive.append((extra, XSTEP))
            while alive:
                nxt = []
                for g_, k_ in alive:
                    ok = True
                    for _ in range(k_):
                        try:
                            next(g_)
                        except StopIteration:
                            ok = False
                            break
                    if ok:
                        nxt.append((g_, k_))
                alive = nxt

        run([chain(stageA(0), stageA2(0), prep(0))])
        gn_prev = None
        for dc in range(ndc):
            more = dc + 1 < ndc
            for q in range(4):
                heads = [4 * q + j for j in range(4)]
                if q < 3:
                    extra = chain(gn(q - 1, dc) if q > 0 else None, prep(q + 1))
                else:
                    extra = chain(gn(2, dc), stageA(dc + 1) if more else None, prep(0) if more else None)
                run([front(h, j, q) for j, h in enumerate(heads)], extra)
                for j, h in enumerate(heads):
                    back(h, j, q)
            gn_prev = gn(3, dc)
            run([stageA2(dc + 1) if more else None, gn_prev])
            gn_prev = None
        S.barrier()
    outproj_phase(S, h_in, h_out, yT_dram, P["w_out"], P["gpost_b"], tag + "o", ntile=ndc)


def rwkv_consts():
    import ml_dtypes
    i = np.arange(128)
    su = (i[:, None] < i[None, :]).astype(np.float32)
    u = (i[:, None] <= i[None, :]).astype(np.float32)
    scanm = np.ones((64, 512), np.float32)
    scanm[:, ::128] = 0.0
    return {
        "identb": np.eye(128, dtype=np.float32).astype(ml_dtypes.bfloat16),
        "identf": np.eye(128, dtype=np.float32),
        "ones64": np.ones((64, 64), np.float32),
        "mask2": np.ascontiguousarray(np.concatenate([su, u], axis=1)),
        "masksl": np.ascontiguousarray(su.T),
        "scanm": scanm,
    }


def _pl(v):
    return np.ascontiguousarray(np.asarray(v, np.float32).reshape(8, 128).T)


def _bcast(v):
    return np.ascontiguousarray(np.broadcast_to(np.asarray(v, np.float32), (128, D)))


def rwkv_host_params(mix_norm_pre, mix_norm_post, mu, w0, a0, k_k, k_a, r_k, gn_w, gn_b):
    hp = np.stack([np.asarray(t, np.float32).reshape(16, 64) for t in
                   (w0, a0, k_k, k_a, r_k, gn_w, gn_b)], axis=0)
    return {
        "gpre_l": _pl(mix_norm_pre), "gpost_b": _bcast(mix_norm_post),
        "mu_l": np.ascontiguousarray(np.asarray(mu, np.float32).reshape(6, 8, 128).transpose(2, 0, 1)),
        "hp": np.ascontiguousarray(hp.transpose(2, 0, 1)),
    }


def outproj_phase(S, h_in, h_out, attnT, w_out, gpost_b, tag, ntile=32):
    P = {"w_out": w_out, "gpost_b": gpost_b}

    def mmf(out, lhsT, rhs, reads, wdep_, start=True, stop=True):
        S.op("pe", lambda e: e.matmul(out, lhsT, rhs, start=start, stop=stop), reads=reads, writes=[wdep_])
    with contextlib.ExitStack() as es:
        def SB(name, shape, dt):
            return sb(S, es, tag + name, shape, dt)
        wo = SB("wo", [128, 8, D], BF16)
        gpost = SB("gpost", [128, D], F32)
        stg = [SB(f"stg{i}", [128, 1024], F32) for i in range(2)]
        G = 4 if ntile % 4 == 0 else 1
        aT = [SB(f"aT{i}", [128, 8, 128 * G], BF16) for i in range(2)]
        xb = [SB(f"xb{i}", [128, D], F32) for i in range(2)]
        tmp = [SB(f"tmp{i}", [128, D], F32) for i in range(2)]
        junk = SB("junk", [128, 512], BF16)
        st = SB("st", [128, 8], F32)
        bk = [ps(S, es, tag + f"bk{i}", [128, 512], F32) for i in range(2)]
        S.dma("sp", gpost[:], P["gpost_b"], writes=[gpost.d()])
        for c in range(8):
            b = stg[c % 2]
            S.dma("sp", b[:, :], P["w_out"][c * 128:(c + 1) * 128, :], writes=[b.d()])
            S.op("act", lambda e, c=c, b=b: e.activation(out=wo[:, c, :], in_=b[:, :], func=AF.Copy), reads=[b.d()])
        S.barrier()
        a_v = attnT.rearrange("(c p) t -> p c t", p=128)
        hv_in = h_in.rearrange("(t p) d -> t p d", p=128)
        hv_out = h_out.rearrange("(t p) d -> t p d", p=128)
        for t in range(ntile):
            a, x, tm = aT[(t // G) % 2], xb[t % 2], tmp[t % 2]
            so = (t % G) * 128
            if t % G == 0:
                S.dma("sp", a[:, :, :], a_v[:, :, t * 128:(t + G) * 128], writes=[a.d()])
            S.dma("sp", x[:], hv_in[t], writes=[x.d()])
            for half in range(2):
                b = bk[half]
                for c in range(8):
                    mmf(b[:, :], a[:, c, so:so + 128], wo[:, c, half * 512:(half + 1) * 512], [a.d()], b.d(),
                        c == 0, c == 7)
                S.op("act", lambda e, b=b, half=half: e.activation(out=junk[:], in_=b[:, :], func=AF.Square,
                                                                   accum_out=st[:, half:half + 1]),
                     reads=[b.d()], writes=[junk.d(), st.d(half), b.d("port")])
                S.op("dve", lambda e, b=b, half=half, tm=tm: e.tensor_tensor(
                    out=tm[:, half * 512:(half + 1) * 512], in0=b[:, :], in1=gpost[:, half * 512:(half + 1) * 512],
                    op=ALU.mult), reads=[b.d()], writes=[tm.d(), b.d("port")])
            S.op("dve", lambda e: e.tensor_tensor(out=st[:, 2:3], in0=st[:, 0:1], in1=st[:, 1:2], op=ALU.add),
                 reads=[st.d(0), st.d(1)], writes=[st.d(2)])
            S.op("dve", lambda e: e.tensor_scalar(out=st[:, 2:3], in0=st[:, 2:3], scalar1=1.0 / D, scalar2=RMS_EPS,
                                                  op0=ALU.mult, op1=ALU.add), reads=[st.d(2)], writes=[st.d(2)])
            S.op("act", lambda e: e.activation(out=st[:, 2:3], in_=st[:, 2:3], func=AF.Sqrt),
                 reads=[st.d(2)], writes=[st.d(2)])
            S.op("dve", lambda e: e.reciprocal(out=st[:, 2:3], in_=st[:, 2:3]), reads=[st.d(2)], writes=[st.d(2)])
            S.op("dve", lambda e, tm=tm, x=x: e.scalar_tensor_tensor(out=tm[:], in0=tm[:], scalar=st[:, 2:3], in1=x[:],
                                                                    op0=ALU.mult, op1=ALU.add),
                 reads=[tm.d(), st.d(2), x.d()], writes=[tm.d()])
            S.dma("pool", hv_out[t], tm[:], reads=[tm.d()])
        S.barrier()


NSA_IN = 2608
NEG = -30000.0


def nsa_consts():
    import ml_dtypes
    bf = ml_dtypes.bfloat16
    slopes = (2.0 ** (-8.0 * np.arange(1, 17, dtype=np.float64) / 16)).astype(np.float32)
    t = np.arange(4096, dtype=np.float64)
    aq = np.zeros((16, 3, 4096), dtype=bf)
    for h in range(16):
        v = (-slopes[h].astype(np.float64) * t).astype(np.float32)
        r = v.copy()
        for k in range(3):
            p = r.astype(bf)
            aq[h, k] = p
            r = (r - p.astype(np.float32)).astype(np.float32)
    j = np.arange(128)
    i = np.arange(512)
    bias_key = np.zeros((128, 16, 32), np.float32)
    bias_cmp = np.zeros((128, 16, 2), np.float32)
    for h in range(16):
        for kt in range(32):
            bias_key[:, h, kt] = slopes[h] * (128 * kt + j)
        for nt in range(2):
            bias_cmp[:, h, nt] = slopes[h] * (16 * (128 * nt + j) + 31)
    cmpmask = np.zeros((128, 8, 512), np.float32)
    for idx in range(8):
        cmpmask[:, idx, :] = np.where(16 * j[:, None] + 31 - i[None, :] <= 512 * idx, 0.0, NEG)
    causal = np.zeros((128, 4, 512), np.float32)
    for r_ in range(4):
        causal[:, r_, :] = np.where(j[:, None] + 128 * r_ <= i[None, :], 0.0, NEG)
    win = np.zeros((128, 8, 512), np.float32)
    for w in range(8):
        dist = i[None, :] - j[:, None] - 128 * (w - 4)
        win[:, w, :] = np.where((dist >= 0) & (dist < 512), 0.0, NEG)
    E = np.zeros((64, 4096), np.float32)
    E[np.arange(4096) // 64, np.arange(4096)] = 1.0
    cs_ = np.arange(255) * 16
    bs_ = np.arange(64) * 64
    ov = np.clip(np.minimum(cs_[:, None] + 32, bs_[None, :] + 64) - np.maximum(cs_[:, None], bs_[None, :]), 0, None) / 32.0
    Caug = np.zeros((256, 65), np.float32)
    Caug[:255, :64] = ov
    Caug[:255, 64] = 1.0
    Caug = Caug.reshape(2, 128, 65).transpose(1, 0, 2)
    vis = np.zeros((128, 32, 64), np.float32)
    add = np.zeros((128, 32, 64), np.float32)
    s = np.arange(64)
    for it in range(32):
        tq = 128 * it + j
        cur = tq // 64
        visible = s[None, :] * 64 <= tq[:, None]
        a = np.where(visible, 0.0, -1.0)
        v = visible.astype(np.float32)
        for (cond, val) in [(s[None, :] == 0, 1e4), (s[None, :] == cur[:, None], 2e4),
                            (s[None, :] == cur[:, None] - 1, 3e4)]:
            a = np.where(cond, val, a)
            v = np.where(cond, 0.0, v)
        vis[:, it, :] = v
        add[:, it, :] = a
    return {
        "alibi_q": aq, "bias_key": bias_key, "bias_cmp": bias_cmp,
        "cmpmask": cmpmask.astype(bf), "causal": causal.astype(bf), "winmask": win.astype(bf),
        "E_all": np.concatenate([E[1:64], np.ones((1, 4096), np.float32)], axis=0).astype(bf), "C_aug": np.ascontiguousarray(Caug).astype(bf),
        "vis": vis, "addt": add,
        "identb": np.eye(128, dtype=np.float32).astype(bf),
    }


def nsa_host_params(mix_norm_pre, mix_norm_post, pe_k, w_ck1, pe_v, w_cv1):
    return {
        "gpre_l": _pl(mix_norm_pre), "gpost_b": _bcast(mix_norm_post),
        "pekT": np.ascontiguousarray(np.asarray(pe_k, np.float32).T),
        "pevT": np.ascontiguousarray(np.asarray(pe_v, np.float32).T),
        "wck1_l": np.ascontiguousarray(np.asarray(w_ck1, np.float32).transpose(1, 0, 2)),
        "wcv1_l": np.ascontiguousarray(np.asarray(w_cv1, np.float32).transpose(1, 0, 2)),
    }


def nsa_phase(S, h_in, h_out, P, C, scr, tag="n"):
    nc = S.nc
    projT, vtokd, gTd, attnT = scr["projT"], scr["vtok"], scr["gT"], scr["attnT"]

    def mmf(out, lhsT, rhs, reads, wdep_, start=True, stop=True):
        S.op("pe", lambda e: e.matmul(out, lhsT, rhs, start=start, stop=stop), reads=reads, writes=[wdep_])

    with contextlib.ExitStack() as es:
        def SB(name, shape, dt):
            return sb(S, es, tag + "1" + name, shape, dt)
        Wb = SB("Wb", [128, 8, NSA_IN], BF16)
        gpre = SB("gpre", [128, 8], F32)
        identb = SB("identb", [128, 128], BF16)
        stg = [SB(f"stg{i}", [128, 1024], F32) for i in range(2)]
        xb = [SB(f"xb{i}", [128, 4, D], F32) for i in range(2)]
        xn = [SB(f"xn{i}", [128, D], BF16) for i in range(2)]
        junk = SB("junk", [128, D], BF16)
        uT = [SB(f"uT{i}", [128, 8, 512], BF16) for i in range(2)]
        ev = [SB(f"ev{i}", [128, 512], BF16) for i in range(4)]
        evf = [SB(f"evf{i}", [48, 512], F32) for i in range(2)]
        evt = [SB(f"evt{i}", [128, 512], BF16) for i in range(2)]
        st = SB("st", [128, 8], F32)
        bankT = ps(S, es, tag + "1bT", [128, 8, 128], BF16)
        banks = [ps(S, es, tag + f"1bk{i}", [128, 512], F32) for i in range(4)]
        S.dma("sp", gpre[:], P["gpre_l"], writes=[gpre.d()])
        S.dma("sp", identb[:], C["identb"], writes=[identb.d()])
        k = 0
        for c in range(8):
            for o in range(0, NSA_IN, 1024):
                wdt = min(1024, NSA_IN - o)
                b = stg[k % 2]
                S.dma("sp", b[:, 0:wdt], P["w_in"][c * 128:(c + 1) * 128, o:o + wdt], writes=[b.d()])
                S.op("act", lambda e, c=c, o=o, wdt=wdt, b=b: e.activation(
                    out=Wb[:, c, o:o + wdt], in_=b[:, 0:wdt], func=AF.Copy, scale=gpre[:, c:c + 1]),
                    reads=[b.d(), gpre.d()])
                k += 1
        S.barrier()
        hv_in = h_in.rearrange("(t s p) d -> t p s d", p=128, s=4)
        chunks = [(o, 128) for o in range(0, 2560, 128)] + [(2560, 48)]
        ke = 0
        for t in range(8):
            x, u = xb[t % 2], uT[t % 2]
            S.dma("sp", x[:, :, :], hv_in[t], writes=[x.d()])
            for s in range(4):
                xnb = xn[s % 2]
                S.op("act", lambda e, x=x, s=s: e.activation(out=junk[:], in_=x[:, s, :], func=AF.Square,
                                                             accum_out=st[:, s:s + 1]),
                     reads=[x.d()], writes=[junk.d(), st.d(s)])
                S.op("dve", lambda e, s=s: e.tensor_scalar(out=st[:, 4 + s:5 + s], in0=st[:, s:s + 1],
                                                           scalar1=1.0 / D, scalar2=RMS_EPS, op0=ALU.mult,
                                                           op1=ALU.add), reads=[st.d(s)], writes=[st.d(4 + s)])
                S.op("act", lambda e, s=s: e.activation(out=st[:, 4 + s:5 + s], in_=st[:, 4 + s:5 + s], func=AF.Sqrt),
                     reads=[st.d(4 + s)], writes=[st.d(4 + s)])
                S.op("dve", lambda e, s=s: e.reciprocal(out=st[:, 4 + s:5 + s], in_=st[:, 4 + s:5 + s]),
                     reads=[st.d(4 + s)], writes=[st.d(4 + s)])
                S.op("act", lambda e, x=x, s=s, xnb=xnb: e.activation(out=xnb[:], in_=x[:, s, :], func=AF.Copy,
                                                                      scale=st[:, 4 + s:5 + s]),
                     reads=[x.d(), st.d(4 + s)], writes=[xnb.d()])
                for c in range(8):
                    S.op("pe", lambda e, c=c, xnb=xnb: e.transpose(out=bankT[:, c, :],
                                                                   in_=xnb[:, c * 128:(c + 1) * 128],
                                                                   identity=identb[:]),
                         reads=[xnb.d()], writes=[bankT.d()])
                S.op("dve", lambda e, s=s, u=u: e.tensor_copy(out=u[:, :, s * 128:(s + 1) * 128], in_=bankT[:, :, :]),
                     reads=[bankT.d()], writes=[u.d()])
            for (o, m) in chunks:
                bk = banks[ke % 3]
                for c in range(8):
                    mmf(bk[0:m, :], Wb[:, c, o:o + m], u[:, c, :], [u.d()], bk.d(), c == 0, c == 7)
                if o == 2560:
                    e_ = evf[ke % 2]
                    S.op("act", lambda e, bk=bk, e_=e_: e.activation(out=e_[:], in_=bk[0:48, :], func=AF.Sigmoid),
                         reads=[bk.d()], writes=[e_.d()])
                    S.dma("pool", gTd[:, t * 512:(t + 1) * 512], e_[:], reads=[e_.d()])
                else:
                    e_ = ev[ke % 4]
                    sc = 0.125 if o < 1024 else 1.0
                    if ke % 2 == 0:
                        S.op("act", lambda e, bk=bk, e_=e_, sc=sc: e.activation(out=e_[:], in_=bk[:, :], func=AF.Copy,
                                                                               scale=sc),
                             reads=[bk.d()], writes=[e_.d()])
                    else:
                        S.op("dve", lambda e, bk=bk, e_=e_, sc=sc: e.tensor_scalar(out=e_[:], in0=bk[:, :], scalar1=sc,
                                                                                  scalar2=None, op0=ALU.mult),
                             reads=[bk.d()], writes=[e_.d()])
                    S.dma("pool", projT[o:o + 128, t * 512:(t + 1) * 512], e_[:], reads=[e_.d()])
                ke += 1
            for s in range(4):
                bk = banks[3]
                for (jj, o) in enumerate((1792, 2304)):
                    for c in range(8):
                        mmf(bk[:, jj * 256:(jj + 1) * 256], u[:, c, s * 128:(s + 1) * 128], Wb[:, c, o:o + 256],
                            [u.d()], bk.d(), c == 0, c == 7)
                e_ = evt[s % 2]
                S.op("act", lambda e, bk=bk, e_=e_: e.activation(out=e_[:], in_=bk[:, :], func=AF.Copy),
                     reads=[bk.d()], writes=[e_.d()])
                S.dma("pool", vtokd[t * 512 + s * 128:t * 512 + (s + 1) * 128, :], e_[:], reads=[e_.d()])
        S.barrier()

    with contextlib.ExitStack() as es:
        def SB(name, shape, dt):
            return sb(S, es, tag + "2" + name, shape, dt)
        ksA = SB("ksA", [128, 4096], BF16)
        kwA = SB("kwA", [128, 4096], BF16)
        kcT = SB("kcT", [64, 4096], BF16)
        vcT = SB("vcT", [64, 4096], BF16)
        vsA = SB("vsA", [128, 32, 128], BF16)
        vwA = SB("vwA", [128, 32, 128], BF16)
        qA = [SB(f"qA{i}", [128, 4096], BF16) for i in range(4)]
        kcmpA = SB("kcmpA", [128, 256], BF16)
        vcmpA = SB("vcmpA", [128, 2, 128], BF16)
        bias_key = SB("bias_key", [128, 16, 32], F32)
        bias_cmp = SB("bias_cmp", [128, 16, 2], F32)
        cmpmask = SB("cmpmask", [128, 8, 512], BF16)
        causal = SB("causal", [128, 4, 512], BF16)
        winmask = SB("winmask", [128, 8, 512], BF16)
        C_aug = SB("C_aug", [128, 2, 65], BF16)
        vis = SB("vis", [128, 32, 64], F32)
        addt = SB("addt", [128, 32, 64], F32)
        identb = SB("identb", [128, 128], BF16)
        w1 = [SB(f"w1_{i}", [64, 32, 64], BF16) for i in range(2)]
        w2 = [SB(f"w2_{i}", [64, 64], BF16) for i in range(2)]
        peT = [SB(f"peT{i}", [64, 32], BF16) for i in range(2)]
        cbias = SB("cbias", [64, 2], F32)
        stgw = SB("stgw", [64, 2048], F32)
        hid = [SB(f"hid{i}", [64, 256], BF16) for i in range(2)]
        eTs = [SB(f"eT{i}", [128, 512], BF16) for i in range(8)]
        imp_acc = SB("imp_acc", [128, 4, 64], F32)
        imp2 = SB("imp2", [128, 64], F32)
        imp3 = SB("imp3", [128, 64], F32)
        m8 = SB("m8", [128, 16], F32)
        mk = SB("mk", [128, 64], F32)
        negq = SB("negq", [128, 64], BF16)
        rq = SB("rq", [128, 4], F32)
        gbc = [SB(f"gbc{i}", [64, 3, 512], F32) for i in range(4)]
        acc = SB("acc", [64, 512], F32)
        rd = SB("rd", [64, 512], F32)
        tb = SB("tb", [64, 512], F32)
        outb = [SB(f"outb{i}", [64, 512], BF16) for i in range(2)]
        cacc = [SB(f"cacc{i}", [64, 512], F32) for i in range(4)]
        bsc = [ps(S, es, tag + f"2sc{i}", [128, 512], F32) for i in range(4)]
        bpv = [ps(S, es, tag + f"2pv{i}", [128, 512], F32) for i in range(2)]
        bimp = ps(S, es, tag + "2imp", [128, 512], F32)
        btr = ps(S, es, tag + "2tr", [128, 512], BF16)
        bcm = bimp
        ring_sc = Ring([(b, b.d()) for b in bsc])
        ring_pv = Ring([(b, b.d()) for b in bpv])
        ring_e = Ring([(b, b.d()) for b in eTs])

        for (t_, nm) in [(bias_key, "bias_key"), (bias_cmp, "bias_cmp"), (cmpmask, "cmpmask"), (causal, "causal"),
                         (winmask, "winmask"), (C_aug, "C_aug"), (vis, "vis"), (addt, "addt"),
                         (identb, "identb")]:
            S.dma("sp", t_[:], C[nm], writes=[t_.d()])
        for bi, (w1n, w2n, pen) in enumerate([("wck1_l", "w_ck2", "pekT"), ("wcv1_l", "w_cv2", "pevT")]):
            S.dma("sp", stgw[:, :].rearrange("p (l c) -> p l c", l=32), P[w1n], writes=[stgw.d()])
            S.op("dve", lambda e, bi=bi: e.tensor_copy(out=w1[bi][:].rearrange("p l c -> p (l c)"), in_=stgw[:, :]),
                 reads=[stgw.d()], writes=[w1[bi].d()])
            S.dma("sp", stgw[:, 0:64], P[w2n], writes=[stgw.d()])
            S.op("dve", lambda e, bi=bi: e.tensor_copy(out=w2[bi][:], in_=stgw[:, 0:64]),
                 reads=[stgw.d()], writes=[w2[bi].d()])
            S.dma("sp", stgw[:, 0:32], P[pen], writes=[stgw.d()])
            S.op("dve", lambda e, bi=bi: e.tensor_copy(out=peT[bi][:], in_=stgw[:, 0:32]),
                 reads=[stgw.d()], writes=[peT[bi].d()])
            for l in range(32):
                mmf(bcm[0:64, 0:1], w1[bi][:, l, :], peT[bi][:, l:l + 1], [w1[bi].d(), peT[bi].d()], bcm.d(),
                    l == 0, l == 31)
            S.op("dve", lambda e, bi=bi: e.tensor_copy(out=cbias[:, bi:bi + 1], in_=bcm[0:64, 0:1]),
                 reads=[bcm.d()], writes=[cbias.d()])
        for t_ in (kwA, kcmpA) + tuple(qA):
            S.op("dve", lambda e, t_=t_: e.memset(t_[64:128, :], 0.0), writes=[t_.d()])
        S.dma("sp", ksA[64:128, :], C["E_all"], writes=[ksA.d()])
        S.dma("sp", kwA[127:128, :], C["E_all"][63:64, :], writes=[kwA.d()])
        S.dma("sp", kcmpA[127:128, :], C["E_all"][63:64, 0:256], writes=[kcmpA.d()])
        S.op("dve", lambda e: e.memset(negq[:], 0.0), writes=[negq.d()])
        S.op("dve", lambda e: e.memset(vcmpA[:, :, 0:64], 0.0), writes=[vcmpA.d()])
        for t_ in (vsA, vwA, vcmpA):
            S.op("dve", lambda e, t_=t_: e.memset(t_[:, :, 64:128], 1.0), writes=[t_.d()])
        S.op("dve", lambda e: e.memset(kcmpA[0:64, 255:256], 0.0), writes=[kcmpA.d()])
        vt_v = vtokd.rearrange("(kt p) d -> p kt d", p=128)
        pend = []
        import os
        LAG = int(os.environ.get('NSA_LAG', '4'))
        S.barrier()

        for g in range(4):
            S.dma("sp", ksA[0:64, :], projT[1536 + g * 64:1536 + (g + 1) * 64, :], writes=[ksA.d()])
            S.dma("sp", kwA[0:64, :], projT[2048 + g * 64:2048 + (g + 1) * 64, :], writes=[kwA.d()])
            S.dma("sp", kcT[:, :], projT[1024 + g * 64:1024 + (g + 1) * 64, :], writes=[kcT.d()])
            S.dma("sp", vcT[:, :], projT[1280 + g * 64:1280 + (g + 1) * 64, :], writes=[vcT.d()])
            S.dma("sp", vsA[:, :, 0:64], vt_v[:, :, g * 64:(g + 1) * 64], writes=[vsA.d()])
            S.dma("sp", vwA[:, :, 0:64], vt_v[:, :, 256 + g * 64:256 + (g + 1) * 64], writes=[vwA.d()])
            for r in range(4):
                h = 4 * g + r
                S.dma("sp", qA[r][0:64, :], projT[h * 64:(h + 1) * 64, :], writes=[qA[r].d()])
                S.dma("sp", qA[r][127:128, :], C["alibi_q"][h, 0:1, :], writes=[qA[r].d()])
            for bi, src in enumerate((kcT, vcT)):
                for l in range(32):
                    mmf(bcm[0:64, 0:255], w1[bi][:, l, :], src[:, l:l + 16 * 254 + 1:16], [src.d()], bcm.d(),
                        l == 0, l == 31)
                S.op("act", lambda e, bi=bi: e.activation(out=hid[bi][:, 0:255], in_=bcm[0:64, 0:255], func=AF.Silu,
                                                          bias=cbias[:, bi:bi + 1]),
                     reads=[bcm.d(), cbias.d()], writes=[hid[bi].d()])
            mmf(bcm[0:64, 0:255], w2[0][:], hid[0][:, 0:255], [hid[0].d()], bcm.d())
            S.op("dve", lambda e: e.tensor_copy(out=kcmpA[0:64, 0:255], in_=bcm[0:64, 0:255]),
                 reads=[bcm.d()], writes=[kcmpA.d()])
            for nt in range(2):
                nn = 128 if nt == 0 else 127
                mmf(bcm[0:nn, 256 + nt * 64:256 + (nt + 1) * 64], hid[1][:, nt * 128:nt * 128 + nn], w2[1][:],
                    [hid[1].d()], bcm.d())
                S.op("dve", lambda e, nt=nt, nn=nn: e.tensor_copy(out=vcmpA[0:nn, nt, 0:64],
                                                                 in_=bcm[0:nn, 256 + nt * 64:256 + (nt + 1) * 64]),
                     reads=[bcm.d()], writes=[vcmpA.d()])

            def push_branch(tiles, qc, done_cb):
                pvb, pvd = ring_pv.get()
                n = len(tiles)
                ets = []
                for ti, (kT, bias, masks, vT, rds) in enumerate(tiles):
                    sc, scd = ring_sc.get()
                    mmf(sc[:, :], kT, qc, rds, scd, True, len(masks) == 0)
                    for mi, (ml, mr, mrd) in enumerate(masks):
                        mmf(sc[:, :], ml, mr, mrd, scd, False, mi == len(masks) - 1)
                    eT, eTd = ring_e.get()
                    S.op("act", lambda e, sc=sc, eT=eT, bias=bias: e.activation(out=eT[:], in_=sc[:, :], func=AF.Exp,
                                                                               bias=bias),
                         reads=[scd], writes=[eTd])
                    ets.append((eT, eTd))
                    flush(LAG - 1)
                    last = ti == n - 1
                    pend.append((pvb, pvd, vT, eT, [eTd] + list(rds), ti == 0, last,
                                 (lambda: done_cb(pvb, pvd, ets)) if last else None))

            def flush(keep=0):
                while len(pend) > keep:
                    pvb, pvd, vT, eT, prds, first, last, cb = pend.pop(0)
                    mmf(pvb[:, :], vT, eT[:], prds, pvd, first, last)
                    if cb is not None:
                        cb()

            def combine(pvb, pvd, gb, b, accb, first):
                if first:
                    S.op("dve", lambda e: e.tensor_scalar(out=rd[:], in0=pvb[64:128, :], scalar1=1e-30, scalar2=None,
                                                          op0=ALU.add), reads=[pvd], writes=[rd.d()])
                    S.op("dve", lambda e: e.reciprocal(out=rd[:], in_=rd[:]), reads=[rd.d()], writes=[rd.d()])
                else:
                    S.op("dve", lambda e: e.reciprocal(out=rd[:], in_=pvb[64:128, :]), reads=[pvd], writes=[rd.d()])
                S.op("dve", lambda e: e.tensor_tensor(out=tb[:], in0=pvb[0:64, :], in1=rd[:], op=ALU.mult),
                     reads=[pvd, rd.d()], writes=[tb.d()])
                if first:
                    S.op("pool", lambda e: e.tensor_tensor(out=accb[:], in0=tb[:], in1=gb[:, b, :], op=ALU.mult),
                         reads=[tb.d(), gb.d()], writes=[accb.d()])
                else:
                    S.op("pool", lambda e: e.tensor_tensor(out=tb[:], in0=tb[:], in1=gb[:, b, :], op=ALU.mult),
                         reads=[tb.d(), gb.d()], writes=[tb.d()])
                    S.op("pool", lambda e: e.tensor_tensor(out=accb[:], in0=accb[:], in1=tb[:], op=ALU.add),
                         reads=[tb.d(), accb.d()], writes=[accb.d()])

            for c in range(8):
                cs_ = slice(c * 512, (c + 1) * 512)
                nts = [0] if c < 4 else [0, 1]
                for r in range(4):
                    h = 4 * g + r
                    tiles = [(kcmpA[:, nt * 128:(nt + 1) * 128], bias_cmp[:, h, nt:nt + 1],
                              [(identb[:], cmpmask[:, c - 4 * nt, :], [])], vcmpA[:, nt, :],
                              [kcmpA.d(), qA[r].d(), vcmpA.d()]) for nt in nts]

                    def cmp_done(pvb, pvd, ets, r=r, h=h, nts=nts, cs_=cs_):
                        for qs in range(4):
                            for ni, nt in enumerate(nts):
                                eT, eTd = ets[ni]
                                mmf(bimp[:, qs * 65:(qs + 1) * 65], eT[:, qs * 128:(qs + 1) * 128], C_aug[:, nt, :],
                                    [eTd], bimp.d(), ni == 0, ni == len(nts) - 1)
                        S.op("dve", lambda e: e.tensor_scalar(
                            out=rq[:], in0=bimp[:, 0:260].rearrange("p (a b) -> p a b", a=4)[:, :, 64], scalar1=1e-30,
                            scalar2=None, op0=ALU.add), reads=[bimp.d()], writes=[rq.d()])
                        S.op("dve", lambda e: e.reciprocal(out=rq[:], in_=rq[:]), reads=[rq.d()], writes=[rq.d()])
                        for qs in range(4):
                            if r == 0:
                                S.op("dve", lambda e, qs=qs: e.tensor_scalar(
                                    out=imp_acc[:, qs, :], in0=bimp[:, qs * 65:qs * 65 + 64], scalar1=rq[:, qs:qs + 1],
                                    scalar2=None, op0=ALU.mult), reads=[bimp.d(), rq.d()], writes=[imp_acc.d()])
                            else:
                                S.op("dve", lambda e, qs=qs: e.scalar_tensor_tensor(
                                    out=imp_acc[:, qs, :], in0=bimp[:, qs * 65:qs * 65 + 64], scalar=rq[:, qs:qs + 1],
                                    in1=imp_acc[:, qs, :], op0=ALU.mult, op1=ALU.add),
                                    reads=[bimp.d(), rq.d(), imp_acc.d()], writes=[imp_acc.d()])
                        gb = gbc[r]
                        S.dma("sp", gb[:], gTd[3 * h:3 * h + 3, cs_].partition_broadcast(64), writes=[gb.d()])
                        combine(pvb, pvd, gb, 0, cacc[r], True)

                    push_branch(tiles, qA[r][:, cs_], cmp_done)
                flush()
                for qs in range(4):
                    it = 4 * c + qs
                    S.op("dve", lambda e, qs=qs, it=it: e.tensor_tensor(out=imp2[:], in0=imp_acc[:, qs, :],
                                                                      in1=vis[:, it, :], op=ALU.mult),
                         reads=[imp_acc.d()], writes=[imp2.d()])
                    S.op("dve", lambda e, it=it: e.tensor_tensor(out=imp2[:], in0=imp2[:], in1=addt[:, it, :],
                                                                 op=ALU.add), reads=[imp2.d()], writes=[imp2.d()])
                    S.op("dve", lambda e: e.max(out=m8[:, 0:8], in_=imp2[:]), reads=[imp2.d()], writes=[m8.d()])
                    S.op("dve", lambda e: e.match_replace(out=imp3[:], in_to_replace=m8[:, 0:8], in_values=imp2[:],
                                                          imm_value=-2.0),
                         reads=[imp2.d(), m8.d()], writes=[imp3.d()])
                    S.op("dve", lambda e: e.max(out=m8[:, 8:16], in_=imp3[:]), reads=[imp3.d()], writes=[m8.d()])
                    S.op("dve", lambda e: e.tensor_scalar(out=mk[:], in0=imp2[:], scalar1=m8[:, 15:16], scalar2=None,
                                                          op0=ALU.is_ge), reads=[imp2.d(), m8.d()], writes=[mk.d()])
                    S.op("dve", lambda e: e.tensor_scalar(out=negq[:, 0:63], in0=mk[:, 1:64], scalar1=-1.0,
                                                          scalar2=-NEG, op0=ALU.add, op1=ALU.mult),
                         reads=[mk.d()], writes=[negq.d()])
                    S.op("pe", lambda e: e.transpose(out=btr[0:64, 0:128], in_=negq[:], identity=identb[:]),
                         reads=[negq.d()], writes=[btr.d()])
                    for r in range(4):
                        S.op("act", lambda e, it=it, r=r: e.activation(out=qA[r][64:127, it * 128:(it + 1) * 128],
                                                                       in_=btr[0:63, 0:128], func=AF.Copy),
                             reads=[btr.d()], writes=[qA[r].d()])
                for r in range(4):
                    h = 4 * g + r
                    gb = gbc[r]
                    tiles = []
                    for kt in range(4 * c + 4):
                        masks = []
                        if kt >= 4 * c:
                            masks.append((identb[:], causal[:, kt - 4 * c, :], []))
                        tiles.append((ksA[:, kt * 128:(kt + 1) * 128], bias_key[:, h, kt:kt + 1], masks,
                                      vsA[:, kt, :], [ksA.d(), qA[r].d(), vsA.d()]))
                    push_branch(tiles, qA[r][:, cs_],
                                lambda pvb, pvd, ets, r=r, gb=gb: combine(pvb, pvd, gb, 1, cacc[r], False))
                    tiles = []
                    for kt in range(max(0, 4 * c - 4), 4 * c + 4):
                        tiles.append((kwA[:, kt * 128:(kt + 1) * 128], bias_key[:, h, kt:kt + 1],
                                      [(identb[:], winmask[:, kt - 4 * c + 4, :], [])], vwA[:, kt, :],
                                      [kwA.d(), qA[r].d(), vwA.d()]))

                    def win_done(pvb, pvd, ets, r=r, h=h, gb=gb, cs_=cs_):
                        combine(pvb, pvd, gb, 2, cacc[r], False)
                        ob = outb[r % 2]
                        S.op("act", lambda e: e.activation(out=ob[:], in_=cacc[r][:], func=AF.Copy),
                             reads=[cacc[r].d()], writes=[ob.d()])
                        S.dma("pool", attnT[h * 64:(h + 1) * 64, cs_], ob[:], reads=[ob.d()])

                    push_branch(tiles, qA[r][:, cs_], win_done)
            flush()
        S.barrier()

    outproj_phase(S, h_in, h_out, attnT, P["w_out"], P["gpost_b"], tag + "3")


_CACHE = {}


def build_program(phases=("f", "n", "f", "f", "r", "f")):
    nc = bass.Bass("TRN2", target_bir_lowering=False)
    A = {}

    def din(name, shape, dt=F32):
        A[name] = nc.dram_tensor(name, list(shape), dt, kind="ExternalInput").ap()
        return A[name]

    def dscr(name, shape, dt=F32):
        return nc.dram_tensor(name, list(shape), dt, kind="Internal").ap()

    din("x", [S_LEN, D])
    for li in range(2):
        for f in (1, 2):
            din(f"f{f}_{li}_wgu", [D, 2 * DFF]); din(f"f{f}_{li}_wdn", [DFF, D])
            din(f"f{f}_{li}_gpre", [128, 8]); din(f"f{f}_{li}_gpost", [128, D])
    ncst = nsa_consts()
    rcst = rwkv_consts()
    for k_, v in ncst.items():
        din("nc_" + k_, v.shape, F32 if v.dtype == np.float32 else BF16)
    for k_, v in rcst.items():
        din("rc_" + k_, v.shape, F32 if v.dtype == np.float32 else BF16)
    NP = {"w_in": din("n_w_in", [D, NSA_IN]), "w_out": din("n_w_out", [D, D]),
          "gpre_l": din("n_gpre_l", [128, 8]), "gpost_b": din("n_gpost_b", [128, D]),
          "pekT": din("n_pekT", [64, 32]), "pevT": din("n_pevT", [64, 32]),
          "wck1_l": din("n_wck1_l", [64, 32, 64]), "wcv1_l": din("n_wcv1_l", [64, 32, 64]),
          "w_ck2": din("n_w_ck2", [64, 64]), "w_cv2": din("n_w_cv2", [64, 64])}
    RP = {"w_in": din("r_w_in", [D, 3360]), "w_out": din("r_w_out", [D, D]), "w_w2": din("r_w_w2", [64, D]),
          "w_a2": din("r_w_a2", [64, D]), "w_g2": din("r_w_g2", [160, D]),
          "gpre_l": din("r_gpre_l", [128, 8]), "gpost_b": din("r_gpost_b", [128, D]),
          "mu_l": din("r_mu_l", [128, 6, 8]), "hp": din("r_hp", [64, 7, 16])}
    y = nc.dram_tensor("y", [S_LEN, D], F32, kind="ExternalOutput").ap()
    hs = [A["x"]] + [dscr(f"h{i}", [S_LEN, D]) for i in range(len(phases) - 1)] + [y]
    scr = {"projT": dscr("projT", [2560, S_LEN], BF16), "vtok": dscr("vtokd", [S_LEN, 512], BF16),
           "gT": dscr("gT", [48, S_LEN]), "attnT": dscr("attnT", [D, S_LEN], BF16)}
    NC_ = {k_: A["nc_" + k_] for k_ in ncst}
    RC_ = {k_: A["rc_" + k_] for k_ in rcst}
    es = contextlib.ExitStack()
    with es:
        S = Sched(nc, es)
        ffn_ids = [(1, 0), (2, 0), (1, 1), (2, 1)]
        fi = 0
        for pi, ph in enumerate(phases):
            hin, hout = hs[pi], hs[pi + 1]
            if ph == "f":
                f, li = ffn_ids[fi]
                fi += 1
                ffn_phase(S, hin, hout, A[f"f{f}_{li}_wgu"], A[f"f{f}_{li}_wdn"], A[f"f{f}_{li}_gpre"],
                          A[f"f{f}_{li}_gpost"], A["rc_identb"], tag=f"f{pi}")
            elif ph == "n":
                nsa_phase(S, hin, hout, NP, NC_, scr)
            elif ph == "r":
                rwkv_phase(S, hin, hout, RP, RC_, scr["attnT"])
        S.finish()
    return nc, ncst, rcst


def kernel(**inp):
    f32 = np.float32
    g = {k_: np.asarray(v) for k_, v in inp.items()}
    if "prog" not in _CACHE:
        _CACHE["prog"] = build_program()
    nc, ncst, rcst = _CACHE["prog"]
    shared = {}
    for li in range(2):
        for f in (1, 2):
            shared[f"f{f}_{li}_wgu"] = np.ascontiguousarray(g[f"ffn{f}_w_gu"][li], f32)
            shared[f"f{f}_{li}_wdn"] = np.ascontiguousarray(g[f"ffn{f}_w_down"][li], f32)
            shared[f"f{f}_{li}_gpre"] = _pl(g[f"ffn{f}_norm_pre"][li])
            shared[f"f{f}_{li}_gpost"] = _bcast(g[f"ffn{f}_norm_post"][li])
    for k_, v in ncst.items():
        shared["nc_" + k_] = v
    for k_, v in rcst.items():
        shared["rc_" + k_] = v
    nh = nsa_host_params(g["mix_norm_pre"][0], g["mix_norm_post"][0], g["nsa_pe_k"][0], g["nsa_w_ck1"][0],
                         g["nsa_pe_v"][0], g["nsa_w_cv1"][0])
    for k_, v in nh.items():
        shared["n_" + k_] = v
    shared["n_w_in"] = np.ascontiguousarray(g["nsa_w_in"][0], f32)
    shared["n_w_out"] = np.ascontiguousarray(g["nsa_w_out"][0], f32)
    shared["n_w_ck2"] = np.ascontiguousarray(g["nsa_w_ck2"][0], f32)
    shared["n_w_cv2"] = np.ascontiguousarray(g["nsa_w_cv2"][0], f32)
    rh = rwkv_host_params(g["mix_norm_pre"][1], g["mix_norm_post"][1], g["rwkv_mu"][0], g["rwkv_w0"][0],
                          g["rwkv_a0"][0], g["rwkv_k_k"][0], g["rwkv_k_a"][0], g["rwkv_r_k"][0],
                          g["rwkv_gn_w"][0], g["rwkv_gn_b"][0])
    for k_, v in rh.items():
        shared["r_" + k_] = v
    for nm in ("w_in", "w_out", "w_w2", "w_a2", "w_g2"):
        shared["r_" + nm] = np.ascontiguousarray(g["rwkv_" + nm][0], f32)
    x = np.asarray(g["x"], f32)
    in_maps = [dict(shared, x=np.ascontiguousarray(x[b])) for b in range(NCORES)]
    res = run_bass_kernel_spmd(nc, in_maps, core_ids=list(range(NCORES)))
    return np.stack([np.asarray(r["y"], f32) for r in res.results], axis=0)
```

```python
import contextlib
import numpy as np
import concourse.bass as bass
import concourse.mybir as mybir
from concourse.bass_utils import run_bass_kernel_spmd

F32 = mybir.dt.float32
BF16 = mybir.dt.bfloat16
AF = mybir.ActivationFunctionType
ALU = mybir.AluOpType
AX = mybir.AxisListType

S_LEN = 4096
D = 1024
DFF = 2816
NCORES = 8
RMS_EPS = 1e-6


class Dep:
    __slots__ = ("w", "r")

    def __init__(self):
        self.w = None
        self.r = {}


class Sched:
    EPOCH = 16000
    NDMA = 24

    def __init__(self, nc, es):
        self.nc = nc
        self.es = es
        self.eng = {"pe": nc.tensor, "act": nc.scalar, "dve": nc.vector,
                    "pool": nc.gpsimd, "sp": nc.sync}
        self.nsem = 0
        self.sem = {e: self._newsem(e) for e in self.eng}
        self.cnt = {e: 0 for e in self.eng}
        self.waited = {e: {} for e in self.eng}
        self.dsem = {"sp": [self._newsem("dsp") for _ in range(16)],
                     "pool": [self._newsem("dpl") for _ in range(8)]}
        self.dcnt = {q: [0] * len(v) for q, v in self.dsem.items()}
        self.dnext = {q: 0 for q in self.dsem}
        self.ninst = 0
        self.nwait = 0
        self.pe_self_sync = False

    def _newsem(self, tag):
        self.nsem += 1
        return self.es.enter_context(self.nc.semaphore(f"s_{tag}_{self.nsem}"))

    def _wait(self, e, toks):
        best = {}
        for (s, v, src) in toks:
            if src == e and e == "pe" and not self.pe_self_sync:
                continue
            k = id(s)
            if k not in best or best[k][1] < v:
                best[k] = (s, v)
        w = self.waited[e]
        for k, (s, v) in best.items():
            if w.get(k, 0) >= v:
                continue
            self.eng[e].wait_ge(s, v)
            self.nwait += 1
            w[k] = v

    def _collect(self, reads, writes):
        toks = []
        for d in reads:
            if d.w is not None:
                toks.append(d.w)
        for d in writes:
            if d.w is not None:
                toks.append(d.w)
            toks.extend(d.r.values())
        return toks

    def _update(self, tok, reads, writes):
        k = id(tok[0])
        for d in reads:
            old = d.r.get(k)
            if old is None or old[1] < tok[1]:
                d.r[k] = tok
        for d in writes:
            d.w = tok
            d.r = {}

    def op(self, e, fn, reads=(), writes=()):
        toks = self._collect(reads, writes)
        self._wait(e, toks)
        ins = fn(self.eng[e])
        if self.cnt[e] >= self.EPOCH:
            self.sem[e] = self._newsem(e)
            self.cnt[e] = 0
        self.cnt[e] += 1
        ins.then_inc(self.sem[e], 1)
        self.ninst += 1
        tok = (self.sem[e], self.cnt[e], e)
        self._update(tok, reads, writes)
        return tok

    def dma(self, q, out, in_, reads=(), writes=(), **kw):
        toks = self._collect(reads, writes)
        dsem, dcnt = self.dsem[q], self.dcnt[q]
        k = self.dnext[q]
        self.dnext[q] = (k + 1) % len(dsem)
        if dcnt[k] >= self.EPOCH:
            toks.append((dsem[k], dcnt[k], None))
            self._wait(q, toks)
            toks = []
            dsem[k] = self._newsem("d" + q)
            dcnt[k] = 0
        if dcnt[k] > 0:
            toks.append((dsem[k], dcnt[k], None))
        self._wait(q, toks)
        ins = self.eng[q].dma_start(out=out, in_=in_, **kw)
        dcnt[k] += 16
        ins.then_inc(dsem[k], 16)
        self.ninst += 1
        tok = (dsem[k], dcnt[k], None)
        self._update(tok, reads, writes)
        return tok

    def _all_dma_toks(self):
        return [(self.dsem[q][k], self.dcnt[q][k], None) for q in self.dsem
                for k in range(len(self.dsem[q])) if self.dcnt[q][k] > 0]

    def barrier(self):
        toks = [(self.sem[e], self.cnt[e], None) for e in self.eng if self.cnt[e] > 0]
        toks += self._all_dma_toks()
        for e in self.eng:
            self._wait(e, toks)

    def finish(self):
        toks = self._all_dma_toks()
        toks += [(self.sem[e], self.cnt[e], None) for e in self.eng if self.cnt[e] > 0]
        self._wait("sp", toks)


class Buf:
    def __init__(self, t):
        self.t = t
        self.deps = {}

    def d(self, key=0):
        dd = self.deps.get(key)
        if dd is None:
            dd = self.deps[key] = Dep()
        return dd

    def __getitem__(self, idx):
        return self.t[idx]


def sb(S, es, name, shape, dt):
    return Buf(es.enter_context(S.nc.sbuf_tensor(name, shape, dt)))


def ps(S, es, name, shape, dt):
    return Buf(es.enter_context(S.nc.psum_tensor(name, shape, dt)))


def ffn_phase(S, h_in, h_out, w_gu, w_down, gpre_l, gpost_b, ident_d, ntiles=16, tag="f"):
    nc = S.nc
    T = 256
    NS = T // 128
    with contextlib.ExitStack() as es:
        wgu = sb(S, es, tag + "wgu", [128, 8, 2 * DFF], BF16)
        wdn = sb(S, es, tag + "wdn", [128, 22, D], BF16)
        gpre = sb(S, es, tag + "gpre", [128, 8], F32)
        gpost = sb(S, es, tag + "gpost", [128, D], F32)
        ident = sb(S, es, tag + "ident", [128, 128], BF16)
        xb = [sb(S, es, tag + f"x{i}", [128, NS, D], F32) for i in range(2)]
        xn = [sb(S, es, tag + f"xn{i}", [128, D], BF16) for i in range(2)]
        xnT = [sb(S, es, tag + f"xnT{i}", [128, 8, T], BF16) for i in range(2)]
        hT = sb(S, es, tag + "hT", [128, 22, T], BF16)
        sg = [sb(S, es, tag + f"sg{i}", [128, T], F32) for i in range(3)]
        ob = [sb(S, es, tag + f"ob{i}", [128, D], F32) for i in range(2)]
        tmp = sb(S, es, tag + "tmp", [128, D], F32)
        junk = sb(S, es, tag + "junk", [128, D], BF16)
        st = sb(S, es, tag + "st", [128, 16], F32)
        pT = ps(S, es, tag + "pT", [128, 8, 128], BF16)
        pGU = [ps(S, es, tag + f"pGU{i}", [128, 2, T], F32) for i in range(3)]
        pF = [ps(S, es, tag + f"pF{i}", [128, 512], F32) for i in range(4)]

        S.dma("sp", gpre[:], gpre_l, writes=[gpre.d()])
        S.dma("sp", gpost[:], gpost_b, writes=[gpost.d()])
        S.dma("sp", ident[:], ident_d, writes=[ident.d()])
        S.op("dve", lambda e: e.tensor_scalar(out=gpost[:], in0=gpost[:], scalar1=0.5, scalar2=None,
                                              op0=ALU.mult), reads=[gpost.d()], writes=[gpost.d()])

        slots = [(xb[i].d(("stg", s_)), xb[i][:, s_, :]) for i in range(2) for s_ in range(NS)]
        HW = 1024
        k = 0

        def conv(dst, view, dep, scale=None):
            nonlocal k
            if k % 2 == 0:
                if scale is None:
                    S.op("act", lambda e: e.activation(out=dst, in_=view, func=AF.Copy), reads=[dep])
                else:
                    S.op("act", lambda e: e.activation(out=dst, in_=view, func=AF.Copy, scale=scale),
                         reads=[dep, gpre.d()])
            else:
                if scale is None:
                    S.op("dve", lambda e: e.tensor_copy(out=dst, in_=view), reads=[dep])
                else:
                    S.op("dve", lambda e: e.tensor_scalar(out=dst, in0=view, scalar1=scale, scalar2=None,
                                                          op0=ALU.mult), reads=[dep, gpre.d()])
            k += 1

        for c in range(8):
            for o in range(0, 2 * DFF, HW):
                wdt = min(HW, 2 * DFF - o)
                dep, sv = slots[k % len(slots)]
                S.dma("sp", sv[:, 0:wdt], w_gu[c * 128:(c + 1) * 128, o:o + wdt], writes=[dep])
                conv(wgu[:, c, o:o + wdt], sv[:, 0:wdt], dep, scale=gpre[:, c:c + 1])
        for j in range(22):
            dep, sv = slots[k % len(slots)]
            S.dma("sp", sv[:, :], w_down[j * 128:(j + 1) * 128, :], writes=[dep])
            conv(wdn[:, j, :], sv[:, :], dep)
        S.barrier()

        hv_in = h_in.rearrange("(t s p) d -> t p s d", p=128, s=NS)
        hv_out = h_out.rearrange("(t s p) d -> t s p d", p=128, s=NS)

        def load(t):
            S.dma("sp", xb[t % 2][:, :, :], hv_in[t], writes=[xb[t % 2].d()])

        def prenorm(t):
            x = xb[t % 2]
            for s in range(NS):
                xnb = xn[s % 2]
                S.op("act", lambda e, x=x, s=s: e.activation(
                    out=junk[:], in_=x[:, s, :], func=AF.Square, accum_out=st[:, s:s + 1]),
                    reads=[x.d()], writes=[junk.d(), st.d(s)])
                S.op("dve", lambda e, s=s: e.tensor_scalar(
                    out=st[:, 4 + s:5 + s], in0=st[:, s:s + 1], scalar1=1.0 / D, scalar2=RMS_EPS,
                    op0=ALU.mult, op1=ALU.add), reads=[st.d(s)], writes=[st.d(4 + s)])
                S.op("act", lambda e, s=s: e.activation(
                    out=st[:, 4 + s:5 + s], in_=st[:, 4 + s:5 + s], func=AF.Sqrt),
                    reads=[st.d(4 + s)], writes=[st.d(4 + s)])
                S.op("dve", lambda e, s=s: e.reciprocal(
                    out=st[:, 4 + s:5 + s], in_=st[:, 4 + s:5 + s]),
                    reads=[st.d(4 + s)], writes=[st.d(4 + s)])
                S.op("act", lambda e, x=x, s=s, xnb=xnb: e.activation(
                    out=xnb[:], in_=x[:, s, :], func=AF.Copy, scale=st[:, 4 + s:5 + s]),
                    reads=[x.d(), st.d(4 + s)], writes=[xnb.d()])
                for c in range(8):
                    S.op("pe", lambda e, c=c, xnb=xnb: e.transpose(
                        out=pT[:, c, :], in_=xnb[:, c * 128:(c + 1) * 128], identity=ident[:]),
                        reads=[xnb.d(), ident.d()], writes=[pT.d()])
                S.op("dve", lambda e, t=t, s=s: e.tensor_copy(
                    out=xnT[t % 2][:, :, s * 128:(s + 1) * 128], in_=pT[:, :, :]),
                    reads=[pT.d()], writes=[xnT[t % 2].d()])

        def gu(t):
            xT = xnT[t % 2]
            for j in range(22):
                pg = pGU[j % 3]
                for half in range(2):
                    col = half * DFF + j * 128
                    for c in range(8):
                        S.op("pe", lambda e, c=c, col=col, half=half, pg=pg: e.matmul(
                            pg[:, half, :], wgu[:, c, col:col + 128], xT[:, c, :],
                            start=(c == 0), stop=(c == 7)),
                            reads=[wgu.d(), xT.d()], writes=[pg.d()])
                sgb = sg[j % 3]
                S.op("act", lambda e, pg=pg, sgb=sgb: e.activation(
                    out=sgb[:], in_=pg[:, 0, :], func=AF.Silu), reads=[pg.d()], writes=[sgb.d()])
                S.op("dve", lambda e, pg=pg, sgb=sgb, j=j: e.tensor_tensor(
                    out=hT[:, j, :], in0=pg[:, 1, :], in1=sgb[:], op=ALU.mult),
                    reads=[pg.d(), sgb.d()], writes=[hT.d()])

        def down(t):
            x = xb[t % 2]
            for s in range(NS):
                pf = [pF[(s % 2) * 2], pF[(s % 2) * 2 + 1]]
                for half in range(2):
                    for j in range(22):
                        S.op("pe", lambda e, j=j, s=s, half=half, pf=pf: e.matmul(
                            pf[half][:, :], hT[:, j, s * 128:(s + 1) * 128],
                            wdn[:, j, half * 512:(half + 1) * 512], start=(j == 0), stop=(j == 21)),
                            reads=[hT.d(), wdn.d()], writes=[pf[half].d()])
                for half in range(2):
                    S.op("act", lambda e, half=half, pf=pf, s=s: e.activation(
                        out=junk[:, 0:512], in_=pf[half][:, :], func=AF.Square,
                        accum_out=st[:, 8 + 2 * s + half:9 + 2 * s + half]),
                        reads=[pf[half].d()], writes=[junk.d(), st.d(8 + 2 * s + half)])
                S.op("dve", lambda e, s=s: e.tensor_tensor(
                    out=st[:, 12 + s:13 + s], in0=st[:, 8 + 2 * s:9 + 2 * s],
                    in1=st[:, 9 + 2 * s:10 + 2 * s], op=ALU.add),
                    reads=[st.d(8 + 2 * s), st.d(9 + 2 * s)], writes=[st.d(12 + s)])
                S.op("dve", lambda e, s=s: e.tensor_scalar(
                    out=st[:, 12 + s:13 + s], in0=st[:, 12 + s:13 + s], scalar1=1.0 / D, scalar2=RMS_EPS,
                    op0=ALU.mult, op1=ALU.add), reads=[st.d(12 + s)], writes=[st.d(12 + s)])
                S.op("act", lambda e, s=s: e.activation(
                    out=st[:, 12 + s:13 + s], in_=st[:, 12 + s:13 + s], func=AF.Sqrt),
                    reads=[st.d(12 + s)], writes=[st.d(12 + s)])
                S.op("dve", lambda e, s=s: e.reciprocal(
                    out=st[:, 12 + s:13 + s], in_=st[:, 12 + s:13 + s]),
                    reads=[st.d(12 + s)], writes=[st.d(12 + s)])
                for half in range(2):
                    S.op("dve", lambda e, half=half, pf=pf: e.tensor_tensor(
                        out=tmp[:, half * 512:(half + 1) * 512], in0=pf[half][:, :],
                        in1=gpost[:, half * 512:(half + 1) * 512], op=ALU.mult),
                        reads=[pf[half].d(), gpost.d()], writes=[tmp.d()])
                o = ob[s % 2]
                S.op("dve", lambda e, s=s, o=o, x=x: e.scalar_tensor_tensor(
                    out=o[:], in0=tmp[:], scalar=st[:, 12 + s:13 + s], in1=x[:, s, :],
                    op0=ALU.mult, op1=ALU.add),
                    reads=[tmp.d(), st.d(12 + s), x.d()], writes=[o.d()])
                S.dma("pool", hv_out[t, s], o[:], reads=[o.d()])

        load(0)
        prenorm(0)
        for t in range(ntiles):
            if t + 1 < ntiles:
                load(t + 1)
            gu(t)
            if t + 1 < ntiles:
                prenorm(t + 1)
            down(t)
        S.barrier()


def _consts():
    import ml_dtypes
    ident = np.eye(128, dtype=np.float32).astype(ml_dtypes.bfloat16)
    return {"ident": ident}


class Ring:
    def __init__(self, views):
        self.views = views
        self.i = 0

    def get(self):
        v = self.views[self.i]
        self.i = (self.i + 1) % len(self.views)
        return v


HP_W0, HP_A0, HP_KK, HP_KA, HP_RK, HP_GNW, HP_GNB = range(7)
GN_EPS = 64e-5


def rwkv_phase(S, h_in, h_out, P, C, yT_dram, ndc=32, tag="r", stage=9, dbg=None):
    nc = S.nc
    import os
    RWBF = False
    F32R = BF16 if RWBF else mybir.dt.float32r
    PADDED = not RWBF

    def W(n):
        return 256 if PADDED else n
    HO = 0 if PADDED else 128
    with contextlib.ExitStack() as es:
        def SB(name, shape, dt):
            return sb(S, es, tag + name, shape, dt)

        Wb = SB("Wb", [128, 8, 3360], BF16)
        ww2 = SB("ww2", [64, D], BF16)
        wa2 = SB("wa2", [64, D], BF16)
        wg2a = SB("wg2a", [128, D], BF16)
        wg2b = SB("wg2b", [32, D], BF16)
        gpre = SB("gpre", [128, 8], F32)
        mu = SB("mu", [128, 6, 8], F32)
        hp = SB("hp", [64, 7, 16], F32)
        identb = SB("identb", [128, 128], BF16)
        identf = SB("identf", [128, 128], F32)
        identr = SB("identr", [128, 256], F32R)
        ones64 = SB("ones64", [64, 64], F32)
        ones64r = SB("ones64r", [64, 64], F32R)
        mask2 = SB("mask2", [128, 256], F32)
        masksl = SB("masksl", [128, 128], F32)
        scanm = SB("scanm", [64, 512], F32)
        xb = SB("xb", [128, D], F32)
        xn = SB("xn", [128, D], BF16)
        junk = SB("junk", [128, 512], BF16)
        uTx = [SB(f"uTx{i}", [128, 8, 129], BF16) for i in range(2)]
        xx = SB("xx", [128, 8, 128], BF16)
        mixb = [SB(f"mix{i}", [128, 8, 128], BF16) for i in range(4)]
        vtok = SB("vtok", [128, D + 256], F32R)
        th3 = SB("th3", [64, 128], BF16)
        p4b = SB("p4b", [64, 128], BF16)
        s5a = SB("s5a", [128, 128], BF16)
        s5b = SB("s5b", [32, 128], BF16)
        yfin = SB("yfin", [128, 8, 128], BF16)
        st = SB("st", [128, 8], F32)
        Sst = SB("Sst", [64, 20, 64], F32R)
        gamC = SB("gamC", [64, 16], F32)
        Q = {n: SB("q_" + n, [64, 4, 128], F32) for n in ["k", "sig", "a", "cs", "kk", "t1", "eneg", "epos", "gt1"]}
        Q["eexc"] = Q["sig"]
        Q["t1r"] = SB("q_t1r", [64, 4, 128], F32R)
        Q["gt1r"] = SB("q_gt1r", [64, 4, 128], F32R)
        QP = [{n: SB(f"qp{p}_" + n, [64, 4, 128], F32) for n in ["r", "kmod", "vT", "g", "y"]} for p in range(2)]
        ARs = [SB(f"AR{p}", [64, 4, 2, 128], F32R) for p in range(2)]
        BTs = [SB(f"BTb{p}", [64, 6, 128], F32R) for p in range(2)]
        KTs = [SB(f"KTb{p}", [64, 6, 128], F32R) for p in range(2)]
        NH = 4
        XB = [[SB(f"X{i}_{j}", [128, 512], F32R) for j in range(2)] for i in range(NH)]
        MRB = [SB(f"MRB{i}", [128, 256], F32R) for i in range(NH)]
        MKb = [SB(f"MK{i}", [128, 256], F32R) for i in range(NH)]
        AXb = [SB(f"AX{i}", [128, 256], F32R) for i in range(NH)]
        PQb = [SB(f"PQ{i}", [128, 320], F32R) for i in range(NH)]
        BKb = [SB(f"BK{i}", [128, 256], F32R) for i in range(NH)]
        GTb = [SB(f"GT{i}", [64, 64], F32R) for i in range(NH)]
        Hsb = [SB(f"Hs{i}", [64, 64], F32) for i in range(NH)]
        RhT = [SB(f"RhT{i}", [64, 256], F32R) for i in range(NH)]

        banks = [ps(S, es, tag + f"bk{i}", [128, 512], F32) for i in range(7)]
        bankT = ps(S, es, tag + "bkT", [128, 8, 128], BF16)
        ring_proj = Ring([(b[:, :], b.d()) for b in banks[0:1]])
        ringF = Ring([(b, b.d()) for b in banks[1:7]])

        for (t, src) in [(gpre, P["gpre_l"]), (mu, P["mu_l"]), (hp, P["hp"]),
                         (identb, C["identb"]), (identf, C["identf"]), (ones64, C["ones64"]),
                         (mask2, C["mask2"]), (masksl, C["masksl"]), (scanm, C["scanm"])]:
            S.dma("sp", t[:], src, writes=[t.d()])
        S.op("dve", lambda e: e.memset(uTx[1][:, :, 128:129], 0.0), writes=[uTx[1].d()])
        kcnt = [0]
        stg2 = Buf(xb.t)
        stg = [(xb, xb[:, 0:512]), (stg2, xb[:, 512:1024])]

        def conv(dst, view, b, scale=None):
            if scale is not None:
                S.op("act", lambda e: e.activation(out=dst, in_=view, func=AF.Copy, scale=scale),
                     reads=[b.d(), gpre.d()])
            elif kcnt[0] % 2 == 0:
                S.op("act", lambda e: e.activation(out=dst, in_=view, func=AF.Copy), reads=[b.d()])
            else:
                S.op("dve", lambda e: e.tensor_copy(out=dst, in_=view), reads=[b.d()])
            kcnt[0] += 1

        for c in range(8):
            for o in range(0, 3360, 512):
                wdt = min(512, 3360 - o)
                b, bv = stg[kcnt[0] % 2]
                S.dma("sp", bv[:, 0:wdt], P["w_in"][c * 128:(c + 1) * 128, o:o + wdt], writes=[b.d()])
                conv(Wb[:, c, o:o + wdt], bv[:, 0:wdt], b, scale=gpre[:, c:c + 1])
        for (dst, src, n) in [(ww2, P["w_w2"], 64), (wa2, P["w_a2"], 64), (wg2a, P["w_g2"][0:128, :], 128),
                              (wg2b, P["w_g2"][128:160, :], 32)]:
            for o in range(0, D, 512):
                b, bv = stg[kcnt[0] % 2]
                S.dma("sp", bv[0:n, :], src[:, o:o + 512], writes=[b.d()])
                conv(dst[0:n, o:o + 512], bv[0:n, :], b)
        S.barrier()

        S.op("dve", lambda e: e.memset(xb[:, 0:512], 0.0), writes=[xb.d()])

        def zero_r(buf, flat, nparts, width, deps):
            for o in range(0, width, 512):
                w_ = min(512, width - o)
                S.op("pool", lambda e, o=o, w_=w_: e.tensor_copy(out=flat[0:nparts, o:o + w_],
                                                                 in_=xb[0:nparts, 0:w_]),
                     reads=[xb.d()], writes=deps)
        zero_r(Sst, Sst[:].rearrange("p a b -> p (a b)"), 64, 20 * 64, [Sst.d(h) for h in range(16)])
        zero_r(identr, identr[:, 128:256], 128, 128, [identr.d()])
        S.op("dve", lambda e: e.tensor_copy(out=identr[:, 0:128], in_=identf[:]), reads=[identf.d()],
             writes=[identr.d()])
        S.op("dve", lambda e: e.tensor_copy(out=ones64r[:], in_=ones64[:]), reads=[ones64.d()],
             writes=[ones64r.d()])
        zero_r(vtok, vtok[:, :], 128, D + 256, [vtok.d()])
        for p_ in range(2):
            zero_r(BTs[p_], BTs[p_][:].rearrange("p a b -> p (a b)"), 64, 6 * 128, [BTs[p_].d()])
            zero_r(KTs[p_], KTs[p_][:].rearrange("p a b -> p (a b)"), 64, 6 * 128, [KTs[p_].d()])
        for t_ in MRB + MKb + AXb + BKb:
            zero_r(t_, t_[:, :], 128, 256, [t_.d()])
        for t_ in [b for row in XB for b in row]:
            zero_r(t_, t_[:, :], 128, 512, [t_.d()])
        for t_ in PQb:
            zero_r(t_, t_[:, :], 128, 320, [t_.d()])
        for t_ in RhT:
            zero_r(t_, t_[:, :], 64, 256, [t_.d()])
        S.barrier()

        hv_in = h_in.rearrange("(t p) d -> t p d", p=128)
        hv_out = h_out.rearrange("(t p) d -> t p d", p=128)

        def bc(idx, q):
            return hp[:, idx, 4 * q:4 * q + 4].unsqueeze(2).to_broadcast([64, 4, 128])

        def f2(b):
            return b[:].rearrange("p a b -> p (a b)")

        def rstd_ops(src_col, dst_col):
            S.op("dve", lambda e: e.tensor_scalar(out=st[:, dst_col:dst_col + 1], in0=st[:, src_col:src_col + 1],
                                                  scalar1=1.0 / D, scalar2=RMS_EPS, op0=ALU.mult, op1=ALU.add),
                 reads=[st.d(src_col)], writes=[st.d(dst_col)])
            S.op("act", lambda e: e.activation(out=st[:, dst_col:dst_col + 1], in_=st[:, dst_col:dst_col + 1],
                                               func=AF.Sqrt), reads=[st.d(dst_col)], writes=[st.d(dst_col)])
            S.op("dve", lambda e: e.reciprocal(out=st[:, dst_col:dst_col + 1], in_=st[:, dst_col:dst_col + 1]),
                 reads=[st.d(dst_col)], writes=[st.d(dst_col)])

        def mm(out, lhsT, rhs, reads, wdep_, start=True, stop=True):
            S.op("pe", lambda e: e.matmul(out, lhsT, rhs, start=start, stop=stop), reads=reads, writes=[wdep_])

        def tt(eng, out, in0, in1, op, reads, writes):
            S.op(eng, lambda e: e.tensor_tensor(out=out, in0=in0, in1=in1, op=op), reads=reads, writes=writes)

        def actf(out, in_, func, reads, writes, **kw):
            S.op("act", lambda e: e.activation(out=out, in_=in_, func=func, **kw), reads=reads, writes=writes)

        def make_mix(i, m, cur):
            for c in range(8):
                S.op("dve", lambda e, c=c: e.scalar_tensor_tensor(
                    out=m[:, c, :], in0=xx[:, c, :], scalar=mu[:, i, c:c + 1], in1=cur[:, c, 1:129],
                    op0=ALU.mult, op1=ALU.add), reads=[xx.d(), cur.d()], writes=[m.d()])
            return m

        Sf = Sst[:].rearrange("p a b -> p (a b)")
        yT_v = yT_dram.rearrange("(c p) t -> p c t", p=128)

        def front(h, slot, q):
            j = h % 4
            AR, BTb, KTb = ARs[q % 2], BTs[q % 2], KTs[q % 2]
            BTf = BTb[:].rearrange("p a b -> p (a b)")
            ARcat = AR[:, j, :, :].rearrange("p a b -> p (a b)")
            AT = AR[:, j, 0, :]
            RT = AR[:, j, 1, :]
            BTh = BTb[:, j, :]
            KTh = KTb[:, j, :]
            vpad = vtok[:, h * 64:h * 64 + W(64)]
            mrb, mk = MRB[slot], MKb[slot]
            x0, x1 = XB[slot]
            ax, pq, bk = AXb[slot], PQb[slot], BKb[slot]
            pb, pd = ringF.get()
            mm(pb[:, 0:256], BTh, ARcat, [BTb.d(), AR.d()], pd)
            tt("dve", x0[:, 0:128], pb[:, 0:128], mask2[:, 0:128], ALU.mult, [pd], [x0.d()])
            tt("dve", mrb[:, 128:256], pb[:, 128:256], mask2[:, 128:256], ALU.mult, [pd], [mrb.d()])
            actf(x0[:, 128:256], identr[:, 0:128], AF.Copy, [identr.d()], [x0.d()])
            pb, pd = ringF.get()
            mm(pb[:, 0:256], KTh, ARcat, [KTb.d(), AR.d()], pd)
            tt("dve", mk[:, 0:256], pb[:, 0:256], mask2[:], ALU.mult, [pd], [mk.d()])
            yield
            pb, pd = ringF.get()
            mm(pb[:, 0:W(128)], AT, BTf[:, j * 128:j * 128 + W(128)], [BTb.d(), AR.d()], pd)
            tt("dve", x0[:, 256:384], pb[:, 0:128], masksl[:], ALU.mult, [pd], [x0.d()])
            pb2, pd2 = ringF.get()
            mm(pb2[:, 0:W(64)], BTh, identr[0:64, 0:W(64)], [BTb.d()], pd2)
            mm(pb2[:, 64:64 + W(64)], KTh, identr[0:64, 0:W(64)], [KTb.d()], pd2)
            actf(bk[:, 0:128], pb2[:, 0:128], AF.Copy, [pd2], [bk.d()])
            yield
            Xc = x0
            for jj in range(1, 7):
                Xn = x1 if Xc is x0 else x0
                pb, pd = ringF.get()
                mm(pb[:, 0:256], Xc[:, 256:384], Xc[:, 0:256], [Xc.d()], pd, True, False)
                mm(pb[:, 128:384], identr[:, 0:128], Xc[:, 128:384], [Xc.d(), identr.d()], pd, False, True)
                mm(pb[:, 256:256 + W(128)], Xc[:, 0:128], Xc[:, 256:256 + W(128)], [Xc.d()], pd)
                actf(Xn[:, 0:384], pb[:, 0:384], AF.Copy, [pd], [Xn.d()])
                if jj == 1:
                    pb3, pd3 = ringF.get()
                    mm(pb3[:, 0:W(64)], AT, identr[0:64, 0:W(64)], [AR.d()], pd3)
                    mm(pb3[:, 64:64 + W(64)], mk[:, 0:128], vpad, [mk.d(), vtok.d()], pd3)
                    actf(ax[:, 0:128], pb3[:, 0:128], AF.Copy, [pd3], [ax.d()])
                yield
                Xc = Xn
            Rfin = x1 if Xc is x0 else x0
            pb, pd = ringF.get()
            mm(pb[:, 0:256], Xc[:, 256:384], Xc[:, 0:256], [Xc.d()], pd, True, False)
            mm(pb[:, 128:384], identr[:, 0:128], Xc[:, 128:384], [Xc.d(), identr.d()], pd, False, True)
            actf(Rfin[:, 0:128], pb[:, 128:256], AF.Copy, [pd], [Rfin.d()])
            yield
            pb, pd = ringF.get()
            mm(pb[:, 0:W(128)], Rfin[:, 0:128], ax[:, 0:W(128)], [Rfin.d(), ax.d()], pd)
            actf(pq[:, 0:128], pb[:, 0:128], AF.Copy, [pd], [pq.d()])
            yield
            gt, hs, rh = GTb[slot], Hsb[slot], RhT[slot]
            pb, pd = ringF.get()
            mm(pb[0:64, 0:W(64)], pq[:, 0:64], bk[:, 0:W(64)], [pq.d(), bk.d()], pd)
            tt("dve", gt[:], pb[0:64, 0:64], identf[0:64, 0:64], ALU.add, [pd], [gt.d()])
            pb2, pd2 = ringF.get()
            mm(pb2[0:64, 0:W(64)], bk[:, 0:64], pq[:, 64:64 + W(64)], [pq.d(), bk.d()], pd2, True, False)
            mm(pb2[0:64, 0:W(64)], bk[:, 64:128], vpad, [bk.d(), vtok.d()], pd2, False, True)
            S.op("dve", lambda e: e.tensor_scalar(out=hs[:], in0=pb2[0:64, 0:64], scalar1=gamC[:, h:h + 1],
                                                  scalar2=None, op0=ALU.mult),
                 reads=[pd2, gamC.d(q)], writes=[hs.d()])
            pb3, pd3 = ringF.get()
            mm(pb3[0:64, 0:256 - HO], pq[:, 0:64], mrb[:, HO:256], [pq.d(), mrb.d()], pd3)
            tt("dve", rh[:, 128:256], pb3[0:64, 128 - HO:256 - HO], RT, ALU.add, [pd3, AR.d()], [rh.d()])
            yield

        def back(h, slot, q):
            j = h % 4
            y = QP[q % 2]["y"]
            vh = vtok[:, h * 64:(h + 1) * 64]
            mrb, mk, pq, gt, hs, rh = MRB[slot], MKb[slot], PQb[slot], GTb[slot], Hsb[slot], RhT[slot]
            pb, pd = ringF.get()
            mm(pb[0:64, 0:256 - HO], pq[:, 64:128], mrb[:, HO:256], [pq.d(), mrb.d()], pd, True, False)
            mm(pb[0:64, 0:256 - HO], vh, mk[:, HO:256], [vtok.d(), mk.d()], pd, False, False)
            mm(pb[0:64, 0:256 - HO], Sst[:, h, :], rh[:, HO:256], [Sst.d(h), rh.d()], pd, False, True)
            actf(y[:, j, :], pb[0:64, 128 - HO:256 - HO], AF.Copy, [pd], [y.d()])
            pb2, pd2 = ringF.get()
            mm(pb2[0:64, 0:W(64)], gt[:], Sf[:, h * 64:h * 64 + W(64)], [gt.d(), Sst.d(h)], pd2)
            S.op("dve", lambda e: e.scalar_tensor_tensor(
                out=Sst[:, h, :], in0=pb2[0:64, 0:64], scalar=gamC[:, h:h + 1], in1=hs[:],
                op0=ALU.mult, op1=ALU.add), reads=[pd2, gamC.d(q), hs.d()], writes=[Sst.d(h)])

        mixes = {}

        def stageA(dc):
            cur, prv = uTx[dc % 2], uTx[(dc + 1) % 2]
            S.dma("sp", xb[:], hv_in[dc], writes=[xb.d()])
            actf(xn[:], xb[:], AF.Square, [xb.d()], [xn.d(), st.d(0)], accum_out=st[:, 0:1])
            rstd_ops(0, 1)
            actf(xn[:], xb[:], AF.Copy, [xb.d(), st.d(1)], [xn.d()], scale=st[:, 1:2])
            for c in range(8):
                S.op("pe", lambda e, c=c: e.transpose(out=bankT[:, c, :], in_=xn[:, c * 128:(c + 1) * 128],
                                                      identity=identb[:]), reads=[xn.d()], writes=[bankT.d()])
            S.op("dve", lambda e: e.tensor_copy(out=cur[:, :, 1:129], in_=bankT[:, :, :]),
                 reads=[bankT.d()], writes=[cur.d()])
            S.op("dve", lambda e: e.tensor_copy(out=cur[:, :, 0:1], in_=prv[:, :, 128:129]),
                 reads=[prv.d()], writes=[cur.d()])
            tt("dve", xx[:], cur[:, :, 0:128], cur[:, :, 1:129], ALU.subtract, [cur.d()], [xx.d()])
            yield
            m3 = make_mix(3, mixb[3], cur)
            pv_, pd = ring_proj.get()
            for c in range(8):
                mm(pv_[0:64, 0:128], Wb[:, c, 3072:3136], m3[:, c, :], [m3.d()], pd, c == 0, c == 7)
            actf(th3[:], pv_[0:64, 0:128], AF.Tanh, [pd], [th3.d()])
            yield
            m4 = make_mix(4, mixb[3], cur)
            pv_, pd = ring_proj.get()
            for c in range(8):
                mm(pv_[0:64, 0:128], Wb[:, c, 3136:3200], m4[:, c, :], [m4.d()], pd, c == 0, c == 7)
            actf(p4b[:], pv_[0:64, 0:128], AF.Copy, [pd], [p4b.d()])
            yield
            m5 = make_mix(5, mixb[3], cur)
            pv_, pd = ring_proj.get()
            for c in range(8):
                mm(pv_[:, 0:128], Wb[:, c, 3200:3328], m5[:, c, :], [m5.d()], pd, c == 0, c == 7)
            for c in range(8):
                mm(pv_[0:32, 128:256], Wb[:, c, 3328:3360], m5[:, c, :], [m5.d()], pd, c == 0, c == 7)
            actf(s5a[:], pv_[:, 0:128], AF.Sigmoid, [pd], [s5a.d()])
            actf(s5b[:], pv_[0:32, 128:256], AF.Sigmoid, [pd], [s5b.d()])
            yield
            mixes[0] = make_mix(0, mixb[0], cur)
            yield
            mixes[1] = make_mix(1, mixb[1], cur)
            yield
            mixes[2] = make_mix(2, mixb[2], cur)
            yield

        def stageA2(dc):
            m2 = mixes[2]
            for half in range(2):
                bb, bbd = ring_proj.get()
                for c in range(8):
                    mm(bb, m2[:, c, :], Wb[:, c, 2048 + half * 512:2048 + (half + 1) * 512],
                       [m2.d()], bbd, c == 0, c == 7)
                actf(vtok[:, half * 512:(half + 1) * 512], bb, AF.Copy, [bbd], [vtok.d()])
                yield

        def prep(q):
            par = q % 2
            PENG = os.environ.get('RW_PENG', 'dve')
            AR, BTb, KTb, qp = ARs[par], BTs[par], KTs[par], QP[par]

            def evac_pairs(pv2, pd2, dst, eng="act"):
                src = pv2[:, 0:256].rearrange("p (a b) -> p a b", a=2)
                dv = dst[:].rearrange("p (a two) b -> p a two b", two=2)
                for half in range(2):
                    if eng == "act":
                        actf(dv[:, :, half, :], src[64 * half:64 * half + 64], AF.Copy, [pd2], [dst.d()])
                    else:
                        S.op("dve", lambda e, half=half: e.tensor_copy(out=dv[:, :, half, :],
                                                                       in_=src[64 * half:64 * half + 64]),
                             reads=[pd2], writes=[dst.d()])

            def proj4(mbuf, colbase, dst, eng="act"):
                pv2, pd2 = ring_proj.get()
                for pi in range(2):
                    pr = 2 * q + pi
                    for c in range(8):
                        mm(pv2[:, pi * 128:(pi + 1) * 128], Wb[:, c, colbase + pr * 128:colbase + (pr + 1) * 128],
                           mbuf[:, c, :], [mbuf.d()], pd2, c == 0, c == 7)
                evac_pairs(pv2, pd2, dst, eng)

            proj4(mixes[0], 0, qp["r"])
            yield
            proj4(mixes[1], 1024, Q["k"], "dve")
            yield
            proj4(mixes[2], 2048, qp["vT"])
            yield
            pv2, pd2 = ring_proj.get()
            for pi in range(2):
                pr = 2 * q + pi
                mm(pv2[:, pi * 128:(pi + 1) * 128], ww2[:, pr * 128:(pr + 1) * 128], th3[:], [th3.d()], pd2)
            evac_pairs(pv2, pd2, Q["sig"], "dve")
            tt(PENG, Q["sig"][:], Q["sig"][:], bc(HP_W0, q), ALU.add, [Q["sig"].d()], [Q["sig"].d()])
            actf(f2(Q["sig"]), f2(Q["sig"]), AF.Sigmoid, [Q["sig"].d()], [Q["sig"].d()])
            yield
            pv2, pd2 = ring_proj.get()
            for pi in range(2):
                pr = 2 * q + pi
                mm(pv2[:, pi * 128:(pi + 1) * 128], wa2[:, pr * 128:(pr + 1) * 128], p4b[:], [p4b.d()], pd2)
            evac_pairs(pv2, pd2, Q["a"], "dve")
            tt(PENG, Q["a"][:], Q["a"][:], bc(HP_A0, q), ALU.add, [Q["a"].d()], [Q["a"].d()])
            actf(f2(Q["a"]), f2(Q["a"]), AF.Sigmoid, [Q["a"].d()], [Q["a"].d()])
            yield
            pv2, pd2 = ring_proj.get()
            for pi in range(2):
                pr = 2 * q + pi
                mm(pv2[:, pi * 128:(pi + 1) * 128], wg2a[:, pr * 128:(pr + 1) * 128], s5a[:], [s5a.d()], pd2,
                   True, False)
                mm(pv2[:, pi * 128:(pi + 1) * 128], wg2b[:, pr * 128:(pr + 1) * 128], s5b[:], [s5b.d()], pd2,
                   False, True)
            evac_pairs(pv2, pd2, qp["g"])
            yield
            S.op(PENG, lambda e: e.tensor_scalar(out=f2(Q["sig"]), in0=f2(Q["sig"]), scalar1=-0.6065306597126334,
                                                  scalar2=None, op0=ALU.mult),
                 reads=[Q["sig"].d()], writes=[Q["sig"].d()])
            S.op("dve", lambda e: e.tensor_tensor_scan(out=f2(Q["cs"]), data0=scanm[:], data1=f2(Q["sig"]),
                                                       initial=0.0, op0=ALU.mult, op1=ALU.add),
                 reads=[Q["sig"].d()], writes=[Q["cs"].d()])
            tt(PENG, Q["kk"][:], Q["k"][:], bc(HP_KK, q), ALU.mult, [Q["k"].d()], [Q["kk"].d()])
            actf(f2(Q["t1r"]), f2(Q["kk"]), AF.Square, [Q["kk"].d()], [Q["t1r"].d()])
            yield
            pv2, pd2 = ring_proj.get()
            mm(pv2[0:64, :], ones64r[:], f2(Q["t1r"]), [Q["t1r"].d()], pd2)
            actf(f2(Q["t1"]), pv2[0:64, :], AF.Sqrt, [pd2], [Q["t1"].d()])
            S.op("dve", lambda e: e.tensor_scalar(out=f2(Q["t1"]), in0=f2(Q["t1"]), scalar1=1e-12, scalar2=None,
                                                  op0=ALU.max), reads=[Q["t1"].d()], writes=[Q["t1"].d()])
            S.op("dve", lambda e: e.reciprocal(out=f2(Q["t1"]), in_=f2(Q["t1"])),
                 reads=[Q["t1"].d()], writes=[Q["t1"].d()])
            tt(PENG, Q["kk"][:], Q["kk"][:], Q["t1"][:], ALU.mult, [Q["kk"].d(), Q["t1"].d()], [Q["kk"].d()])
            yield
            S.op("dve", lambda e: e.scalar_tensor_tensor(out=Q["t1"][:], in0=Q["a"][:], scalar=-1.0, in1=bc(HP_KA, q),
                                                         op0=ALU.add, op1=ALU.mult),
                 reads=[Q["a"].d()], writes=[Q["t1"].d()])
            S.op("dve", lambda e: e.scalar_tensor_tensor(out=f2(qp["kmod"]), in0=f2(Q["t1"]), scalar=1.0,
                                                         in1=f2(Q["k"]), op0=ALU.add, op1=ALU.mult),
                 reads=[Q["t1"].d(), Q["k"].d()], writes=[qp["kmod"].d()])
            actf(f2(Q["eneg"]), f2(Q["cs"]), AF.Exp, [Q["cs"].d()], [Q["eneg"].d()], scale=-1.0)
            actf(f2(Q["epos"]), f2(Q["cs"]), AF.Exp, [Q["cs"].d()], [Q["epos"].d()])
            yield
            tt(PENG, Q["eexc"][:], Q["cs"][:], Q["sig"][:], ALU.subtract, [Q["cs"].d(), Q["sig"].d()],
               [Q["eexc"].d()])
            actf(f2(Q["eexc"]), f2(Q["eexc"]), AF.Exp, [Q["eexc"].d()], [Q["eexc"].d()])
            S.op("dve", lambda e: e.tensor_copy(out=gamC[:, 4 * q:4 * q + 4], in_=Q["epos"][:, :, 127]),
                 reads=[Q["epos"].d()], writes=[gamC.d(q)])
            yield
            S.op("dve", lambda e: e.scalar_tensor_tensor(out=AR[:, :, 0, :], in0=Q["kk"][:], scalar=-1.0,
                                                         in1=Q["eexc"][:], op0=ALU.mult, op1=ALU.mult),
                 reads=[Q["kk"].d(), Q["eexc"].d()], writes=[AR.d()])
            tt(PENG, AR[:, :, 1, :], qp["r"][:], Q["epos"][:], ALU.mult, [qp["r"].d(), Q["epos"].d()], [AR.d()])
            yield
            tt(PENG, Q["t1"][:], Q["kk"][:], Q["a"][:], ALU.mult, [Q["kk"].d(), Q["a"].d()], [Q["t1"].d()])
            tt(PENG, BTb[:, 0:4, :], Q["t1"][:], Q["eneg"][:], ALU.mult, [Q["t1"].d(), Q["eneg"].d()], [BTb.d()])
            tt(PENG, KTb[:, 0:4, :], qp["kmod"][:], Q["eneg"][:], ALU.mult, [qp["kmod"].d(), Q["eneg"].d()],
               [KTb.d()])
            yield

        def gn(q, dc):
            qp = QP[q % 2]
            y, t1, t1r = qp["y"], Q["gt1"], Q["gt1r"]
            heads = [4 * q + j for j in range(4)]
            actf(f2(t1r), f2(y), AF.Copy, [y.d()], [t1r.d()])
            pv2, pd2 = ring_proj.get()
            mm(pv2[0:64, :], ones64r[:], f2(t1r), [t1r.d()], pd2)
            S.op("dve", lambda e: e.tensor_scalar(out=f2(t1), in0=pv2[0:64, :], scalar1=1.0 / 64, scalar2=None,
                                                  op0=ALU.mult), reads=[pd2], writes=[t1.d()])
            tt("dve", y[:], y[:], t1[:], ALU.subtract, [y.d(), t1.d()], [y.d()])
            actf(f2(t1r), f2(y), AF.Square, [y.d()], [t1r.d()])
            yield
            pv2, pd2 = ring_proj.get()
            mm(pv2[0:64, :], ones64r[:], f2(t1r), [t1r.d()], pd2)
            S.op("dve", lambda e: e.tensor_scalar(out=f2(t1), in0=pv2[0:64, :], scalar1=1.0 / 64,
                                                  scalar2=GN_EPS, op0=ALU.mult, op1=ALU.add),
                 reads=[pd2], writes=[t1.d()])
            actf(f2(t1), f2(t1), AF.Sqrt, [t1.d()], [t1.d()])
            S.op("dve", lambda e: e.reciprocal(out=f2(t1), in_=f2(t1)), reads=[t1.d()], writes=[t1.d()])
            tt("dve", y[:], y[:], t1[:], ALU.mult, [y.d(), t1.d()], [y.d()])
            yield
            tt("pool", y[:], y[:], bc(HP_GNW, q), ALU.mult, [y.d()], [y.d()])
            tt("pool", y[:], y[:], bc(HP_GNB, q), ALU.add, [y.d()], [y.d()])
            tt("pool", t1[:], qp["r"][:], qp["kmod"][:], ALU.mult, [qp["r"].d(), qp["kmod"].d()], [t1.d()])
            tt("dve", t1r[:], t1[:], bc(HP_RK, q), ALU.mult, [t1.d()], [t1r.d()])
            yield
            pv2, pd2 = ring_proj.get()
            mm(pv2[0:64, :], ones64r[:], f2(t1r), [t1r.d()], pd2)
            tt("dve", t1[:], pv2[0:64, :].rearrange("p (a b) -> p a b", a=4), qp["vT"][:], ALU.mult,
               [pd2, qp["vT"].d()], [t1.d()])
            tt("pool", y[:], y[:], t1[:], ALU.add, [y.d(), t1.d()], [y.d()])
            yield
            for j, h in enumerate(heads):
                po = 64 * (h % 2)
                tt("dve", yfin[po:po + 64, h // 2, :], y[:, j, :], qp["g"][:, j, :], ALU.mult,
                   [y.d(), qp["g"].d()], [yfin.d()])
            if q == 3:
                S.dma("pool", yT_v[:, :, dc * 128:(dc + 1) * 128], yfin[:, :, :], reads=[yfin.d()])
            yield

        def chain(*gens):
            for g_ in gens:
                if g_ is not None:
                    yield from g_

        XSTEP = int(os.environ.get("RW_XSTEP", "1"))

        def run(gens, extra=None):
            alive = [(g_, 1) for g_ in gens if g_ is not None]
            if extra is not None:
                if os.environ.get("RW_XFIRST"):
                    alive.insert(0, (extra, XSTEP))
                else:
                    alive.append((extra, XSTEP))
            while alive:
                nxt = []
                for g_, k_ in alive:
                    ok = True
                    for _ in range(k_):
                        try:
                            next(g_)
                        except StopIteration:
                            ok = False
                            break
                    if ok:
                        nxt.append((g_, k_))
                alive = nxt

        run([chain(stageA(0), stageA2(0), prep(0))])
        gn_prev = None
        for dc in range(ndc):
            more = dc + 1 < ndc
            for q in range(4):
                heads = [4 * q + j for j in range(4)]
                if q < 3:
                    extra = chain(gn(q - 1, dc) if q > 0 else None, prep(q + 1))
                else:
                    extra = chain(gn(2, dc), stageA(dc + 1) if more else None, prep(0) if more else None)
                run([front(h, j, q) for j, h in enumerate(heads)], extra)
                for j, h in enumerate(heads):
                    back(h, j, q)
            gn_prev = gn(3, dc)
            run([stageA2(dc + 1) if more else None, gn_prev])
            gn_prev = None
        S.barrier()
    outproj_phase(S, h_in, h_out, yT_dram, P["w_out"], P["gpost_b"], tag + "o", ntile=ndc)


def rwkv_consts():
    import ml_dtypes
    i = np.arange(128)
    su = (i[:, None] < i[None, :]).astype(np.float32)
    u = (i[:, None] <= i[None, :]).astype(np.float32)
    scanm = np.ones((64, 512), np.float32)
    scanm[:, ::128] = 0.0
    return {
        "identb": np.eye(128, dtype=np.float32).astype(ml_dtypes.bfloat16),
        "identf": np.eye(128, dtype=np.float32),
        "ones64": np.ones((64, 64), np.float32),
        "mask2": np.ascontiguousarray(np.concatenate([su, u], axis=1)),
        "masksl": np.ascontiguousarray(su.T),
        "scanm": scanm,
    }


def _pl(v):
    return np.ascontiguousarray(np.asarray(v, np.float32).reshape(8, 128).T)


def _bcast(v):
    return np.ascontiguousarray(np.broadcast_to(np.asarray(v, np.float32), (128, D)))


def rwkv_host_params(mix_norm_pre, mix_norm_post, mu, w0, a0, k_k, k_a, r_k, gn_w, gn_b):
    hp = np.stack([np.asarray(t, np.float32).reshape(16, 64) for t in
                   (w0, a0, k_k, k_a, r_k, gn_w, gn_b)], axis=0)
    return {
        "gpre_l": _pl(mix_norm_pre), "gpost_b": _bcast(mix_norm_post),
        "mu_l": np.ascontiguousarray(np.asarray(mu, np.float32).reshape(6, 8, 128).transpose(2, 0, 1)),
        "hp": np.ascontiguousarray(hp.transpose(2, 0, 1)),
    }


def outproj_phase(S, h_in, h_out, attnT, w_out, gpost_b, tag, ntile=32):
    P = {"w_out": w_out, "gpost_b": gpost_b}

    def mmf(out, lhsT, rhs, reads, wdep_, start=True, stop=True):
        S.op("pe", lambda e: e.matmul(out, lhsT, rhs, start=start, stop=stop), reads=reads, writes=[wdep_])
    with contextlib.ExitStack() as es:
        def SB(name, shape, dt):
            return sb(S, es, tag + name, shape, dt)
        wo = SB("wo", [128, 8, D], BF16)
        gpost = SB("gpost", [128, D], F32)
        stg = [SB(f"stg{i}", [128, 1024], F32) for i in range(2)]
        G = 4 if ntile % 4 == 0 else 1
        aT = [SB(f"aT{i}", [128, 8, 128 * G], BF16) for i in range(2)]
        xb = [SB(f"xb{i}", [128, D], F32) for i in range(2)]
        tmp = [SB(f"tmp{i}", [128, D], F32) for i in range(2)]
        junk = SB("junk", [128, 512], BF16)
        st = SB("st", [128, 8], F32)
        bk = [ps(S, es, tag + f"bk{i}", [128, 512], F32) for i in range(2)]
        S.dma("sp", gpost[:], P["gpost_b"], writes=[gpost.d()])
        for c in range(8):
            b = stg[c % 2]
            S.dma("sp", b[:, :], P["w_out"][c * 128:(c + 1) * 128, :], writes=[b.d()])
            S.op("act", lambda e, c=c, b=b: e.activation(out=wo[:, c, :], in_=b[:, :], func=AF.Copy), reads=[b.d()])
        S.barrier()
        a_v = attnT.rearrange("(c p) t -> p c t", p=128)
        hv_in = h_in.rearrange("(t p) d -> t p d", p=128)
        hv_out = h_out.rearrange("(t p) d -> t p d", p=128)
        for t in range(ntile):
            a, x, tm = aT[(t // G) % 2], xb[t % 2], tmp[t % 2]
            so = (t % G) * 128
            if t % G == 0:
                S.dma("sp", a[:, :, :], a_v[:, :, t * 128:(t + G) * 128], writes=[a.d()])
            S.dma("sp", x[:], hv_in[t], writes=[x.d()])
            for half in range(2):
                b = bk[half]
                for c in range(8):
                    mmf(b[:, :], a[:, c, so:so + 128], wo[:, c, half * 512:(half + 1) * 512], [a.d()], b.d(),
                        c == 0, c == 7)
                S.op("act", lambda e, b=b, half=half: e.activation(out=junk[:], in_=b[:, :], func=AF.Square,
                                                                   accum_out=st[:, half:half + 1]),
                     reads=[b.d()], writes=[junk.d(), st.d(half), b.d("port")])
                S.op("dve", lambda e, b=b, half=half, tm=tm: e.tensor_tensor(
                    out=tm[:, half * 512:(half + 1) * 512], in0=b[:, :], in1=gpost[:, half * 512:(half + 1) * 512],
                    op=ALU.mult), reads=[b.d()], writes=[tm.d(), b.d("port")])
            S.op("dve", lambda e: e.tensor_tensor(out=st[:, 2:3], in0=st[:, 0:1], in1=st[:, 1:2], op=ALU.add),
                 reads=[st.d(0), st.d(1)], writes=[st.d(2)])
            S.op("dve", lambda e: e.tensor_scalar(out=st[:, 2:3], in0=st[:, 2:3], scalar1=1.0 / D, scalar2=RMS_EPS,
                                                  op0=ALU.mult, op1=ALU.add), reads=[st.d(2)], writes=[st.d(2)])
            S.op("act", lambda e: e.activation(out=st[:, 2:3], in_=st[:, 2:3], func=AF.Sqrt),
                 reads=[st.d(2)], writes=[st.d(2)])
            S.op("dve", lambda e: e.reciprocal(out=st[:, 2:3], in_=st[:, 2:3]), reads=[st.d(2)], writes=[st.d(2)])
            S.op("dve", lambda e, tm=tm, x=x: e.scalar_tensor_tensor(out=tm[:], in0=tm[:], scalar=st[:, 2:3], in1=x[:],
                                                                    op0=ALU.mult, op1=ALU.add),
                 reads=[tm.d(), st.d(2), x.d()], writes=[tm.d()])
            S.dma("pool", hv_out[t], tm[:], reads=[tm.d()])
        S.barrier()


NSA_IN = 2608
NEG = -30000.0


def nsa_consts():
    import ml_dtypes
    bf = ml_dtypes.bfloat16
    slopes = (2.0 ** (-8.0 * np.arange(1, 17, dtype=np.float64) / 16)).astype(np.float32)
    t = np.arange(4096, dtype=np.float64)
    aq = np.zeros((16, 3, 4096), dtype=bf)
    for h in range(16):
        v = (-slopes[h].astype(np.float64) * t).astype(np.float32)
        r = v.copy()
        for k in range(3):
            p = r.astype(bf)
            aq[h, k] = p
            r = (r - p.astype(np.float32)).astype(np.float32)
    j = np.arange(128)
    i = np.arange(512)
    bias_key = np.zeros((128, 16, 32), np.float32)
    bias_cmp = np.zeros((128, 16, 2), np.float32)
    for h in range(16):
        for kt in range(32):
            bias_key[:, h, kt] = slopes[h] * (128 * kt + j)
        for nt in range(2):
            bias_cmp[:, h, nt] = slopes[h] * (16 * (128 * nt + j) + 31)
    cmpmask = np.zeros((128, 8, 512), np.float32)
    for idx in range(8):
        cmpmask[:, idx, :] = np.where(16 * j[:, None] + 31 - i[None, :] <= 512 * idx, 0.0, NEG)
    causal = np.zeros((128, 4, 512), np.float32)
    for r_ in range(4):
        causal[:, r_, :] = np.where(j[:, None] + 128 * r_ <= i[None, :], 0.0, NEG)
    win = np.zeros((128, 8, 512), np.float32)
    for w in range(8):
        dist = i[None, :] - j[:, None] - 128 * (w - 4)
        win[:, w, :] = np.where((dist >= 0) & (dist < 512), 0.0, NEG)
    E = np.zeros((64, 4096), np.float32)
    E[np.arange(4096) // 64, np.arange(4096)] = 1.0
    cs_ = np.arange(255) * 16
    bs_ = np.arange(64) * 64
    ov = np.clip(np.minimum(cs_[:, None] + 32, bs_[None, :] + 64) - np.maximum(cs_[:, None], bs_[None, :]), 0, None) / 32.0
    Caug = np.zeros((256, 65), np.float32)
    Caug[:255, :64] = ov
    Caug[:255, 64] = 1.0
    Caug = Caug.reshape(2, 128, 65).transpose(1, 0, 2)
    vis = np.zeros((128, 32, 64), np.float32)
    add = np.zeros((128, 32, 64), np.float32)
    s = np.arange(64)
    for it in range(32):
        tq = 128 * it + j
        cur = tq // 64
        visible = s[None, :] * 64 <= tq[:, None]
        a = np.where(visible, 0.0, -1.0)
        v = visible.astype(np.float32)
        for (cond, val) in [(s[None, :] == 0, 1e4), (s[None, :] == cur[:, None], 2e4),
                            (s[None, :] == cur[:, None] - 1, 3e4)]:
            a = np.where(cond, val, a)
            v = np.where(cond, 0.0, v)
        vis[:, it, :] = v
        add[:, it, :] = a
    return {
        "alibi_q": aq, "bias_key": bias_key, "bias_cmp": bias_cmp,
        "cmpmask": cmpmask.astype(bf), "causal": causal.astype(bf), "winmask": win.astype(bf),
        "E_all": np.concatenate([E[1:64], np.ones((1, 4096), np.float32)], axis=0).astype(bf), "C_aug": np.ascontiguousarray(Caug).astype(bf),
        "vis": vis, "addt": add,
        "identb": np.eye(128, dtype=np.float32).astype(bf),
    }


def nsa_host_params(mix_norm_pre, mix_norm_post, pe_k, w_ck1, pe_v, w_cv1):
    return {
        "gpre_l": _pl(mix_norm_pre), "gpost_b": _bcast(mix_norm_post),
        "pekT": np.ascontiguousarray(np.asarray(pe_k, np.float32).T),
        "pevT": np.ascontiguousarray(np.asarray(pe_v, np.float32).T),
        "wck1_l": np.ascontiguousarray(np.asarray(w_ck1, np.float32).transpose(1, 0, 2)),
        "wcv1_l": np.ascontiguousarray(np.asarray(w_cv1, np.float32).transpose(1, 0, 2)),
    }


def nsa_phase(S, h_in, h_out, P, C, scr, tag="n"):
    nc = S.nc
    projT, vtokd, gTd, attnT = scr["projT"], scr["vtok"], scr["gT"], scr["attnT"]

    def mmf(out, lhsT, rhs, reads, wdep_, start=True, stop=True):
        S.op("pe", lambda e: e.matmul(out, lhsT, rhs, start=start, stop=stop), reads=reads, writes=[wdep_])

    with contextlib.ExitStack() as es:
        def SB(name, shape, dt):
            return sb(S, es, tag + "1" + name, shape, dt)
        Wb = SB("Wb", [128, 8, NSA_IN], BF16)
        gpre = SB("gpre", [128, 8], F32)
        identb = SB("identb", [128, 128], BF16)
        stg = [SB(f"stg{i}", [128, 1024], F32) for i in range(2)]
        xb = [SB(f"xb{i}", [128, 4, D], F32) for i in range(2)]
        xn = [SB(f"xn{i}", [128, D], BF16) for i in range(2)]
        junk = SB("junk", [128, D], BF16)
        uT = [SB(f"uT{i}", [128, 8, 512], BF16) for i in range(2)]
        ev = [SB(f"ev{i}", [128, 512], BF16) for i in range(4)]
        evf = [SB(f"evf{i}", [48, 512], F32) for i in range(2)]
        evt = [SB(f"evt{i}", [128, 512], BF16) for i in range(2)]
        st = SB("st", [128, 8], F32)
        bankT = ps(S, es, tag + "1bT", [128, 8, 128], BF16)
        banks = [ps(S, es, tag + f"1bk{i}", [128, 512], F32) for i in range(4)]
        S.dma("sp", gpre[:], P["gpre_l"], writes=[gpre.d()])
        S.dma("sp", identb[:], C["identb"], writes=[identb.d()])
        k = 0
        for c in range(8):
            for o in range(0, NSA_IN, 1024):
                wdt = min(1024, NSA_IN - o)
                b = stg[k % 2]
                S.dma("sp", b[:, 0:wdt], P["w_in"][c * 128:(c + 1) * 128, o:o + wdt], writes=[b.d()])
                S.op("act", lambda e, c=c, o=o, wdt=wdt, b=b: e.activation(
                    out=Wb[:, c, o:o + wdt], in_=b[:, 0:wdt], func=AF.Copy, scale=gpre[:, c:c + 1]),
                    reads=[b.d(), gpre.d()])
                k += 1
        S.barrier()
        hv_in = h_in.rearrange("(t s p) d -> t p s d", p=128, s=4)
        chunks = [(o, 128) for o in range(0, 2560, 128)] + [(2560, 48)]
        ke = 0
        for t in range(8):
            x, u = xb[t % 2], uT[t % 2]
            S.dma("sp", x[:, :, :], hv_in[t], writes=[x.d()])
            for s in range(4):
                xnb = xn[s % 2]
                S.op("act", lambda e, x=x, s=s: e.activation(out=junk[:], in_=x[:, s, :], func=AF.Square,
                                                             accum_out=st[:, s:s + 1]),
                     reads=[x.d()], writes=[junk.d(), st.d(s)])
                S.op("dve", lambda e, s=s: e.tensor_scalar(out=st[:, 4 + s:5 + s], in0=st[:, s:s + 1],
                                                           scalar1=1.0 / D, scalar2=RMS_EPS, op0=ALU.mult,
                                                           op1=ALU.add), reads=[st.d(s)], writes=[st.d(4 + s)])
                S.op("act", lambda e, s=s: e.activation(out=st[:, 4 + s:5 + s], in_=st[:, 4 + s:5 + s], func=AF.Sqrt),
                     reads=[st.d(4 + s)], writes=[st.d(4 + s)])
                S.op("dve", lambda e, s=s: e.reciprocal(out=st[:, 4 + s:5 + s], in_=st[:, 4 + s:5 + s]),
                     reads=[st.d(4 + s)], writes=[st.d(4 + s)])
                S.op("act", lambda e, x=x, s=s, xnb=xnb: e.activation(out=xnb[:], in_=x[:, s, :], func=AF.Copy,
                                                                      scale=st[:, 4 + s:5 + s]),
                     reads=[x.d(), st.d(4 + s)], writes=[xnb.d()])
                for c in range(8):
                    S.op("pe", lambda e, c=c, xnb=xnb: e.transpose(out=bankT[:, c, :],
                                                                   in_=xnb[:, c * 128:(c + 1) * 128],
                                                                   identity=identb[:]),
                         reads=[xnb.d()], writes=[bankT.d()])
                S.op("dve", lambda e, s=s, u=u: e.tensor_copy(out=u[:, :, s * 128:(s + 1) * 128], in_=bankT[:, :, :]),
                     reads=[bankT.d()], writes=[u.d()])
            for (o, m) in chunks:
                bk = banks[ke % 3]
                for c in range(8):
                    mmf(bk[0:m, :], Wb[:, c, o:o + m], u[:, c, :], [u.d()], bk.d(), c == 0, c == 7)
                if o == 2560:
                    e_ = evf[ke % 2]
                    S.op("act", lambda e, bk=bk, e_=e_: e.activation(out=e_[:], in_=bk[0:48, :], func=AF.Sigmoid),
                         reads=[bk.d()], writes=[e_.d()])
                    S.dma("pool", gTd[:, t * 512:(t + 1) * 512], e_[:], reads=[e_.d()])
                else:
                    e_ = ev[ke % 4]
                    sc = 0.125 if o < 1024 else 1.0
                    if ke % 2 == 0:
                        S.op("act", lambda e, bk=bk, e_=e_, sc=sc: e.activation(out=e_[:], in_=bk[:, :], func=AF.Copy,
                                                                               scale=sc),
                             reads=[bk.d()], writes=[e_.d()])
                    else:
                        S.op("dve", lambda e, bk=bk, e_=e_, sc=sc: e.tensor_scalar(out=e_[:], in0=bk[:, :], scalar1=sc,
                                                                                  scalar2=None, op0=ALU.mult),
                             reads=[bk.d()], writes=[e_.d()])
                    S.dma("pool", projT[o:o + 128, t * 512:(t + 1) * 512], e_[:], reads=[e_.d()])
                ke += 1
            for s in range(4):
                bk = banks[3]
                for (jj, o) in enumerate((1792, 2304)):
                    for c in range(8):
                        mmf(bk[:, jj * 256:(jj + 1) * 256], u[:, c, s * 128:(s + 1) * 128], Wb[:, c, o:o + 256],
                            [u.d()], bk.d(), c == 0, c == 7)
                e_ = evt[s % 2]
                S.op("act", lambda e, bk=bk, e_=e_: e.activation(out=e_[:], in_=bk[:, :], func=AF.Copy),
                     reads=[bk.d()], writes=[e_.d()])
                S.dma("pool", vtokd[t * 512 + s * 128:t * 512 + (s + 1) * 128, :], e_[:], reads=[e_.d()])
        S.barrier()

    with contextlib.ExitStack() as es:
        def SB(name, shape, dt):
            return sb(S, es, tag + "2" + name, shape, dt)
        ksA = SB("ksA", [128, 4096], BF16)
        kwA = SB("kwA", [128, 4096], BF16)
        kcT = SB("kcT", [64, 4096], BF16)
        vcT = SB("vcT", [64, 4096], BF16)
        vsA = SB("vsA", [128, 32, 128], BF16)
        vwA = SB("vwA", [128, 32, 128], BF16)
        qA = [SB(f"qA{i}", [128, 4096], BF16) for i in range(4)]
        kcmpA = SB("kcmpA", [128, 256], BF16)
        vcmpA = SB("vcmpA", [128, 2, 128], BF16)
        bias_key = SB("bias_key", [128, 16, 32], F32)
        bias_cmp = SB("bias_cmp", [128, 16, 2], F32)
        cmpmask = SB("cmpmask", [128, 8, 512], BF16)
        causal = SB("causal", [128, 4, 512], BF16)
        winmask = SB("winmask", [128, 8, 512], BF16)
        C_aug = SB("C_aug", [128, 2, 65], BF16)
        vis = SB("vis", [128, 32, 64], F32)
        addt = SB("addt", [128, 32, 64], F32)
        identb = SB("identb", [128, 128], BF16)
        w1 = [SB(f"w1_{i}", [64, 32, 64], BF16) for i in range(2)]
        w2 = [SB(f"w2_{i}", [64, 64], BF16) for i in range(2)]
        peT = [SB(f"peT{i}", [64, 32], BF16) for i in range(2)]
        cbias = SB("cbias", [64, 2], F32)
        stgw = SB("stgw", [64, 2048], F32)
        hid = [SB(f"hid{i}", [64, 256], BF16) for i in range(2)]
        eTs = [SB(f"eT{i}", [128, 512], BF16) for i in range(8)]
        imp_acc = SB("imp_acc", [128, 4, 64], F32)
        imp2 = SB("imp2", [128, 64], F32)
        imp3 = SB("imp3", [128, 64], F32)
        m8 = SB("m8", [128, 16], F32)
        mk = SB("mk", [128, 64], F32)
        negq = SB("negq", [128, 64], BF16)
        rq = SB("rq", [128, 4], F32)
        gbc = [SB(f"gbc{i}", [64, 3, 512], F32) for i in range(4)]
        acc = SB("acc", [64, 512], F32)
        rd = SB("rd", [64, 512], F32)
        tb = SB("tb", [64, 512], F32)
        outb = [SB(f"outb{i}", [64, 512], BF16) for i in range(2)]
        cacc = [SB(f"cacc{i}", [64, 512], F32) for i in range(4)]
        bsc = [ps(S, es, tag + f"2sc{i}", [128, 512], F32) for i in range(4)]
        bpv = [ps(S, es, tag + f"2pv{i}", [128, 512], F32) for i in range(2)]
        bimp = ps(S, es, tag + "2imp", [128, 512], F32)
        btr = ps(S, es, tag + "2tr", [128, 512], BF16)
        bcm = bimp
        ring_sc = Ring([(b, b.d()) for b in bsc])
        ring_pv = Ring([(b, b.d()) for b in bpv])
        ring_e = Ring([(b, b.d()) for b in eTs])

        for (t_, nm) in [(bias_key, "bias_key"), (bias_cmp, "bias_cmp"), (cmpmask, "cmpmask"), (causal, "causal"),
                         (winmask, "winmask"), (C_aug, "C_aug"), (vis, "vis"), (addt, "addt"),
                         (identb, "identb")]:
            S.dma("sp", t_[:], C[nm], writes=[t_.d()])
        for bi, (w1n, w2n, pen) in enumerate([("wck1_l", "w_ck2", "pekT"), ("wcv1_l", "w_cv2", "pevT")]):
            S.dma("sp", stgw[:, :].rearrange("p (l c) -> p l c", l=32), P[w1n], writes=[stgw.d()])
            S.op("dve", lambda e, bi=bi: e.tensor_copy(out=w1[bi][:].rearrange("p l c -> p (l c)"), in_=stgw[:, :]),
                 reads=[stgw.d()], writes=[w1[bi].d()])
            S.dma("sp", stgw[:, 0:64], P[w2n], writes=[stgw.d()])
            S.op("dve", lambda e, bi=bi: e.tensor_copy(out=w2[bi][:], in_=stgw[:, 0:64]),
                 reads=[stgw.d()], writes=[w2[bi].d()])
            S.dma("sp", stgw[:, 0:32], P[pen], writes=[stgw.d()])
            S.op("dve", lambda e, bi=bi: e.tensor_copy(out=peT[bi][:], in_=stgw[:, 0:32]),
                 reads=[stgw.d()], writes=[peT[bi].d()])
            for l in range(32):
                mmf(bcm[0:64, 0:1], w1[bi][:, l, :], peT[bi][:, l:l + 1], [w1[bi].d(), peT[bi].d()], bcm.d(),
                    l == 0, l == 31)
            S.op("dve", lambda e, bi=bi: e.tensor_copy(out=cbias[:, bi:bi + 1], in_=bcm[0:64, 0:1]),
                 reads=[bcm.d()], writes=[cbias.d()])
        for t_ in (kwA, kcmpA) + tuple(qA):
            S.op("dve", lambda e, t_=t_: e.memset(t_[64:128, :], 0.0), writes=[t_.d()])
        S.dma("sp", ksA[64:128, :], C["E_all"], writes=[ksA.d()])
        S.dma("sp", kwA[127:128, :], C["E_all"][63:64, :], writes=[kwA.d()])
        S.dma("sp", kcmpA[127:128, :], C["E_all"][63:64, 0:256], writes=[kcmpA.d()])
        S.op("dve", lambda e: e.memset(negq[:], 0.0), writes=[negq.d()])
        S.op("dve", lambda e: e.memset(vcmpA[:, :, 0:64], 0.0), writes=[vcmpA.d()])
        for t_ in (vsA, vwA, vcmpA):
            S.op("dve", lambda e, t_=t_: e.memset(t_[:, :, 64:128], 1.0), writes=[t_.d()])
        S.op("dve", lambda e: e.memset(kcmpA[0:64, 255:256], 0.0), writes=[kcmpA.d()])
        vt_v = vtokd.rearrange("(kt p) d -> p kt d", p=128)
        pend = []
        import os
        LAG = int(os.environ.get('NSA_LAG', '4'))
        S.barrier()

        for g in range(4):
            S.dma("sp", ksA[0:64, :], projT[1536 + g * 64:1536 + (g + 1) * 64, :], writes=[ksA.d()])
            S.dma("sp", kwA[0:64, :], projT[2048 + g * 64:2048 + (g + 1) * 64, :], writes=[kwA.d()])
            S.dma("sp", kcT[:, :], projT[1024 + g * 64:1024 + (g + 1) * 64, :], writes=[kcT.d()])
            S.dma("sp", vcT[:, :], projT[1280 + g * 64:1280 + (g + 1) * 64, :], writes=[vcT.d()])
            S.dma("sp", vsA[:, :, 0:64], vt_v[:, :, g * 64:(g + 1) * 64], writes=[vsA.d()])
            S.dma("sp", vwA[:, :, 0:64], vt_v[:, :, 256 + g * 64:256 + (g + 1) * 64], writes=[vwA.d()])
            for r in range(4):
                h = 4 * g + r
                S.dma("sp", qA[r][0:64, :], projT[h * 64:(h + 1) * 64, :], writes=[qA[r].d()])
                S.dma("sp", qA[r][127:128, :], C["alibi_q"][h, 0:1, :], writes=[qA[r].d()])
            for bi, src in enumerate((kcT, vcT)):
                for l in range(32):
                    mmf(bcm[0:64, 0:255], w1[bi][:, l, :], src[:, l:l + 16 * 254 + 1:16], [src.d()], bcm.d(),
                        l == 0, l == 31)
                S.op("act", lambda e, bi=bi: e.activation(out=hid[bi][:, 0:255], in_=bcm[0:64, 0:255], func=AF.Silu,
                                                          bias=cbias[:, bi:bi + 1]),
                     reads=[bcm.d(), cbias.d()], writes=[hid[bi].d()])
            mmf(bcm[0:64, 0:255], w2[0][:], hid[0][:, 0:255], [hid[0].d()], bcm.d())
            S.op("dve", lambda e: e.tensor_copy(out=kcmpA[0:64, 0:255], in_=bcm[0:64, 0:255]),
                 reads=[bcm.d()], writes=[kcmpA.d()])
            for nt in range(2):
                nn = 128 if nt == 0 else 127
                mmf(bcm[0:nn, 256 + nt * 64:256 + (nt + 1) * 64], hid[1][:, nt * 128:nt * 128 + nn], w2[1][:],
                    [hid[1].d()], bcm.d())
                S.op("dve", lambda e, nt=nt, nn=nn: e.tensor_copy(out=vcmpA[0:nn, nt, 0:64],
                                                                 in_=bcm[0:nn, 256 + nt * 64:256 + (nt + 1) * 64]),
                     reads=[bcm.d()], writes=[vcmpA.d()])

            def push_branch(tiles, qc, done_cb):
                pvb, pvd = ring_pv.get()
                n = len(tiles)
                ets = []
                for ti, (kT, bias, masks, vT, rds) in enumerate(tiles):
                    sc, scd = ring_sc.get()
                    mmf(sc[:, :], kT, qc, rds, scd, True, len(masks) == 0)
                    for mi, (ml, mr, mrd) in enumerate(masks):
                        mmf(sc[:, :], ml, mr, mrd, scd, False, mi == len(masks) - 1)
                    eT, eTd = ring_e.get()
                    S.op("act", lambda e, sc=sc, eT=eT, bias=bias: e.activation(out=eT[:], in_=sc[:, :], func=AF.Exp,
                                                                               bias=bias),
                         reads=[scd], writes=[eTd])
                    ets.append((eT, eTd))
                    flush(LAG - 1)
                    last = ti == n - 1
                    pend.append((pvb, pvd, vT, eT, [eTd] + list(rds), ti == 0, last,
                                 (lambda: done_cb(pvb, pvd, ets)) if last else None))

            def flush(keep=0):
                while len(pend) > keep:
                    pvb, pvd, vT, eT, prds, first, last, cb = pend.pop(0)
                    mmf(pvb[:, :], vT, eT[:], prds, pvd, first, last)
                    if cb is not None:
                        cb()

            def combine(pvb, pvd, gb, b, accb, first):
                if first:
                    S.op("dve", lambda e: e.tensor_scalar(out=rd[:], in0=pvb[64:128, :], scalar1=1e-30, scalar2=None,
                                                          op0=ALU.add), reads=[pvd], writes=[rd.d()])
                    S.op("dve", lambda e: e.reciprocal(out=rd[:], in_=rd[:]), reads=[rd.d()], writes=[rd.d()])
                else:
                    S.op("dve", lambda e: e.reciprocal(out=rd[:], in_=pvb[64:128, :]), reads=[pvd], writes=[rd.d()])
                S.op("dve", lambda e: e.tensor_tensor(out=tb[:], in0=pvb[0:64, :], in1=rd[:], op=ALU.mult),
                     reads=[pvd, rd.d()], writes=[tb.d()])
                if first:
                    S.op("pool", lambda e: e.tensor_tensor(out=accb[:], in0=tb[:], in1=gb[:, b, :], op=ALU.mult),
                         reads=[tb.d(), gb.d()], writes=[accb.d()])
                else:
                    S.op("pool", lambda e: e.tensor_tensor(out=tb[:], in0=tb[:], in1=gb[:, b, :], op=ALU.mult),
                         reads=[tb.d(), gb.d()], writes=[tb.d()])
                    S.op("pool", lambda e: e.tensor_tensor(out=accb[:], in0=accb[:], in1=tb[:], op=ALU.add),
                         reads=[tb.d(), accb.d()], writes=[accb.d()])

            for c in range(8):
                cs_ = slice(c * 512, (c + 1) * 512)
                nts = [0] if c < 4 else [0, 1]
                for r in range(4):
                    h = 4 * g + r
                    tiles = [(kcmpA[:, nt * 128:(nt + 1) * 128], bias_cmp[:, h, nt:nt + 1],
                              [(identb[:], cmpmask[:, c - 4 * nt, :], [])], vcmpA[:, nt, :],
                              [kcmpA.d(), qA[r].d(), vcmpA.d()]) for nt in nts]

                    def cmp_done(pvb, pvd, ets, r=r, h=h, nts=nts, cs_=cs_):
                        for qs in range(4):
                            for ni, nt in enumerate(nts):
                                eT, eTd = ets[ni]
                                mmf(bimp[:, qs * 65:(qs + 1) * 65], eT[:, qs * 128:(qs + 1) * 128], C_aug[:, nt, :],
                                    [eTd], bimp.d(), ni == 0, ni == len(nts) - 1)
                        S.op("dve", lambda e: e.tensor_scalar(
                            out=rq[:], in0=bimp[:, 0:260].rearrange("p (a b) -> p a b", a=4)[:, :, 64], scalar1=1e-30,
                            scalar2=None, op0=ALU.add), reads=[bimp.d()], writes=[rq.d()])
                        S.op("dve", lambda e: e.reciprocal(out=rq[:], in_=rq[:]), reads=[rq.d()], writes=[rq.d()])
                        for qs in range(4):
                            if r == 0:
                                S.op("dve", lambda e, qs=qs: e.tensor_scalar(
                                    out=imp_acc[:, qs, :], in0=bimp[:, qs * 65:qs * 65 + 64], scalar1=rq[:, qs:qs + 1],
                                    scalar2=None, op0=ALU.mult), reads=[bimp.d(), rq.d()], writes=[imp_acc.d()])
                            else:
                                S.op("dve", lambda e, qs=qs: e.scalar_tensor_tensor(
                                    out=imp_acc[:, qs, :], in0=bimp[:, qs * 65:qs * 65 + 64], scalar=rq[:, qs:qs + 1],
                                    in1=imp_acc[:, qs, :], op0=ALU.mult, op1=ALU.add),
                                    reads=[bimp.d(), rq.d(), imp_acc.d()], writes=[imp_acc.d()])
                        gb = gbc[r]
                        S.dma("sp", gb[:], gTd[3 * h:3 * h + 3, cs_].partition_broadcast(64), writes=[gb.d()])
                        combine(pvb, pvd, gb, 0, cacc[r], True)

                    push_branch(tiles, qA[r][:, cs_], cmp_done)
                flush()
                for qs in range(4):
                    it = 4 * c + qs
                    S.op("dve", lambda e, qs=qs, it=it: e.tensor_tensor(out=imp2[:], in0=imp_acc[:, qs, :],
                                                                      in1=vis[:, it, :], op=ALU.mult),
                         reads=[imp_acc.d()], writes=[imp2.d()])
                    S.op("dve", lambda e, it=it: e.tensor_tensor(out=imp2[:], in0=imp2[:], in1=addt[:, it, :],
                                                                 op=ALU.add), reads=[imp2.d()], writes=[imp2.d()])
                    S.op("dve", lambda e: e.max(out=m8[:, 0:8], in_=imp2[:]), reads=[imp2.d()], writes=[m8.d()])
                    S.op("dve", lambda e: e.match_replace(out=imp3[:], in_to_replace=m8[:, 0:8], in_values=imp2[:],
                                                          imm_value=-2.0),
                         reads=[imp2.d(), m8.d()], writes=[imp3.d()])
                    S.op("dve", lambda e: e.max(out=m8[:, 8:16], in_=imp3[:]), reads=[imp3.d()], writes=[m8.d()])
                    S.op("dve", lambda e: e.tensor_scalar(out=mk[:], in0=imp2[:], scalar1=m8[:, 15:16], scalar2=None,
                                                          op0=ALU.is_ge), reads=[imp2.d(), m8.d()], writes=[mk.d()])
                    S.op("dve", lambda e: e.tensor_scalar(out=negq[:, 0:63], in0=mk[:, 1:64], scalar1=-1.0,
                                                          scalar2=-NEG, op0=ALU.add, op1=ALU.mult),
                         reads=[mk.d()], writes=[negq.d()])
                    S.op("pe", lambda e: e.transpose(out=btr[0:64, 0:128], in_=negq[:], identity=identb[:]),
                         reads=[negq.d()], writes=[btr.d()])
                    for r in range(4):
                        S.op("act", lambda e, it=it, r=r: e.activation(out=qA[r][64:127, it * 128:(it + 1) * 128],
                                                                       in_=btr[0:63, 0:128], func=AF.Copy),
                             reads=[btr.d()], writes=[qA[r].d()])
                for r in range(4):
                    h = 4 * g + r
                    gb = gbc[r]
                    tiles = []
                    for kt in range(4 * c + 4):
                        masks = []
                        if kt >= 4 * c:
                            masks.append((identb[:], causal[:, kt - 4 * c, :], []))
                        tiles.append((ksA[:, kt * 128:(kt + 1) * 128], bias_key[:, h, kt:kt + 1], masks,
                                      vsA[:, kt, :], [ksA.d(), qA[r].d(), vsA.d()]))
                    push_branch(tiles, qA[r][:, cs_],
                                lambda pvb, pvd, ets, r=r, gb=gb: combine(pvb, pvd, gb, 1, cacc[r], False))
                    tiles = []
                    for kt in range(max(0, 4 * c - 4), 4 * c + 4):
                        tiles.append((kwA[:, kt * 128:(kt + 1) * 128], bias_key[:, h, kt:kt + 1],
                                      [(identb[:], winmask[:, kt - 4 * c + 4, :], [])], vwA[:, kt, :],
                                      [kwA.d(), qA[r].d(), vwA.d()]))

                    def win_done(pvb, pvd, ets, r=r, h=h, gb=gb, cs_=cs_):
                        combine(pvb, pvd, gb, 2, cacc[r], False)
                        ob = outb[r % 2]
                        S.op("act", lambda e: e.activation(out=ob[:], in_=cacc[r][:], func=AF.Copy),
                             reads=[cacc[r].d()], writes=[ob.d()])
                        S.dma("pool", attnT[h * 64:(h + 1) * 64, cs_], ob[:], reads=[ob.d()])

                    push_branch(tiles, qA[r][:, cs_], win_done)
            flush()
        S.barrier()

    outproj_phase(S, h_in, h_out, attnT, P["w_out"], P["gpost_b"], tag + "3")


_CACHE = {}


def build_program(phases=("f", "n", "f", "f", "r", "f")):
    nc = bass.Bass("TRN2", target_bir_lowering=False)
    A = {}

    def din(name, shape, dt=F32):
        A[name] = nc.dram_tensor(name, list(shape), dt, kind="ExternalInput").ap()
        return A[name]

    def dscr(name, shape, dt=F32):
        return nc.dram_tensor(name, list(shape), dt, kind="Internal").ap()

    din("x", [S_LEN, D])
    for li in range(2):
        for f in (1, 2):
            din(f"f{f}_{li}_wgu", [D, 2 * DFF]); din(f"f{f}_{li}_wdn", [DFF, D])
            din(f"f{f}_{li}_gpre", [128, 8]); din(f"f{f}_{li}_gpost", [128, D])
    ncst = nsa_consts()
    rcst = rwkv_consts()
    for k_, v in ncst.items():
        din("nc_" + k_, v.shape, F32 if v.dtype == np.float32 else BF16)
    for k_, v in rcst.items():
        din("rc_" + k_, v.shape, F32 if v.dtype == np.float32 else BF16)
    NP = {"w_in": din("n_w_in", [D, NSA_IN]), "w_out": din("n_w_out", [D, D]),
          "gpre_l": din("n_gpre_l", [128, 8]), "gpost_b": din("n_gpost_b", [128, D]),
          "pekT": din("n_pekT", [64, 32]), "pevT": din("n_pevT", [64, 32]),
          "wck1_l": din("n_wck1_l", [64, 32, 64]), "wcv1_l": din("n_wcv1_l", [64, 32, 64]),
          "w_ck2": din("n_w_ck2", [64, 64]), "w_cv2": din("n_w_cv2", [64, 64])}
    RP = {"w_in": din("r_w_in", [D, 3360]), "w_out": din("r_w_out", [D, D]), "w_w2": din("r_w_w2", [64, D]),
          "w_a2": din("r_w_a2", [64, D]), "w_g2": din("r_w_g2", [160, D]),
          "gpre_l": din("r_gpre_l", [128, 8]), "gpost_b": din("r_gpost_b", [128, D]),
          "mu_l": din("r_mu_l", [128, 6, 8]), "hp": din("r_hp", [64, 7, 16])}
    y = nc.dram_tensor("y", [S_LEN, D], F32, kind="ExternalOutput").ap()
    hs = [A["x"]] + [dscr(f"h{i}", [S_LEN, D]) for i in range(len(phases) - 1)] + [y]
    scr = {"projT": dscr("projT", [2560, S_LEN], BF16), "vtok": dscr("vtokd", [S_LEN, 512], BF16),
           "gT": dscr("gT", [48, S_LEN]), "attnT": dscr("attnT", [D, S_LEN], BF16)}
    NC_ = {k_: A["nc_" + k_] for k_ in ncst}
    RC_ = {k_: A["rc_" + k_] for k_ in rcst}
    es = contextlib.ExitStack()
    with es:
        S = Sched(nc, es)
        ffn_ids = [(1, 0), (2, 0), (1, 1), (2, 1)]
        fi = 0
        for pi, ph in enumerate(phases):
            hin, hout = hs[pi], hs[pi + 1]
            if ph == "f":
                f, li = ffn_ids[fi]
                fi += 1
                ffn_phase(S, hin, hout, A[f"f{f}_{li}_wgu"], A[f"f{f}_{li}_wdn"], A[f"f{f}_{li}_gpre"],
                          A[f"f{f}_{li}_gpost"], A["rc_identb"], tag=f"f{pi}")
            elif ph == "n":
                nsa_phase(S, hin, hout, NP, NC_, scr)
            elif ph == "r":
                rwkv_phase(S, hin, hout, RP, RC_, scr["attnT"])
        S.finish()
    return nc, ncst, rcst


def kernel(**inp):
    f32 = np.float32
    g = {k_: np.asarray(v) for k_, v in inp.items()}
    if "prog" not in _CACHE:
        _CACHE["prog"] = build_program()
    nc, ncst, rcst = _CACHE["prog"]
    shared = {}
    for li in range(2):
        for f in (1, 2):
            shared[f"f{f}_{li}_wgu"] = np.ascontiguousarray(g[f"ffn{f}_w_gu"][li], f32)
            shared[f"f{f}_{li}_wdn"] = np.ascontiguousarray(g[f"ffn{f}_w_down"][li], f32)
            shared[f"f{f}_{li}_gpre"] = _pl(g[f"ffn{f}_norm_pre"][li])
            shared[f"f{f}_{li}_gpost"] = _bcast(g[f"ffn{f}_norm_post"][li])
    for k_, v in ncst.items():
        shared["nc_" + k_] = v
    for k_, v in rcst.items():
        shared["rc_" + k_] = v
    nh = nsa_host_params(g["mix_norm_pre"][0], g["mix_norm_post"][0], g["nsa_pe_k"][0], g["nsa_w_ck1"][0],
                         g["nsa_pe_v"][0], g["nsa_w_cv1"][0])
    for k_, v in nh.items():
        shared["n_" + k_] = v
    shared["n_w_in"] = np.ascontiguousarray(g["nsa_w_in"][0], f32)
    shared["n_w_out"] = np.ascontiguousarray(g["nsa_w_out"][0], f32)
    shared["n_w_ck2"] = np.ascontiguousarray(g["nsa_w_ck2"][0], f32)
    shared["n_w_cv2"] = np.ascontiguousarray(g["nsa_w_cv2"][0], f32)
    rh = rwkv_host_params(g["mix_norm_pre"][1], g["mix_norm_post"][1], g["rwkv_mu"][0], g["rwkv_w0"][0],
                          g["rwkv_a0"][0], g["rwkv_k_k"][0], g["rwkv_k_a"][0], g["rwkv_r_k"][0],
                          g["rwkv_gn_w"][0], g["rwkv_gn_b"][0])
    for k_, v in rh.items():
        shared["r_" + k_] = v
    for nm in ("w_in", "w_out", "w_w2", "w_a2", "w_g2"):
        shared["r_" + nm] = np.ascontiguousarray(g["rwkv_" + nm][0], f32)
    x = np.asarray(g["x"], f32)
    in_maps = [dict(shared, x=np.ascontiguousarray(x[b])) for b in range(NCORES)]
    res = run_bass_kernel_spmd(nc, in_maps, core_ids=list(range(NCORES)))
    return np.stack([np.asarray(r["y"], f32) for r in res.results], axis=0)
```
